# Optimizing a Trainium2 kernel written in Bass

```python
import math
import jax
import jax.numpy as jnp
from jax import lax
import numpy as np

D_MODEL = 1024
BATCH = 16
SEQ = 2048
DEPTH = 2

CTX_LEN = 256
GRID_W = 64
EPS = 1e-6

HY_WIDTH = 256
HY_HEADS = 4
HY_SHORT = 3
HY_EMB = 33
HY_BANDS = (HY_EMB - 1) // 2
HY_HIDDEN = 64
HY_DECAY_TARGET = 1e-2
HY_SHORT_DECAY_PCT = 0.3
HY_LONG_DECAY_PCT = 1.5

SSD_HEADS = 8
SSD_HEAD_DIM = 64
SSD_INNER = SSD_HEADS * SSD_HEAD_DIM
SSD_GROUPS = 2
SSD_STATE = 128
SSD_CONV = 3
SSD_CHUNK = 128
SSD_XBC = SSD_INNER + 2 * SSD_GROUPS * SSD_STATE

FN_WIDTH = 256
FN_GROUPS = 4
FN_GROUP_DIM = FN_WIDTH // FN_GROUPS

D_MIX = HY_WIDTH + SSD_INNER + FN_WIDTH

OFF_HY = 0
OFF_Z = OFF_HY + 3 * HY_WIDTH
OFF_XBC = OFF_Z + SSD_INNER
OFF_DT = OFF_XBC + SSD_XBC
OFF_FN = OFF_DT + 2 * SSD_HEADS
D_IN_PROJ = OFF_FN + FN_WIDTH

PEER_HEADS = 8
PEER_TOPK = 16
N_KEYS = 128
N_EXPERTS = N_KEYS * N_KEYS
PEER_QDIM = 256
PEER_HALF = PEER_QDIM // 2
PEER_BLOCK = 128

kernel_name = 'hybrid_hyena_ssd_fnet_peer_dit'


def rmsnorm(x, g):
    xf = x.astype(jnp.float32)
    y = xf * lax.rsqrt(jnp.mean(xf * xf, axis=-1, keepdims=True) + EPS)
    return y.astype(x.dtype) * g


def modulate(h, shift, scale):
    return h * (1 + scale) + shift


def dwconv(u, w, b):
    k_w, ch = w.shape
    y = lax.conv_general_dilated(u, w.astype(u.dtype)[:, None, :], window_strides=(1,),
                                 padding=[(k_w // 2, k_w // 2)],
                                 dimension_numbers=('NWC', 'WIO', 'NWC'), feature_group_count=ch)
    return y + b


def hyena_filter(seq_len, w1, b1, w2, b2, w3, freq):
    f32 = jnp.float32
    t = jnp.linspace(0.0, 1.0, seq_len, dtype=f32)[:, None]
    w = 2.0 * math.pi * jnp.arange(seq_len, dtype=f32)[:, None] / seq_len
    f = jnp.linspace(1e-4, HY_BANDS - 1, HY_BANDS, dtype=f32)[None, :]
    z = jnp.concatenate([t, jnp.cos(f * w), -jnp.sin(f * w)], axis=-1)
    fr = freq.astype(f32)
    h = jnp.sin(fr * (z @ w1.astype(f32) + b1.astype(f32)))
    h = jnp.sin(fr * (h @ w2.astype(f32) + b2.astype(f32)))
    h = h @ w3.astype(f32)
    max_decay = math.log(HY_DECAY_TARGET) / HY_SHORT_DECAY_PCT
    min_decay = math.log(HY_DECAY_TARGET) / HY_LONG_DECAY_PCT
    deltas = jnp.abs(jnp.linspace(min_decay, max_decay, HY_WIDTH, dtype=f32))
    window = jnp.exp(-t * deltas)
    h_fwd = h[:, :HY_WIDTH] * window
    h_bwd = h[:, HY_WIDTH:] * window
    k = jnp.concatenate([h_fwd, jnp.zeros((1, HY_WIDTH), f32), jnp.flip(h_bwd[1:], axis=0)], axis=0)
    return k * lax.rsqrt(jnp.sum(k * k, axis=0, keepdims=True) + EPS)


def long_conv(u, k, bias):
    seq_len = u.shape[1]
    uf = u.astype(jnp.float32)
    u_hat = jnp.fft.rfft(uf, n=2 * seq_len, axis=1)
    k_hat = jnp.fft.rfft(k, n=2 * seq_len, axis=0)
    y = jnp.fft.irfft(u_hat * k_hat[None], n=2 * seq_len, axis=1)[:, :seq_len]
    return (y + uf * bias.astype(jnp.float32)).astype(u.dtype)


def hyena_mixer(u, conv_w, conv_b, w1, b1, w2, b2, w3, freq, bias):
    u = dwconv(u, conv_w, conv_b)
    v, x1, x2 = jnp.split(u, 3, axis=-1)
    k = hyena_filter(u.shape[1], w1, b1, w2, b2, w3, freq)
    return x1 * long_conv(x2 * v, k, bias)


def fourier_mixer(u):
    b, seq_len, _ = u.shape
    uf = u.astype(jnp.float32).reshape(b, seq_len, FN_GROUPS, FN_GROUP_DIM)
    y = jnp.fft.fft2(uf, axes=(1, 3), norm='ortho').real
    return y.reshape(b, seq_len, FN_WIDTH).astype(u.dtype)


def ssd_scan(x, dt, a, bm, cm, h0, with_output):
    f32 = jnp.float32
    b, seq_len, n_heads, hd = x.shape
    n_groups, n_state = bm.shape[2], bm.shape[3]
    hg = n_heads // n_groups
    q = SSD_CHUNK
    nc = seq_len // q
    xdt = (x.astype(f32) * dt[..., None]).reshape(b, nc, q, n_groups, hg, hd)
    a_cum = jnp.cumsum((dt * a).reshape(b, nc, q, n_groups, hg), axis=2)
    bc = bm.astype(f32).reshape(b, nc, q, n_groups, n_state)
    decay_end = jnp.exp(a_cum[:, :, -1:] - a_cum)
    states = jnp.einsum('bcjgn,bcjgkp->bcgkpn', bc, decay_end[..., None] * xdt)
    chunk_decay = jnp.exp(a_cum[:, :, -1])

    def step(h, inp):
        s, d = inp
        return h * d[..., None, None] + s, h

    h_last, h_in = lax.scan(step, h0, (jnp.moveaxis(states, 1, 0), jnp.moveaxis(chunk_decay, 1, 0)))
    if not with_output:
        return h_last
    h_in = jnp.moveaxis(h_in, 0, 1)
    cc = cm.astype(f32).reshape(b, nc, q, n_groups, n_state)
    lower = jnp.tril(jnp.ones((q, q), dtype=bool))[:, :, None, None]
    seg = a_cum[:, :, :, None] - a_cum[:, :, None, :]
    lmat = jnp.where(lower, jnp.exp(jnp.where(lower, seg, 0.0)), 0.0)
    cb = jnp.einsum('bcign,bcjgn->bcijg', cc, bc)
    y_diag = jnp.einsum('bcijgk,bcjgkp->bcigkp', cb[..., None] * lmat, xdt)
    y_off = jnp.einsum('bcign,bcgkpn->bcigkp', cc, h_in) * jnp.exp(a_cum)[..., None]
    return (y_diag + y_off).reshape(b, seq_len, n_heads, hd), h_last


def gated_group_rmsnorm(y, z, g):
    b, seq_len, _ = y.shape
    yf = (y * jax.nn.silu(z.astype(jnp.float32))).reshape(b, seq_len, SSD_GROUPS, SSD_INNER // SSD_GROUPS)
    yf = yf * lax.rsqrt(jnp.mean(yf * yf, axis=-1, keepdims=True) + EPS)
    return yf.reshape(b, seq_len, SSD_INNER).astype(z.dtype) * g


def ssd_mixer(z, xbc, dt_raw, conv_w, conv_b, dt_bias, a_log, d_skip, norm_g, h0_fwd, h0_bwd, with_output):
    f32 = jnp.float32
    b, seq_len, _ = xbc.shape
    gn = SSD_GROUPS * SSD_STATE
    xbc = jax.nn.silu(dwconv(xbc, conv_w, conv_b))
    x = xbc[..., :SSD_INNER].reshape(b, seq_len, SSD_HEADS, SSD_HEAD_DIM)
    bm = xbc[..., SSD_INNER:SSD_INNER + gn].reshape(b, seq_len, SSD_GROUPS, SSD_STATE)
    cm = xbc[..., SSD_INNER + gn:].reshape(b, seq_len, SSD_GROUPS, SSD_STATE)
    dt = jax.nn.softplus(dt_raw.astype(f32).reshape(b, seq_len, 2, SSD_HEADS) + dt_bias.astype(f32))
    a = -jnp.exp(a_log.astype(f32))
    flip = lambda arr: jnp.flip(arr, axis=1)
    out_f = ssd_scan(x, dt[:, :, 0], a[0], bm, cm, h0_fwd, with_output)
    out_b = ssd_scan(flip(x), flip(dt[:, :, 1]), a[1], flip(bm), flip(cm), h0_bwd, with_output)
    if not with_output:
        return None, out_f, out_b
    y_f, h_f = out_f
    y_b, h_b = out_b
    y = y_f + flip(y_b) + d_skip.astype(f32)[:, None] * x.astype(f32)
    y = gated_group_rmsnorm(y.reshape(b, seq_len, SSD_INNER), z, norm_g)
    return y, h_f, h_b


def peer_ffn(h, wq, k1, k2, u_tab, v_tab):
    f32 = jnp.float32
    b, seq_len, d = h.shape
    n_tok = b * seq_len
    hf = h.reshape(n_tok, d)
    q = (hf @ wq).reshape(n_tok, PEER_HEADS, 2, PEER_HALF).astype(f32)
    s1 = jnp.einsum('thd,nd->thn', q[:, :, 0], k1.astype(f32))
    s2 = jnp.einsum('thd,nd->thn', q[:, :, 1], k2.astype(f32))
    v1, i1 = lax.top_k(s1, PEER_TOPK)
    v2, i2 = lax.top_k(s2, PEER_TOPK)
    cand = (v1[..., :, None] + v2[..., None, :]).reshape(n_tok, PEER_HEADS, PEER_TOPK * PEER_TOPK)
    sc, ci = lax.top_k(cand, PEER_TOPK)
    e1 = jnp.take_along_axis(i1, ci // PEER_TOPK, axis=-1)
    e2 = jnp.take_along_axis(i2, ci % PEER_TOPK, axis=-1)
    idx = (e1 * N_KEYS + e2).reshape(n_tok, PEER_HEADS * PEER_TOPK)
    gates = jax.nn.softmax(sc, axis=-1).reshape(n_tok, PEER_HEADS * PEER_TOPK)
    nb = n_tok // PEER_BLOCK

    def block(args):
        hb, ib, gb = args
        us = jnp.take(u_tab, ib, axis=0)
        act = jax.nn.gelu(jnp.einsum('td,tkd->tk', hb, us).astype(f32)) * gb
        vs = jnp.take(v_tab, ib, axis=0)
        return jnp.einsum('tk,tkd->td', act.astype(hb.dtype), vs)

    out = lax.map(block, (hf.reshape(nb, PEER_BLOCK, d), idx.reshape(nb, PEER_BLOCK, -1),
                          gates.reshape(nb, PEER_BLOCK, -1)))
    return out.reshape(b, seq_len, d).astype(h.dtype)


def trunk_layer(xl, xc, c, c_ctx, p, ctx_out):
    f32 = jnp.float32
    mod_l = (jax.nn.silu(c) @ p['w_ada'] + p['b_ada'])[:, None, :]
    mod_c = jax.nn.silu(c_ctx) @ p['w_ada'] + p['b_ada']
    sh1, sc1, ga1, sh2, sc2, ga2 = jnp.split(mod_l, 6, axis=-1)
    csh1, csc1, cga1, csh2, csc2, cga2 = jnp.split(mod_c, 6, axis=-1)
    pl = modulate(rmsnorm(xl, p['g_norm1']), sh1, sc1) @ p['w_in']
    pc = modulate(rmsnorm(xc, p['g_norm1']), csh1, csc1) @ p['w_in']
    ssd_w = (p['ssd_conv_w'], p['ssd_conv_b'], p['ssd_dt_bias'], p['ssd_a_log'], p['ssd_d'], p['ssd_norm_g'])

    def ssd_cols(proj):
        return proj[..., OFF_Z:OFF_XBC], proj[..., OFF_XBC:OFF_DT], proj[..., OFF_DT:OFF_FN]

    h0 = jnp.zeros((xl.shape[0], SSD_GROUPS, SSD_HEADS // SSD_GROUPS, SSD_HEAD_DIM, SSD_STATE), f32)
    yc_ssd, hc_fwd, hc_bwd = ssd_mixer(*ssd_cols(pc), *ssd_w, h0, h0, ctx_out)
    yl_ssd, _, _ = ssd_mixer(*ssd_cols(pl), *ssd_w, hc_fwd, hc_bwd, True)

    def token_mix(proj, y_ssd):
        y_hy = hyena_mixer(proj[..., OFF_HY:OFF_Z], p['hy_conv_w'], p['hy_conv_b'], p['hf_w1'], p['hf_b1'],
                           p['hf_w2'], p['hf_b2'], p['hf_w3'], p['hf_freq'], p['hy_bias'])
        y_fn = fourier_mixer(proj[..., OFF_FN:D_IN_PROJ])
        return jnp.concatenate([y_hy, y_ssd.astype(y_hy.dtype), y_fn], axis=-1) @ p['w_out']

    def channel_mix(h_in, shift, scale):
        return peer_ffn(modulate(rmsnorm(h_in, p['g_norm2']), shift, scale),
                        p['peer_wq'], p['peer_k1'], p['peer_k2'], p['peer_u'], p['peer_v'])

    xl = xl + ga1 * token_mix(pl, yl_ssd)
    xl = xl + ga2 * channel_mix(xl, sh2, sc2)
    if ctx_out:
        xc = xc + cga1 * token_mix(pc, yc_ssd)
        xc = xc + cga2 * channel_mix(xc, csh2, csc2)
    return xl, xc


def setup_inputs(seed: int = 0) -> dict:
    key = jax.random.key(seed)
    ks = list(jax.random.split(key, 40))
    f32 = jnp.float32

    def nrm(shape, scale):
        return jax.random.normal(ks.pop(), shape, f32) * scale

    nl = DEPTH
    x = nrm((BATCH, SEQ, D_MODEL), 1.0)
    c = nrm((BATCH, D_MODEL), 1.0)
    ctx = nrm((BATCH, CTX_LEN, D_MODEL), 1.0)
    c_ctx = nrm((D_MODEL,), 1.0)
    w_ada = nrm((nl, D_MODEL, 6 * D_MODEL), D_MODEL ** -0.5)
    b_ada = nrm((nl, 6 * D_MODEL), 0.02)
    g_norm1 = 1.0 + nrm((nl, D_MODEL), 0.02)
    g_norm2 = 1.0 + nrm((nl, D_MODEL), 0.02)
    w_in = nrm((nl, D_MODEL, D_IN_PROJ), D_MODEL ** -0.5)
    hy_conv_w = nrm((nl, HY_SHORT, 3 * HY_WIDTH), HY_SHORT ** -0.5)
    hy_conv_b = nrm((nl, 3 * HY_WIDTH), 0.02)
    hf_w1 = nrm((nl, HY_EMB, HY_HIDDEN), HY_EMB ** -0.5)
    hf_b1 = nrm((nl, HY_HIDDEN), 0.1)
    hf_w2 = nrm((nl, HY_HIDDEN, HY_HIDDEN), HY_HIDDEN ** -0.5)
    hf_b2 = nrm((nl, HY_HIDDEN), 0.1)
    hf_w3 = nrm((nl, HY_HIDDEN, 2 * HY_WIDTH), HY_HIDDEN ** -0.5)
    hf_freq = 1.0 + nrm((nl, HY_HIDDEN), 0.1)
    hy_bias = nrm((nl, HY_WIDTH), 0.5)
    ssd_conv_w = nrm((nl, SSD_CONV, SSD_XBC), SSD_CONV ** -0.5)
    ssd_conv_b = nrm((nl, SSD_XBC), 0.02)
    dt0 = jnp.exp(jax.random.uniform(ks.pop(), (nl, 2, SSD_HEADS), f32, math.log(1e-3), math.log(1e-1)))
    ssd_dt_bias = dt0 + jnp.log(-jnp.expm1(-dt0))
    ssd_a_log = jnp.log(jax.random.uniform(ks.pop(), (nl, 2, SSD_HEADS), f32, 1.0, 16.0))
    ssd_d = 1.0 + nrm((nl, SSD_HEADS), 0.1)
    ssd_norm_g = 1.0 + nrm((nl, SSD_INNER), 0.02)
    w_out = nrm((nl, D_MIX, D_MODEL), D_MIX ** -0.5)
    peer_wq = nrm((nl, D_MODEL, PEER_HEADS * PEER_QDIM), D_MODEL ** -0.5)
    peer_k1 = nrm((nl, N_KEYS, PEER_HALF), PEER_HALF ** -0.5)
    peer_k2 = nrm((nl, N_KEYS, PEER_HALF), PEER_HALF ** -0.5)
    peer_u = nrm((nl, N_EXPERTS, D_MODEL), D_MODEL ** -0.5)
    peer_v = nrm((nl, N_EXPERTS, D_MODEL), PEER_HEADS ** -0.5)
    g_final = 1.0 + nrm((D_MODEL,), 0.02)
    return {'x': x, 'c': c, 'ctx': ctx, 'c_ctx': c_ctx, 'w_ada': w_ada, 'b_ada': b_ada,
            'g_norm1': g_norm1, 'g_norm2': g_norm2, 'w_in': w_in, 'hy_conv_w': hy_conv_w,
            'hy_conv_b': hy_conv_b, 'hf_w1': hf_w1, 'hf_b1': hf_b1, 'hf_w2': hf_w2, 'hf_b2': hf_b2,
            'hf_w3': hf_w3, 'hf_freq': hf_freq, 'hy_bias': hy_bias, 'ssd_conv_w': ssd_conv_w,
            'ssd_conv_b': ssd_conv_b, 'ssd_dt_bias': ssd_dt_bias, 'ssd_a_log': ssd_a_log, 'ssd_d': ssd_d,
            'ssd_norm_g': ssd_norm_g, 'w_out': w_out, 'peer_wq': peer_wq, 'peer_k1': peer_k1,
            'peer_k2': peer_k2, 'peer_u': peer_u, 'peer_v': peer_v, 'g_final': g_final}


def reference(x, c, ctx, c_ctx, w_ada, b_ada, g_norm1, g_norm2, w_in, hy_conv_w, hy_conv_b, hf_w1, hf_b1,
              hf_w2, hf_b2, hf_w3, hf_freq, hy_bias, ssd_conv_w, ssd_conv_b, ssd_dt_bias, ssd_a_log, ssd_d,
              ssd_norm_g, w_out, peer_wq, peer_k1, peer_k2, peer_u, peer_v, g_final):
    xl, xc = x, ctx
    for i in range(DEPTH):
        p = {'w_ada': w_ada[i], 'b_ada': b_ada[i], 'g_norm1': g_norm1[i], 'g_norm2': g_norm2[i],
             'w_in': w_in[i], 'hy_conv_w': hy_conv_w[i], 'hy_conv_b': hy_conv_b[i], 'hf_w1': hf_w1[i],
             'hf_b1': hf_b1[i], 'hf_w2': hf_w2[i], 'hf_b2': hf_b2[i], 'hf_w3': hf_w3[i], 'hf_freq': hf_freq[i],
             'hy_bias': hy_bias[i], 'ssd_conv_w': ssd_conv_w[i], 'ssd_conv_b': ssd_conv_b[i],
             'ssd_dt_bias': ssd_dt_bias[i], 'ssd_a_log': ssd_a_log[i], 'ssd_d': ssd_d[i],
             'ssd_norm_g': ssd_norm_g[i], 'w_out': w_out[i], 'peer_wq': peer_wq[i], 'peer_k1': peer_k1[i],
             'peer_k2': peer_k2[i], 'peer_u': peer_u[i], 'peer_v': peer_v[i]}
        xl, xc = trunk_layer(xl, xc, c, c_ctx, p, i < DEPTH - 1)
    return rmsnorm(xl, g_final)
```

```python
import math
from contextlib import ExitStack
import numpy as np
import ml_dtypes
import concourse.bass as bass
import concourse.mybir as mybir
from concourse.bass_utils import run_bass_kernel_spmd

F32 = mybir.dt.float32
BF16 = mybir.dt.bfloat16
AF = mybir.ActivationFunctionType
ALU = mybir.AluOpType
AX = mybir.AxisListType

NDS = 20
D = 1024
L = 2048
LC = 256
NLAYER = 2
EPS = 1e-6
NCOL = 2576
OFF_Z, OFF_XBC, OFF_DT, OFF_FN = 768, 1280, 2304, 2320
PI = math.pi
import os
SSD_MODE = os.environ.get('SSD_MODE', 'full')


class FW:
    def __init__(self, nc):
        self.nc = nc
        self.eng = dict(pe=nc.tensor, dve=nc.vector, act=nc.scalar, pool=nc.gpsimd, sp=nc.sync)
        self.esem = {k: nc.alloc_semaphore("es_" + k) for k in self.eng}
        self.ecnt = {k: 0 for k in self.eng}
        self.dsem = [nc.alloc_semaphore("ds_%d" % i) for i in range(NDS)]
        self.dcnt = [0] * NDS
        self.waited = {k: {} for k in self.eng}
        self.buf = {}
        self.dsem_of = {}
        self.rr = 0
        self.ninst = 0

    def _b(self, key):
        b = self.buf.get(key)
        if b is None:
            b = dict(w=None, r={})
            self.buf[key] = b
        return b

    def _wait(self, en, tok):
        if tok is None:
            return
        kind, who, n = tok
        if kind == 'e':
            if who == en and en == 'pe':
                return
            if self.waited[en].get(('e', who), 0) >= n:
                return
            self.eng[en].wait_ge(self.esem[who], n)
            self.waited[en][('e', who)] = n
        else:
            val = self.dcnt[who]
            if self.waited[en].get(('d', who), 0) >= n:
                return
            self.eng[en].wait_ge(self.dsem[who], 16 * val)
            self.waited[en][('d', who)] = val

    def _deps(self, en, reads, writes):
        for k in reads:
            self._wait(en, self._b(k)['w'])
        for k in writes:
            b = self._b(k)
            self._wait(en, b['w'])
            for t in b['r'].values():
                self._wait(en, t)

    def _commit(self, tok, reads, writes):
        for k in writes:
            self.buf[k] = dict(w=tok, r={})
        for k in reads:
            if k in writes:
                continue
            self._b(k)['r'][(tok[0], tok[1])] = tok

    def op(self, en, fn, reads=(), writes=()):
        self._deps(en, reads, writes)
        ins = fn(self.eng[en])
        self.ecnt[en] += 1
        ins.then_inc(self.esem[en], 1)
        tok = ('e', en, self.ecnt[en])
        self._commit(tok, reads, writes)
        self.ninst += 1
        return tok

    def dma(self, en, out, in_, reads=(), writes=(), **kw):
        self._deps(en, reads, writes)
        key = writes[0] if writes else ('anon',)
        idx = self.dsem_of.get(key)
        if idx is None:
            idx = self.rr % NDS
            self.rr += 1
            self.dsem_of[key] = idx
        ins = self.eng[en].dma_start(out=out, in_=in_, **kw)
        self.dcnt[idx] += 1
        ins.then_inc(self.dsem[idx], 16)
        tok = ('d', idx, self.dcnt[idx])
        self._commit(tok, reads, writes)
        self.ninst += 1
        return tok

    def barrier(self):
        for en in self.eng:
            for who in self.eng:
                if who != en and self.ecnt[who] > self.waited[en].get(('e', who), 0):
                    self.eng[en].wait_ge(self.esem[who], self.ecnt[who])
                    self.waited[en][('e', who)] = self.ecnt[who]
            for i in range(NDS):
                if self.dcnt[i] > self.waited[en].get(('d', i), 0):
                    self.eng[en].wait_ge(self.dsem[i], 16 * self.dcnt[i])
                    self.waited[en][('d', i)] = self.dcnt[i]
        self.buf = {}

    def mm(self, out, lhsT, rhs, start=True, stop=True, reads=(), writes=()):
        return self.op('pe', lambda e: e.matmul(out, lhsT, rhs, start=start, stop=stop), reads, writes)

    def tr(self, out, in_, ident, reads=(), writes=()):
        return self.op('pe', lambda e: e.transpose(out, in_, ident), reads, writes)

    def act(self, out, in_, func, bias=None, scale=None, reads=(), writes=()):
        kw = {}
        if bias is not None:
            kw['bias'] = bias
        if scale is not None:
            kw['scale'] = scale
        return self.op('act', lambda e: e.activation(out=out, in_=in_, func=func, **kw), reads, writes)

    def ts(self, en, out, in0, s1, s2, op0, op1=None, reads=(), writes=()):
        kw = {}
        if op1 is not None:
            kw['op1'] = op1
        return self.op(en, lambda e: e.tensor_scalar(out, in0, s1, s2, op0, **kw), reads, writes)

    def tt(self, en, out, in0, in1, op, reads=(), writes=()):
        return self.op(en, lambda e: e.tensor_tensor(out, in0, in1, op), reads, writes)

    def stt(self, en, out, in0, scalar, in1, op0, op1, reads=(), writes=()):
        en = 'dve'
        return self.op(en, lambda e: e.scalar_tensor_tensor(out, in0, scalar, in1, op0, op1), reads, writes)

    def cp(self, en, out, in_, reads=(), writes=()):
        if en == 'act':
            return self.op('act', lambda e: e.copy(out, in_), reads, writes)
        return self.op(en, lambda e: e.tensor_copy(out, in_), reads, writes)

    def memset(self, en, ap, val, writes=()):
        return self.op(en, lambda e: e.memset(ap, val), (), writes)


_CONST = None


def _bf(a):
    return np.ascontiguousarray(a.astype(ml_dtypes.bfloat16))


def _consts():
    global _CONST
    if _CONST is not None:
        return _CONST
    c = {}
    c['ident'] = np.eye(128, dtype=np.float32)
    j = np.arange(128)[:, None]
    i = np.arange(128)[None, :]
    c['uinc'] = (j <= i).astype(np.float32)
    c['linc'] = (j >= i).astype(np.float32)
    c['maskf'] = np.where(i >= j, 0.0, -1e4).astype(np.float32)
    c['maskb'] = np.where(j >= i, 0.0, -1e4).astype(np.float32)
    c['ones'] = np.ones((128, 128), np.float32)
    a = np.arange(64)
    ang = 2 * np.pi * np.outer(a, a) / 64.0
    cb = np.zeros((128, 128)); sb = np.zeros((128, 128))
    for g in range(2):
        cb[g * 64:(g + 1) * 64, g * 64:(g + 1) * 64] = np.cos(ang) / 8.0
        sb[g * 64:(g + 1) * 64, g * 64:(g + 1) * 64] = np.sin(ang) / 8.0
    c['cbd'] = cb.astype(np.float32)
    c['sbd'] = sb.astype(np.float32)
    for tag, Lq in (('l', L), ('c', LC)):
        nsc = Lq // 128
        N = 2 * Lq
        nf = Lq + 1
        nfc = (nf + 127) // 128
        t = np.linspace(0.0, 1.0, Lq, dtype=np.float32)[:, None]
        w = (2.0 * np.pi * np.arange(Lq, dtype=np.float32)[:, None] / Lq).astype(np.float32)
        f = np.linspace(1e-4, 15, 16, dtype=np.float32)[None, :]
        z = np.concatenate([t, np.cos(f * w), -np.sin(f * w)], axis=-1).astype(np.float32)
        c['zT_' + tag] = np.ascontiguousarray(z.T)
        max_decay = math.log(1e-2) / 0.3
        min_decay = math.log(1e-2) / 1.5
        deltas = np.abs(np.linspace(min_decay, max_decay, 256, dtype=np.float32))
        win = np.exp(-t * deltas).astype(np.float32)
        winb = win.copy()
        winb[0] = 0.0
        lay = lambda m: np.ascontiguousarray(m.reshape(nsc, 128, 256).transpose(1, 0, 2))
        c['win_' + tag] = np.stack([lay(win), lay(winb)]).astype(np.float32)
        s = np.arange(Lq, dtype=np.float64)[:, None]
        ff = np.arange(nfc * 128, dtype=np.float64)[None, :]
        th = 2 * np.pi * s * ff / N
        valid = (ff < nf)
        Cf = np.cos(th) * valid
        Sf = -np.sin(th) * valid
        fl = lambda m: np.ascontiguousarray(m.reshape(nsc, 128, nfc, 128).transpose(2, 1, 0, 3))
        c['cf_' + tag] = _bf(fl(Cf))
        c['sf_' + tag] = _bf(fl(Sf))
        TB = min(512, Lq)
        ntb = Lq // TB
        fcol = np.arange(nfc * 128, dtype=np.float64)[:, None]
        tt = np.arange(Lq, dtype=np.float64)[None, :]
        wgt = np.where((fcol == 0) | (fcol == Lq), 1.0, 2.0) * (fcol < nf) / N
        th2 = 2 * np.pi * fcol * tt / N
        Ci = wgt * np.cos(th2)
        Si = -wgt * np.sin(th2)
        il = lambda m: np.ascontiguousarray(m.reshape(nfc, 128, ntb, TB).transpose(2, 1, 0, 3))
        c['ci_' + tag] = _bf(il(Ci))
        c['si_' + tag] = _bf(il(Si))
        t1 = np.arange(Lq, dtype=np.float64)
        th3 = 2 * np.pi * np.outer(t1, t1) / Lq
        CL = np.cos(th3) / math.sqrt(Lq)
        SLn = -np.sin(th3) / math.sqrt(Lq)
        ll = lambda m: np.ascontiguousarray(m.reshape(nsc, 128, ntb, TB).transpose(2, 1, 0, 3))
        c['cl_' + tag] = _bf(ll(CL))
        c['sl_' + tag] = _bf(ll(SLn))
    _CONST = c
    return c


def _pc(v, n):
    return np.ascontiguousarray(np.asarray(v, np.float32).reshape(n, 128).T)


class Prog:
    def __init__(self, layers=(0, 1), phases=None, dbg=False):
        self.layers = layers
        self.phases = phases
        self.dbg = dbg
        nc = bass.Bass("TRN2", target_bir_lowering=False)
        self.nc = nc
        self.fw = FW(nc)
        self.inputs = {}
        self.uid = 0
        self.ps = [nc.alloc_psum_tensor("psb%d" % i, [128, 512], F32) for i in range(8)]

    def inp(self, name, shape, dt=F32):
        t = self.nc.dram_tensor(name, list(shape), dt, kind="ExternalInput").ap()
        self.inputs[name] = t
        return t

    def scratch(self, name, shape, dt=F32, out=False):
        kind = "ExternalOutput" if (out or self.dbg) else "Internal"
        return self.nc.dram_tensor(name, list(shape), dt, kind=kind).ap()

    def sb(self, name, shape, dt):
        self.uid += 1
        return self.nc.sbuf_tensor('%s_u%d' % (name, self.uid), shape, dt)

    def want(self, ph):
        return self.phases is None or ph in self.phases


def _declare(P):
    c = _consts()
    I = {}
    I['xT'] = P.inp('xT', [2, D, L])
    I['ctxT'] = P.inp('ctxT', [2, D, LC])
    I['cT'] = P.inp('cT', [128, 8, 3])
    I['w_ada'] = P.inp('w_ada', [NLAYER, D, 6 * D])
    I['b_adaT'] = P.inp('b_adaT', [NLAYER, 128, 48])
    I['gn1'] = P.inp('gn1', [NLAYER, 128, 8])
    I['gn2'] = P.inp('gn2', [NLAYER, 128, 8])
    I['gfin'] = P.inp('gfin', [128, 8])
    I['w_in'] = P.inp('w_in', [NLAYER, D, NCOL])
    I['hycw'] = P.inp('hycw', [NLAYER, 128, 6, 4])
    I['hfw1'] = P.inp('hfw1', [NLAYER, 33, 64])
    I['hfw2'] = P.inp('hfw2', [NLAYER, 64, 64])
    I['hfw3'] = P.inp('hfw3', [NLAYER, 64, 512])
    I['hfv'] = P.inp('hfv', [NLAYER, 64, 3])
    I['hybias'] = P.inp('hybias', [NLAYER, 128, 2])
    I['sscw'] = P.inp('sscw', [NLAYER, 128, 8, 4])
    I['dtb'] = P.inp('dtb', [NLAYER, 16, 1])
    I['alog'] = P.inp('alog', [NLAYER, 1, 16])
    I['dskip'] = P.inp('dskip', [NLAYER, 1, 8])
    I['sng'] = P.inp('sng', [NLAYER, 128, 4])
    I['w_out'] = P.inp('w_out', [NLAYER, D, D])
    I['wq'] = P.inp('wq', [NLAYER, D, 2048])
    I['k1T'] = P.inp('k1T', [NLAYER, 128, 128])
    I['k2T'] = P.inp('k2T', [NLAYER, 128, 128])
    I['uT'] = P.inp('uT', [NLAYER, D, 16384])
    I['v'] = P.inp('v', [NLAYER, 16384, D])
    for k, a in c.items():
        I[k] = P.inp('c_' + k, a.shape, BF16 if a.dtype == ml_dtypes.bfloat16 else F32)
    return I


class Ctx:
    pass


def build(layers=(0, 1), phases=None, dbg=False, final=True):
    P = Prog(layers, phases, dbg)
    nc, fw = P.nc, P.fw
    I = _declare(P)
    S = Ctx()
    S.res = {}
    for b in range(2):
        S.res[(b, 'l')] = P.scratch('res_l%d' % b, [D, L])
        S.res[(b, 'c')] = P.scratch('res_c%d' % b, [D, LC])
    S.pl = {}
    S.mix = {}
    for b in range(2):
        S.pl[(b, 'l')] = P.scratch('pl_l%d' % b, [NCOL, L])
        S.pl[(b, 'c')] = P.scratch('pl_c%d' % b, [NCOL, LC])
        S.mix[(b, 'l')] = P.scratch('mix_l%d' % b, [D, L], BF16)
        S.mix[(b, 'c')] = P.scratch('mix_c%d' % b, [D, LC], BF16)
    S.khat = {'l': P.scratch('khat_l', [2, 17 * 128, 256]), 'c': P.scratch('khat_c', [2, 3 * 128, 256])}
    S.ubf = P.scratch('ubf', [64, 128, 8, 256], BF16)
    S.vbf = P.scratch('vbf', [64, 128, 2, 1024], BF16)
    S.outT = P.scratch('outT', [2, D, L], F32, out=True)

    A = lambda n, sh, dt=F32: nc.alloc_sbuf_tensor('sb_' + n, sh, dt)
    S.ident = A('ident', [128, 128]); S.ones = A('ones', [128, 128])
    S.uinc = A('uinc', [128, 128]); S.linc = A('linc', [128, 128])
    S.maskf = A('maskf', [128, 128]); S.maskb = A('maskb', [128, 128])
    S.modT = A('modT', [128, 48, 3])
    S.G1 = A('G1', [128, 8, 3]); S.G2 = A('G2', [128, 8, 3])
    S.gfin = A('gfin', [128, 8]); S.zero8 = A('zero8', [128, 8])
    S.hyn = A('hyn', [128, 2, 2])
    for nm in ('ident', 'ones', 'uinc', 'linc', 'maskf', 'maskb'):
        fw.dma('sp', getattr(S, nm)[:], I[nm], writes=[nm])
    fw.dma('sp', S.gfin[:], I['gfin'], writes=['gfin'])
    fw.memset('pool', S.zero8[:], 0.0, writes=['zero8'])
    S.epsc = A('epsc', [128, 1])
    fw.memset('pool', S.epsc[:], EPS, writes=['epsc'])
    S.negpi = A('negpi', [128, 1])
    fw.memset('pool', S.negpi[:], -PI, writes=['negpi'])
    fw.barrier()

    def src(l, seq):
        b, kind = seq
        if l == layers[0] and l == 0:
            return I['xT'][b] if kind == 'l' else I['ctxT'][b]
        return S.res[seq]

    for l in layers:
        ctx_out = l < NLAYER - 1
        if P.want('mod'):
            phase_mod(P, I, S, l)
        if P.want('proj'):
            phase_proj(P, I, S, l, src)
        if P.want('filt'):
            phase_filt(P, I, S, l, 'l')
            if ctx_out:
                phase_filt(P, I, S, l, 'c')
        if P.want('hy'):
            phase_hy(P, I, S, l, 'l')
            if ctx_out:
                phase_hy(P, I, S, l, 'c')
        if P.want('fn'):
            phase_fn(P, I, S, l, 'l')
            if ctx_out:
                phase_fn(P, I, S, l, 'c')
        if P.want('ssd'):
            phase_ssd(P, I, S, l, ctx_out)
        if P.want('out'):
            phase_out(P, I, S, l, src, ctx_out)
        if P.want('peer'):
            phase_peer(P, I, S, l, ctx_out)
    if final and P.want('final'):
        phase_final(P, I, S)
    if dbg:
        dh = P.scratch('dbg_hyn', [128, 4])
        fw.dma('sp', dh, S.hyn[:].rearrange("p a b -> p (a b)"), writes=['dbg_hyn'])
    fw.barrier()
    return P


def phase_mod(P, I, S, l):
    nc, fw = P.nc, P.fw
    with ExitStack() as es:
        cin = es.enter_context(P.sb('m_c', [128, 8, 3], F32))
        sc = es.enter_context(P.sb('m_sc', [128, 8, 3], F32))
        w0 = es.enter_context(P.sb('m_w0', [128, 8, 512], F32))
        w1 = es.enter_context(P.sb('m_w1', [128, 8, 512], F32))
        bada = es.enter_context(P.sb('m_b', [128, 48], F32))
        g1 = es.enter_context(P.sb('m_g1', [128, 8], F32))
        g2 = es.enter_context(P.sb('m_g2', [128, 8], F32))
        tmp = es.enter_context(P.sb('m_t', [128, 8, 3], F32))
        wb = [w0, w1]
        fw.dma('sp', cin[:], I['cT'], writes=['m_c'])
        fw.dma('sp', bada[:], I['b_adaT'][l], writes=['m_b'])
        fw.dma('sp', g1[:], I['gn1'][l], writes=['m_g1'])
        fw.dma('sp', g2[:], I['gn2'][l], writes=['m_g2'])
        fw.act(sc[:], cin[:], AF.Silu, reads=['m_c'], writes=['m_sc'])
        wv = I['w_ada'][l].rearrange("(dc p) n -> p dc n", p=128)
        for blk in range(12):
            w = wb[blk % 2]
            wk = 'm_w%d' % (blk % 2)
            fw.dma('sp', w[:], wv[:, :, blk * 512:(blk + 1) * 512], writes=[wk])
            for j in range(4):
                cc = blk * 4 + j
                pk = ('ps', cc % 2)
                pt = P.ps[cc % 2][:, 0:3]
                for dc in range(8):
                    fw.mm(pt, w[:, dc, j * 128:(j + 1) * 128], sc[:, dc, :], start=(dc == 0), stop=(dc == 7),
                          reads=[wk, 'm_sc'], writes=[pk])
                fw.ts('dve', S.modT[:, cc, :], pt, bada[:, cc:cc + 1], None, ALU.add, reads=[pk, 'm_b'], writes=['modT'])
        for (G, g, gk, c0, nm) in ((S.G1, g1, 'm_g1', 8, 'G1'), (S.G2, g2, 'm_g2', 32, 'G2')):
            fw.ts('dve', tmp[:], S.modT[:, c0:c0 + 8, :], 1.0, None, ALU.add, reads=['modT'], writes=['m_t'])
            fw.tt('dve', G[:], tmp[:], g[:, :].unsqueeze(2).to_broadcast([128, 8, 3]), ALU.mult, reads=['m_t', gk], writes=[nm])
        fw.barrier()


def seqs_of(ctx_too=True):
    out = []
    for b in range(2):
        out.append((b, 'l'))
        if ctx_too:
            out.append((b, 'c'))
    return out


def seq_len(seq):
    return L if seq[1] == 'l' else LC


def seq_col(seq):
    return seq[0] if seq[1] == 'l' else 2


def normmod(P, S, xt, xk, T, Gap, shap, hm, hk, sq, rstd, psk, pst):
    fw = P.fw
    fw.act(sq[:, :, :T], xt[:, :, :T], AF.Square, reads=[xk], writes=[('nm_sq', dc) for dc in range(8)])
    for dc in range(8):
        fw.mm(pst[:, :T], S.ones[:], sq[:, dc, :T], start=(dc == 0), stop=(dc == 7), reads=[('nm_sq', dc), 'ones'], writes=[psk])
    fw.act(rstd[:, :T], pst[:, :T], AF.Sqrt, bias=S.epsc[:, 0:1], scale=1.0 / D, reads=[psk, 'epsc'], writes=['nm_rstd'])
    fw.op('dve', lambda e: e.reciprocal(rstd[:, :T], rstd[:, :T]), reads=['nm_rstd'], writes=['nm_rstd'])
    for dc in range(8):
        en = 'dve' if dc % 2 == 0 else 'pool'
        fw.stt(en, sq[:, dc, :T], xt[:, dc, :T], Gap[:, dc:dc + 1], rstd[:, :T], ALU.mult, ALU.mult,
               reads=[xk, 'nm_rstd'], writes=[('nm_sq', dc)])
        fw.act(hm[:, dc, :T], sq[:, dc, :T], AF.Identity, bias=shap[:, dc:dc + 1], reads=[('nm_sq', dc)], writes=[(hk, dc)])
    return [(hk, dc) for dc in range(8)]


def load_cast_weight(P, dst, dstk, srcv, ncols, stg, stgk):
    fw = P.fw
    nb = (ncols + 511) // 512
    for blk in range(nb):
        c0 = blk * 512
        c1 = min(ncols, c0 + 512)
        st = stg[blk % 2]
        sk = stgk[blk % 2]
        fw.dma('sp', st[:, :, :c1 - c0], srcv[:, :, c0:c1], writes=[sk])
        fw.cp('dve' if blk % 2 == 0 else 'pool', dst[:, :, c0:c1], st[:, :, :c1 - c0], reads=[sk], writes=[(dstk, blk)])
    return [(dstk, blk) for blk in range(nb)]


def phase_proj(P, I, S, l, src):
    nc, fw = P.nc, P.fw
    with ExitStack() as es:
        winb = es.enter_context(P.sb('p_win', [128, 8, NCOL], BF16))
        s0 = es.enter_context(P.sb('p_s0', [128, 8, 512], F32))
        s1 = es.enter_context(P.sb('p_s1', [128, 8, 512], F32))
        sq = es.enter_context(P.sb('p_sq', [128, 8, 512], F32))
        rstd = es.enter_context(P.sb('p_rstd', [128, 512], F32))
        hm = es.enter_context(P.sb('p_hm', [128, 8, 512], BF16))
        ob = es.enter_context(P.sb('p_o', [128, 4, 512], F32))
        wkeys = load_cast_weight(P, winb, 'p_win', I['w_in'][l].rearrange("(dc p) n -> p dc n", p=128), NCOL, [s0, s1], ['p_s0', 'p_s1'])
        xb = [s0, s1]
        chunks = [(c0, min(128, NCOL - c0)) for c0 in range(0, OFF_DT, 128)] + [(OFF_DT, 16)] + [(OFF_FN, 128), (OFF_FN + 128, 128)]
        it = 0
        oi = 0
        for seq in seqs_of(True):
            Lq = seq_len(seq)
            T = min(512, Lq)
            col = seq_col(seq)
            xv = src(l, seq).rearrange("(dc p) t -> p dc t", p=128)
            for tb in range(Lq // T):
                xt = xb[it % 2]
                xk = 'p_s%d' % (it % 2)
                it += 1
                fw.dma('sp', xt[:, :, :T], xv[:, :, tb * T:(tb + 1) * T], reads=[('res', seq)], writes=[xk])
                hk = normmod(P, S, xt, xk, T, S.G1[:, :, col], S.modT[:, 0:8, col], hm, 'p_hm', sq, rstd, ('ps', 0), P.ps[0])
                for ci, (c0, cw) in enumerate(chunks):
                    pb = 1 + ci % 3
                    pt = P.ps[pb][:cw, :T]
                    for dc in range(8):
                        fw.mm(pt, winb[:, dc, c0:c0 + cw], hm[:, dc, :T], start=(dc == 0), stop=(dc == 7),
                              reads=wkeys + hk if dc in (0, 7) else (), writes=[('ps', pb)])
                    o = ob[:cw, oi % 4, :T]
                    ok = ('p_o', oi % 4)
                    oi += 1
                    if ci % 2 == 0:
                        fw.cp('act', o, pt, reads=[('ps', pb)], writes=[ok])
                    else:
                        fw.cp('dve', o, pt, reads=[('ps', pb)], writes=[ok])
                    fw.dma('sp', S.pl[seq][c0:c0 + cw, tb * T:(tb + 1) * T], o, reads=[ok], writes=[('pl', seq, ci, tb)])
        fw.barrier()


def phase_final(P, I, S):
    nc, fw = P.nc, P.fw
    with ExitStack() as es:
        s0 = es.enter_context(P.sb('f_s0', [128, 8, 512], F32))
        s1 = es.enter_context(P.sb('f_s1', [128, 8, 512], F32))
        sq = es.enter_context(P.sb('f_sq', [128, 8, 512], F32))
        rstd = es.enter_context(P.sb('f_rstd', [128, 512], F32))
        o0 = es.enter_context(P.sb('f_o0', [128, 8, 512], F32))
        o1 = es.enter_context(P.sb('f_o1', [128, 8, 512], F32))
        xb = [s0, s1]
        ob = [o0, o1]
        it = 0
        for b in range(2):
            xv = S.res[(b, 'l')].rearrange("(dc p) t -> p dc t", p=128)
            ov = S.outT[b].rearrange("(dc p) t -> p dc t", p=128)
            for tb in range(L // 512):
                xt = xb[it % 2]; xk = 'f_s%d' % (it % 2)
                o = ob[it % 2]; ok = 'f_o%d' % (it % 2)
                it += 1
                fw.dma('sp', xt[:], xv[:, :, tb * 512:(tb + 1) * 512], writes=[xk])
                hk = normmod(P, S, xt, xk, 512, S.gfin, S.zero8, o, ok + 'h', sq, rstd, ('ps', 0), P.ps[0])
                fw.dma('sp', ov[:, :, tb * 512:(tb + 1) * 512], o[:], reads=hk, writes=[('outT', b, tb)])
        fw.barrier()


def prep_shared(inp):
    f = lambda a: np.ascontiguousarray(np.asarray(a, np.float32))
    sh = {}
    sh['w_ada'] = f(inp['w_ada'])
    sh['b_adaT'] = np.stack([_pc(inp['b_ada'][l], 48) for l in range(NLAYER)])
    sh['gn1'] = np.stack([_pc(inp['g_norm1'][l], 8) for l in range(NLAYER)])
    sh['gn2'] = np.stack([_pc(inp['g_norm2'][l], 8) for l in range(NLAYER)])
    sh['gfin'] = _pc(inp['g_final'], 8)
    sh['w_in'] = f(inp['w_in'])
    hy = []
    for l in range(NLAYER):
        m = np.concatenate([np.asarray(inp['hy_conv_w'][l], np.float32), np.asarray(inp['hy_conv_b'][l], np.float32)[None]], 0)
        hy.append(np.ascontiguousarray(m.reshape(4, 6, 128).transpose(2, 1, 0)))
    sh['hycw'] = np.stack(hy)
    sh['hfw1'] = f(inp['hf_w1']); sh['hfw2'] = f(inp['hf_w2']); sh['hfw3'] = f(inp['hf_w3'])
    sh['hfv'] = np.ascontiguousarray(np.stack([inp['hf_b1'], inp['hf_b2'], inp['hf_freq']], axis=-1).astype(np.float32))
    sh['hybias'] = np.stack([_pc(inp['hy_bias'][l], 2) for l in range(NLAYER)])
    ss = []
    for l in range(NLAYER):
        m = np.concatenate([np.asarray(inp['ssd_conv_w'][l], np.float32), np.asarray(inp['ssd_conv_b'][l], np.float32)[None]], 0)
        ss.append(np.ascontiguousarray(m.reshape(4, 8, 128).transpose(2, 1, 0)))
    sh['sscw'] = np.stack(ss)
    sh['dtb'] = f(np.asarray(inp['ssd_dt_bias']).reshape(NLAYER, 16, 1))
    sh['alog'] = f(np.asarray(inp['ssd_a_log']).reshape(NLAYER, 1, 16))
    sh['dskip'] = f(np.asarray(inp['ssd_d']).reshape(NLAYER, 1, 8))
    sh['sng'] = np.stack([_pc(inp['ssd_norm_g'][l], 4) for l in range(NLAYER)])
    sh['w_out'] = f(inp['w_out'])
    sh['wq'] = f(inp['peer_wq'])
    sh['k1T'] = np.ascontiguousarray(np.asarray(inp['peer_k1'], np.float32).transpose(0, 2, 1))
    sh['k2T'] = np.ascontiguousarray(np.asarray(inp['peer_k2'], np.float32).transpose(0, 2, 1))
    sh['uT'] = np.ascontiguousarray(np.asarray(inp['peer_u'], np.float32).transpose(0, 2, 1))
    sh['v'] = f(inp['peer_v'])
    for k, a in _consts().items():
        sh['c_' + k] = a
    return sh


def prep_core(inp, core):
    m = {}
    x = np.asarray(inp['x'], np.float32)[2 * core:2 * core + 2]
    cx = np.asarray(inp['ctx'], np.float32)[2 * core:2 * core + 2]
    m['xT'] = np.ascontiguousarray(x.transpose(0, 2, 1))
    m['ctxT'] = np.ascontiguousarray(cx.transpose(0, 2, 1))
    cv = np.stack([np.asarray(inp['c'], np.float32)[2 * core], np.asarray(inp['c'], np.float32)[2 * core + 1],
                   np.asarray(inp['c_ctx'], np.float32)], axis=-1)
    m['cT'] = np.ascontiguousarray(cv.reshape(8, 128, 3).transpose(1, 0, 2))
    return m


_PROG = None


def kernel(**inputs):
    global _PROG
    if _PROG is None:
        _PROG = build()
    P = _PROG
    sh = prep_shared(inputs)
    in_maps = []
    for core in range(8):
        m = dict(sh)
        m.update(prep_core(inputs, core))
        in_maps.append({k: m[k] for k in P.inputs})
    res = run_bass_kernel_spmd(P.nc, in_maps, core_ids=list(range(8)))
    outs = [np.asarray(r['outT']).transpose(0, 2, 1) for r in res.results]
    return np.ascontiguousarray(np.concatenate(outs, axis=0).astype(np.float32))


def phase_fn(P, I, S, l, tag):
    nc, fw = P.nc, P.fw
    Lq = L if tag == 'l' else LC
    nsc = Lq // 128
    TB = min(512, Lq)
    ntb = Lq // TB
    with ExitStack() as es:
        ut = es.enter_context(P.sb('fn_ut', [128, 4, Lq], F32))
        cbd = es.enter_context(P.sb('fn_cbd', [128, 128], F32))
        sbd = es.enter_context(P.sb('fn_sbd', [128, 128], F32))
        atok = es.enter_context(P.sb('fn_a', [128, nsc, 512], BF16))
        btok = es.enter_context(P.sb('fn_b', [128, nsc, 512], BF16))
        cl = es.enter_context(P.sb('fn_cl', [128, nsc, TB], BF16))
        sl = es.enter_context(P.sb('fn_sl', [128, nsc, TB], BF16))
        ob = es.enter_context(P.sb('fn_o', [128, 2, TB], BF16))
        fw.dma('sp', cbd[:], I['cbd'], writes=['fn_cbd'])
        fw.dma('sp', sbd[:], I['sbd'], writes=['fn_sbd'])
        for b in range(2):
            for ch in range(2):
                fw.dma('sp', ut[:, b * 2 + ch, :], S.pl[(b, tag)][OFF_FN + ch * 128:OFF_FN + (ch + 1) * 128, :], writes=[('fn_ut', b * 2 + ch)])
        utk = [('fn_ut', m) for m in range(4)]
        for tc in range(nsc):
            pa, pb = 2 * (tc % 2), 2 * (tc % 2) + 1
            for m in range(4):
                fw.mm(P.ps[pa][:, m * 128:(m + 1) * 128], ut[:, m, tc * 128:(tc + 1) * 128], cbd[:], reads=utk + ['fn_cbd'], writes=[('ps', pa)])
                fw.mm(P.ps[pb][:, m * 128:(m + 1) * 128], ut[:, m, tc * 128:(tc + 1) * 128], sbd[:], reads=utk + ['fn_sbd'], writes=[('ps', pb)])
            fw.cp('act', atok[:, tc, :], P.ps[pa][:, :], reads=[('ps', pa)], writes=[('fn_a', tc)])
            fw.cp('dve', btok[:, tc, :], P.ps[pb][:, :], reads=[('ps', pb)], writes=[('fn_b', tc)])
        ak = [('fn_a', tc) for tc in range(nsc)]
        bk = [('fn_b', tc) for tc in range(nsc)]
        oi = 0
        for tb in range(ntb):
            fw.dma('sp', cl[:], I['cl_' + tag][tb], writes=['fn_cl'])
            fw.dma('sp', sl[:], I['sl_' + tag][tb], writes=['fn_sl'])
            for m in range(4):
                b, ch = m // 2, m % 2
                pi = 4 + m % 2
                pt = P.ps[pi][:, :TB]
                for tc in range(nsc):
                    fw.mm(pt, atok[:, tc, m * 128:(m + 1) * 128], cl[:, tc, :], start=(tc == 0), stop=False,
                          reads=ak + ['fn_cl'] if tc in (0, nsc - 1) else (), writes=[('ps', pi)])
                for tc in range(nsc):
                    fw.mm(pt, btok[:, tc, m * 128:(m + 1) * 128], sl[:, tc, :], start=False, stop=(tc == nsc - 1),
                          reads=bk + ['fn_sl'] if tc in (0, nsc - 1) else (), writes=[('ps', pi)])
                o = ob[:, oi % 2, :]
                ok = ('fn_o', oi % 2)
                oi += 1
                fw.cp('act' if m % 2 == 0 else 'dve', o, pt, reads=[('ps', pi)], writes=[ok])
                fw.dma('sp', S.mix[(b, tag)][768 + ch * 128:768 + (ch + 1) * 128, tb * TB:(tb + 1) * TB], o, reads=[ok], writes=[('mixfn', b, ch, tb)])
        fw.barrier()


def phase_out(P, I, S, l, src, ctx_out):
    nc, fw = P.nc, P.fw
    with ExitStack() as es:
        wout = es.enter_context(P.sb('o_w', [128, 8, 1024], BF16))
        s0 = es.enter_context(P.sb('o_s0', [128, 8, 512], F32))
        s1 = es.enter_context(P.sb('o_s1', [128, 8, 512], F32))
        m0 = es.enter_context(P.sb('o_m0', [128, 8, 512], BF16))
        m1 = es.enter_context(P.sb('o_m1', [128, 8, 512], BF16))
        x0 = es.enter_context(P.sb('o_x0', [128, 8, 512], F32))
        x1 = es.enter_context(P.sb('o_x1', [128, 8, 512], F32))
        wkeys = load_cast_weight(P, wout, 'o_w', I['w_out'][l].rearrange("(dc p) n -> p dc n", p=128), 1024, [s0, s1], ['o_s0', 'o_s1'])
        xb, mb, ob = [s0, s1], [m0, m1], [x0, x1]
        it = 0
        for seq in seqs_of(ctx_out):
            Lq = seq_len(seq)
            T = min(512, Lq)
            col = seq_col(seq)
            xv = src(l, seq).rearrange("(dc p) t -> p dc t", p=128)
            mv = S.mix[seq].rearrange("(dc p) t -> p dc t", p=128)
            rv = S.res[seq].rearrange("(dc p) t -> p dc t", p=128)
            for tb in range(Lq // T):
                k = it % 2
                it += 1
                xt, mx, xo = xb[k], mb[k], ob[k]
                fw.dma('sp', xt[:, :, :T], xv[:, :, tb * T:(tb + 1) * T], writes=['o_s%d' % k])
                fw.dma('sp', mx[:, :, :T], mv[:, :, tb * T:(tb + 1) * T], writes=['o_m%d' % k])
                for dch in range(8):
                    pi = dch % 4
                    pt = P.ps[pi][:, :T]
                    for cc in range(8):
                        fw.mm(pt, wout[:, cc, dch * 128:(dch + 1) * 128], mx[:, cc, :T], start=(cc == 0), stop=(cc == 7),
                              reads=wkeys + ['o_m%d' % k] if cc in (0, 7) else (), writes=[('ps', pi)])
                    fw.stt('dve', xo[:, dch, :T], pt, S.modT[:, 16 + dch, col:col + 1], xt[:, dch, :T], ALU.mult, ALU.add,
                           reads=[('ps', pi), 'o_s%d' % k, 'modT'], writes=[('o_x%d' % k, dch)])
                fw.dma('sp', rv[:, :, tb * T:(tb + 1) * T], xo[:, :, :T], reads=[('o_x%d' % k, d8) for d8 in range(8)], writes=[('resw', seq, tb)])
        fw.barrier()


def phase_filt(P, I, S, l, tag):
    nc, fw = P.nc, P.fw
    Lq = L if tag == 'l' else LC
    li = 0 if tag == 'l' else 1
    nsc = Lq // 128
    nfc = (Lq + 1 + 127) // 128
    T = min(512, Lq)
    with ExitStack() as es:
        zT = es.enter_context(P.sb('fl_z', [33, Lq], F32))
        w1 = es.enter_context(P.sb('fl_w1', [33, 64], F32))
        w2 = es.enter_context(P.sb('fl_w2', [64, 64], F32))
        w3 = es.enter_context(P.sb('fl_w3', [64, 512], F32))
        hv = es.enter_context(P.sb('fl_hv', [64, 3], F32))
        fb = es.enter_context(P.sb('fl_fb', [64, 2], F32))
        h1 = es.enter_context(P.sb('fl_h1', [64, Lq], F32))
        h2 = es.enter_context(P.sb('fl_h2', [64, Lq], F32))
        win = es.enter_context(P.sb('fl_win', [128, 2, nsc, 256], F32))
        pm = es.enter_context(P.sb('fl_pm', [128, nsc, 256], BF16))
        mmn = es.enter_context(P.sb('fl_mm', [128, nsc, 256], BF16))
        acc = es.enter_context(P.sb('fl_acc', [128, 256], F32))
        t1 = es.enter_context(P.sb('fl_t1', [128, 2, 256], F32))
        t2 = es.enter_context(P.sb('fl_t2', [128, 2, 256], F32))
        tmp = es.enter_context(P.sb('fl_tmp', [64, 512], F32))
        tmpk = es.enter_context(P.sb('fl_tmpk', [64, 512], F32))
        cf = es.enter_context(P.sb('fl_cf', [128, 2, nsc, 128], BF16))
        sf = es.enter_context(P.sb('fl_sf', [128, 2, nsc, 128], BF16))
        ko = es.enter_context(P.sb('fl_ko', [128, 2, 2, 256], F32))
        ntmp = es.enter_context(P.sb('fl_n', [128, 2], F32))
        fw.dma('sp', zT[:], I['zT_' + tag], writes=['fl_z'])
        fw.dma('sp', w1[:], I['hfw1'][l], writes=['fl_w1'])
        fw.dma('sp', w2[:], I['hfw2'][l], writes=['fl_w2'])
        fw.dma('sp', w3[:], I['hfw3'][l], writes=['fl_w3'])
        fw.dma('sp', hv[:], I['hfv'][l], writes=['fl_hv'])
        for v in range(2):
            fw.dma('sp', win[:, v, :, :], I['win_' + tag][v], writes=[('fl_win', v)])
        fw.ts('dve', fb[:], hv[:, 0:2], hv[:, 2:3], None, ALU.mult, reads=['fl_hv'], writes=['fl_fb'])
        fw.memset('pool', acc[:], 0.0, writes=['fl_acc'])

        def sin_layer(dst, dk, w, wk, K, srcT, sk, col):
            for blk in range(Lq // T):
                pi = blk % 2
                pt = P.ps[pi][:64, :T]
                fw.mm(pt, w[:K, :64], srcT[:K, blk * T:(blk + 1) * T], reads=[wk, sk], writes=[('ps', pi)])
                fw.ts('dve', tmp[:, :T], pt, hv[:, 2:3], fb[:, col:col + 1], ALU.mult, ALU.add, reads=[('ps', pi), 'fl_hv', 'fl_fb'], writes=['fl_tmp'])
                MAGIC = 12582912.0
                fw.ts('dve', tmpk[:, :T], tmp[:, :T], 1.0 / (2.0 * PI), MAGIC, ALU.mult, ALU.add, reads=['fl_tmp'], writes=['fl_tmpk'])
                fw.ts('dve', tmpk[:, :T], tmpk[:, :T], MAGIC, None, ALU.subtract, reads=['fl_tmpk'], writes=['fl_tmpk'])
                fw.stt('dve', tmp[:, :T], tmpk[:, :T], -2.0 * PI, tmp[:, :T], ALU.mult, ALU.add, reads=['fl_tmpk', 'fl_tmp'], writes=['fl_tmp'])
                fw.act(dst[:, blk * T:(blk + 1) * T], tmp[:, :T], AF.Sin, reads=['fl_tmp'], writes=[dk])

        sin_layer(h1, 'fl_h1', w1, 'fl_w1', 33, zT, 'fl_z', 0)
        sin_layer(h2, 'fl_h2', w2, 'fl_w2', 64, h1, 'fl_h1', 1)
        for sc in range(nsc):
            pi = 2 + sc % 2
            pt = P.ps[pi]
            fw.mm(pt[:, :], h2[:64, sc * 128:(sc + 1) * 128], w3[:64, :], reads=['fl_h2', 'fl_w3'], writes=[('ps', pi)])
            fw.tt('dve', t1[:], pt[:, :].rearrange("p (v c) -> p v c", v=2), win[:, :, sc, :], ALU.mult,
                  reads=[('ps', pi), ('fl_win', 0), ('fl_win', 1)], writes=['fl_t1'])
            fw.tt('pool', pm[:, sc, :], t1[:, 0, :], t1[:, 1, :], ALU.add, reads=['fl_t1'], writes=[('fl_pm', sc)])
            fw.tt('pool', mmn[:, sc, :], t1[:, 0, :], t1[:, 1, :], ALU.subtract, reads=['fl_t1'], writes=[('fl_mm', sc)])
            fw.act(t2[:], t1[:], AF.Square, reads=['fl_t1'], writes=['fl_t2'])
            fw.tt('pool', acc[:], acc[:], t2[:, 0, :], ALU.add, reads=['fl_t2'], writes=['fl_acc'])
            fw.tt('pool', acc[:], acc[:], t2[:, 1, :], ALU.add, reads=['fl_t2'], writes=['fl_acc'])
        for ch in range(2):
            fw.mm(P.ps[0][:, ch:ch + 1], acc[:, ch * 128:(ch + 1) * 128], S.ones[:, 0:1], reads=['fl_acc', 'ones'], writes=[('ps', 0)])
        fw.act(ntmp[:], P.ps[0][:, 0:2], AF.Sqrt, bias=S.epsc[:, 0:1], reads=[('ps', 0), 'epsc'], writes=['fl_n'])
        fw.op('dve', lambda e: e.reciprocal(S.hyn[:, :, li], ntmp[:]), reads=['fl_n'], writes=[('hyn', li)])
        pmk = [('fl_pm', sc) for sc in range(nsc)]
        mmk = [('fl_mm', sc) for sc in range(nsc)]
        for fc in range(nfc):
            fsz = 128 if fc < nfc - 1 else 1
            k = fc % 2
            fw.dma('sp', cf[:, k, :, :], I['cf_' + tag][fc], writes=[('fl_cf', k)])
            fw.dma('sp', sf[:, k, :, :], I['sf_' + tag][fc], writes=[('fl_sf', k)])
            pa, pb = 4 + 2 * k, 5 + 2 * k
            for sc in range(nsc):
                fw.mm(P.ps[pa][:fsz, :256], cf[:, k, sc, :fsz], pm[:, sc, :], start=(sc == 0), stop=(sc == nsc - 1),
                      reads=pmk + [('fl_cf', k)] if sc in (0, nsc - 1) else (), writes=[('ps', pa)])
            for sc in range(nsc):
                fw.mm(P.ps[pb][:fsz, :256], sf[:, k, sc, :fsz], mmn[:, sc, :], start=(sc == 0), stop=(sc == nsc - 1),
                      reads=mmk + [('fl_sf', k)] if sc in (0, nsc - 1) else (), writes=[('ps', pb)])
            fw.cp('act', ko[:fsz, k, 0, :], P.ps[pa][:fsz, :256], reads=[('ps', pa)], writes=[('fl_ko', k, 0)])
            fw.cp('dve', ko[:fsz, k, 1, :], P.ps[pb][:fsz, :256], reads=[('ps', pb)], writes=[('fl_ko', k, 1)])
            for v in range(2):
                fw.dma('sp', S.khat[tag][v, fc * 128:fc * 128 + fsz, :], ko[:fsz, k, v, :], reads=[('fl_ko', k, v)], writes=[('khat', tag, v, fc)])
        fw.barrier()


def phase_hy(P, I, S, l, tag):
    nc, fw = P.nc, P.fw
    Lq = L if tag == 'l' else LC
    li = 0 if tag == 'l' else 1
    nsc = Lq // 128
    nfc = (Lq + 1 + 127) // 128
    TB = min(512, Lq)
    ntb = Lq // TB
    with ExitStack() as es:
        u = es.enter_context(P.sb('hy_u', [128, 4, Lq], F32))
        x1c = es.enter_context(P.sb('hy_x1', [128, 4, Lq], BF16))
        utok = es.enter_context(P.sb('hy_ut', [128, nsc, 512], BF16))
        wre = es.enter_context(P.sb('hy_wre', [128, nfc, 512], BF16))
        wim = es.enter_context(P.sb('hy_wim', [128, nfc, 512], BF16))
        cw = es.enter_context(P.sb('hy_cw', [128, 6, 4], F32))
        hb = es.enter_context(P.sb('hy_hb', [128, 2], F32))
        fw.dma('sp', cw[:], I['hycw'][l], writes=['hy_cw'])
        fw.dma('sp', hb[:], I['hybias'][l], writes=['hy_hb'])
        with ExitStack() as es2:
            raw = es2.enter_context(P.sb('hy_raw', [128, 3, Lq + 2], F32))
            cv = [es2.enter_context(P.sb('hy_cv%d' % k, [128, Lq], F32)) for k in range(3)]
            fw.memset('pool', raw[:, :, 0:1], 0.0, writes=['hy_raw_h0'])
            fw.memset('pool', raw[:, :, Lq + 1:Lq + 2], 0.0, writes=['hy_raw_h1'])
            for m in range(4):
                b, ch = m // 2, m % 2
                for k in range(3):
                    r0 = k * 256 + ch * 128
                    fw.dma('sp', raw[:, k, 1:Lq + 1], S.pl[(b, tag)][r0:r0 + 128, :], writes=[('hy_raw', k)])
                    ci = 2 * k + ch
                    rk = [('hy_raw', k), 'hy_raw_h0', 'hy_raw_h1', 'hy_cw']
                    fw.ts('dve', cv[k][:], raw[:, k, 0:Lq], cw[:, ci, 0:1], cw[:, ci, 3:4], ALU.mult, ALU.add, reads=rk, writes=[('hy_cv', k)])
                    fw.stt('dve', cv[k][:], raw[:, k, 1:Lq + 1], cw[:, ci, 1:2], cv[k][:], ALU.mult, ALU.add, reads=rk, writes=[('hy_cv', k)])
                    fw.stt('dve', cv[k][:], raw[:, k, 2:Lq + 2], cw[:, ci, 2:3], cv[k][:], ALU.mult, ALU.add, reads=rk, writes=[('hy_cv', k)])
                fw.tt('pool', u[:, m, :], cv[2][:], cv[0][:], ALU.mult, reads=[('hy_cv', 2), ('hy_cv', 0)], writes=[('hy_u', m)])
                fw.cp('act', x1c[:, m, :], cv[1][:], reads=[('hy_cv', 1)], writes=[('hy_x1', m)])
            uk = [('hy_u', m) for m in range(4)]
            for sc in range(nsc):
                pi = sc % 2
                for m in range(4):
                    fw.tr(P.ps[pi][:, m * 128:(m + 1) * 128], u[:, m, sc * 128:(sc + 1) * 128], S.ident[:], reads=uk + ['ident'], writes=[('ps', pi)])
                fw.cp('act' if sc % 2 == 0 else 'dve', utok[:, sc, :], P.ps[pi][:, :], reads=[('ps', pi)], writes=[('hy_ut', sc)])
            fw.barrier()
        with ExitStack() as es2:
            cf = es2.enter_context(P.sb('hy_cf', [128, 2, nsc, 128], BF16))
            sf = es2.enter_context(P.sb('hy_sf', [128, 2, nsc, 128], BF16))
            kk = es2.enter_context(P.sb('hy_kk', [128, 2, 2, 256], F32))
            tq = [es2.enter_context(P.sb('hy_t%d' % q, [128, 2, 256], F32)) for q in range(4)]
            for fc in range(nfc):
                fsz = 128 if fc < nfc - 1 else 1
                k = fc % 2
                fw.dma('sp', cf[:, k, :, :], I['cf_' + tag][fc], writes=[('hy_cf', k)])
                fw.dma('sp', sf[:, k, :, :], I['sf_' + tag][fc], writes=[('hy_sf', k)])
                for v in range(2):
                    fw.dma('sp', kk[:fsz, k, v, :], S.khat[tag][v, fc * 128:fc * 128 + fsz, :], writes=[('hy_kk', k, v)])
                pa, pb = 2 + 2 * k, 3 + 2 * k
                for sc in range(nsc):
                    fw.mm(P.ps[pa][:fsz, :], cf[:, k, sc, :fsz], utok[:, sc, :], start=(sc == 0), stop=(sc == nsc - 1),
                          reads=[('hy_cf', k)] if sc in (0, nsc - 1) else (), writes=[('ps', pa)])
                for sc in range(nsc):
                    fw.mm(P.ps[pb][:fsz, :], sf[:, k, sc, :fsz], utok[:, sc, :], start=(sc == 0), stop=(sc == nsc - 1),
                          reads=[('hy_sf', k)] if sc in (0, nsc - 1) else (), writes=[('ps', pb)])
                ure = P.ps[pa][:fsz, :].rearrange("p (b c) -> p b c", b=2)
                uim = P.ps[pb][:fsz, :].rearrange("p (b c) -> p b c", b=2)
                kre = kk[:fsz, k, 0, :].unsqueeze(1).to_broadcast([fsz, 2, 256])
                kim = kk[:fsz, k, 1, :].unsqueeze(1).to_broadcast([fsz, 2, 256])
                kr = [('hy_kk', k, 0), ('hy_kk', k, 1)]
                fw.tt('dve', tq[0][:fsz], ure, kre, ALU.mult, reads=[('ps', pa)] + kr, writes=[('hy_t', 0)])
                fw.tt('dve', tq[1][:fsz], uim, kim, ALU.mult, reads=[('ps', pb)] + kr, writes=[('hy_t', 1)])
                fw.tt('dve', tq[2][:fsz], ure, kim, ALU.mult, reads=[('ps', pa)] + kr, writes=[('hy_t', 2)])
                fw.tt('dve', tq[3][:fsz], uim, kre, ALU.mult, reads=[('ps', pb)] + kr, writes=[('hy_t', 3)])
                fw.tt('pool', wre[:fsz, fc, :].rearrange("p (b c) -> p b c", b=2), tq[0][:fsz], tq[1][:fsz], ALU.subtract,
                      reads=[('hy_t', 0), ('hy_t', 1)], writes=[('hy_wre', fc)])
                fw.tt('pool', wim[:fsz, fc, :].rearrange("p (b c) -> p b c", b=2), tq[2][:fsz], tq[3][:fsz], ALU.add,
                      reads=[('hy_t', 2), ('hy_t', 3)], writes=[('hy_wim', fc)])
            fw.barrier()
        with ExitStack() as es2:
            ci_t = es2.enter_context(P.sb('hy_ci', [128, nfc, TB], BF16))
            si_t = es2.enter_context(P.sb('hy_si', [128, nfc, TB], BF16))
            at = es2.enter_context(P.sb('hy_at', [128, 2, TB], F32))
            ob = es2.enter_context(P.sb('hy_o', [128, 2, TB], BF16))
            oi = 0
            for tb in range(ntb):
                fw.dma('sp', ci_t[:], I['ci_' + tag][tb], writes=['hy_ci'])
                fw.dma('sp', si_t[:], I['si_' + tag][tb], writes=['hy_si'])
                for m in range(4):
                    b, ch = m // 2, m % 2
                    pi = m % 2
                    pt = P.ps[pi][:, :TB]
                    for fc in range(nfc):
                        fsz = 128 if fc < nfc - 1 else 1
                        fw.mm(pt, wre[:fsz, fc, m * 128:(m + 1) * 128], ci_t[:fsz, fc, :], start=(fc == 0), stop=False,
                              reads=['hy_ci'] if fc in (0, nfc - 1) else (), writes=[('ps', pi)])
                    for fc in range(nfc - 1):
                        fw.mm(pt, wim[:, fc, m * 128:(m + 1) * 128], si_t[:, fc, :], start=False, stop=(fc == nfc - 2),
                              reads=['hy_si'] if fc in (0, nfc - 2) else (), writes=[('ps', pi)])
                    a = at[:, oi % 2, :]
                    o = ob[:, oi % 2, :]
                    ak, ok = ('hy_at', oi % 2), ('hy_o', oi % 2)
                    oi += 1
                    fw.ts('dve', a, pt, S.hyn[:, ch, li:li + 1], None, ALU.mult, reads=[('ps', pi)], writes=[ak])
                    fw.stt('dve', a, u[:, m, tb * TB:(tb + 1) * TB], hb[:, ch:ch + 1], a, ALU.mult, ALU.add, reads=['hy_hb'], writes=[ak])
                    fw.tt('pool', o, a, x1c[:, m, tb * TB:(tb + 1) * TB], ALU.mult, reads=[ak], writes=[ok])
                    fw.dma('sp', S.mix[(b, tag)][ch * 128:(ch + 1) * 128, tb * TB:(tb + 1) * TB], o, reads=[ok], writes=[('mixhy', b, ch, tb)])
            fw.barrier()


def phase_ssd(P, I, S, l, ctx_out):
    nc, fw = P.nc, P.fw
    ps = P.ps
    with ExitStack() as es:
        E = lambda n, sh, dt=F32: es.enter_context(P.sb(n, sh, dt))
        cw = E('sd_cw', [128, 8, 4]); dtb = E('sd_dtb', [16, 1]); arow = E('sd_arow', [128, 16]); dskb = E('sd_dsk', [128, 8])
        sng = E('sd_sng', [128, 4])
        Hs = [E('sd_H%d' % d, [128, 512]) for d in range(2)]
        xtok = E('sd_xtok', [128, 16, 512]); btok = E('sd_btok', [128, 16, 256]); cbt = E('sd_cbt', [128, 16, 2, 128])
        ct = E('sd_ct', [128, 2, L]); dttok = E('sd_dttok', [128, 16, 16]); ytok = E('sd_ytok', [128, 16, 512])
        raw = E('sd_raw', [128, 8, 514]); xa = E('sd_xa', [128, 8, 512]); dtT = E('sd_dtT', [16, L])
        dtA = E('sd_dtA', [128, 16]); ac16 = E('sd_acum', [128, 16]); acum = ac16[:, 0:8]; t8 = E('sd_t8', [128, 8]); dend = E('sd_dend', [128, 8])
        cdec = E('sd_cdec', [128, 8]); eac = E('sd_eac', [128, 8]); xdt = E('sd_xdt', [128, 8, 64]); xdte = E('sd_xdte', [128, 8, 64])
        segt = [E('sd_seg%d' % i, [128, 128]) for i in range(2)]
        dtAb = [E('sd_dtAb%d' % i, [128, 128]) for i in range(2)]
        mtt = [E('sd_mt%d' % i, [128, 128]) for i in range(2)]
        yo = E('sd_yo', [128, 8, 64]); tmpy = E('sd_tmpy', [128, 512]); hsc = E('sd_hsc', [128, 8, 64])
        zt = E('sd_zt', [128, 4, 128]); yg = E('sd_yg', [128, 4, 128]); sqg = E('sd_sqg', [128, 4, 128]); rs = E('sd_rs', [128, 2, 128])
        og = E('sd_og', [128, 4, 128], BF16)
        fw.dma('sp', cw[:], I['sscw'][l], writes=['sd_cw'])
        fw.dma('sp', dtb[:], I['dtb'][l], writes=['sd_dtb'])
        fw.dma('sp', arow[:], I['alog'][l].partition_broadcast(128), writes=['sd_arow'])
        fw.dma('sp', dskb[:], I['dskip'][l].partition_broadcast(128), writes=['sd_dsk'])
        fw.dma('sp', sng[:], I['sng'][l], writes=['sd_sng'])
        fw.act(arow[:], arow[:], AF.Exp, reads=['sd_arow'], writes=['sd_arow'])
        fw.ts('dve', arow[:], arow[:], -1.0, None, ALU.mult, reads=['sd_arow'], writes=['sd_arow'])
        tri = [S.uinc, S.linc]
        msk = [S.maskf, S.maskb]

        def stage_a(seq):
            Lq = seq_len(seq)
            T = min(512, Lq)
            plv = S.pl[seq][OFF_XBC:OFF_XBC + 1024, :].rearrange("(cc p) t -> p cc t", p=128)
            fw.dma('sp', dtT[:, :Lq], S.pl[seq][OFF_DT:OFF_DT + 16, :], writes=['sd_dtT'])
            fw.act(dtT[:, :Lq], dtT[:, :Lq], AF.Exp, bias=dtb[:, 0:1], reads=['sd_dtT', 'sd_dtb'], writes=['sd_dtT'])
            fw.ts('dve', dtT[:, :Lq], dtT[:, :Lq], 1.0, None, ALU.add, reads=['sd_dtT'], writes=['sd_dtT'])
            fw.act(dtT[:, :Lq], dtT[:, :Lq], AF.Ln, reads=['sd_dtT'], writes=['sd_dtT'])
            for tb in range(Lq // T):
                t0 = tb * T
                lo, hi = max(0, t0 - 1), min(Lq, t0 + T + 1)
                d0 = lo - (t0 - 1)
                if tb == 0:
                    fw.memset('pool', raw[:, :, 0:1], 0.0, writes=['sd_raw'])
                if tb == Lq // T - 1:
                    fw.memset('pool', raw[:, :, T + 1:T + 2], 0.0, writes=['sd_raw'])
                fw.dma('sp', raw[:, :, d0:d0 + hi - lo], plv[:, :, lo:hi], writes=['sd_raw'])
                for cc in range(8):
                    k = ('sd_xa', cc)
                    fw.ts('dve', xa[:, cc, :T], raw[:, cc, 0:T], cw[:, cc, 0:1], cw[:, cc, 3:4], ALU.mult, ALU.add, reads=['sd_raw', 'sd_cw'], writes=[k])
                    fw.stt('dve', xa[:, cc, :T], raw[:, cc, 1:T + 1], cw[:, cc, 1:2], xa[:, cc, :T], ALU.mult, ALU.add, reads=['sd_raw'], writes=[k])
                    fw.stt('dve', xa[:, cc, :T], raw[:, cc, 2:T + 2], cw[:, cc, 2:3], xa[:, cc, :T], ALU.mult, ALU.add, reads=['sd_raw'], writes=[k])
                    fw.act(xa[:, cc, :T], xa[:, cc, :T], AF.Silu, reads=[k], writes=[k])
                xk = [('sd_xa', cc) for cc in range(8)]
                fw.cp('pool', ct[:, :, t0:t0 + T], xa[:, 6:8, :T], reads=xk, writes=['sd_ct'])
                for j in range(T // 128):
                    c = tb * (T // 128) + j
                    sl = slice(j * 128, (j + 1) * 128)
                    for k in range(4):
                        fw.tr(ps[1][:, k * 128:(k + 1) * 128], xa[:, k, sl], S.ident[:], reads=xk, writes=[('ps', 1)])
                    fw.cp('act', xtok[:, c, :], ps[1][:, :], reads=[('ps', 1)], writes=[('sd_xtok', c)])
                    for g in range(2):
                        fw.tr(ps[2][:, g * 128:(g + 1) * 128], xa[:, 4 + g, sl], S.ident[:], reads=xk, writes=[('ps', 2)])
                        fw.mm(ps[2][:, 256 + g * 128:256 + (g + 1) * 128], xa[:, 4 + g, sl], xa[:, 6 + g, sl], reads=xk, writes=[('ps', 2)])
                    fw.cp('dve', btok[:, c, :], ps[2][:, 0:256], reads=[('ps', 2)], writes=[('sd_btok', c)])
                    fw.cp('dve', cbt[:, c, :, :], ps[2][:, 256:512].rearrange("p (g i) -> p g i", g=2), reads=[('ps', 2)], writes=[('sd_cbt', c)])
                    fw.tr(ps[3][:, 0:16], dtT[:16, t0 + j * 128:t0 + (j + 1) * 128], S.ident[:16, :16], reads=['sd_dtT'], writes=[('ps', 3)])
                    fw.cp('dve', dttok[:, c, :], ps[3][:, 0:16], reads=[('ps', 3)], writes=[('sd_dttok', c)])

        def scan(seq, d, with_output, final_out):
            Lq = seq_len(seq)
            ncq = Lq // 128
            H = Hs[d]
            hk = 'sd_H%d' % d
            d8 = slice(d * 8, (d + 1) * 8)
            order = range(ncq) if d == 0 else range(ncq - 1, -1, -1)
            for c in order:
                fw.tt('pool', dtA[:], dttok[:, c, :], arow[:], ALU.mult, reads=[('sd_dttok', c), 'sd_arow'], writes=['sd_dtA'])
                fw.mm(ps[0][:, 0:8], tri[d][:], dtA[:, d8], reads=['sd_dtA'], writes=[('ps', 0)])
                fw.mm(ps[0][:, 8:16], S.ones[:], dtA[:, d8], reads=['sd_dtA'], writes=[('ps', 0)])
                fw.cp('dve', ac16[:], ps[0][:, 0:16], reads=[('ps', 0)], writes=['sd_acum'])
                fw.tt('pool', t8[:], ac16[:, 8:16], acum, ALU.subtract, reads=['sd_acum'], writes=['sd_t8'])
                fw.act(dend[:], t8[:], AF.Exp, reads=['sd_t8'], writes=['sd_dend'])
                fw.act(cdec[:], ac16[:, 8:16], AF.Exp, reads=['sd_acum'], writes=['sd_cdec'])
                fw.tt('pool', xdt[:], xtok[:, c, :].rearrange("p (h q) -> p h q", h=8),
                      dttok[:, c, d8].unsqueeze(2).to_broadcast([128, 8, 64]), ALU.mult, reads=[('sd_xtok', c), ('sd_dttok', c)], writes=['sd_xdt'])
                if with_output:
                    fw.act(eac[:], acum,  AF.Exp, reads=['sd_acum'], writes=['sd_eac'])
                    for hd in range(8):
                        g = hd // 4
                        pa = 1 + hd % 2
                        kcol = d * 8 + hd
                        fw.cp('pool', dtAb[hd % 2][:], dtA[:, kcol:kcol + 1].to_broadcast([128, 128]), reads=['sd_dtA'], writes=[('sd_dtAb', hd % 2)])
                        fw.mm(ps[pa][:, 0:128], dtAb[hd % 2][:], tri[d][:], reads=[('sd_dtAb', hd % 2)], writes=[('ps', pa)])
                        sg, mt = segt[hd % 2], mtt[hd % 2]
                        sk, mk = ('sd_seg', hd % 2), ('sd_mt', hd % 2)
                        fw.stt('dve', sg[:], ps[pa][:, 0:128], acum[:, hd:hd + 1], msk[d][:], ALU.subtract, ALU.min,
                               reads=[('ps', pa), 'sd_acum'], writes=[sk])
                        fw.act(sg[:], sg[:], AF.Exp, reads=[sk], writes=[sk])
                        fw.tt('pool', mt[:], sg[:], cbt[:, c, g, :], ALU.mult, reads=[sk, ('sd_cbt', c)], writes=[mk])
                        fw.mm(ps[3][:, hd * 64:(hd + 1) * 64], mt[:], xdt[:, hd, :], reads=[mk, 'sd_xdt'], writes=[('ps', 3)])
                    for g in range(2):
                        fw.mm(ps[4][:, g * 256:(g + 1) * 256], ct[:, g, c * 128:(c + 1) * 128], H[:, g * 256:(g + 1) * 256],
                              reads=['sd_ct', hk], writes=[('ps', 4)])
                    fw.tt('dve', yo[:], ps[4][:, :].rearrange("p (h q) -> p h q", h=8), eac[:, :].unsqueeze(2).to_broadcast([128, 8, 64]),
                          ALU.mult, reads=[('ps', 4), 'sd_eac'], writes=['sd_yo'])
                    yflat = yo[:].rearrange("p h q -> p (h q)")
                    if d == 0:
                        fw.tt('dve', ytok[:, c, :], ps[3][:, :], yflat, ALU.add, reads=[('ps', 3), 'sd_yo'], writes=[('sd_ytok', c)])
                        fw.tt('pool', tmpy[:].rearrange("p (h q) -> p h q", h=8), xtok[:, c, :].rearrange("p (h q) -> p h q", h=8),
                              dskb[:, :].unsqueeze(2).to_broadcast([128, 8, 64]), ALU.mult, reads=[('sd_xtok', c), 'sd_dsk'], writes=['sd_tmpy'])
                        fw.tt('pool', ytok[:, c, :], ytok[:, c, :], tmpy[:], ALU.add, reads=['sd_tmpy'], writes=[('sd_ytok', c)])
                    else:
                        fw.tt('dve', tmpy[:], ps[3][:, :], yflat, ALU.add, reads=[('ps', 3), 'sd_yo'], writes=['sd_tmpy'])
                        fw.tt('pool', ytok[:, c, :], ytok[:, c, :], tmpy[:], ALU.add, reads=['sd_tmpy'], writes=[('sd_ytok', c)])
                fw.tt('pool', xdte[:], xdt[:], dend[:, :].unsqueeze(2).to_broadcast([128, 8, 64]), ALU.mult, reads=['sd_xdt', 'sd_dend'], writes=['sd_xdte'])
                for g in range(2):
                    fw.mm(ps[5][:, g * 256:(g + 1) * 256], btok[:, c, g * 128:(g + 1) * 128],
                          xdte[:, g * 4:(g + 1) * 4, :].rearrange("p h q -> p (h q)"), reads=[('sd_btok', c), 'sd_xdte'], writes=[('ps', 5)])
                fw.tt('pool', hsc[:], H[:].rearrange("p (h q) -> p h q", h=8), cdec[:, :].unsqueeze(2).to_broadcast([128, 8, 64]), ALU.mult,
                      reads=[hk, 'sd_cdec'], writes=['sd_hsc'])
                fw.tt('dve', H[:], hsc[:].rearrange("p h q -> p (h q)"), ps[5][:, :], ALU.add, reads=['sd_hsc', ('ps', 5)], writes=[hk])
                if final_out:
                    tk = slice(c * 128, (c + 1) * 128)
                    for k in range(4):
                        fw.tr(ps[6][:, k * 128:(k + 1) * 128], ytok[:, c, k * 128:(k + 1) * 128], S.ident[:], reads=[('sd_ytok', c)], writes=[('ps', 6)])
                    fw.dma('sp', zt[:], S.pl[seq][OFF_Z:OFF_Z + 512, tk].rearrange("(k p) t -> p k t", p=128), writes=['sd_zt'])
                    fw.act(zt[:], zt[:], AF.Silu, reads=['sd_zt'], writes=['sd_zt'])
                    fw.tt('dve', yg[:], ps[6][:, :].rearrange("p (k t) -> p k t", k=4), zt[:], ALU.mult, reads=[('ps', 6), 'sd_zt'], writes=['sd_yg'])
                    fw.act(sqg[:], yg[:], AF.Square, reads=['sd_yg'], writes=['sd_sqg'])
                    for g in range(2):
                        fw.mm(ps[7][:, g * 128:(g + 1) * 128], S.ones[:], sqg[:, 2 * g, :], start=True, stop=False, reads=['sd_sqg'], writes=[('ps', 7)])
                        fw.mm(ps[7][:, g * 128:(g + 1) * 128], S.ones[:], sqg[:, 2 * g + 1, :], start=False, stop=True, reads=['sd_sqg'], writes=[('ps', 7)])
                    fw.act(rs[:], ps[7][:, 0:256].rearrange("p (g t) -> p g t", g=2), AF.Sqrt, bias=S.epsc[:, 0:1], scale=1.0 / 256.0,
                           reads=[('ps', 7)], writes=['sd_rs'])
                    fw.op('dve', lambda e: e.reciprocal(rs[:], rs[:]), reads=['sd_rs'], writes=['sd_rs'])
                    for k in range(4):
                        fw.stt('dve', og[:, k, :], yg[:, k, :], sng[:, k:k + 1], rs[:, k // 2, :], ALU.mult, ALU.mult,
                               reads=['sd_yg', 'sd_rs', 'sd_sng'], writes=['sd_og'])
                    fw.dma('sp', S.mix[seq][256:768, tk].rearrange("(k p) t -> p k t", p=128), og[:], reads=['sd_og'], writes=[('mixssd', seq, c)])

        for b in range(2):
            for d in range(2):
                fw.memset('pool', Hs[d][:], 0.0, writes=['sd_H%d' % d])
            stage_a((b, 'c'))
            if SSD_MODE != 'a':
                scan((b, 'c'), 0, ctx_out, False)
                scan((b, 'c'), 1, ctx_out, ctx_out)
            stage_a((b, 'l'))
            if SSD_MODE != 'a':
                scan((b, 'l'), 0, True, False)
                scan((b, 'l'), 1, True, True)
        fw.barrier()


def phase_peer(P, I, S, l, ctx_out):
    nc, fw = P.nc, P.fw
    ps = P.ps
    T = 256
    NB = 4
    with ExitStack() as es2:
        cs = [es2.enter_context(P.sb('pe_cs%d' % i, [128, 2048], F32)) for i in range(4)]
        cbs = [es2.enter_context(P.sb('pe_cb%d' % i, [128, 2048], BF16)) for i in range(4)]
        uv = I['uT'][l].rearrange("(a p) e -> p a e", p=128)
        n = 0
        for blk in range(64):
            k = n % 4
            n += 1
            fw.dma('sp', cs[k][:].rearrange("p (a e) -> p a e", a=8), uv[:, :, blk * 256:(blk + 1) * 256], writes=['pe_cs%d' % k])
            fw.cp('dve' if k % 2 == 0 else 'pool', cbs[k][:], cs[k][:], reads=['pe_cs%d' % k], writes=['pe_cb%d' % k])
            fw.dma('act', S.ubf[blk].rearrange("p a e -> p (a e)"), cbs[k][:], reads=['pe_cb%d' % k], writes=[('ubf', blk)])
        for blk in range(64):
            k = n % 4
            n += 1
            fw.dma('sp', cs[k][:].rearrange("p (ii d) -> p ii d", ii=2),
                   I['v'][l][blk * 256:(blk + 1) * 256, :].rearrange("(ii j) d -> j ii d", j=128), writes=['pe_cs%d' % k])
            fw.cp('dve' if k % 2 == 0 else 'pool', cbs[k][:], cs[k][:], reads=['pe_cs%d' % k], writes=['pe_cb%d' % k])
            fw.dma('act', S.vbf[blk].rearrange("p ii d -> p (ii d)"), cbs[k][:], reads=['pe_cb%d' % k], writes=[('vbf', blk)])
        fw.barrier()
    with ExitStack() as es:
        E = lambda n, sh, dt=F32: es.enter_context(P.sb(n, sh, dt))
        wq = E('pe_wq', [128, 8, 2048], BF16)
        st = [E('pe_s%d' % i, [128, 8, 256]) for i in range(2)]
        k1 = E('pe_k1', [128, 128], BF16); k2 = E('pe_k2', [128, 128], BF16)
        gbuf = E('pe_g', [128, 128, T], BF16)
        ub = E('pe_ub', [128, 2, 8, 256], BF16); vb = E('pe_vb', [128, 2, 2, 1024], BF16)
        h2 = E('pe_h2', [128, 8, T], BF16); sq = E('pe_sq', [128, 8, T]); rstd = E('pe_rstd', [128, T])
        qT = E('pe_qT', [128, 16, T], BF16); s12 = E('pe_s12', [128, 16, 128]); v12 = E('pe_v12', [128, 16, 16])
        wk = E('pe_wk', [128, 128]); wk2 = E('pe_wk2', [128, 256]); t16 = E('pe_t16', [128, 8, 16])
        e16 = E('pe_e16', [128, 8, 16]); zz = E('pe_z', [128, 8]); mz = E('pe_mz', [128, 8])
        thr = E('pe_thr', [128, 8, 16]); bia = E('pe_bia', [128, 8, 16]); pc = E('pe_pc', [128, 3, T])
        Et = [E('pe_E%d' % i, [128, 128]) for i in range(8)]
        Wt = [E('pe_W%d' % i, [128, 128], BF16) for i in range(8)]
        Pt = [E('pe_P%d' % i, [128, 128], BF16) for i in range(8)]
        qr1 = E('pe_qr1', [128, 2, 4, 128], BF16)
        qr2 = E('pe_qr2', [128, 2, 4, 128], BF16)
        gst = [E('pe_gs%d' % i, [128, T]) for i in range(2)]
        At = [E('pe_A%d' % i, [128, T], BF16) for i in range(2)]
        xo = E('pe_xo', [128, 8, T])
        cand = sq
        wv = I['wq'][l].rearrange("(dc p) n -> p dc n", p=128)
        n = 0
        for blk in range(8):
            k = n % 2
            n += 1
            fw.dma('sp', st[k][:], wv[:, :, blk * 256:(blk + 1) * 256], writes=['pe_s%d' % k])
            fw.cp('dve' if k == 0 else 'pool', wq[:, :, blk * 256:(blk + 1) * 256], st[k][:], reads=['pe_s%d' % k], writes=[('pe_wq', blk)])
        for (kt, nm, key) in ((k1, 'k1T', 'pe_k1'), (k2, 'k2T', 'pe_k2')):
            k = n % 2
            n += 1
            fw.dma('sp', st[k][:, 0, 0:128], I[nm][l], writes=['pe_s%d' % k])
            fw.cp('dve', kt[:], st[k][:, 0, 0:128], reads=['pe_s%d' % k], writes=[key])
        fw.barrier()

        def load_tables(blk):
            bb = blk % 2
            fw.dma('sp', ub[:, bb, :, :], S.ubf[blk], writes=[('pe_ub', bb)])
            fw.dma('sp', vb[:, bb, :, :], S.vbf[blk], writes=[('pe_vb', bb)])

        gi = 0
        for seq in seqs_of(ctx_out):
            Lq = seq_len(seq)
            col = seq_col(seq)
            rv = S.res[seq].rearrange("(dc p) t -> p dc t", p=128)
            for grp in range(Lq // T):
                g0 = grp * T
                xt = st[gi % 2]
                xk = 'pe_s%d' % (gi % 2)
                gi += 1
                fw.dma('sp', xt[:], rv[:, :, g0:g0 + T], writes=[xk])
                hk = normmod(P, S, xt, xk, T, S.G2[:, :, col], S.modT[:, 24:32, col], h2, 'pe_h2', sq, rstd, ('ps', 4), ps[4])
                for qc in range(16):
                    pi = 4 + qc % 4
                    for dc in range(8):
                        fw.mm(ps[pi][:, :T], wq[:, dc, qc * 128:(qc + 1) * 128], h2[:, dc, :], start=(dc == 0), stop=(dc == 7),
                              reads=hk if dc in (0, 7) else (), writes=[('ps', pi)])
                    fw.cp('act' if qc % 2 == 0 else 'dve', qT[:, qc, :], ps[pi][:, :T], reads=[('ps', pi)], writes=[('pe_qT', qc)])
                qk = [('pe_qT', qc) for qc in range(16)]
                sqk = [('nm_sq', dc) for dc in range(8)]
                for tt in range(2):
                    tsl = slice(tt * 128, (tt + 1) * 128)
                    for h in range(8):
                        fw.mm(ps[4 + h // 4][:, (h % 4) * 128:(h % 4 + 1) * 128], qT[:, 2 * h, tsl], k1[:], reads=qk + ['pe_k1'], writes=[('ps', 4 + h // 4)])
                        fw.mm(ps[6 + h // 4][:, (h % 4) * 128:(h % 4 + 1) * 128], qT[:, 2 * h + 1, tsl], k2[:], reads=qk + ['pe_k2'], writes=[('ps', 6 + h // 4)])
                    for q4 in range(4):
                        fw.cp('dve' if q4 % 2 == 0 else 'act', s12[:, q4 * 4:(q4 + 1) * 4, :], ps[4 + q4][:, :].rearrange("p (h i) -> p h i", h=4),
                              reads=[('ps', 4 + q4)], writes=[('pe_s12', q4)])
                    sk = [('pe_s12', q4) for q4 in range(4)]
                    for idx in range(16):
                        fw.op('dve', lambda e, idx=idx: e.max(out=v12[:, idx, 0:8], in_=s12[:, idx, :]), reads=sk, writes=['pe_v12'])
                        fw.op('dve', lambda e, idx=idx: e.match_replace(out=wk[:], in_to_replace=v12[:, idx, 0:8], in_values=s12[:, idx, :], imm_value=-1e30),
                              reads=['pe_v12'], writes=['pe_wk'])
                        fw.op('dve', lambda e, idx=idx: e.max(out=v12[:, idx, 8:16], in_=wk[:]), reads=['pe_wk'], writes=['pe_v12'])
                    fw.tt('dve', cand[:].rearrange("p h (a b) -> p h a b", a=16), v12[:, 0:8, :].unsqueeze(3).to_broadcast([128, 8, 16, 16]),
                          v12[:, 8:16, :].unsqueeze(2).to_broadcast([128, 8, 16, 16]), ALU.add, reads=['pe_v12'], writes=sqk)
                    for h in range(8):
                        fw.op('dve', lambda e, h=h: e.max(out=t16[:, h, 0:8], in_=cand[:, h, :]), reads=sqk, writes=['pe_t16'])
                        fw.op('dve', lambda e, h=h: e.match_replace(out=wk2[:], in_to_replace=t16[:, h, 0:8], in_values=cand[:, h, :], imm_value=-1e30),
                              reads=['pe_t16'] + sqk, writes=['pe_wk2'])
                        fw.op('dve', lambda e, h=h: e.max(out=t16[:, h, 8:16], in_=wk2[:]), reads=['pe_wk2'], writes=['pe_t16'])
                    fw.tt('dve', e16[:], t16[:], t16[:, :, 0:1].to_broadcast([128, 8, 16]), ALU.subtract, reads=['pe_t16'], writes=['pe_e16'])
                    fw.act(e16[:], e16[:], AF.Exp, reads=['pe_e16'], writes=['pe_e16'])
                    fw.op('dve', lambda e: e.reduce_sum(out=zz[:], in_=e16[:], axis=AX.X), reads=['pe_e16'], writes=['pe_z'])
                    fw.act(zz[:], zz[:], AF.Ln, reads=['pe_z'], writes=['pe_z'])
                    fw.tt('dve', mz[:], t16[:, :, 0], zz[:], ALU.add, reads=['pe_t16', 'pe_z'], writes=['pe_mz'])
                    fw.tt('dve', thr[:], t16[:, :, 15:16].to_broadcast([128, 8, 16]), v12[:, 0:8, :], ALU.subtract, reads=['pe_t16', 'pe_v12'], writes=['pe_thr'])
                    fw.tt('dve', bia[:], v12[:, 0:8, :], mz[:, :].unsqueeze(2).to_broadcast([128, 8, 16]), ALU.subtract, reads=['pe_mz', 'pe_v12'], writes=['pe_bia'])
                    srcs = [(v12[:, 0:8, :], 'pe_v12'), (thr[:], 'pe_thr'), (bia[:], 'pe_bia')]
                    for slot, (sap, skey) in enumerate(srcs):
                        fw.tr(ps[4][:, slot * 128:(slot + 1) * 128], sap.rearrange("p h a -> p (h a)"), S.ident[:], reads=[skey], writes=[('ps', 4)])
                    fw.cp('dve', pc[:, :, tsl], ps[4][:, 0:384].rearrange("p (s t) -> p s t", s=3), reads=[('ps', 4)], writes=['pe_pc'])
                load_tables(0)
                load_tables(1)
                NQ = T // 4

                def quad_F(u):
                    qb = u % 2
                    t0 = 4 * u
                    for half, qr, kk_ in ((0, qr1, k1), (1, qr2, k2)):
                        src_ap = qT[:, half:16:2, t0:t0 + 4].rearrange("p h t -> p t h").unsqueeze(3).to_broadcast([128, 4, 8, 16])
                        fw.cp('pool', qr[:, qb, :, :].rearrange("p t (h a) -> p t h a", h=8), src_ap,
                              reads=qk if u < 2 else (), writes=[('pe_qr%d' % half, qb)])
                    for k in range(4):
                        t = t0 + k
                        xb = (t // 2) % 4
                        c0 = (t % 2) * 256
                        fw.mm(ps[xb][:, c0:c0 + 128], qr1[:, qb, k, :], k1[:], reads=[('pe_qr0', qb)], writes=[('ps', xb)])
                        fw.mm(ps[xb][:, c0 + 128:c0 + 256], qr2[:, qb, k, :], k2[:], reads=[('pe_qr1', qb)], writes=[('ps', xb)])

                def quad_B(u):
                    t0 = 4 * u
                    gb = 4 + u % 2
                    for k in range(4):
                        t = t0 + k
                        xb = (t // 2) % 4
                        c0 = (t % 2) * 256
                        q = t % 8
                        ek, wkk, pk = ('pe_E', q), ('pe_W', q), ('pe_P', q)
                        fw.act(Et[q][:], ps[xb][:, c0 + 128:c0 + 256], AF.Exp, bias=pc[:, 2, t:t + 1], reads=[('ps', xb), 'pe_pc'], writes=[ek])
                        fw.stt('dve', Wt[q][:], ps[xb][:, c0 + 128:c0 + 256], pc[:, 1, t:t + 1], Et[q][:], ALU.is_ge, ALU.mult,
                               reads=[('ps', xb), ek], writes=[wkk])
                        fw.ts('dve', Pt[q][:], ps[xb][:, c0:c0 + 128], pc[:, 0, t:t + 1], None, ALU.is_equal, reads=[('ps', xb), ek], writes=[pk])
                        fw.mm(ps[gb][:, k * 128:(k + 1) * 128], Wt[q][:], Pt[q][:], reads=[wkk, pk], writes=[('ps', gb)])

                def quad_C(u):
                    gb = 4 + u % 2
                    fw.cp('act', gbuf[:, :, 4 * u:4 * u + 4].rearrange("p i t -> p t i"),
                          ps[gb][:, :].rearrange("p (t i) -> p t i", t=4), reads=[('ps', gb)], writes=[('pe_g', u % 2)])

                for u in range(NQ + 2):
                    if u < NQ:
                        quad_F(u)
                    if 1 <= u <= NQ:
                        quad_B(u - 1)
                    if u >= 2:
                        quad_C(u - 2)
                gk = [('pe_g', q) for q in range(2)]

                def dense_S(i):
                    bb, ii = (i // 2) % 2, i % 2
                    pi = 4 + i % 2
                    for dc in range(8):
                        fw.mm(ps[pi][:, :T], ub[:, bb, dc, ii * 128:(ii + 1) * 128], h2[:, dc, :], start=(dc == 0), stop=(dc == 7),
                              reads=[('pe_ub', bb)] if dc in (0, 7) else (), writes=[('ps', pi)])

                def dense_O(i):
                    bb, ii = (i // 2) % 2, i % 2
                    pi = 4 + i % 2
                    fw.act(gst[i % 2][:], ps[pi][:, :T], AF.Gelu_apprx_tanh, reads=[('ps', pi)], writes=[('pe_gs', i % 2)])
                    fw.tt('pool' if i % 2 == 0 else 'dve', At[i % 2][:], gst[i % 2][:], gbuf[:, i, :], ALU.mult,
                          reads=[('pe_gs', i % 2)] + (gk if i < 2 else []), writes=[('pe_A', i % 2)])
                    for dch in range(8):
                        bk, rg = dch // 2, (dch % 2) * 256
                        fw.mm(ps[bk][:, rg:rg + T], vb[:, bb, ii, dch * 128:(dch + 1) * 128], At[i % 2][:],
                              start=(i == 0 and dch % 2 == 0), stop=(i == 127),
                              reads=[('pe_A', i % 2), ('pe_vb', bb)] if dch in (0, 7) else (), writes=[('ps', bk)])

                dense_S(0)
                for i in range(128):
                    if i + 1 < 128:
                        dense_S(i + 1)
                    dense_O(i)
                    if i % 2 == 1 and (i + 1) // 2 + 1 < 64:
                        load_tables((i + 1) // 2 + 1)
                for dch in range(8):
                    bk, rg = dch // 2, (dch % 2) * 256
                    fw.stt('dve', xo[:, dch, :], ps[bk][:, rg:rg + T], S.modT[:, 40 + dch, col:col + 1], xt[:, dch, :], ALU.mult, ALU.add,
                           reads=[('ps', bk), xk], writes=[('pe_xo', dch)])
                fw.dma('sp', rv[:, :, g0:g0 + T], xo[:], reads=[('pe_xo', d8) for d8 in range(8)], writes=[('resp', seq, grp)])
        fw.barrier()
```

```python
import math
from contextlib import ExitStack
import numpy as np
import ml_dtypes
import concourse.bass as bass
import concourse.mybir as mybir
from concourse.bass_utils import run_bass_kernel_spmd

F32 = mybir.dt.float32
BF16 = mybir.dt.bfloat16
AF = mybir.ActivationFunctionType
ALU = mybir.AluOpType
AX = mybir.AxisListType

NDS = 20
D = 1024
L = 2048
LC = 256
NLAYER = 2
EPS = 1e-6
NCOL = 2576
OFF_Z, OFF_XBC, OFF_DT, OFF_FN = 768, 1280, 2304, 2320
PI = math.pi
import os
SSD_MODE = os.environ.get('SSD_MODE', 'full')


class FW:
    def __init__(self, nc):
        self.nc = nc
        self.eng = dict(pe=nc.tensor, dve=nc.vector, act=nc.scalar, pool=nc.gpsimd, sp=nc.sync)
        self.esem = {k: nc.alloc_semaphore("es_" + k) for k in self.eng}
        self.ecnt = {k: 0 for k in self.eng}
        self.dsem = [nc.alloc_semaphore("ds_%d" % i) for i in range(NDS)]
        self.dcnt = [0] * NDS
        self.waited = {k: {} for k in self.eng}
        self.buf = {}
        self.dsem_of = {}
        self.rr = 0
        self.ninst = 0

    def _b(self, key):
        b = self.buf.get(key)
        if b is None:
            b = dict(w=None, r={})
            self.buf[key] = b
        return b

    def _wait(self, en, tok):
        if tok is None:
            return
        kind, who, n = tok
        if kind == 'e':
            if who == en and en == 'pe':
                return
            if self.waited[en].get(('e', who), 0) >= n:
                return
            self.eng[en].wait_ge(self.esem[who], n)
            self.waited[en][('e', who)] = n
        else:
            val = self.dcnt[who]
            if self.waited[en].get(('d', who), 0) >= n:
                return
            self.eng[en].wait_ge(self.dsem[who], 16 * val)
            self.waited[en][('d', who)] = val

    def _deps(self, en, reads, writes):
        for k in reads:
            self._wait(en, self._b(k)['w'])
        for k in writes:
            b = self._b(k)
            self._wait(en, b['w'])
            for t in b['r'].values():
                self._wait(en, t)

    def _commit(self, tok, reads, writes):
        for k in writes:
            self.buf[k] = dict(w=tok, r={})
        for k in reads:
            if k in writes:
                continue
            self._b(k)['r'][(tok[0], tok[1])] = tok

    def op(self, en, fn, reads=(), writes=()):
        self._deps(en, reads, writes)
        ins = fn(self.eng[en])
        self.ecnt[en] += 1
        ins.then_inc(self.esem[en], 1)
        tok = ('e', en, self.ecnt[en])
        self._commit(tok, reads, writes)
        self.ninst += 1
        return tok

    def dma(self, en, out, in_, reads=(), writes=(), **kw):
        self._deps(en, reads, writes)
        key = writes[0] if writes else ('anon',)
        idx = self.dsem_of.get(key)
        if idx is None:
            idx = self.rr % NDS
            self.rr += 1
            self.dsem_of[key] = idx
        ins = self.eng[en].dma_start(out=out, in_=in_, **kw)
        self.dcnt[idx] += 1
        ins.then_inc(self.dsem[idx], 16)
        tok = ('d', idx, self.dcnt[idx])
        self._commit(tok, reads, writes)
        self.ninst += 1
        return tok

    def barrier(self):
        for en in self.eng:
            for who in self.eng:
                if who != en and self.ecnt[who] > self.waited[en].get(('e', who), 0):
                    self.eng[en].wait_ge(self.esem[who], self.ecnt[who])
                    self.waited[en][('e', who)] = self.ecnt[who]
            for i in range(NDS):
                if self.dcnt[i] > self.waited[en].get(('d', i), 0):
                    self.eng[en].wait_ge(self.dsem[i], 16 * self.dcnt[i])
                    self.waited[en][('d', i)] = self.dcnt[i]
        self.buf = {}

    def mm(self, out, lhsT, rhs, start=True, stop=True, reads=(), writes=()):
        return self.op('pe', lambda e: e.matmul(out, lhsT, rhs, start=start, stop=stop), reads, writes)

    def tr(self, out, in_, ident, reads=(), writes=()):
        return self.op('pe', lambda e: e.transpose(out, in_, ident), reads, writes)

    def act(self, out, in_, func, bias=None, scale=None, reads=(), writes=()):
        kw = {}
        if bias is not None:
            kw['bias'] = bias
        if scale is not None:
            kw['scale'] = scale
        return self.op('act', lambda e: e.activation(out=out, in_=in_, func=func, **kw), reads, writes)

    def ts(self, en, out, in0, s1, s2, op0, op1=None, reads=(), writes=()):
        kw = {}
        if op1 is not None:
            kw['op1'] = op1
        return self.op(en, lambda e: e.tensor_scalar(out, in0, s1, s2, op0, **kw), reads, writes)

    def tt(self, en, out, in0, in1, op, reads=(), writes=()):
        return self.op(en, lambda e: e.tensor_tensor(out, in0, in1, op), reads, writes)

    def stt(self, en, out, in0, scalar, in1, op0, op1, reads=(), writes=()):
        en = 'dve'
        return self.op(en, lambda e: e.scalar_tensor_tensor(out, in0, scalar, in1, op0, op1), reads, writes)

    def cp(self, en, out, in_, reads=(), writes=()):
        if en == 'act':
            return self.op('act', lambda e: e.copy(out, in_), reads, writes)
        return self.op(en, lambda e: e.tensor_copy(out, in_), reads, writes)

    def memset(self, en, ap, val, writes=()):
        return self.op(en, lambda e: e.memset(ap, val), (), writes)


_CONST = None


def _bf(a):
    return np.ascontiguousarray(a.astype(ml_dtypes.bfloat16))


def _consts():
    global _CONST
    if _CONST is not None:
        return _CONST
    c = {}
    c['ident'] = np.eye(128, dtype=np.float32)
    j = np.arange(128)[:, None]
    i = np.arange(128)[None, :]
    c['uinc'] = (j <= i).astype(np.float32)
    c['linc'] = (j >= i).astype(np.float32)
    c['maskf'] = np.where(i >= j, 0.0, -1e4).astype(np.float32)
    c['maskb'] = np.where(j >= i, 0.0, -1e4).astype(np.float32)
    c['ones'] = np.ones((128, 128), np.float32)
    a = np.arange(64)
    ang = 2 * np.pi * np.outer(a, a) / 64.0
    cb = np.zeros((128, 128)); sb = np.zeros((128, 128))
    for g in range(2):
        cb[g * 64:(g + 1) * 64, g * 64:(g + 1) * 64] = np.cos(ang) / 8.0
        sb[g * 64:(g + 1) * 64, g * 64:(g + 1) * 64] = np.sin(ang) / 8.0
    c['cbd'] = cb.astype(np.float32)
    c['sbd'] = sb.astype(np.float32)
    for tag, Lq in (('l', L), ('c', LC)):
        nsc = Lq // 128
        N = 2 * Lq
        nf = Lq + 1
        nfc = (nf + 127) // 128
        t = np.linspace(0.0, 1.0, Lq, dtype=np.float32)[:, None]
        w = (2.0 * np.pi * np.arange(Lq, dtype=np.float32)[:, None] / Lq).astype(np.float32)
        f = np.linspace(1e-4, 15, 16, dtype=np.float32)[None, :]
        z = np.concatenate([t, np.cos(f * w), -np.sin(f * w)], axis=-1).astype(np.float32)
        c['zT_' + tag] = np.ascontiguousarray(z.T)
        max_decay = math.log(1e-2) / 0.3
        min_decay = math.log(1e-2) / 1.5
        deltas = np.abs(np.linspace(min_decay, max_decay, 256, dtype=np.float32))
        win = np.exp(-t * deltas).astype(np.float32)
        winb = win.copy()
        winb[0] = 0.0
        lay = lambda m: np.ascontiguousarray(m.reshape(nsc, 128, 256).transpose(1, 0, 2))
        c['win_' + tag] = np.stack([lay(win), lay(winb)]).astype(np.float32)
        s = np.arange(Lq, dtype=np.float64)[:, None]
        ff = np.arange(nfc * 128, dtype=np.float64)[None, :]
        th = 2 * np.pi * s * ff / N
        valid = (ff < nf)
        Cf = np.cos(th) * valid
        Sf = -np.sin(th) * valid
        fl = lambda m: np.ascontiguousarray(m.reshape(nsc, 128, nfc, 128).transpose(2, 1, 0, 3))
        c['cf_' + tag] = _bf(fl(Cf))
        c['sf_' + tag] = _bf(fl(Sf))
        TB = min(512, Lq)
        ntb = Lq // TB
        fcol = np.arange(nfc * 128, dtype=np.float64)[:, None]
        tt = np.arange(Lq, dtype=np.float64)[None, :]
        wgt = np.where((fcol == 0) | (fcol == Lq), 1.0, 2.0) * (fcol < nf) / N
        th2 = 2 * np.pi * fcol * tt / N
        Ci = wgt * np.cos(th2)
        Si = -wgt * np.sin(th2)
        il = lambda m: np.ascontiguousarray(m.reshape(nfc, 128, ntb, TB).transpose(2, 1, 0, 3))
        c['ci_' + tag] = _bf(il(Ci))
        c['si_' + tag] = _bf(il(Si))
        t1 = np.arange(Lq, dtype=np.float64)
        th3 = 2 * np.pi * np.outer(t1, t1) / Lq
        CL = np.cos(th3) / math.sqrt(Lq)
        SLn = -np.sin(th3) / math.sqrt(Lq)
        ll = lambda m: np.ascontiguousarray(m.reshape(nsc, 128, ntb, TB).transpose(2, 1, 0, 3))
        c['cl_' + tag] = _bf(ll(CL))
        c['sl_' + tag] = _bf(ll(SLn))
    _CONST = c
    return c


def _pc(v, n):
    return np.ascontiguousarray(np.asarray(v, np.float32).reshape(n, 128).T)


class Prog:
    def __init__(self, layers=(0, 1), phases=None, dbg=False):
        self.layers = layers
        self.phases = phases
        self.dbg = dbg
        nc = bass.Bass("TRN2", target_bir_lowering=False)
        self.nc = nc
        self.fw = FW(nc)
        self.inputs = {}
        self.uid = 0
        self.ps = [nc.alloc_psum_tensor("psb%d" % i, [128, 512], F32) for i in range(8)]

    def inp(self, name, shape, dt=F32):
        t = self.nc.dram_tensor(name, list(shape), dt, kind="ExternalInput").ap()
        self.inputs[name] = t
        return t

    def scratch(self, name, shape, dt=F32, out=False):
        kind = "ExternalOutput" if (out or self.dbg) else "Internal"
        return self.nc.dram_tensor(name, list(shape), dt, kind=kind).ap()

    def sb(self, name, shape, dt):
        self.uid += 1
        return self.nc.sbuf_tensor('%s_u%d' % (name, self.uid), shape, dt)

    def want(self, ph):
        return self.phases is None or ph in self.phases


def _declare(P):
    c = _consts()
    I = {}
    I['xT'] = P.inp('xT', [2, D, L])
    I['ctxT'] = P.inp('ctxT', [2, D, LC])
    I['cT'] = P.inp('cT', [128, 8, 3])
    I['w_ada'] = P.inp('w_ada', [NLAYER, D, 6 * D])
    I['b_adaT'] = P.inp('b_adaT', [NLAYER, 128, 48])
    I['gn1'] = P.inp('gn1', [NLAYER, 128, 8])
    I['gn2'] = P.inp('gn2', [NLAYER, 128, 8])
    I['gfin'] = P.inp('gfin', [128, 8])
    I['w_in'] = P.inp('w_in', [NLAYER, D, NCOL])
    I['hycw'] = P.inp('hycw', [NLAYER, 128, 6, 4])
    I['hfw1'] = P.inp('hfw1', [NLAYER, 33, 64])
    I['hfw2'] = P.inp('hfw2', [NLAYER, 64, 64])
    I['hfw3'] = P.inp('hfw3', [NLAYER, 64, 512])
    I['hfv'] = P.inp('hfv', [NLAYER, 64, 3])
    I['hybias'] = P.inp('hybias', [NLAYER, 128, 2])
    I['sscw'] = P.inp('sscw', [NLAYER, 128, 8, 4])
    I['dtb'] = P.inp('dtb', [NLAYER, 16, 1])
    I['alog'] = P.inp('alog', [NLAYER, 1, 16])
    I['dskip'] = P.inp('dskip', [NLAYER, 1, 8])
    I['sng'] = P.inp('sng', [NLAYER, 128, 4])
    I['w_out'] = P.inp('w_out', [NLAYER, D, D])
    I['wq'] = P.inp('wq', [NLAYER, D, 2048])
    I['k1T'] = P.inp('k1T', [NLAYER, 128, 128])
    I['k2T'] = P.inp('k2T', [NLAYER, 128, 128])
    I['uT'] = P.inp('uT', [NLAYER, D, 16384])
    I['v'] = P.inp('v', [NLAYER, 16384, D])
    for k, a in c.items():
        I[k] = P.inp('c_' + k, a.shape, BF16 if a.dtype == ml_dtypes.bfloat16 else F32)
    return I


class Ctx:
    pass


def build(layers=(0, 1), phases=None, dbg=False, final=True):
    P = Prog(layers, phases, dbg)
    nc, fw = P.nc, P.fw
    I = _declare(P)
    S = Ctx()
    S.res = {}
    for b in range(2):
        S.res[(b, 'l')] = P.scratch('res_l%d' % b, [D, L])
        S.res[(b, 'c')] = P.scratch('res_c%d' % b, [D, LC])
    S.pl = {}
    S.mix = {}
    for b in range(2):
        S.pl[(b, 'l')] = P.scratch('pl_l%d' % b, [NCOL, L])
        S.pl[(b, 'c')] = P.scratch('pl_c%d' % b, [NCOL, LC])
        S.mix[(b, 'l')] = P.scratch('mix_l%d' % b, [D, L], BF16)
        S.mix[(b, 'c')] = P.scratch('mix_c%d' % b, [D, LC], BF16)
    S.khat = {'l': P.scratch('khat_l', [2, 17 * 128, 256]), 'c': P.scratch('khat_c', [2, 3 * 128, 256])}
    S.ubf = P.scratch('ubf', [64, 128, 8, 256], BF16)
    S.vbf = P.scratch('vbf', [64, 128, 2, 1024], BF16)
    S.outT = P.scratch('outT', [2, D, L], F32, out=True)

    A = lambda n, sh, dt=F32: nc.alloc_sbuf_tensor('sb_' + n, sh, dt)
    S.ident = A('ident', [128, 128]); S.ones = A('ones', [128, 128])
    S.uinc = A('uinc', [128, 128]); S.linc = A('linc', [128, 128])
    S.maskf = A('maskf', [128, 128]); S.maskb = A('maskb', [128, 128])
    S.modT = A('modT', [128, 48, 3])
    S.G1 = A('G1', [128, 8, 3]); S.G2 = A('G2', [128, 8, 3])
    S.gfin = A('gfin', [128, 8]); S.zero8 = A('zero8', [128, 8])
    S.hyn = A('hyn', [128, 2, 2])
    for nm in ('ident', 'ones', 'uinc', 'linc', 'maskf', 'maskb'):
        fw.dma('sp', getattr(S, nm)[:], I[nm], writes=[nm])
    fw.dma('sp', S.gfin[:], I['gfin'], writes=['gfin'])
    fw.memset('pool', S.zero8[:], 0.0, writes=['zero8'])
    S.epsc = A('epsc', [128, 1])
    fw.memset('pool', S.epsc[:], EPS, writes=['epsc'])
    S.negpi = A('negpi', [128, 1])
    fw.memset('pool', S.negpi[:], -PI, writes=['negpi'])
    fw.barrier()

    def src(l, seq):
        b, kind = seq
        if l == layers[0] and l == 0:
            return I['xT'][b] if kind == 'l' else I['ctxT'][b]
        return S.res[seq]

    for l in layers:
        ctx_out = l < NLAYER - 1
        if P.want('mod'):
            phase_mod(P, I, S, l)
        if P.want('proj'):
            phase_proj(P, I, S, l, src)
        if P.want('filt'):
            phase_filt(P, I, S, l, 'l')
            if ctx_out:
                phase_filt(P, I, S, l, 'c')
        if P.want('hy'):
            phase_hy(P, I, S, l, 'l')
            if ctx_out:
                phase_hy(P, I, S, l, 'c')
        if P.want('fn'):
            phase_fn(P, I, S, l, 'l')
            if ctx_out:
                phase_fn(P, I, S, l, 'c')
        if P.want('ssd'):
            phase_ssd(P, I, S, l, ctx_out)
        if P.want('out'):
            phase_out(P, I, S, l, src, ctx_out)
        if P.want('peer'):
            phase_peer(P, I, S, l, ctx_out)
    if final and P.want('final'):
        phase_final(P, I, S)
    if dbg:
        dh = P.scratch('dbg_hyn', [128, 4])
        fw.dma('sp', dh, S.hyn[:].rearrange("p a b -> p (a b)"), writes=['dbg_hyn'])
    fw.barrier()
    return P


def phase_mod(P, I, S, l):
    nc, fw = P.nc, P.fw
    with ExitStack() as es:
        cin = es.enter_context(P.sb('m_c', [128, 8, 3], F32))
        sc = es.enter_context(P.sb('m_sc', [128, 8, 3], F32))
        w0 = es.enter_context(P.sb('m_w0', [128, 8, 512], F32))
        w1 = es.enter_context(P.sb('m_w1', [128, 8, 512], F32))
        bada = es.enter_context(P.sb('m_b', [128, 48], F32))
        g1 = es.enter_context(P.sb('m_g1', [128, 8], F32))
        g2 = es.enter_context(P.sb('m_g2', [128, 8], F32))
        tmp = es.enter_context(P.sb('m_t', [128, 8, 3], F32))
        wb = [w0, w1]
        fw.dma('sp', cin[:], I['cT'], writes=['m_c'])
        fw.dma('sp', bada[:], I['b_adaT'][l], writes=['m_b'])
        fw.dma('sp', g1[:], I['gn1'][l], writes=['m_g1'])
        fw.dma('sp', g2[:], I['gn2'][l], writes=['m_g2'])
        fw.act(sc[:], cin[:], AF.Silu, reads=['m_c'], writes=['m_sc'])
        wv = I['w_ada'][l].rearrange("(dc p) n -> p dc n", p=128)
        for blk in range(12):
            w = wb[blk % 2]
            wk = 'm_w%d' % (blk % 2)
            fw.dma('sp', w[:], wv[:, :, blk * 512:(blk + 1) * 512], writes=[wk])
            for j in range(4):
                cc = blk * 4 + j
                pk = ('ps', cc % 2)
                pt = P.ps[cc % 2][:, 0:3]
                for dc in range(8):
                    fw.mm(pt, w[:, dc, j * 128:(j + 1) * 128], sc[:, dc, :], start=(dc == 0), stop=(dc == 7),
                          reads=[wk, 'm_sc'], writes=[pk])
                fw.ts('dve', S.modT[:, cc, :], pt, bada[:, cc:cc + 1], None, ALU.add, reads=[pk, 'm_b'], writes=['modT'])
        for (G, g, gk, c0, nm) in ((S.G1, g1, 'm_g1', 8, 'G1'), (S.G2, g2, 'm_g2', 32, 'G2')):
            fw.ts('dve', tmp[:], S.modT[:, c0:c0 + 8, :], 1.0, None, ALU.add, reads=['modT'], writes=['m_t'])
            fw.tt('dve', G[:], tmp[:], g[:, :].unsqueeze(2).to_broadcast([128, 8, 3]), ALU.mult, reads=['m_t', gk], writes=[nm])
        fw.barrier()


def seqs_of(ctx_too=True):
    out = []
    for b in range(2):
        out.append((b, 'l'))
        if ctx_too:
            out.append((b, 'c'))
    return out


def seq_len(seq):
    return L if seq[1] == 'l' else LC


def seq_col(seq):
    return seq[0] if seq[1] == 'l' else 2


def normmod(P, S, xt, xk, T, Gap, shap, hm, hk, sq, rstd, psk, pst):
    fw = P.fw
    fw.act(sq[:, :, :T], xt[:, :, :T], AF.Square, reads=[xk], writes=[('nm_sq', dc) for dc in range(8)])
    for dc in range(8):
        fw.mm(pst[:, :T], S.ones[:], sq[:, dc, :T], start=(dc == 0), stop=(dc == 7), reads=[('nm_sq', dc), 'ones'], writes=[psk])
    fw.act(rstd[:, :T], pst[:, :T], AF.Sqrt, bias=S.epsc[:, 0:1], scale=1.0 / D, reads=[psk, 'epsc'], writes=['nm_rstd'])
    fw.op('dve', lambda e: e.reciprocal(rstd[:, :T], rstd[:, :T]), reads=['nm_rstd'], writes=['nm_rstd'])
    for dc in range(8):
        en = 'dve' if dc % 2 == 0 else 'pool'
        fw.stt(en, sq[:, dc, :T], xt[:, dc, :T], Gap[:, dc:dc + 1], rstd[:, :T], ALU.mult, ALU.mult,
               reads=[xk, 'nm_rstd'], writes=[('nm_sq', dc)])
        fw.act(hm[:, dc, :T], sq[:, dc, :T], AF.Identity, bias=shap[:, dc:dc + 1], reads=[('nm_sq', dc)], writes=[(hk, dc)])
    return [(hk, dc) for dc in range(8)]


def load_cast_weight(P, dst, dstk, srcv, ncols, stg, stgk):
    fw = P.fw
    nb = (ncols + 511) // 512
    for blk in range(nb):
        c0 = blk * 512
        c1 = min(ncols, c0 + 512)
        st = stg[blk % 2]
        sk = stgk[blk % 2]
        fw.dma('sp', st[:, :, :c1 - c0], srcv[:, :, c0:c1], writes=[sk])
        fw.cp('dve' if blk % 2 == 0 else 'pool', dst[:, :, c0:c1], st[:, :, :c1 - c0], reads=[sk], writes=[(dstk, blk)])
    return [(dstk, blk) for blk in range(nb)]


def phase_proj(P, I, S, l, src):
    nc, fw = P.nc, P.fw
    with ExitStack() as es:
        winb = es.enter_context(P.sb('p_win', [128, 8, NCOL], BF16))
        s0 = es.enter_context(P.sb('p_s0', [128, 8, 512], F32))
        s1 = es.enter_context(P.sb('p_s1', [128, 8, 512], F32))
        sq = es.enter_context(P.sb('p_sq', [128, 8, 512], F32))
        rstd = es.enter_context(P.sb('p_rstd', [128, 512], F32))
        hm = es.enter_context(P.sb('p_hm', [128, 8, 512], BF16))
        ob = es.enter_context(P.sb('p_o', [128, 4, 512], F32))
        wkeys = load_cast_weight(P, winb, 'p_win', I['w_in'][l].rearrange("(dc p) n -> p dc n", p=128), NCOL, [s0, s1], ['p_s0', 'p_s1'])
        xb = [s0, s1]
        chunks = [(c0, min(128, NCOL - c0)) for c0 in range(0, OFF_DT, 128)] + [(OFF_DT, 16)] + [(OFF_FN, 128), (OFF_FN + 128, 128)]
        it = 0
        oi = 0
        for seq in seqs_of(True):
            Lq = seq_len(seq)
            T = min(512, Lq)
            col = seq_col(seq)
            xv = src(l, seq).rearrange("(dc p) t -> p dc t", p=128)
            for tb in range(Lq // T):
                xt = xb[it % 2]
                xk = 'p_s%d' % (it % 2)
                it += 1
                fw.dma('sp', xt[:, :, :T], xv[:, :, tb * T:(tb + 1) * T], reads=[('res', seq)], writes=[xk])
                hk = normmod(P, S, xt, xk, T, S.G1[:, :, col], S.modT[:, 0:8, col], hm, 'p_hm', sq, rstd, ('ps', 0), P.ps[0])
                for ci, (c0, cw) in enumerate(chunks):
                    pb = 1 + ci % 3
                    pt = P.ps[pb][:cw, :T]
                    for dc in range(8):
                        fw.mm(pt, winb[:, dc, c0:c0 + cw], hm[:, dc, :T], start=(dc == 0), stop=(dc == 7),
                              reads=wkeys + hk if dc in (0, 7) else (), writes=[('ps', pb)])
                    o = ob[:cw, oi % 4, :T]
                    ok = ('p_o', oi % 4)
                    oi += 1
                    if ci % 2 == 0:
                        fw.cp('act', o, pt, reads=[('ps', pb)], writes=[ok])
                    else:
                        fw.cp('dve', o, pt, reads=[('ps', pb)], writes=[ok])
                    fw.dma('sp', S.pl[seq][c0:c0 + cw, tb * T:(tb + 1) * T], o, reads=[ok], writes=[('pl', seq, ci, tb)])
        fw.barrier()


def phase_final(P, I, S):
    nc, fw = P.nc, P.fw
    with ExitStack() as es:
        s0 = es.enter_context(P.sb('f_s0', [128, 8, 512], F32))
        s1 = es.enter_context(P.sb('f_s1', [128, 8, 512], F32))
        sq = es.enter_context(P.sb('f_sq', [128, 8, 512], F32))
        rstd = es.enter_context(P.sb('f_rstd', [128, 512], F32))
        o0 = es.enter_context(P.sb('f_o0', [128, 8, 512], F32))
        o1 = es.enter_context(P.sb('f_o1', [128, 8, 512], F32))
        xb = [s0, s1]
        ob = [o0, o1]
        it = 0
        for b in range(2):
            xv = S.res[(b, 'l')].rearrange("(dc p) t -> p dc t", p=128)
            ov = S.outT[b].rearrange("(dc p) t -> p dc t", p=128)
            for tb in range(L // 512):
                xt = xb[it % 2]; xk = 'f_s%d' % (it % 2)
                o = ob[it % 2]; ok = 'f_o%d' % (it % 2)
                it += 1
                fw.dma('sp', xt[:], xv[:, :, tb * 512:(tb + 1) * 512], writes=[xk])
                hk = normmod(P, S, xt, xk, 512, S.gfin, S.zero8, o, ok + 'h', sq, rstd, ('ps', 0), P.ps[0])
                fw.dma('sp', ov[:, :, tb * 512:(tb + 1) * 512], o[:], reads=hk, writes=[('outT', b, tb)])
        fw.barrier()


def prep_shared(inp):
    f = lambda a: np.ascontiguousarray(np.asarray(a, np.float32))
    sh = {}
    sh['w_ada'] = f(inp['w_ada'])
    sh['b_adaT'] = np.stack([_pc(inp['b_ada'][l], 48) for l in range(NLAYER)])
    sh['gn1'] = np.stack([_pc(inp['g_norm1'][l], 8) for l in range(NLAYER)])
    sh['gn2'] = np.stack([_pc(inp['g_norm2'][l], 8) for l in range(NLAYER)])
    sh['gfin'] = _pc(inp['g_final'], 8)
    sh['w_in'] = f(inp['w_in'])
    hy = []
    for l in range(NLAYER):
        m = np.concatenate([np.asarray(inp['hy_conv_w'][l], np.float32), np.asarray(inp['hy_conv_b'][l], np.float32)[None]], 0)
        hy.append(np.ascontiguousarray(m.reshape(4, 6, 128).transpose(2, 1, 0)))
    sh['hycw'] = np.stack(hy)
    sh['hfw1'] = f(inp['hf_w1']); sh['hfw2'] = f(inp['hf_w2']); sh['hfw3'] = f(inp['hf_w3'])
    sh['hfv'] = np.ascontiguousarray(np.stack([inp['hf_b1'], inp['hf_b2'], inp['hf_freq']], axis=-1).astype(np.float32))
    sh['hybias'] = np.stack([_pc(inp['hy_bias'][l], 2) for l in range(NLAYER)])
    ss = []
    for l in range(NLAYER):
        m = np.concatenate([np.asarray(inp['ssd_conv_w'][l], np.float32), np.asarray(inp['ssd_conv_b'][l], np.float32)[None]], 0)
        ss.append(np.ascontiguousarray(m.reshape(4, 8, 128).transpose(2, 1, 0)))
    sh['sscw'] = np.stack(ss)
    sh['dtb'] = f(np.asarray(inp['ssd_dt_bias']).reshape(NLAYER, 16, 1))
    sh['alog'] = f(np.asarray(inp['ssd_a_log']).reshape(NLAYER, 1, 16))
    sh['dskip'] = f(np.asarray(inp['ssd_d']).reshape(NLAYER, 1, 8))
    sh['sng'] = np.stack([_pc(inp['ssd_norm_g'][l], 4) for l in range(NLAYER)])
    sh['w_out'] = f(inp['w_out'])
    sh['wq'] = f(inp['peer_wq'])
    sh['k1T'] = np.ascontiguousarray(np.asarray(inp['peer_k1'], np.float32).transpose(0, 2, 1))
    sh['k2T'] = np.ascontiguousarray(np.asarray(inp['peer_k2'], np.float32).transpose(0, 2, 1))
    sh['uT'] = np.ascontiguousarray(np.asarray(inp['peer_u'], np.float32).transpose(0, 2, 1))
    sh['v'] = f(inp['peer_v'])
    for k, a in _consts().items():
        sh['c_' + k] = a
    return sh


def prep_core(inp, core):
    m = {}
    x = np.asarray(inp['x'], np.float32)[2 * core:2 * core + 2]
    cx = np.asarray(inp['ctx'], np.float32)[2 * core:2 * core + 2]
    m['xT'] = np.ascontiguousarray(x.transpose(0, 2, 1))
    m['ctxT'] = np.ascontiguousarray(cx.transpose(0, 2, 1))
    cv = np.stack([np.asarray(inp['c'], np.float32)[2 * core], np.asarray(inp['c'], np.float32)[2 * core + 1],
                   np.asarray(inp['c_ctx'], np.float32)], axis=-1)
    m['cT'] = np.ascontiguousarray(cv.reshape(8, 128, 3).transpose(1, 0, 2))
    return m


_PROG = None


def kernel(**inputs):
    global _PROG
    if _PROG is None:
        _PROG = build()
    P = _PROG
    sh = prep_shared(inputs)
    in_maps = []
    for core in range(8):
        m = dict(sh)
        m.update(prep_core(inputs, core))
        in_maps.append({k: m[k] for k in P.inputs})
    res = run_bass_kernel_spmd(P.nc, in_maps, core_ids=list(range(8)))
    outs = [np.asarray(r['outT']).transpose(0, 2, 1) for r in res.results]
    return np.ascontiguousarray(np.concatenate(outs, axis=0).astype(np.float32))


def phase_fn(P, I, S, l, tag):
    nc, fw = P.nc, P.fw
    Lq = L if tag == 'l' else LC
    nsc = Lq // 128
    TB = min(512, Lq)
    ntb = Lq // TB
    with ExitStack() as es:
        ut = es.enter_context(P.sb('fn_ut', [128, 4, Lq], F32))
        cbd = es.enter_context(P.sb('fn_cbd', [128, 128], F32))
        sbd = es.enter_context(P.sb('fn_sbd', [128, 128], F32))
        atok = es.enter_context(P.sb('fn_a', [128, nsc, 512], BF16))
        btok = es.enter_context(P.sb('fn_b', [128, nsc, 512], BF16))
        cl = es.enter_context(P.sb('fn_cl', [128, nsc, TB], BF16))
        sl = es.enter_context(P.sb('fn_sl', [128, nsc, TB], BF16))
        ob = es.enter_context(P.sb('fn_o', [128, 2, TB], BF16))
        fw.dma('sp', cbd[:], I['cbd'], writes=['fn_cbd'])
        fw.dma('sp', sbd[:], I['sbd'], writes=['fn_sbd'])
        for b in range(2):
            for ch in range(2):
                fw.dma('sp', ut[:, b * 2 + ch, :], S.pl[(b, tag)][OFF_FN + ch * 128:OFF_FN + (ch + 1) * 128, :], writes=[('fn_ut', b * 2 + ch)])
        utk = [('fn_ut', m) for m in range(4)]
        for tc in range(nsc):
            pa, pb = 2 * (tc % 2), 2 * (tc % 2) + 1
            for m in range(4):
                fw.mm(P.ps[pa][:, m * 128:(m + 1) * 128], ut[:, m, tc * 128:(tc + 1) * 128], cbd[:], reads=utk + ['fn_cbd'], writes=[('ps', pa)])
                fw.mm(P.ps[pb][:, m * 128:(m + 1) * 128], ut[:, m, tc * 128:(tc + 1) * 128], sbd[:], reads=utk + ['fn_sbd'], writes=[('ps', pb)])
            fw.cp('act', atok[:, tc, :], P.ps[pa][:, :], reads=[('ps', pa)], writes=[('fn_a', tc)])
            fw.cp('dve', btok[:, tc, :], P.ps[pb][:, :], reads=[('ps', pb)], writes=[('fn_b', tc)])
        ak = [('fn_a', tc) for tc in range(nsc)]
        bk = [('fn_b', tc) for tc in range(nsc)]
        oi = 0
        for tb in range(ntb):
            fw.dma('sp', cl[:], I['cl_' + tag][tb], writes=['fn_cl'])
            fw.dma('sp', sl[:], I['sl_' + tag][tb], writes=['fn_sl'])
            for m in range(4):
                b, ch = m // 2, m % 2
                pi = 4 + m % 2
                pt = P.ps[pi][:, :TB]
                for tc in range(nsc):
                    fw.mm(pt, atok[:, tc, m * 128:(m + 1) * 128], cl[:, tc, :], start=(tc == 0), stop=False,
                          reads=ak + ['fn_cl'] if tc in (0, nsc - 1) else (), writes=[('ps', pi)])
                for tc in range(nsc):
                    fw.mm(pt, btok[:, tc, m * 128:(m + 1) * 128], sl[:, tc, :], start=False, stop=(tc == nsc - 1),
                          reads=bk + ['fn_sl'] if tc in (0, nsc - 1) else (), writes=[('ps', pi)])
                o = ob[:, oi % 2, :]
                ok = ('fn_o', oi % 2)
                oi += 1
                fw.cp('act' if m % 2 == 0 else 'dve', o, pt, reads=[('ps', pi)], writes=[ok])
                fw.dma('sp', S.mix[(b, tag)][768 + ch * 128:768 + (ch + 1) * 128, tb * TB:(tb + 1) * TB], o, reads=[ok], writes=[('mixfn', b, ch, tb)])
        fw.barrier()


def phase_out(P, I, S, l, src, ctx_out):
    nc, fw = P.nc, P.fw
    with ExitStack() as es:
        wout = es.enter_context(P.sb('o_w', [128, 8, 1024], BF16))
        s0 = es.enter_context(P.sb('o_s0', [128, 8, 512], F32))
        s1 = es.enter_context(P.sb('o_s1', [128, 8, 512], F32))
        m0 = es.enter_context(P.sb('o_m0', [128, 8, 512], BF16))
        m1 = es.enter_context(P.sb('o_m1', [128, 8, 512], BF16))
        x0 = es.enter_context(P.sb('o_x0', [128, 8, 512], F32))
        x1 = es.enter_context(P.sb('o_x1', [128, 8, 512], F32))
        wkeys = load_cast_weight(P, wout, 'o_w', I['w_out'][l].rearrange("(dc p) n -> p dc n", p=128), 1024, [s0, s1], ['o_s0', 'o_s1'])
        xb, mb, ob = [s0, s1], [m0, m1], [x0, x1]
        it = 0
        for seq in seqs_of(ctx_out):
            Lq = seq_len(seq)
            T = min(512, Lq)
            col = seq_col(seq)
            xv = src(l, seq).rearrange("(dc p) t -> p dc t", p=128)
            mv = S.mix[seq].rearrange("(dc p) t -> p dc t", p=128)
            rv = S.res[seq].rearrange("(dc p) t -> p dc t", p=128)
            for tb in range(Lq // T):
                k = it % 2
                it += 1
                xt, mx, xo = xb[k], mb[k], ob[k]
                fw.dma('sp', xt[:, :, :T], xv[:, :, tb * T:(tb + 1) * T], writes=['o_s%d' % k])
                fw.dma('sp', mx[:, :, :T], mv[:, :, tb * T:(tb + 1) * T], writes=['o_m%d' % k])
                for dch in range(8):
                    pi = dch % 4
                    pt = P.ps[pi][:, :T]
                    for cc in range(8):
                        fw.mm(pt, wout[:, cc, dch * 128:(dch + 1) * 128], mx[:, cc, :T], start=(cc == 0), stop=(cc == 7),
                              reads=wkeys + ['o_m%d' % k] if cc in (0, 7) else (), writes=[('ps', pi)])
                    fw.stt('dve', xo[:, dch, :T], pt, S.modT[:, 16 + dch, col:col + 1], xt[:, dch, :T], ALU.mult, ALU.add,
                           reads=[('ps', pi), 'o_s%d' % k, 'modT'], writes=[('o_x%d' % k, dch)])
                fw.dma('sp', rv[:, :, tb * T:(tb + 1) * T], xo[:, :, :T], reads=[('o_x%d' % k, d8) for d8 in range(8)], writes=[('resw', seq, tb)])
        fw.barrier()


def phase_filt(P, I, S, l, tag):
    nc, fw = P.nc, P.fw
    Lq = L if tag == 'l' else LC
    li = 0 if tag == 'l' else 1
    nsc = Lq // 128
    nfc = (Lq + 1 + 127) // 128
    T = min(512, Lq)
    with ExitStack() as es:
        zT = es.enter_context(P.sb('fl_z', [33, Lq], F32))
        w1 = es.enter_context(P.sb('fl_w1', [33, 64], F32))
        w2 = es.enter_context(P.sb('fl_w2', [64, 64], F32))
        w3 = es.enter_context(P.sb('fl_w3', [64, 512], F32))
        hv = es.enter_context(P.sb('fl_hv', [64, 3], F32))
        fb = es.enter_context(P.sb('fl_fb', [64, 2], F32))
        h1 = es.enter_context(P.sb('fl_h1', [64, Lq], F32))
        h2 = es.enter_context(P.sb('fl_h2', [64, Lq], F32))
        win = es.enter_context(P.sb('fl_win', [128, 2, nsc, 256], F32))
        pm = es.enter_context(P.sb('fl_pm', [128, nsc, 256], BF16))
        mmn = es.enter_context(P.sb('fl_mm', [128, nsc, 256], BF16))
        acc = es.enter_context(P.sb('fl_acc', [128, 256], F32))
        t1 = es.enter_context(P.sb('fl_t1', [128, 2, 256], F32))
        t2 = es.enter_context(P.sb('fl_t2', [128, 2, 256], F32))
        tmp = es.enter_context(P.sb('fl_tmp', [64, 512], F32))
        tmpk = es.enter_context(P.sb('fl_tmpk', [64, 512], F32))
        cf = es.enter_context(P.sb('fl_cf', [128, 2, nsc, 128], BF16))
        sf = es.enter_context(P.sb('fl_sf', [128, 2, nsc, 128], BF16))
        ko = es.enter_context(P.sb('fl_ko', [128, 2, 2, 256], F32))
        ntmp = es.enter_context(P.sb('fl_n', [128, 2], F32))
        fw.dma('sp', zT[:], I['zT_' + tag], writes=['fl_z'])
        fw.dma('sp', w1[:], I['hfw1'][l], writes=['fl_w1'])
        fw.dma('sp', w2[:], I['hfw2'][l], writes=['fl_w2'])
        fw.dma('sp', w3[:], I['hfw3'][l], writes=['fl_w3'])
        fw.dma('sp', hv[:], I['hfv'][l], writes=['fl_hv'])
        for v in range(2):
            fw.dma('sp', win[:, v, :, :], I['win_' + tag][v], writes=[('fl_win', v)])
        fw.ts('dve', fb[:], hv[:, 0:2], hv[:, 2:3], None, ALU.mult, reads=['fl_hv'], writes=['fl_fb'])
        fw.memset('pool', acc[:], 0.0, writes=['fl_acc'])

        def sin_layer(dst, dk, w, wk, K, srcT, sk, col):
            for blk in range(Lq // T):
                pi = blk % 2
                pt = P.ps[pi][:64, :T]
                fw.mm(pt, w[:K, :64], srcT[:K, blk * T:(blk + 1) * T], reads=[wk, sk], writes=[('ps', pi)])
                fw.ts('dve', tmp[:, :T], pt, hv[:, 2:3], fb[:, col:col + 1], ALU.mult, ALU.add, reads=[('ps', pi), 'fl_hv', 'fl_fb'], writes=['fl_tmp'])
                MAGIC = 12582912.0
                fw.ts('dve', tmpk[:, :T], tmp[:, :T], 1.0 / (2.0 * PI), MAGIC, ALU.mult, ALU.add, reads=['fl_tmp'], writes=['fl_tmpk'])
                fw.ts('dve', tmpk[:, :T], tmpk[:, :T], MAGIC, None, ALU.subtract, reads=['fl_tmpk'], writes=['fl_tmpk'])
                fw.stt('dve', tmp[:, :T], tmpk[:, :T], -2.0 * PI, tmp[:, :T], ALU.mult, ALU.add, reads=['fl_tmpk', 'fl_tmp'], writes=['fl_tmp'])
                fw.act(dst[:, blk * T:(blk + 1) * T], tmp[:, :T], AF.Sin, reads=['fl_tmp'], writes=[dk])

        sin_layer(h1, 'fl_h1', w1, 'fl_w1', 33, zT, 'fl_z', 0)
        sin_layer(h2, 'fl_h2', w2, 'fl_w2', 64, h1, 'fl_h1', 1)
        for sc in range(nsc):
            pi = 2 + sc % 2
            pt = P.ps[pi]
            fw.mm(pt[:, :], h2[:64, sc * 128:(sc + 1) * 128], w3[:64, :], reads=['fl_h2', 'fl_w3'], writes=[('ps', pi)])
            fw.tt('dve', t1[:], pt[:, :].rearrange("p (v c) -> p v c", v=2), win[:, :, sc, :], ALU.mult,
                  reads=[('ps', pi), ('fl_win', 0), ('fl_win', 1)], writes=['fl_t1'])
            fw.tt('pool', pm[:, sc, :], t1[:, 0, :], t1[:, 1, :], ALU.add, reads=['fl_t1'], writes=[('fl_pm', sc)])
            fw.tt('pool', mmn[:, sc, :], t1[:, 0, :], t1[:, 1, :], ALU.subtract, reads=['fl_t1'], writes=[('fl_mm', sc)])
            fw.act(t2[:], t1[:], AF.Square, reads=['fl_t1'], writes=['fl_t2'])
            fw.tt('pool', acc[:], acc[:], t2[:, 0, :], ALU.add, reads=['fl_t2'], writes=['fl_acc'])
            fw.tt('pool', acc[:], acc[:], t2[:, 1, :], ALU.add, reads=['fl_t2'], writes=['fl_acc'])
        for ch in range(2):
            fw.mm(P.ps[0][:, ch:ch + 1], acc[:, ch * 128:(ch + 1) * 128], S.ones[:, 0:1], reads=['fl_acc', 'ones'], writes=[('ps', 0)])
        fw.act(ntmp[:], P.ps[0][:, 0:2], AF.Sqrt, bias=S.epsc[:, 0:1], reads=[('ps', 0), 'epsc'], writes=['fl_n'])
        fw.op('dve', lambda e: e.reciprocal(S.hyn[:, :, li], ntmp[:]), reads=['fl_n'], writes=[('hyn', li)])
        pmk = [('fl_pm', sc) for sc in range(nsc)]
        mmk = [('fl_mm', sc) for sc in range(nsc)]
        for fc in range(nfc):
            fsz = 128 if fc < nfc - 1 else 1
            k = fc % 2
            fw.dma('sp', cf[:, k, :, :], I['cf_' + tag][fc], writes=[('fl_cf', k)])
            fw.dma('sp', sf[:, k, :, :], I['sf_' + tag][fc], writes=[('fl_sf', k)])
            pa, pb = 4 + 2 * k, 5 + 2 * k
            for sc in range(nsc):
                fw.mm(P.ps[pa][:fsz, :256], cf[:, k, sc, :fsz], pm[:, sc, :], start=(sc == 0), stop=(sc == nsc - 1),
                      reads=pmk + [('fl_cf', k)] if sc in (0, nsc - 1) else (), writes=[('ps', pa)])
            for sc in range(nsc):
                fw.mm(P.ps[pb][:fsz, :256], sf[:, k, sc, :fsz], mmn[:, sc, :], start=(sc == 0), stop=(sc == nsc - 1),
                      reads=mmk + [('fl_sf', k)] if sc in (0, nsc - 1) else (), writes=[('ps', pb)])
            fw.cp('act', ko[:fsz, k, 0, :], P.ps[pa][:fsz, :256], reads=[('ps', pa)], writes=[('fl_ko', k, 0)])
            fw.cp('dve', ko[:fsz, k, 1, :], P.ps[pb][:fsz, :256], reads=[('ps', pb)], writes=[('fl_ko', k, 1)])
            for v in range(2):
                fw.dma('sp', S.khat[tag][v, fc * 128:fc * 128 + fsz, :], ko[:fsz, k, v, :], reads=[('fl_ko', k, v)], writes=[('khat', tag, v, fc)])
        fw.barrier()


def phase_hy(P, I, S, l, tag):
    nc, fw = P.nc, P.fw
    Lq = L if tag == 'l' else LC
    li = 0 if tag == 'l' else 1
    nsc = Lq // 128
    nfc = (Lq + 1 + 127) // 128
    TB = min(512, Lq)
    ntb = Lq // TB
    with ExitStack() as es:
        u = es.enter_context(P.sb('hy_u', [128, 4, Lq], F32))
        x1c = es.enter_context(P.sb('hy_x1', [128, 4, Lq], BF16))
        utok = es.enter_context(P.sb('hy_ut', [128, nsc, 512], BF16))
        wre = es.enter_context(P.sb('hy_wre', [128, nfc, 512], BF16))
        wim = es.enter_context(P.sb('hy_wim', [128, nfc, 512], BF16))
        cw = es.enter_context(P.sb('hy_cw', [128, 6, 4], F32))
        hb = es.enter_context(P.sb('hy_hb', [128, 2], F32))
        fw.dma('sp', cw[:], I['hycw'][l], writes=['hy_cw'])
        fw.dma('sp', hb[:], I['hybias'][l], writes=['hy_hb'])
        with ExitStack() as es2:
            raw = es2.enter_context(P.sb('hy_raw', [128, 3, Lq + 2], F32))
            cv = [es2.enter_context(P.sb('hy_cv%d' % k, [128, Lq], F32)) for k in range(3)]
            fw.memset('pool', raw[:, :, 0:1], 0.0, writes=['hy_raw_h0'])
            fw.memset('pool', raw[:, :, Lq + 1:Lq + 2], 0.0, writes=['hy_raw_h1'])
            for m in range(4):
                b, ch = m // 2, m % 2
                for k in range(3):
                    r0 = k * 256 + ch * 128
                    fw.dma('sp', raw[:, k, 1:Lq + 1], S.pl[(b, tag)][r0:r0 + 128, :], writes=[('hy_raw', k)])
                    ci = 2 * k + ch
                    rk = [('hy_raw', k), 'hy_raw_h0', 'hy_raw_h1', 'hy_cw']
                    fw.ts('dve', cv[k][:], raw[:, k, 0:Lq], cw[:, ci, 0:1], cw[:, ci, 3:4], ALU.mult, ALU.add, reads=rk, writes=[('hy_cv', k)])
                    fw.stt('dve', cv[k][:], raw[:, k, 1:Lq + 1], cw[:, ci, 1:2], cv[k][:], ALU.mult, ALU.add, reads=rk, writes=[('hy_cv', k)])
                    fw.stt('dve', cv[k][:], raw[:, k, 2:Lq + 2], cw[:, ci, 2:3], cv[k][:], ALU.mult, ALU.add, reads=rk, writes=[('hy_cv', k)])
                fw.tt('pool', u[:, m, :], cv[2][:], cv[0][:], ALU.mult, reads=[('hy_cv', 2), ('hy_cv', 0)], writes=[('hy_u', m)])
                fw.cp('act', x1c[:, m, :], cv[1][:], reads=[('hy_cv', 1)], writes=[('hy_x1', m)])
            uk = [('hy_u', m) for m in range(4)]
            for sc in range(nsc):
                pi = sc % 2
                for m in range(4):
                    fw.tr(P.ps[pi][:, m * 128:(m + 1) * 128], u[:, m, sc * 128:(sc + 1) * 128], S.ident[:], reads=uk + ['ident'], writes=[('ps', pi)])
                fw.cp('act' if sc % 2 == 0 else 'dve', utok[:, sc, :], P.ps[pi][:, :], reads=[('ps', pi)], writes=[('hy_ut', sc)])
            fw.barrier()
        with ExitStack() as es2:
            cf = es2.enter_context(P.sb('hy_cf', [128, 2, nsc, 128], BF16))
            sf = es2.enter_context(P.sb('hy_sf', [128, 2, nsc, 128], BF16))
            kk = es2.enter_context(P.sb('hy_kk', [128, 2, 2, 256], F32))
            tq = [es2.enter_context(P.sb('hy_t%d' % q, [128, 2, 256], F32)) for q in range(4)]
            for fc in range(nfc):
                fsz = 128 if fc < nfc - 1 else 1
                k = fc % 2
                fw.dma('sp', cf[:, k, :, :], I['cf_' + tag][fc], writes=[('hy_cf', k)])
                fw.dma('sp', sf[:, k, :, :], I['sf_' + tag][fc], writes=[('hy_sf', k)])
                for v in range(2):
                    fw.dma('sp', kk[:fsz, k, v, :], S.khat[tag][v, fc * 128:fc * 128 + fsz, :], writes=[('hy_kk', k, v)])
                pa, pb = 2 + 2 * k, 3 + 2 * k
                for sc in range(nsc):
                    fw.mm(P.ps[pa][:fsz, :], cf[:, k, sc, :fsz], utok[:, sc, :], start=(sc == 0), stop=(sc == nsc - 1),
                          reads=[('hy_cf', k)] if sc in (0, nsc - 1) else (), writes=[('ps', pa)])
                for sc in range(nsc):
                    fw.mm(P.ps[pb][:fsz, :], sf[:, k, sc, :fsz], utok[:, sc, :], start=(sc == 0), stop=(sc == nsc - 1),
                          reads=[('hy_sf', k)] if sc in (0, nsc - 1) else (), writes=[('ps', pb)])
                ure = P.ps[pa][:fsz, :].rearrange("p (b c) -> p b c", b=2)
                uim = P.ps[pb][:fsz, :].rearrange("p (b c) -> p b c", b=2)
                kre = kk[:fsz, k, 0, :].unsqueeze(1).to_broadcast([fsz, 2, 256])
                kim = kk[:fsz, k, 1, :].unsqueeze(1).to_broadcast([fsz, 2, 256])
                kr = [('hy_kk', k, 0), ('hy_kk', k, 1)]
                fw.tt('dve', tq[0][:fsz], ure, kre, ALU.mult, reads=[('ps', pa)] + kr, writes=[('hy_t', 0)])
                fw.tt('dve', tq[1][:fsz], uim, kim, ALU.mult, reads=[('ps', pb)] + kr, writes=[('hy_t', 1)])
                fw.tt('dve', tq[2][:fsz], ure, kim, ALU.mult, reads=[('ps', pa)] + kr, writes=[('hy_t', 2)])
                fw.tt('dve', tq[3][:fsz], uim, kre, ALU.mult, reads=[('ps', pb)] + kr, writes=[('hy_t', 3)])
                fw.tt('pool', wre[:fsz, fc, :].rearrange("p (b c) -> p b c", b=2), tq[0][:fsz], tq[1][:fsz], ALU.subtract,
                      reads=[('hy_t', 0), ('hy_t', 1)], writes=[('hy_wre', fc)])
                fw.tt('pool', wim[:fsz, fc, :].rearrange("p (b c) -> p b c", b=2), tq[2][:fsz], tq[3][:fsz], ALU.add,
                      reads=[('hy_t', 2), ('hy_t', 3)], writes=[('hy_wim', fc)])
            fw.barrier()
        with ExitStack() as es2:
            ci_t = es2.enter_context(P.sb('hy_ci', [128, nfc, TB], BF16))
            si_t = es2.enter_context(P.sb('hy_si', [128, nfc, TB], BF16))
            at = es2.enter_context(P.sb('hy_at', [128, 2, TB], F32))
            ob = es2.enter_context(P.sb('hy_o', [128, 2, TB], BF16))
            oi = 0
            for tb in range(ntb):
                fw.dma('sp', ci_t[:], I['ci_' + tag][tb], writes=['hy_ci'])
                fw.dma('sp', si_t[:], I['si_' + tag][tb], writes=['hy_si'])
                for m in range(4):
                    b, ch = m // 2, m % 2
                    pi = m % 2
                    pt = P.ps[pi][:, :TB]
                    for fc in range(nfc):
                        fsz = 128 if fc < nfc - 1 else 1
                        fw.mm(pt, wre[:fsz, fc, m * 128:(m + 1) * 128], ci_t[:fsz, fc, :], start=(fc == 0), stop=False,
                              reads=['hy_ci'] if fc in (0, nfc - 1) else (), writes=[('ps', pi)])
                    for fc in range(nfc - 1):
                        fw.mm(pt, wim[:, fc, m * 128:(m + 1) * 128], si_t[:, fc, :], start=False, stop=(fc == nfc - 2),
                              reads=['hy_si'] if fc in (0, nfc - 2) else (), writes=[('ps', pi)])
                    a = at[:, oi % 2, :]
                    o = ob[:, oi % 2, :]
                    ak, ok = ('hy_at', oi % 2), ('hy_o', oi % 2)
                    oi += 1
                    fw.ts('dve', a, pt, S.hyn[:, ch, li:li + 1], None, ALU.mult, reads=[('ps', pi)], writes=[ak])
                    fw.stt('dve', a, u[:, m, tb * TB:(tb + 1) * TB], hb[:, ch:ch + 1], a, ALU.mult, ALU.add, reads=['hy_hb'], writes=[ak])
                    fw.tt('pool', o, a, x1c[:, m, tb * TB:(tb + 1) * TB], ALU.mult, reads=[ak], writes=[ok])
                    fw.dma('sp', S.mix[(b, tag)][ch * 128:(ch + 1) * 128, tb * TB:(tb + 1) * TB], o, reads=[ok], writes=[('mixhy', b, ch, tb)])
            fw.barrier()


def phase_ssd(P, I, S, l, ctx_out):
    nc, fw = P.nc, P.fw
    ps = P.ps
    with ExitStack() as es:
        E = lambda n, sh, dt=F32: es.enter_context(P.sb(n, sh, dt))
        cw = E('sd_cw', [128, 8, 4]); dtb = E('sd_dtb', [16, 1]); arow = E('sd_arow', [128, 16]); dskb = E('sd_dsk', [128, 8])
        sng = E('sd_sng', [128, 4])
        Hs = [E('sd_H%d' % d, [128, 512]) for d in range(2)]
        xtok = E('sd_xtok', [128, 16, 512]); btok = E('sd_btok', [128, 16, 256]); cbt = E('sd_cbt', [128, 16, 2, 128])
        ct = E('sd_ct', [128, 2, L]); dttok = E('sd_dttok', [128, 16, 16]); ytok = E('sd_ytok', [128, 16, 512])
        raw = E('sd_raw', [128, 8, 514]); xa = E('sd_xa', [128, 8, 512]); dtT = E('sd_dtT', [16, L])
        dtA = E('sd_dtA', [128, 16]); ac16 = E('sd_acum', [128, 16]); acum = ac16[:, 0:8]; t8 = E('sd_t8', [128, 8]); dend = E('sd_dend', [128, 8])
        cdec = E('sd_cdec', [128, 8]); eac = E('sd_eac', [128, 8]); xdt = E('sd_xdt', [128, 8, 64]); xdte = E('sd_xdte', [128, 8, 64])
        segt8 = E('sd_seg8', [128, 8, 128]); dtAb8 = E('sd_dtAb8', [128, 8, 128]); mt8 = E('sd_mt8', [128, 8, 128])
        yo = E('sd_yo', [128, 8, 64]); tmpy = E('sd_tmpy', [128, 512]); hsc = E('sd_hsc', [128, 8, 64])
        zt = E('sd_zt', [128, 4, 128]); yg = E('sd_yg', [128, 4, 128]); sqg = E('sd_sqg', [128, 4, 128]); rs = E('sd_rs', [128, 2, 128])
        og = E('sd_og', [128, 4, 128], BF16)
        fw.dma('sp', cw[:], I['sscw'][l], writes=['sd_cw'])
        fw.dma('sp', dtb[:], I['dtb'][l], writes=['sd_dtb'])
        fw.dma('sp', arow[:], I['alog'][l].partition_broadcast(128), writes=['sd_arow'])
        fw.dma('sp', dskb[:], I['dskip'][l].partition_broadcast(128), writes=['sd_dsk'])
        fw.dma('sp', sng[:], I['sng'][l], writes=['sd_sng'])
        fw.act(arow[:], arow[:], AF.Exp, reads=['sd_arow'], writes=['sd_arow'])
        fw.ts('dve', arow[:], arow[:], -1.0, None, ALU.mult, reads=['sd_arow'], writes=['sd_arow'])
        tri = [S.uinc, S.linc]
        msk = [S.maskf, S.maskb]

        def stage_a(seq):
            Lq = seq_len(seq)
            T = min(512, Lq)
            plv = S.pl[seq][OFF_XBC:OFF_XBC + 1024, :].rearrange("(cc p) t -> p cc t", p=128)
            fw.dma('sp', dtT[:, :Lq], S.pl[seq][OFF_DT:OFF_DT + 16, :], writes=['sd_dtT'])
            fw.act(dtT[:, :Lq], dtT[:, :Lq], AF.Exp, bias=dtb[:, 0:1], reads=['sd_dtT', 'sd_dtb'], writes=['sd_dtT'])
            fw.ts('dve', dtT[:, :Lq], dtT[:, :Lq], 1.0, None, ALU.add, reads=['sd_dtT'], writes=['sd_dtT'])
            fw.act(dtT[:, :Lq], dtT[:, :Lq], AF.Ln, reads=['sd_dtT'], writes=['sd_dtT'])
            for tb in range(Lq // T):
                t0 = tb * T
                lo, hi = max(0, t0 - 1), min(Lq, t0 + T + 1)
                d0 = lo - (t0 - 1)
                if tb == 0:
                    fw.memset('pool', raw[:, :, 0:1], 0.0, writes=['sd_raw'])
                if tb == Lq // T - 1:
                    fw.memset('pool', raw[:, :, T + 1:T + 2], 0.0, writes=['sd_raw'])
                fw.dma('sp', raw[:, :, d0:d0 + hi - lo], plv[:, :, lo:hi], writes=['sd_raw'])
                for cc in range(8):
                    k = ('sd_xa', cc)
                    fw.ts('dve', xa[:, cc, :T], raw[:, cc, 0:T], cw[:, cc, 0:1], cw[:, cc, 3:4], ALU.mult, ALU.add, reads=['sd_raw', 'sd_cw'], writes=[k])
                    fw.stt('dve', xa[:, cc, :T], raw[:, cc, 1:T + 1], cw[:, cc, 1:2], xa[:, cc, :T], ALU.mult, ALU.add, reads=['sd_raw'], writes=[k])
                    fw.stt('dve', xa[:, cc, :T], raw[:, cc, 2:T + 2], cw[:, cc, 2:3], xa[:, cc, :T], ALU.mult, ALU.add, reads=['sd_raw'], writes=[k])
                    fw.act(xa[:, cc, :T], xa[:, cc, :T], AF.Silu, reads=[k], writes=[k])
                xk = [('sd_xa', cc) for cc in range(8)]
                fw.cp('pool', ct[:, :, t0:t0 + T], xa[:, 6:8, :T], reads=xk, writes=['sd_ct'])
                for j in range(T // 128):
                    c = tb * (T // 128) + j
                    sl = slice(j * 128, (j + 1) * 128)
                    for k in range(4):
                        fw.tr(ps[1][:, k * 128:(k + 1) * 128], xa[:, k, sl], S.ident[:], reads=xk, writes=[('ps', 1)])
                    fw.cp('act', xtok[:, c, :], ps[1][:, :], reads=[('ps', 1)], writes=[('sd_xtok', c)])
                    for g in range(2):
                        fw.tr(ps[2][:, g * 128:(g + 1) * 128], xa[:, 4 + g, sl], S.ident[:], reads=xk, writes=[('ps', 2)])
                        fw.mm(ps[2][:, 256 + g * 128:256 + (g + 1) * 128], xa[:, 4 + g, sl], xa[:, 6 + g, sl], reads=xk, writes=[('ps', 2)])
                    fw.cp('dve', btok[:, c, :], ps[2][:, 0:256], reads=[('ps', 2)], writes=[('sd_btok', c)])
                    fw.cp('dve', cbt[:, c, :, :], ps[2][:, 256:512].rearrange("p (g i) -> p g i", g=2), reads=[('ps', 2)], writes=[('sd_cbt', c)])
                    fw.tr(ps[3][:, 0:16], dtT[:16, t0 + j * 128:t0 + (j + 1) * 128], S.ident[:16, :16], reads=['sd_dtT'], writes=[('ps', 3)])
                    fw.cp('dve', dttok[:, c, :], ps[3][:, 0:16], reads=[('ps', 3)], writes=[('sd_dttok', c)])

        def scan(seq, d, with_output, final_out):
            Lq = seq_len(seq)
            ncq = Lq // 128
            H = Hs[d]
            hk = 'sd_H%d' % d
            d8 = slice(d * 8, (d + 1) * 8)
            order = range(ncq) if d == 0 else range(ncq - 1, -1, -1)
            for c in order:
                fw.tt('pool', dtA[:], dttok[:, c, :], arow[:], ALU.mult, reads=[('sd_dttok', c), 'sd_arow'], writes=['sd_dtA'])
                fw.mm(ps[0][:, 0:8], tri[d][:], dtA[:, d8], reads=['sd_dtA'], writes=[('ps', 0)])
                fw.mm(ps[0][:, 8:16], S.ones[:], dtA[:, d8], reads=['sd_dtA'], writes=[('ps', 0)])
                fw.cp('dve', ac16[:], ps[0][:, 0:16], reads=[('ps', 0)], writes=['sd_acum'])
                fw.tt('pool', t8[:], ac16[:, 8:16], acum, ALU.subtract, reads=['sd_acum'], writes=['sd_t8'])
                fw.act(dend[:], t8[:], AF.Exp, reads=['sd_t8'], writes=['sd_dend'])
                fw.act(cdec[:], ac16[:, 8:16], AF.Exp, reads=['sd_acum'], writes=['sd_cdec'])
                fw.tt('pool', xdt[:], xtok[:, c, :].rearrange("p (h q) -> p h q", h=8),
                      dttok[:, c, d8].unsqueeze(2).to_broadcast([128, 8, 64]), ALU.mult, reads=[('sd_xtok', c), ('sd_dttok', c)], writes=['sd_xdt'])
                if with_output:
                    fw.act(eac[:], acum,  AF.Exp, reads=['sd_acum'], writes=['sd_eac'])
                    fw.cp('dve', dtAb8[:], dtA[:, d8].unsqueeze(2).to_broadcast([128, 8, 128]), reads=['sd_dtA'], writes=['sd_dtAb'])
                    for hd in range(8):
                        pa = 1 + hd // 4
                        fw.mm(ps[pa][:, (hd % 4) * 128:(hd % 4 + 1) * 128], dtAb8[:, hd, :], tri[d][:], reads=['sd_dtAb'], writes=[('ps', pa)])
                    for hf in range(2):
                        fw.tt('dve', segt8[:, 4 * hf:4 * hf + 4, :], ps[1 + hf][:, :].rearrange("p (h i) -> p h i", h=4),
                              acum[:, 4 * hf:4 * hf + 4].unsqueeze(2).to_broadcast([128, 4, 128]), ALU.subtract,
                              reads=[('ps', 1 + hf), 'sd_acum'], writes=['sd_seg8'])
                    fw.tt('dve', segt8[:], segt8[:], msk[d][:, :].unsqueeze(1).to_broadcast([128, 8, 128]), ALU.min,
                          reads=['sd_seg8'], writes=['sd_seg8'])
                    fw.act(segt8[:], segt8[:], AF.Exp, reads=['sd_seg8'], writes=['sd_seg8'])
                    fw.tt('dve', mt8[:].rearrange("p (g h) i -> p g h i", g=2), segt8[:].rearrange("p (g h) i -> p g h i", g=2),
                          cbt[:, c, :, :].unsqueeze(2).to_broadcast([128, 2, 4, 128]), ALU.mult, reads=['sd_seg8', ('sd_cbt', c)], writes=['sd_mt8'])
                    for hd in range(8):
                        fw.mm(ps[3][:, hd * 64:(hd + 1) * 64], mt8[:, hd, :], xdt[:, hd, :], reads=['sd_mt8', 'sd_xdt'], writes=[('ps', 3)])
                    for g in range(2):
                        fw.mm(ps[4][:, g * 256:(g + 1) * 256], ct[:, g, c * 128:(c + 1) * 128], H[:, g * 256:(g + 1) * 256],
                              reads=['sd_ct', hk], writes=[('ps', 4)])
                    fw.tt('dve', yo[:], ps[4][:, :].rearrange("p (h q) -> p h q", h=8), eac[:, :].unsqueeze(2).to_broadcast([128, 8, 64]),
                          ALU.mult, reads=[('ps', 4), 'sd_eac'], writes=['sd_yo'])
                    yflat = yo[:].rearrange("p h q -> p (h q)")
                    if d == 0:
                        fw.tt('dve', ytok[:, c, :], ps[3][:, :], yflat, ALU.add, reads=[('ps', 3), 'sd_yo'], writes=[('sd_ytok', c)])
                        fw.tt('pool', tmpy[:].rearrange("p (h q) -> p h q", h=8), xtok[:, c, :].rearrange("p (h q) -> p h q", h=8),
                              dskb[:, :].unsqueeze(2).to_broadcast([128, 8, 64]), ALU.mult, reads=[('sd_xtok', c), 'sd_dsk'], writes=['sd_tmpy'])
                        fw.tt('pool', ytok[:, c, :], ytok[:, c, :], tmpy[:], ALU.add, reads=['sd_tmpy'], writes=[('sd_ytok', c)])
                    else:
                        fw.tt('dve', tmpy[:], ps[3][:, :], yflat, ALU.add, reads=[('ps', 3), 'sd_yo'], writes=['sd_tmpy'])
                        fw.tt('pool', ytok[:, c, :], ytok[:, c, :], tmpy[:], ALU.add, reads=['sd_tmpy'], writes=[('sd_ytok', c)])
                fw.tt('pool', xdte[:], xdt[:], dend[:, :].unsqueeze(2).to_broadcast([128, 8, 64]), ALU.mult, reads=['sd_xdt', 'sd_dend'], writes=['sd_xdte'])
                for g in range(2):
                    fw.mm(ps[5][:, g * 256:(g + 1) * 256], btok[:, c, g * 128:(g + 1) * 128],
                          xdte[:, g * 4:(g + 1) * 4, :].rearrange("p h q -> p (h q)"), reads=[('sd_btok', c), 'sd_xdte'], writes=[('ps', 5)])
                fw.tt('pool', hsc[:], H[:].rearrange("p (h q) -> p h q", h=8), cdec[:, :].unsqueeze(2).to_broadcast([128, 8, 64]), ALU.mult,
                      reads=[hk, 'sd_cdec'], writes=['sd_hsc'])
                fw.tt('dve', H[:], hsc[:].rearrange("p h q -> p (h q)"), ps[5][:, :], ALU.add, reads=['sd_hsc', ('ps', 5)], writes=[hk])
                if final_out:
                    tk = slice(c * 128, (c + 1) * 128)
                    for k in range(4):
                        fw.tr(ps[6][:, k * 128:(k + 1) * 128], ytok[:, c, k * 128:(k + 1) * 128], S.ident[:], reads=[('sd_ytok', c)], writes=[('ps', 6)])
                    fw.dma('sp', zt[:], S.pl[seq][OFF_Z:OFF_Z + 512, tk].rearrange("(k p) t -> p k t", p=128), writes=['sd_zt'])
                    fw.act(zt[:], zt[:], AF.Silu, reads=['sd_zt'], writes=['sd_zt'])
                    fw.tt('dve', yg[:], ps[6][:, :].rearrange("p (k t) -> p k t", k=4), zt[:], ALU.mult, reads=[('ps', 6), 'sd_zt'], writes=['sd_yg'])
                    fw.act(sqg[:], yg[:], AF.Square, reads=['sd_yg'], writes=['sd_sqg'])
                    for g in range(2):
                        fw.mm(ps[7][:, g * 128:(g + 1) * 128], S.ones[:], sqg[:, 2 * g, :], start=True, stop=False, reads=['sd_sqg'], writes=[('ps', 7)])
                        fw.mm(ps[7][:, g * 128:(g + 1) * 128], S.ones[:], sqg[:, 2 * g + 1, :], start=False, stop=True, reads=['sd_sqg'], writes=[('ps', 7)])
                    fw.act(rs[:], ps[7][:, 0:256].rearrange("p (g t) -> p g t", g=2), AF.Sqrt, bias=S.epsc[:, 0:1], scale=1.0 / 256.0,
                           reads=[('ps', 7)], writes=['sd_rs'])
                    fw.op('dve', lambda e: e.reciprocal(rs[:], rs[:]), reads=['sd_rs'], writes=['sd_rs'])
                    for k in range(4):
                        fw.stt('dve', og[:, k, :], yg[:, k, :], sng[:, k:k + 1], rs[:, k // 2, :], ALU.mult, ALU.mult,
                               reads=['sd_yg', 'sd_rs', 'sd_sng'], writes=['sd_og'])
                    fw.dma('sp', S.mix[seq][256:768, tk].rearrange("(k p) t -> p k t", p=128), og[:], reads=['sd_og'], writes=[('mixssd', seq, c)])

        for b in range(2):
            for d in range(2):
                fw.memset('pool', Hs[d][:], 0.0, writes=['sd_H%d' % d])
            stage_a((b, 'c'))
            if SSD_MODE != 'a':
                scan((b, 'c'), 0, ctx_out, False)
                scan((b, 'c'), 1, ctx_out, ctx_out)
            stage_a((b, 'l'))
            if SSD_MODE != 'a':
                scan((b, 'l'), 0, True, False)
                scan((b, 'l'), 1, True, True)
        fw.barrier()


def phase_peer(P, I, S, l, ctx_out):
    nc, fw = P.nc, P.fw
    ps = P.ps
    T = 256
    NB = 4
    with ExitStack() as es2:
        cs = [es2.enter_context(P.sb('pe_cs%d' % i, [128, 2048], F32)) for i in range(4)]
        cbs = [es2.enter_context(P.sb('pe_cb%d' % i, [128, 2048], BF16)) for i in range(4)]
        uv = I['uT'][l].rearrange("(a p) e -> p a e", p=128)
        n = 0
        for blk in range(64):
            k = n % 4
            n += 1
            fw.dma('sp', cs[k][:].rearrange("p (a e) -> p a e", a=8), uv[:, :, blk * 256:(blk + 1) * 256], writes=['pe_cs%d' % k])
            fw.cp('dve' if k % 2 == 0 else 'pool', cbs[k][:], cs[k][:], reads=['pe_cs%d' % k], writes=['pe_cb%d' % k])
            fw.dma('act', S.ubf[blk].rearrange("p a e -> p (a e)"), cbs[k][:], reads=['pe_cb%d' % k], writes=[('ubf', blk)])
        for blk in range(64):
            k = n % 4
            n += 1
            fw.dma('sp', cs[k][:].rearrange("p (ii d) -> p ii d", ii=2),
                   I['v'][l][blk * 256:(blk + 1) * 256, :].rearrange("(ii j) d -> j ii d", j=128), writes=['pe_cs%d' % k])
            fw.cp('dve' if k % 2 == 0 else 'pool', cbs[k][:], cs[k][:], reads=['pe_cs%d' % k], writes=['pe_cb%d' % k])
            fw.dma('act', S.vbf[blk].rearrange("p ii d -> p (ii d)"), cbs[k][:], reads=['pe_cb%d' % k], writes=[('vbf', blk)])
        fw.barrier()
    with ExitStack() as es:
        E = lambda n, sh, dt=F32: es.enter_context(P.sb(n, sh, dt))
        wq = E('pe_wq', [128, 8, 2048], BF16)
        st = [E('pe_s%d' % i, [128, 8, 256]) for i in range(2)]
        k1 = E('pe_k1', [128, 128], BF16); k2 = E('pe_k2', [128, 128], BF16)
        gbuf = E('pe_g', [128, 128, T], BF16)
        ub = E('pe_ub', [128, 2, 8, 256], BF16); vb = E('pe_vb', [128, 2, 2, 1024], BF16)
        h2 = E('pe_h2', [128, 8, T], BF16); sq = E('pe_sq', [128, 8, T]); rstd = E('pe_rstd', [128, T])
        qT = E('pe_qT', [128, 16, T], BF16); s12 = E('pe_s12', [128, 16, 128]); v12 = E('pe_v12', [128, 16, 16])
        wk = E('pe_wk', [128, 128]); wk2 = E('pe_wk2', [128, 256]); t16 = E('pe_t16', [128, 8, 16])
        e16 = E('pe_e16', [128, 8, 16]); zz = E('pe_z', [128, 8]); mz = E('pe_mz', [128, 8])
        thr = E('pe_thr', [128, 8, 16]); bia = E('pe_bia', [128, 8, 16]); pc = E('pe_pc', [128, 3, T])
        Et = [E('pe_E%d' % i, [128, 128]) for i in range(8)]
        Wt = [E('pe_W%d' % i, [128, 128], BF16) for i in range(8)]
        Pt = [E('pe_P%d' % i, [128, 128], BF16) for i in range(8)]
        qr1 = E('pe_qr1', [128, 2, 4, 128], BF16)
        qr2 = E('pe_qr2', [128, 2, 4, 128], BF16)
        gst = [E('pe_gs%d' % i, [128, T]) for i in range(2)]
        At = [E('pe_A%d' % i, [128, T], BF16) for i in range(2)]
        xo = E('pe_xo', [128, 8, T])
        cand = sq
        wv = I['wq'][l].rearrange("(dc p) n -> p dc n", p=128)
        n = 0
        for blk in range(8):
            k = n % 2
            n += 1
            fw.dma('sp', st[k][:], wv[:, :, blk * 256:(blk + 1) * 256], writes=['pe_s%d' % k])
            fw.cp('dve' if k == 0 else 'pool', wq[:, :, blk * 256:(blk + 1) * 256], st[k][:], reads=['pe_s%d' % k], writes=[('pe_wq', blk)])
        for (kt, nm, key) in ((k1, 'k1T', 'pe_k1'), (k2, 'k2T', 'pe_k2')):
            k = n % 2
            n += 1
            fw.dma('sp', st[k][:, 0, 0:128], I[nm][l], writes=['pe_s%d' % k])
            fw.cp('dve', kt[:], st[k][:, 0, 0:128], reads=['pe_s%d' % k], writes=[key])
        fw.barrier()

        def load_tables(blk):
            bb = blk % 2
            fw.dma('sp', ub[:, bb, :, :], S.ubf[blk], writes=[('pe_ub', bb)])
            fw.dma('sp', vb[:, bb, :, :], S.vbf[blk], writes=[('pe_vb', bb)])

        gi = 0
        for seq in seqs_of(ctx_out):
            Lq = seq_len(seq)
            col = seq_col(seq)
            rv = S.res[seq].rearrange("(dc p) t -> p dc t", p=128)
            for grp in range(Lq // T):
                g0 = grp * T
                xt = st[gi % 2]
                xk = 'pe_s%d' % (gi % 2)
                gi += 1
                fw.dma('sp', xt[:], rv[:, :, g0:g0 + T], writes=[xk])
                hk = normmod(P, S, xt, xk, T, S.G2[:, :, col], S.modT[:, 24:32, col], h2, 'pe_h2', sq, rstd, ('ps', 4), ps[4])
                for qc in range(16):
                    pi = 4 + qc % 4
                    for dc in range(8):
                        fw.mm(ps[pi][:, :T], wq[:, dc, qc * 128:(qc + 1) * 128], h2[:, dc, :], start=(dc == 0), stop=(dc == 7),
                              reads=hk if dc in (0, 7) else (), writes=[('ps', pi)])
                    fw.cp('act' if qc % 2 == 0 else 'dve', qT[:, qc, :], ps[pi][:, :T], reads=[('ps', pi)], writes=[('pe_qT', qc)])
                qk = [('pe_qT', qc) for qc in range(16)]
                sqk = [('nm_sq', dc) for dc in range(8)]
                for tt in range(2):
                    tsl = slice(tt * 128, (tt + 1) * 128)
                    for h in range(8):
                        fw.mm(ps[4 + h // 4][:, (h % 4) * 128:(h % 4 + 1) * 128], qT[:, 2 * h, tsl], k1[:], reads=qk + ['pe_k1'], writes=[('ps', 4 + h // 4)])
                        fw.mm(ps[6 + h // 4][:, (h % 4) * 128:(h % 4 + 1) * 128], qT[:, 2 * h + 1, tsl], k2[:], reads=qk + ['pe_k2'], writes=[('ps', 6 + h // 4)])
                    for q4 in range(4):
                        fw.cp('dve' if q4 % 2 == 0 else 'act', s12[:, q4 * 4:(q4 + 1) * 4, :], ps[4 + q4][:, :].rearrange("p (h i) -> p h i", h=4),
                              reads=[('ps', 4 + q4)], writes=[('pe_s12', q4)])
                    sk = [('pe_s12', q4) for q4 in range(4)]
                    for idx in range(16):
                        fw.op('dve', lambda e, idx=idx: e.max(out=v12[:, idx, 0:8], in_=s12[:, idx, :]), reads=sk, writes=['pe_v12'])
                        fw.op('dve', lambda e, idx=idx: e.match_replace(out=wk[:], in_to_replace=v12[:, idx, 0:8], in_values=s12[:, idx, :], imm_value=-1e30),
                              reads=['pe_v12'], writes=['pe_wk'])
                        fw.op('dve', lambda e, idx=idx: e.max(out=v12[:, idx, 8:16], in_=wk[:]), reads=['pe_wk'], writes=['pe_v12'])
                    fw.tt('dve', cand[:].rearrange("p h (a b) -> p h a b", a=16), v12[:, 0:8, :].unsqueeze(3).to_broadcast([128, 8, 16, 16]),
                          v12[:, 8:16, :].unsqueeze(2).to_broadcast([128, 8, 16, 16]), ALU.add, reads=['pe_v12'], writes=sqk)
                    for h in range(8):
                        fw.op('dve', lambda e, h=h: e.max(out=t16[:, h, 0:8], in_=cand[:, h, :]), reads=sqk, writes=['pe_t16'])
                        fw.op('dve', lambda e, h=h: e.match_replace(out=wk2[:], in_to_replace=t16[:, h, 0:8], in_values=cand[:, h, :], imm_value=-1e30),
                              reads=['pe_t16'] + sqk, writes=['pe_wk2'])
                        fw.op('dve', lambda e, h=h: e.max(out=t16[:, h, 8:16], in_=wk2[:]), reads=['pe_wk2'], writes=['pe_t16'])
                    fw.tt('dve', e16[:], t16[:], t16[:, :, 0:1].to_broadcast([128, 8, 16]), ALU.subtract, reads=['pe_t16'], writes=['pe_e16'])
                    fw.act(e16[:], e16[:], AF.Exp, reads=['pe_e16'], writes=['pe_e16'])
                    fw.op('dve', lambda e: e.reduce_sum(out=zz[:], in_=e16[:], axis=AX.X), reads=['pe_e16'], writes=['pe_z'])
                    fw.act(zz[:], zz[:], AF.Ln, reads=['pe_z'], writes=['pe_z'])
                    fw.tt('dve', mz[:], t16[:, :, 0], zz[:], ALU.add, reads=['pe_t16', 'pe_z'], writes=['pe_mz'])
                    fw.tt('dve', thr[:], t16[:, :, 15:16].to_broadcast([128, 8, 16]), v12[:, 0:8, :], ALU.subtract, reads=['pe_t16', 'pe_v12'], writes=['pe_thr'])
                    fw.tt('dve', bia[:], v12[:, 0:8, :], mz[:, :].unsqueeze(2).to_broadcast([128, 8, 16]), ALU.subtract, reads=['pe_mz', 'pe_v12'], writes=['pe_bia'])
                    srcs = [(v12[:, 0:8, :], 'pe_v12'), (thr[:], 'pe_thr'), (bia[:], 'pe_bia')]
                    for slot, (sap, skey) in enumerate(srcs):
                        fw.tr(ps[4][:, slot * 128:(slot + 1) * 128], sap.rearrange("p h a -> p (h a)"), S.ident[:], reads=[skey], writes=[('ps', 4)])
                    fw.cp('dve', pc[:, :, tsl], ps[4][:, 0:384].rearrange("p (s t) -> p s t", s=3), reads=[('ps', 4)], writes=['pe_pc'])
                load_tables(0)
                load_tables(1)
                NQ = T // 4

                def quad_F(u):
                    qb = u % 2
                    t0 = 4 * u
                    for half, qr, kk_ in ((0, qr1, k1), (1, qr2, k2)):
                        src_ap = qT[:, half:16:2, t0:t0 + 4].rearrange("p h t -> p t h").unsqueeze(3).to_broadcast([128, 4, 8, 16])
                        fw.cp('pool' if half == 0 else 'dve', qr[:, qb, :, :].rearrange("p t (h a) -> p t h a", h=8), src_ap,
                              reads=qk if u < 2 else (), writes=[('pe_qr%d' % half, qb)])
                    for k in range(4):
                        t = t0 + k
                        xb = (t // 2) % 4
                        c0 = (t % 2) * 256
                        fw.mm(ps[xb][:, c0:c0 + 128], qr1[:, qb, k, :], k1[:], reads=[('pe_qr0', qb)], writes=[('ps', xb)])
                        fw.mm(ps[xb][:, c0 + 128:c0 + 256], qr2[:, qb, k, :], k2[:], reads=[('pe_qr1', qb)], writes=[('ps', xb)])

                def quad_B(u):
                    t0 = 4 * u
                    gb = 4 + u % 2
                    for k in range(4):
                        t = t0 + k
                        xb = (t // 2) % 4
                        c0 = (t % 2) * 256
                        q = t % 8
                        ek, wkk, pk = ('pe_E', q), ('pe_W', q), ('pe_P', q)
                        fw.act(Et[q][:], ps[xb][:, c0 + 128:c0 + 256], AF.Exp, bias=pc[:, 2, t:t + 1], reads=[('ps', xb), 'pe_pc'], writes=[ek])
                        fw.stt('dve', Wt[q][:], ps[xb][:, c0 + 128:c0 + 256], pc[:, 1, t:t + 1], Et[q][:], ALU.is_ge, ALU.mult,
                               reads=[('ps', xb), ek], writes=[wkk])
                        fw.ts('dve', Pt[q][:], ps[xb][:, c0:c0 + 128], pc[:, 0, t:t + 1], None, ALU.is_equal, reads=[('ps', xb), ek], writes=[pk])
                        fw.mm(ps[gb][:, k * 128:(k + 1) * 128], Wt[q][:], Pt[q][:], reads=[wkk, pk], writes=[('ps', gb)])

                def quad_C(u):
                    gb = 4 + u % 2
                    fw.cp('act', gbuf[:, :, 4 * u:4 * u + 4].rearrange("p i t -> p t i"),
                          ps[gb][:, :].rearrange("p (t i) -> p t i", t=4), reads=[('ps', gb)], writes=[('pe_g', u % 2)])

                for u in range(NQ + 2):
                    if u < NQ:
                        quad_F(u)
                    if 1 <= u <= NQ:
                        quad_B(u - 1)
                    if u >= 2:
                        quad_C(u - 2)
                gk = [('pe_g', q) for q in range(2)]

                def dense_S(i):
                    bb, ii = (i // 2) % 2, i % 2
                    pi = 4 + i % 2
                    for dc in range(8):
                        fw.mm(ps[pi][:, :T], ub[:, bb, dc, ii * 128:(ii + 1) * 128], h2[:, dc, :], start=(dc == 0), stop=(dc == 7),
                              reads=[('pe_ub', bb)] if dc in (0, 7) else (), writes=[('ps', pi)])

                def dense_O(i):
                    bb, ii = (i // 2) % 2, i % 2
                    pi = 4 + i % 2
                    fw.act(gst[i % 2][:], ps[pi][:, :T], AF.Gelu_apprx_tanh, reads=[('ps', pi)], writes=[('pe_gs', i % 2)])
                    fw.tt('pool' if i % 2 == 0 else 'dve', At[i % 2][:], gst[i % 2][:], gbuf[:, i, :], ALU.mult,
                          reads=[('pe_gs', i % 2)] + (gk if i < 2 else []), writes=[('pe_A', i % 2)])
                    for dch in range(8):
                        bk, rg = dch // 2, (dch % 2) * 256
                        fw.mm(ps[bk][:, rg:rg + T], vb[:, bb, ii, dch * 128:(dch + 1) * 128], At[i % 2][:],
                              start=(i == 0 and dch % 2 == 0), stop=(i == 127),
                              reads=[('pe_A', i % 2), ('pe_vb', bb)] if dch in (0, 7) else (), writes=[('ps', bk)])

                dense_S(0)
                for i in range(128):
                    if i + 1 < 128:
                        dense_S(i + 1)
                    dense_O(i)
                    if i % 2 == 1 and (i + 1) // 2 + 1 < 64:
                        load_tables((i + 1) // 2 + 1)
                for dch in range(8):
                    bk, rg = dch // 2, (dch % 2) * 256
                    fw.stt('dve', xo[:, dch, :], ps[bk][:, rg:rg + T], S.modT[:, 40 + dch, col:col + 1], xt[:, dch, :], ALU.mult, ALU.add,
                           reads=[('ps', bk), xk], writes=[('pe_xo', dch)])
                fw.dma('sp', rv[:, :, g0:g0 + T], xo[:], reads=[('pe_xo', d8) for d8 in range(8)], writes=[('resp', seq, grp)])
        fw.barrier()
```

```python
import math
from contextlib import ExitStack
import numpy as np
import ml_dtypes
import concourse.bass as bass
import concourse.mybir as mybir
from concourse.bass_utils import run_bass_kernel_spmd

F32 = mybir.dt.float32
BF16 = mybir.dt.bfloat16
AF = mybir.ActivationFunctionType
ALU = mybir.AluOpType
AX = mybir.AxisListType

NDS = 20
D = 1024
L = 2048
LC = 256
NLAYER = 2
EPS = 1e-6
NCOL = 2576
OFF_Z, OFF_XBC, OFF_DT, OFF_FN = 768, 1280, 2304, 2320
PI = math.pi
import os
SSD_MODE = os.environ.get('SSD_MODE', 'full')


class FW:
    def __init__(self, nc):
        self.nc = nc
        self.eng = dict(pe=nc.tensor, dve=nc.vector, act=nc.scalar, pool=nc.gpsimd, sp=nc.sync)
        self.esem = {k: nc.alloc_semaphore("es_" + k) for k in self.eng}
        self.ecnt = {k: 0 for k in self.eng}
        self.dsem = [nc.alloc_semaphore("ds_%d" % i) for i in range(NDS)]
        self.dcnt = [0] * NDS
        self.waited = {k: {} for k in self.eng}
        self.buf = {}
        self.dsem_of = {}
        self.rr = 0
        self.ninst = 0

    def _b(self, key):
        b = self.buf.get(key)
        if b is None:
            b = dict(w=None, r={})
            self.buf[key] = b
        return b

    def _wait(self, en, tok):
        if tok is None:
            return
        kind, who, n = tok
        if kind == 'e':
            if who == en and en == 'pe':
                return
            if self.waited[en].get(('e', who), 0) >= n:
                return
            self.eng[en].wait_ge(self.esem[who], n)
            self.waited[en][('e', who)] = n
        else:
            val = self.dcnt[who]
            if self.waited[en].get(('d', who), 0) >= n:
                return
            self.eng[en].wait_ge(self.dsem[who], 16 * val)
            self.waited[en][('d', who)] = val

    def _deps(self, en, reads, writes):
        for k in reads:
            self._wait(en, self._b(k)['w'])
        for k in writes:
            b = self._b(k)
            self._wait(en, b['w'])
            for t in b['r'].values():
                self._wait(en, t)

    def _commit(self, tok, reads, writes):
        for k in writes:
            self.buf[k] = dict(w=tok, r={})
        for k in reads:
            if k in writes:
                continue
            self._b(k)['r'][(tok[0], tok[1])] = tok

    def op(self, en, fn, reads=(), writes=()):
        self._deps(en, reads, writes)
        ins = fn(self.eng[en])
        self.ecnt[en] += 1
        ins.then_inc(self.esem[en], 1)
        tok = ('e', en, self.ecnt[en])
        self._commit(tok, reads, writes)
        self.ninst += 1
        return tok

    def dma(self, en, out, in_, reads=(), writes=(), **kw):
        self._deps(en, reads, writes)
        key = writes[0] if writes else ('anon',)
        idx = self.dsem_of.get(key)
        if idx is None:
            idx = self.rr % NDS
            self.rr += 1
            self.dsem_of[key] = idx
        ins = self.eng[en].dma_start(out=out, in_=in_, **kw)
        self.dcnt[idx] += 1
        ins.then_inc(self.dsem[idx], 16)
        tok = ('d', idx, self.dcnt[idx])
        self._commit(tok, reads, writes)
        self.ninst += 1
        return tok

    def barrier(self):
        for en in self.eng:
            for who in self.eng:
                if who != en and self.ecnt[who] > self.waited[en].get(('e', who), 0):
                    self.eng[en].wait_ge(self.esem[who], self.ecnt[who])
                    self.waited[en][('e', who)] = self.ecnt[who]
            for i in range(NDS):
                if self.dcnt[i] > self.waited[en].get(('d', i), 0):
                    self.eng[en].wait_ge(self.dsem[i], 16 * self.dcnt[i])
                    self.waited[en][('d', i)] = self.dcnt[i]
        self.buf = {}

    def mm(self, out, lhsT, rhs, start=True, stop=True, reads=(), writes=()):
        return self.op('pe', lambda e: e.matmul(out, lhsT, rhs, start=start, stop=stop), reads, writes)

    def tr(self, out, in_, ident, reads=(), writes=()):
        return self.op('pe', lambda e: e.transpose(out, in_, ident), reads, writes)

    def act(self, out, in_, func, bias=None, scale=None, reads=(), writes=()):
        kw = {}
        if bias is not None:
            kw['bias'] = bias
        if scale is not None:
            kw['scale'] = scale
        return self.op('act', lambda e: e.activation(out=out, in_=in_, func=func, **kw), reads, writes)

    def ts(self, en, out, in0, s1, s2, op0, op1=None, reads=(), writes=()):
        kw = {}
        if op1 is not None:
            kw['op1'] = op1
        return self.op(en, lambda e: e.tensor_scalar(out, in0, s1, s2, op0, **kw), reads, writes)

    def tt(self, en, out, in0, in1, op, reads=(), writes=()):
        return self.op(en, lambda e: e.tensor_tensor(out, in0, in1, op), reads, writes)

    def stt(self, en, out, in0, scalar, in1, op0, op1, reads=(), writes=()):
        en = 'dve'
        return self.op(en, lambda e: e.scalar_tensor_tensor(out, in0, scalar, in1, op0, op1), reads, writes)

    def cp(self, en, out, in_, reads=(), writes=()):
        if en == 'act':
            return self.op('act', lambda e: e.copy(out, in_), reads, writes)
        return self.op(en, lambda e: e.tensor_copy(out, in_), reads, writes)

    def memset(self, en, ap, val, writes=()):
        return self.op(en, lambda e: e.memset(ap, val), (), writes)


_CONST = None


def _bf(a):
    return np.ascontiguousarray(a.astype(ml_dtypes.bfloat16))


def _consts():
    global _CONST
    if _CONST is not None:
        return _CONST
    c = {}
    c['ident'] = np.eye(128, dtype=np.float32)
    j = np.arange(128)[:, None]
    i = np.arange(128)[None, :]
    c['uinc'] = (j <= i).astype(np.float32)
    c['linc'] = (j >= i).astype(np.float32)
    c['maskf'] = np.where(i >= j, 0.0, -1e4).astype(np.float32)
    c['maskb'] = np.where(j >= i, 0.0, -1e4).astype(np.float32)
    c['ones'] = np.ones((128, 128), np.float32)
    a = np.arange(64)
    ang = 2 * np.pi * np.outer(a, a) / 64.0
    cb = np.zeros((128, 128)); sb = np.zeros((128, 128))
    for g in range(2):
        cb[g * 64:(g + 1) * 64, g * 64:(g + 1) * 64] = np.cos(ang) / 8.0
        sb[g * 64:(g + 1) * 64, g * 64:(g + 1) * 64] = np.sin(ang) / 8.0
    c['cbd'] = cb.astype(np.float32)
    c['sbd'] = sb.astype(np.float32)
    for tag, Lq in (('l', L), ('c', LC)):
        nsc = Lq // 128
        N = 2 * Lq
        nf = Lq + 1
        nfc = (nf + 127) // 128
        t = np.linspace(0.0, 1.0, Lq, dtype=np.float32)[:, None]
        w = (2.0 * np.pi * np.arange(Lq, dtype=np.float32)[:, None] / Lq).astype(np.float32)
        f = np.linspace(1e-4, 15, 16, dtype=np.float32)[None, :]
        z = np.concatenate([t, np.cos(f * w), -np.sin(f * w)], axis=-1).astype(np.float32)
        c['zT_' + tag] = np.ascontiguousarray(z.T)
        max_decay = math.log(1e-2) / 0.3
        min_decay = math.log(1e-2) / 1.5
        deltas = np.abs(np.linspace(min_decay, max_decay, 256, dtype=np.float32))
        win = np.exp(-t * deltas).astype(np.float32)
        winb = win.copy()
        winb[0] = 0.0
        lay = lambda m: np.ascontiguousarray(m.reshape(nsc, 128, 256).transpose(1, 0, 2))
        c['win_' + tag] = np.stack([lay(win), lay(winb)]).astype(np.float32)
        s = np.arange(Lq, dtype=np.float64)[:, None]
        ff = np.arange(nfc * 128, dtype=np.float64)[None, :]
        th = 2 * np.pi * s * ff / N
        valid = (ff < nf)
        Cf = np.cos(th) * valid
        Sf = -np.sin(th) * valid
        fl = lambda m: np.ascontiguousarray(m.reshape(nsc, 128, nfc, 128).transpose(2, 1, 0, 3))
        c['cf_' + tag] = _bf(fl(Cf))
        c['sf_' + tag] = _bf(fl(Sf))
        TB = min(512, Lq)
        ntb = Lq // TB
        fcol = np.arange(nfc * 128, dtype=np.float64)[:, None]
        tt = np.arange(Lq, dtype=np.float64)[None, :]
        wgt = np.where((fcol == 0) | (fcol == Lq), 1.0, 2.0) * (fcol < nf) / N
        th2 = 2 * np.pi * fcol * tt / N
        Ci = wgt * np.cos(th2)
        Si = -wgt * np.sin(th2)
        il = lambda m: np.ascontiguousarray(m.reshape(nfc, 128, ntb, TB).transpose(2, 1, 0, 3))
        c['ci_' + tag] = _bf(il(Ci))
        c['si_' + tag] = _bf(il(Si))
        t1 = np.arange(Lq, dtype=np.float64)
        th3 = 2 * np.pi * np.outer(t1, t1) / Lq
        CL = np.cos(th3) / math.sqrt(Lq)
        SLn = -np.sin(th3) / math.sqrt(Lq)
        ll = lambda m: np.ascontiguousarray(m.reshape(nsc, 128, ntb, TB).transpose(2, 1, 0, 3))
        c['cl_' + tag] = _bf(ll(CL))
        c['sl_' + tag] = _bf(ll(SLn))
    _CONST = c
    return c


def _pc(v, n):
    return np.ascontiguousarray(np.asarray(v, np.float32).reshape(n, 128).T)


class Prog:
    def __init__(self, layers=(0, 1), phases=None, dbg=False):
        self.layers = layers
        self.phases = phases
        self.dbg = dbg
        nc = bass.Bass("TRN2", target_bir_lowering=False)
        self.nc = nc
        self.fw = FW(nc)
        self.inputs = {}
        self.uid = 0
        self.ps = [nc.alloc_psum_tensor("psb%d" % i, [128, 512], F32) for i in range(8)]

    def inp(self, name, shape, dt=F32):
        t = self.nc.dram_tensor(name, list(shape), dt, kind="ExternalInput").ap()
        self.inputs[name] = t
        return t

    def scratch(self, name, shape, dt=F32, out=False):
        kind = "ExternalOutput" if (out or self.dbg) else "Internal"
        return self.nc.dram_tensor(name, list(shape), dt, kind=kind).ap()

    def sb(self, name, shape, dt):
        self.uid += 1
        return self.nc.sbuf_tensor('%s_u%d' % (name, self.uid), shape, dt)

    def want(self, ph):
        return self.phases is None or ph in self.phases


def _declare(P):
    c = _consts()
    I = {}
    I['xT'] = P.inp('xT', [2, D, L])
    I['ctxT'] = P.inp('ctxT', [2, D, LC])
    I['cT'] = P.inp('cT', [128, 8, 3])
    I['w_ada'] = P.inp('w_ada', [NLAYER, D, 6 * D])
    I['b_adaT'] = P.inp('b_adaT', [NLAYER, 128, 48])
    I['gn1'] = P.inp('gn1', [NLAYER, 128, 8])
    I['gn2'] = P.inp('gn2', [NLAYER, 128, 8])
    I['gfin'] = P.inp('gfin', [128, 8])
    I['w_in'] = P.inp('w_in', [NLAYER, D, NCOL])
    I['hycw'] = P.inp('hycw', [NLAYER, 128, 6, 4])
    I['hfw1'] = P.inp('hfw1', [NLAYER, 33, 64])
    I['hfw2'] = P.inp('hfw2', [NLAYER, 64, 64])
    I['hfw3'] = P.inp('hfw3', [NLAYER, 64, 512])
    I['hfv'] = P.inp('hfv', [NLAYER, 64, 3])
    I['hybias'] = P.inp('hybias', [NLAYER, 128, 2])
    I['sscw'] = P.inp('sscw', [NLAYER, 128, 8, 4])
    I['dtb'] = P.inp('dtb', [NLAYER, 16, 1])
    I['alog'] = P.inp('alog', [NLAYER, 1, 16])
    I['dskip'] = P.inp('dskip', [NLAYER, 1, 8])
    I['sng'] = P.inp('sng', [NLAYER, 128, 4])
    I['w_out'] = P.inp('w_out', [NLAYER, D, D])
    I['wq'] = P.inp('wq', [NLAYER, D, 2048])
    I['k1T'] = P.inp('k1T', [NLAYER, 128, 128])
    I['k2T'] = P.inp('k2T', [NLAYER, 128, 128])
    I['uT'] = P.inp('uT', [NLAYER, D, 16384])
    I['v'] = P.inp('v', [NLAYER, 16384, D])
    for k, a in c.items():
        I[k] = P.inp('c_' + k, a.shape, BF16 if a.dtype == ml_dtypes.bfloat16 else F32)
    return I


class Ctx:
    pass


def build(layers=(0, 1), phases=None, dbg=False, final=True):
    P = Prog(layers, phases, dbg)
    nc, fw = P.nc, P.fw
    I = _declare(P)
    S = Ctx()
    S.res = {}
    for b in range(2):
        S.res[(b, 'l')] = P.scratch('res_l%d' % b, [D, L])
        S.res[(b, 'c')] = P.scratch('res_c%d' % b, [D, LC])
    S.pl = {}
    S.mix = {}
    for b in range(2):
        S.pl[(b, 'l')] = P.scratch('pl_l%d' % b, [NCOL, L])
        S.pl[(b, 'c')] = P.scratch('pl_c%d' % b, [NCOL, LC])
        S.mix[(b, 'l')] = P.scratch('mix_l%d' % b, [D, L], BF16)
        S.mix[(b, 'c')] = P.scratch('mix_c%d' % b, [D, LC], BF16)
    S.khat = {'l': P.scratch('khat_l', [2, 17 * 128, 256]), 'c': P.scratch('khat_c', [2, 3 * 128, 256])}
    S.ubf = P.scratch('ubf', [64, 128, 8, 256], BF16)
    S.vbf = P.scratch('vbf', [64, 128, 2, 1024], BF16)
    S.outT = P.scratch('outT', [2, D, L], F32, out=True)

    A = lambda n, sh, dt=F32: nc.alloc_sbuf_tensor('sb_' + n, sh, dt)
    S.ident = A('ident', [128, 128]); S.ones = A('ones', [128, 128])
    S.uinc = A('uinc', [128, 128]); S.linc = A('linc', [128, 128])
    S.maskf = A('maskf', [128, 128]); S.maskb = A('maskb', [128, 128])
    S.modT = A('modT', [128, 48, 3])
    S.G1 = A('G1', [128, 8, 3]); S.G2 = A('G2', [128, 8, 3])
    S.gfin = A('gfin', [128, 8]); S.zero8 = A('zero8', [128, 8])
    S.hyn = A('hyn', [128, 2, 2])
    for nm in ('ident', 'ones', 'uinc', 'linc', 'maskf', 'maskb'):
        fw.dma('sp', getattr(S, nm)[:], I[nm], writes=[nm])
    fw.dma('sp', S.gfin[:], I['gfin'], writes=['gfin'])
    fw.memset('pool', S.zero8[:], 0.0, writes=['zero8'])
    S.epsc = A('epsc', [128, 1])
    fw.memset('pool', S.epsc[:], EPS, writes=['epsc'])
    S.negpi = A('negpi', [128, 1])
    fw.memset('pool', S.negpi[:], -PI, writes=['negpi'])
    fw.barrier()

    def src(l, seq):
        b, kind = seq
        if l == layers[0] and l == 0:
            return I['xT'][b] if kind == 'l' else I['ctxT'][b]
        return S.res[seq]

    for l in layers:
        ctx_out = l < NLAYER - 1
        if P.want('mod'):
            phase_mod(P, I, S, l)
        if P.want('proj'):
            phase_proj(P, I, S, l, src)
        if P.want('filt'):
            phase_filt(P, I, S, l, 'l')
            if ctx_out:
                phase_filt(P, I, S, l, 'c')
        if P.want('hy'):
            phase_hy(P, I, S, l, 'l')
            if ctx_out:
                phase_hy(P, I, S, l, 'c')
        if P.want('fn'):
            phase_fn(P, I, S, l, 'l')
            if ctx_out:
                phase_fn(P, I, S, l, 'c')
        if P.want('ssd'):
            phase_ssd(P, I, S, l, ctx_out)
        if P.want('out'):
            phase_out(P, I, S, l, src, ctx_out)
        if P.want('peer'):
            phase_peer(P, I, S, l, ctx_out)
    if final and P.want('final'):
        phase_final(P, I, S)
    if dbg:
        dh = P.scratch('dbg_hyn', [128, 4])
        fw.dma('sp', dh, S.hyn[:].rearrange("p a b -> p (a b)"), writes=['dbg_hyn'])
    fw.barrier()
    return P


def phase_mod(P, I, S, l):
    nc, fw = P.nc, P.fw
    with ExitStack() as es:
        cin = es.enter_context(P.sb('m_c', [128, 8, 3], F32))
        sc = es.enter_context(P.sb('m_sc', [128, 8, 3], F32))
        w0 = es.enter_context(P.sb('m_w0', [128, 8, 512], F32))
        w1 = es.enter_context(P.sb('m_w1', [128, 8, 512], F32))
        bada = es.enter_context(P.sb('m_b', [128, 48], F32))
        g1 = es.enter_context(P.sb('m_g1', [128, 8], F32))
        g2 = es.enter_context(P.sb('m_g2', [128, 8], F32))
        tmp = es.enter_context(P.sb('m_t', [128, 8, 3], F32))
        wb = [w0, w1]
        fw.dma('sp', cin[:], I['cT'], writes=['m_c'])
        fw.dma('sp', bada[:], I['b_adaT'][l], writes=['m_b'])
        fw.dma('sp', g1[:], I['gn1'][l], writes=['m_g1'])
        fw.dma('sp', g2[:], I['gn2'][l], writes=['m_g2'])
        fw.act(sc[:], cin[:], AF.Silu, reads=['m_c'], writes=['m_sc'])
        wv = I['w_ada'][l].rearrange("(dc p) n -> p dc n", p=128)
        for blk in range(12):
            w = wb[blk % 2]
            wk = 'm_w%d' % (blk % 2)
            fw.dma('sp', w[:], wv[:, :, blk * 512:(blk + 1) * 512], writes=[wk])
            for j in range(4):
                cc = blk * 4 + j
                pk = ('ps', cc % 2)
                pt = P.ps[cc % 2][:, 0:3]
                for dc in range(8):
                    fw.mm(pt, w[:, dc, j * 128:(j + 1) * 128], sc[:, dc, :], start=(dc == 0), stop=(dc == 7),
                          reads=[wk, 'm_sc'], writes=[pk])
                fw.ts('dve', S.modT[:, cc, :], pt, bada[:, cc:cc + 1], None, ALU.add, reads=[pk, 'm_b'], writes=['modT'])
        for (G, g, gk, c0, nm) in ((S.G1, g1, 'm_g1', 8, 'G1'), (S.G2, g2, 'm_g2', 32, 'G2')):
            fw.ts('dve', tmp[:], S.modT[:, c0:c0 + 8, :], 1.0, None, ALU.add, reads=['modT'], writes=['m_t'])
            fw.tt('dve', G[:], tmp[:], g[:, :].unsqueeze(2).to_broadcast([128, 8, 3]), ALU.mult, reads=['m_t', gk], writes=[nm])
        fw.barrier()


def seqs_of(ctx_too=True):
    out = []
    for b in range(2):
        out.append((b, 'l'))
        if ctx_too:
            out.append((b, 'c'))
    return out


def seq_len(seq):
    return L if seq[1] == 'l' else LC


def seq_col(seq):
    return seq[0] if seq[1] == 'l' else 2


def normmod(P, S, xt, xk, T, Gap, shap, hm, hk, sq, rstd, psk, pst):
    fw = P.fw
    fw.act(sq[:, :, :T], xt[:, :, :T], AF.Square, reads=[xk], writes=[('nm_sq', dc) for dc in range(8)])
    for dc in range(8):
        fw.mm(pst[:, :T], S.ones[:], sq[:, dc, :T], start=(dc == 0), stop=(dc == 7), reads=[('nm_sq', dc), 'ones'], writes=[psk])
    fw.act(rstd[:, :T], pst[:, :T], AF.Sqrt, bias=S.epsc[:, 0:1], scale=1.0 / D, reads=[psk, 'epsc'], writes=['nm_rstd'])
    fw.op('dve', lambda e: e.reciprocal(rstd[:, :T], rstd[:, :T]), reads=['nm_rstd'], writes=['nm_rstd'])
    for dc in range(8):
        en = 'dve' if dc % 2 == 0 else 'pool'
        fw.stt(en, sq[:, dc, :T], xt[:, dc, :T], Gap[:, dc:dc + 1], rstd[:, :T], ALU.mult, ALU.mult,
               reads=[xk, 'nm_rstd'], writes=[('nm_sq', dc)])
        fw.act(hm[:, dc, :T], sq[:, dc, :T], AF.Identity, bias=shap[:, dc:dc + 1], reads=[('nm_sq', dc)], writes=[(hk, dc)])
    return [(hk, dc) for dc in range(8)]


def load_cast_weight(P, dst, dstk, srcv, ncols, stg, stgk):
    fw = P.fw
    nb = (ncols + 511) // 512
    for blk in range(nb):
        c0 = blk * 512
        c1 = min(ncols, c0 + 512)
        st = stg[blk % 2]
        sk = stgk[blk % 2]
        fw.dma('sp', st[:, :, :c1 - c0], srcv[:, :, c0:c1], writes=[sk])
        fw.cp('dve' if blk % 2 == 0 else 'pool', dst[:, :, c0:c1], st[:, :, :c1 - c0], reads=[sk], writes=[(dstk, blk)])
    return [(dstk, blk) for blk in range(nb)]


def phase_proj(P, I, S, l, src):
    nc, fw = P.nc, P.fw
    with ExitStack() as es:
        winb = es.enter_context(P.sb('p_win', [128, 8, NCOL], BF16))
        s0 = es.enter_context(P.sb('p_s0', [128, 8, 512], F32))
        s1 = es.enter_context(P.sb('p_s1', [128, 8, 512], F32))
        sq = es.enter_context(P.sb('p_sq', [128, 8, 512], F32))
        rstd = es.enter_context(P.sb('p_rstd', [128, 512], F32))
        hm = es.enter_context(P.sb('p_hm', [128, 8, 512], BF16))
        ob = es.enter_context(P.sb('p_o', [128, 4, 512], F32))
        wkeys = load_cast_weight(P, winb, 'p_win', I['w_in'][l].rearrange("(dc p) n -> p dc n", p=128), NCOL, [s0, s1], ['p_s0', 'p_s1'])
        xb = [s0, s1]
        chunks = [(c0, min(128, NCOL - c0)) for c0 in range(0, OFF_DT, 128)] + [(OFF_DT, 16)] + [(OFF_FN, 128), (OFF_FN + 128, 128)]
        it = 0
        oi = 0
        for seq in seqs_of(True):
            Lq = seq_len(seq)
            T = min(512, Lq)
            col = seq_col(seq)
            xv = src(l, seq).rearrange("(dc p) t -> p dc t", p=128)
            for tb in range(Lq // T):
                xt = xb[it % 2]
                xk = 'p_s%d' % (it % 2)
                it += 1
                fw.dma('sp', xt[:, :, :T], xv[:, :, tb * T:(tb + 1) * T], reads=[('res', seq)], writes=[xk])
                hk = normmod(P, S, xt, xk, T, S.G1[:, :, col], S.modT[:, 0:8, col], hm, 'p_hm', sq, rstd, ('ps', 0), P.ps[0])
                for ci, (c0, cw) in enumerate(chunks):
                    pb = 1 + ci % 3
                    pt = P.ps[pb][:cw, :T]
                    for dc in range(8):
                        fw.mm(pt, winb[:, dc, c0:c0 + cw], hm[:, dc, :T], start=(dc == 0), stop=(dc == 7),
                              reads=wkeys + hk if dc in (0, 7) else (), writes=[('ps', pb)])
                    o = ob[:cw, oi % 4, :T]
                    ok = ('p_o', oi % 4)
                    oi += 1
                    if ci % 2 == 0:
                        fw.cp('act', o, pt, reads=[('ps', pb)], writes=[ok])
                    else:
                        fw.cp('dve', o, pt, reads=[('ps', pb)], writes=[ok])
                    fw.dma('sp', S.pl[seq][c0:c0 + cw, tb * T:(tb + 1) * T], o, reads=[ok], writes=[('pl', seq, ci, tb)])
        fw.barrier()


def phase_final(P, I, S):
    nc, fw = P.nc, P.fw
    with ExitStack() as es:
        s0 = es.enter_context(P.sb('f_s0', [128, 8, 512], F32))
        s1 = es.enter_context(P.sb('f_s1', [128, 8, 512], F32))
        sq = es.enter_context(P.sb('f_sq', [128, 8, 512], F32))
        rstd = es.enter_context(P.sb('f_rstd', [128, 512], F32))
        o0 = es.enter_context(P.sb('f_o0', [128, 8, 512], F32))
        o1 = es.enter_context(P.sb('f_o1', [128, 8, 512], F32))
        xb = [s0, s1]
        ob = [o0, o1]
        it = 0
        for b in range(2):
            xv = S.res[(b, 'l')].rearrange("(dc p) t -> p dc t", p=128)
            ov = S.outT[b].rearrange("(dc p) t -> p dc t", p=128)
            for tb in range(L // 512):
                xt = xb[it % 2]; xk = 'f_s%d' % (it % 2)
                o = ob[it % 2]; ok = 'f_o%d' % (it % 2)
                it += 1
                fw.dma('sp', xt[:], xv[:, :, tb * 512:(tb + 1) * 512], writes=[xk])
                hk = normmod(P, S, xt, xk, 512, S.gfin, S.zero8, o, ok + 'h', sq, rstd, ('ps', 0), P.ps[0])
                fw.dma('sp', ov[:, :, tb * 512:(tb + 1) * 512], o[:], reads=hk, writes=[('outT', b, tb)])
        fw.barrier()


def prep_shared(inp):
    f = lambda a: np.ascontiguousarray(np.asarray(a, np.float32))
    sh = {}
    sh['w_ada'] = f(inp['w_ada'])
    sh['b_adaT'] = np.stack([_pc(inp['b_ada'][l], 48) for l in range(NLAYER)])
    sh['gn1'] = np.stack([_pc(inp['g_norm1'][l], 8) for l in range(NLAYER)])
    sh['gn2'] = np.stack([_pc(inp['g_norm2'][l], 8) for l in range(NLAYER)])
    sh['gfin'] = _pc(inp['g_final'], 8)
    sh['w_in'] = f(inp['w_in'])
    hy = []
    for l in range(NLAYER):
        m = np.concatenate([np.asarray(inp['hy_conv_w'][l], np.float32), np.asarray(inp['hy_conv_b'][l], np.float32)[None]], 0)
        hy.append(np.ascontiguousarray(m.reshape(4, 6, 128).transpose(2, 1, 0)))
    sh['hycw'] = np.stack(hy)
    sh['hfw1'] = f(inp['hf_w1']); sh['hfw2'] = f(inp['hf_w2']); sh['hfw3'] = f(inp['hf_w3'])
    sh['hfv'] = np.ascontiguousarray(np.stack([inp['hf_b1'], inp['hf_b2'], inp['hf_freq']], axis=-1).astype(np.float32))
    sh['hybias'] = np.stack([_pc(inp['hy_bias'][l], 2) for l in range(NLAYER)])
    ss = []
    for l in range(NLAYER):
        m = np.concatenate([np.asarray(inp['ssd_conv_w'][l], np.float32), np.asarray(inp['ssd_conv_b'][l], np.float32)[None]], 0)
        ss.append(np.ascontiguousarray(m.reshape(4, 8, 128).transpose(2, 1, 0)))
    sh['sscw'] = np.stack(ss)
    sh['dtb'] = f(np.asarray(inp['ssd_dt_bias']).reshape(NLAYER, 16, 1))
    sh['alog'] = f(np.asarray(inp['ssd_a_log']).reshape(NLAYER, 1, 16))
    sh['dskip'] = f(np.asarray(inp['ssd_d']).reshape(NLAYER, 1, 8))
    sh['sng'] = np.stack([_pc(inp['ssd_norm_g'][l], 4) for l in range(NLAYER)])
    sh['w_out'] = f(inp['w_out'])
    sh['wq'] = f(inp['peer_wq'])
    sh['k1T'] = np.ascontiguousarray(np.asarray(inp['peer_k1'], np.float32).transpose(0, 2, 1))
    sh['k2T'] = np.ascontiguousarray(np.asarray(inp['peer_k2'], np.float32).transpose(0, 2, 1))
    sh['uT'] = np.ascontiguousarray(np.asarray(inp['peer_u'], np.float32).transpose(0, 2, 1))
    sh['v'] = f(inp['peer_v'])
    for k, a in _consts().items():
        sh['c_' + k] = a
    return sh


def prep_core(inp, core):
    m = {}
    x = np.asarray(inp['x'], np.float32)[2 * core:2 * core + 2]
    cx = np.asarray(inp['ctx'], np.float32)[2 * core:2 * core + 2]
    m['xT'] = np.ascontiguousarray(x.transpose(0, 2, 1))
    m['ctxT'] = np.ascontiguousarray(cx.transpose(0, 2, 1))
    cv = np.stack([np.asarray(inp['c'], np.float32)[2 * core], np.asarray(inp['c'], np.float32)[2 * core + 1],
                   np.asarray(inp['c_ctx'], np.float32)], axis=-1)
    m['cT'] = np.ascontiguousarray(cv.reshape(8, 128, 3).transpose(1, 0, 2))
    return m


_PROG = None


def kernel(**inputs):
    global _PROG
    if _PROG is None:
        _PROG = build()
    P = _PROG
    sh = prep_shared(inputs)
    in_maps = []
    for core in range(8):
        m = dict(sh)
        m.update(prep_core(inputs, core))
        in_maps.append({k: m[k] for k in P.inputs})
    res = run_bass_kernel_spmd(P.nc, in_maps, core_ids=list(range(8)))
    outs = [np.asarray(r['outT']).transpose(0, 2, 1) for r in res.results]
    return np.ascontiguousarray(np.concatenate(outs, axis=0).astype(np.float32))


def phase_fn(P, I, S, l, tag):
    nc, fw = P.nc, P.fw
    Lq = L if tag == 'l' else LC
    nsc = Lq // 128
    TB = min(512, Lq)
    ntb = Lq // TB
    with ExitStack() as es:
        ut = es.enter_context(P.sb('fn_ut', [128, 4, Lq], F32))
        cbd = es.enter_context(P.sb('fn_cbd', [128, 128], F32))
        sbd = es.enter_context(P.sb('fn_sbd', [128, 128], F32))
        atok = es.enter_context(P.sb('fn_a', [128, nsc, 512], BF16))
        btok = es.enter_context(P.sb('fn_b', [128, nsc, 512], BF16))
        cl = es.enter_context(P.sb('fn_cl', [128, nsc, TB], BF16))
        sl = es.enter_context(P.sb('fn_sl', [128, nsc, TB], BF16))
        ob = es.enter_context(P.sb('fn_o', [128, 2, TB], BF16))
        fw.dma('sp', cbd[:], I['cbd'], writes=['fn_cbd'])
        fw.dma('sp', sbd[:], I['sbd'], writes=['fn_sbd'])
        for b in range(2):
            for ch in range(2):
                fw.dma('sp', ut[:, b * 2 + ch, :], S.pl[(b, tag)][OFF_FN + ch * 128:OFF_FN + (ch + 1) * 128, :], writes=[('fn_ut', b * 2 + ch)])
        utk = [('fn_ut', m) for m in range(4)]
        for tc in range(nsc):
            pa, pb = 2 * (tc % 2), 2 * (tc % 2) + 1
            for m in range(4):
                fw.mm(P.ps[pa][:, m * 128:(m + 1) * 128], ut[:, m, tc * 128:(tc + 1) * 128], cbd[:], reads=utk + ['fn_cbd'], writes=[('ps', pa)])
                fw.mm(P.ps[pb][:, m * 128:(m + 1) * 128], ut[:, m, tc * 128:(tc + 1) * 128], sbd[:], reads=utk + ['fn_sbd'], writes=[('ps', pb)])
            fw.cp('act', atok[:, tc, :], P.ps[pa][:, :], reads=[('ps', pa)], writes=[('fn_a', tc)])
            fw.cp('dve', btok[:, tc, :], P.ps[pb][:, :], reads=[('ps', pb)], writes=[('fn_b', tc)])
        ak = [('fn_a', tc) for tc in range(nsc)]
        bk = [('fn_b', tc) for tc in range(nsc)]
        oi = 0
        for tb in range(ntb):
            fw.dma('sp', cl[:], I['cl_' + tag][tb], writes=['fn_cl'])
            fw.dma('sp', sl[:], I['sl_' + tag][tb], writes=['fn_sl'])
            for m in range(4):
                b, ch = m // 2, m % 2
                pi = 4 + m % 2
                pt = P.ps[pi][:, :TB]
                for tc in range(nsc):
                    fw.mm(pt, atok[:, tc, m * 128:(m + 1) * 128], cl[:, tc, :], start=(tc == 0), stop=False,
                          reads=ak + ['fn_cl'] if tc in (0, nsc - 1) else (), writes=[('ps', pi)])
                for tc in range(nsc):
                    fw.mm(pt, btok[:, tc, m * 128:(m + 1) * 128], sl[:, tc, :], start=False, stop=(tc == nsc - 1),
                          reads=bk + ['fn_sl'] if tc in (0, nsc - 1) else (), writes=[('ps', pi)])
                o = ob[:, oi % 2, :]
                ok = ('fn_o', oi % 2)
                oi += 1
                fw.cp('act' if m % 2 == 0 else 'dve', o, pt, reads=[('ps', pi)], writes=[ok])
                fw.dma('sp', S.mix[(b, tag)][768 + ch * 128:768 + (ch + 1) * 128, tb * TB:(tb + 1) * TB], o, reads=[ok], writes=[('mixfn', b, ch, tb)])
        fw.barrier()


def phase_out(P, I, S, l, src, ctx_out):
    nc, fw = P.nc, P.fw
    with ExitStack() as es:
        wout = es.enter_context(P.sb('o_w', [128, 8, 1024], BF16))
        s0 = es.enter_context(P.sb('o_s0', [128, 8, 512], F32))
        s1 = es.enter_context(P.sb('o_s1', [128, 8, 512], F32))
        m0 = es.enter_context(P.sb('o_m0', [128, 8, 512], BF16))
        m1 = es.enter_context(P.sb('o_m1', [128, 8, 512], BF16))
        x0 = es.enter_context(P.sb('o_x0', [128, 8, 512], F32))
        x1 = es.enter_context(P.sb('o_x1', [128, 8, 512], F32))
        wkeys = load_cast_weight(P, wout, 'o_w', I['w_out'][l].rearrange("(dc p) n -> p dc n", p=128), 1024, [s0, s1], ['o_s0', 'o_s1'])
        xb, mb, ob = [s0, s1], [m0, m1], [x0, x1]
        it = 0
        for seq in seqs_of(ctx_out):
            Lq = seq_len(seq)
            T = min(512, Lq)
            col = seq_col(seq)
            xv = src(l, seq).rearrange("(dc p) t -> p dc t", p=128)
            mv = S.mix[seq].rearrange("(dc p) t -> p dc t", p=128)
            rv = S.res[seq].rearrange("(dc p) t -> p dc t", p=128)
            for tb in range(Lq // T):
                k = it % 2
                it += 1
                xt, mx, xo = xb[k], mb[k], ob[k]
                fw.dma('sp', xt[:, :, :T], xv[:, :, tb * T:(tb + 1) * T], writes=['o_s%d' % k])
                fw.dma('sp', mx[:, :, :T], mv[:, :, tb * T:(tb + 1) * T], writes=['o_m%d' % k])
                for dch in range(8):
                    pi = dch % 4
                    pt = P.ps[pi][:, :T]
                    for cc in range(8):
                        fw.mm(pt, wout[:, cc, dch * 128:(dch + 1) * 128], mx[:, cc, :T], start=(cc == 0), stop=(cc == 7),
                              reads=wkeys + ['o_m%d' % k] if cc in (0, 7) else (), writes=[('ps', pi)])
                    fw.stt('dve', xo[:, dch, :T], pt, S.modT[:, 16 + dch, col:col + 1], xt[:, dch, :T], ALU.mult, ALU.add,
                           reads=[('ps', pi), 'o_s%d' % k, 'modT'], writes=[('o_x%d' % k, dch)])
                fw.dma('sp', rv[:, :, tb * T:(tb + 1) * T], xo[:, :, :T], reads=[('o_x%d' % k, d8) for d8 in range(8)], writes=[('resw', seq, tb)])
        fw.barrier()


def phase_filt(P, I, S, l, tag):
    nc, fw = P.nc, P.fw
    Lq = L if tag == 'l' else LC
    li = 0 if tag == 'l' else 1
    nsc = Lq // 128
    nfc = (Lq + 1 + 127) // 128
    T = min(512, Lq)
    with ExitStack() as es:
        zT = es.enter_context(P.sb('fl_z', [33, Lq], F32))
        w1 = es.enter_context(P.sb('fl_w1', [33, 64], F32))
        w2 = es.enter_context(P.sb('fl_w2', [64, 64], F32))
        w3 = es.enter_context(P.sb('fl_w3', [64, 512], F32))
        hv = es.enter_context(P.sb('fl_hv', [64, 3], F32))
        fb = es.enter_context(P.sb('fl_fb', [64, 2], F32))
        h1 = es.enter_context(P.sb('fl_h1', [64, Lq], F32))
        h2 = es.enter_context(P.sb('fl_h2', [64, Lq], F32))
        win = es.enter_context(P.sb('fl_win', [128, 2, nsc, 256], F32))
        pm = es.enter_context(P.sb('fl_pm', [128, nsc, 256], BF16))
        mmn = es.enter_context(P.sb('fl_mm', [128, nsc, 256], BF16))
        acc = es.enter_context(P.sb('fl_acc', [128, 256], F32))
        t1 = es.enter_context(P.sb('fl_t1', [128, 2, 256], F32))
        t2 = es.enter_context(P.sb('fl_t2', [128, 2, 256], F32))
        tmp = es.enter_context(P.sb('fl_tmp', [64, 512], F32))
        tmpk = es.enter_context(P.sb('fl_tmpk', [64, 512], F32))
        cf = es.enter_context(P.sb('fl_cf', [128, 2, nsc, 128], BF16))
        sf = es.enter_context(P.sb('fl_sf', [128, 2, nsc, 128], BF16))
        ko = es.enter_context(P.sb('fl_ko', [128, 2, 2, 256], F32))
        ntmp = es.enter_context(P.sb('fl_n', [128, 2], F32))
        fw.dma('sp', zT[:], I['zT_' + tag], writes=['fl_z'])
        fw.dma('sp', w1[:], I['hfw1'][l], writes=['fl_w1'])
        fw.dma('sp', w2[:], I['hfw2'][l], writes=['fl_w2'])
        fw.dma('sp', w3[:], I['hfw3'][l], writes=['fl_w3'])
        fw.dma('sp', hv[:], I['hfv'][l], writes=['fl_hv'])
        for v in range(2):
            fw.dma('sp', win[:, v, :, :], I['win_' + tag][v], writes=[('fl_win', v)])
        fw.ts('dve', fb[:], hv[:, 0:2], hv[:, 2:3], None, ALU.mult, reads=['fl_hv'], writes=['fl_fb'])
        fw.memset('pool', acc[:], 0.0, writes=['fl_acc'])

        def sin_layer(dst, dk, w, wk, K, srcT, sk, col):
            for blk in range(Lq // T):
                pi = blk % 2
                pt = P.ps[pi][:64, :T]
                fw.mm(pt, w[:K, :64], srcT[:K, blk * T:(blk + 1) * T], reads=[wk, sk], writes=[('ps', pi)])
                fw.ts('dve', tmp[:, :T], pt, hv[:, 2:3], fb[:, col:col + 1], ALU.mult, ALU.add, reads=[('ps', pi), 'fl_hv', 'fl_fb'], writes=['fl_tmp'])
                MAGIC = 12582912.0
                fw.ts('dve', tmpk[:, :T], tmp[:, :T], 1.0 / (2.0 * PI), MAGIC, ALU.mult, ALU.add, reads=['fl_tmp'], writes=['fl_tmpk'])
                fw.ts('dve', tmpk[:, :T], tmpk[:, :T], MAGIC, None, ALU.subtract, reads=['fl_tmpk'], writes=['fl_tmpk'])
                fw.stt('dve', tmp[:, :T], tmpk[:, :T], -2.0 * PI, tmp[:, :T], ALU.mult, ALU.add, reads=['fl_tmpk', 'fl_tmp'], writes=['fl_tmp'])
                fw.act(dst[:, blk * T:(blk + 1) * T], tmp[:, :T], AF.Sin, reads=['fl_tmp'], writes=[dk])

        sin_layer(h1, 'fl_h1', w1, 'fl_w1', 33, zT, 'fl_z', 0)
        sin_layer(h2, 'fl_h2', w2, 'fl_w2', 64, h1, 'fl_h1', 1)
        for sc in range(nsc):
            pi = 2 + sc % 2
            pt = P.ps[pi]
            fw.mm(pt[:, :], h2[:64, sc * 128:(sc + 1) * 128], w3[:64, :], reads=['fl_h2', 'fl_w3'], writes=[('ps', pi)])
            fw.tt('dve', t1[:], pt[:, :].rearrange("p (v c) -> p v c", v=2), win[:, :, sc, :], ALU.mult,
                  reads=[('ps', pi), ('fl_win', 0), ('fl_win', 1)], writes=['fl_t1'])
            fw.tt('pool', pm[:, sc, :], t1[:, 0, :], t1[:, 1, :], ALU.add, reads=['fl_t1'], writes=[('fl_pm', sc)])
            fw.tt('pool', mmn[:, sc, :], t1[:, 0, :], t1[:, 1, :], ALU.subtract, reads=['fl_t1'], writes=[('fl_mm', sc)])
            fw.act(t2[:], t1[:], AF.Square, reads=['fl_t1'], writes=['fl_t2'])
            fw.tt('pool', acc[:], acc[:], t2[:, 0, :], ALU.add, reads=['fl_t2'], writes=['fl_acc'])
            fw.tt('pool', acc[:], acc[:], t2[:, 1, :], ALU.add, reads=['fl_t2'], writes=['fl_acc'])
        for ch in range(2):
            fw.mm(P.ps[0][:, ch:ch + 1], acc[:, ch * 128:(ch + 1) * 128], S.ones[:, 0:1], reads=['fl_acc', 'ones'], writes=[('ps', 0)])
        fw.act(ntmp[:], P.ps[0][:, 0:2], AF.Sqrt, bias=S.epsc[:, 0:1], reads=[('ps', 0), 'epsc'], writes=['fl_n'])
        fw.op('dve', lambda e: e.reciprocal(S.hyn[:, :, li], ntmp[:]), reads=['fl_n'], writes=[('hyn', li)])
        pmk = [('fl_pm', sc) for sc in range(nsc)]
        mmk = [('fl_mm', sc) for sc in range(nsc)]
        for fc in range(nfc):
            fsz = 128 if fc < nfc - 1 else 1
            k = fc % 2
            fw.dma('sp', cf[:, k, :, :], I['cf_' + tag][fc], writes=[('fl_cf', k)])
            fw.dma('sp', sf[:, k, :, :], I['sf_' + tag][fc], writes=[('fl_sf', k)])
            pa, pb = 4 + 2 * k, 5 + 2 * k
            for sc in range(nsc):
                fw.mm(P.ps[pa][:fsz, :256], cf[:, k, sc, :fsz], pm[:, sc, :], start=(sc == 0), stop=(sc == nsc - 1),
                      reads=pmk + [('fl_cf', k)] if sc in (0, nsc - 1) else (), writes=[('ps', pa)])
            for sc in range(nsc):
                fw.mm(P.ps[pb][:fsz, :256], sf[:, k, sc, :fsz], mmn[:, sc, :], start=(sc == 0), stop=(sc == nsc - 1),
                      reads=mmk + [('fl_sf', k)] if sc in (0, nsc - 1) else (), writes=[('ps', pb)])
            fw.cp('act', ko[:fsz, k, 0, :], P.ps[pa][:fsz, :256], reads=[('ps', pa)], writes=[('fl_ko', k, 0)])
            fw.cp('dve', ko[:fsz, k, 1, :], P.ps[pb][:fsz, :256], reads=[('ps', pb)], writes=[('fl_ko', k, 1)])
            for v in range(2):
                fw.dma('sp', S.khat[tag][v, fc * 128:fc * 128 + fsz, :], ko[:fsz, k, v, :], reads=[('fl_ko', k, v)], writes=[('khat', tag, v, fc)])
        fw.barrier()


def phase_hy(P, I, S, l, tag):
    nc, fw = P.nc, P.fw
    Lq = L if tag == 'l' else LC
    li = 0 if tag == 'l' else 1
    nsc = Lq // 128
    nfc = (Lq + 1 + 127) // 128
    TB = min(512, Lq)
    ntb = Lq // TB
    with ExitStack() as es:
        u = es.enter_context(P.sb('hy_u', [128, 4, Lq], F32))
        x1c = es.enter_context(P.sb('hy_x1', [128, 4, Lq], BF16))
        utok = es.enter_context(P.sb('hy_ut', [128, nsc, 512], BF16))
        wre = es.enter_context(P.sb('hy_wre', [128, nfc, 512], BF16))
        wim = es.enter_context(P.sb('hy_wim', [128, nfc, 512], BF16))
        cw = es.enter_context(P.sb('hy_cw', [128, 6, 4], F32))
        hb = es.enter_context(P.sb('hy_hb', [128, 2], F32))
        fw.dma('sp', cw[:], I['hycw'][l], writes=['hy_cw'])
        fw.dma('sp', hb[:], I['hybias'][l], writes=['hy_hb'])
        with ExitStack() as es2:
            raw = es2.enter_context(P.sb('hy_raw', [128, 3, Lq + 2], F32))
            cv = [es2.enter_context(P.sb('hy_cv%d' % k, [128, Lq], F32)) for k in range(3)]
            fw.memset('pool', raw[:, :, 0:1], 0.0, writes=['hy_raw_h0'])
            fw.memset('pool', raw[:, :, Lq + 1:Lq + 2], 0.0, writes=['hy_raw_h1'])
            for m in range(4):
                b, ch = m // 2, m % 2
                for k in range(3):
                    r0 = k * 256 + ch * 128
                    fw.dma('sp', raw[:, k, 1:Lq + 1], S.pl[(b, tag)][r0:r0 + 128, :], writes=[('hy_raw', k)])
                    ci = 2 * k + ch
                    rk = [('hy_raw', k), 'hy_raw_h0', 'hy_raw_h1', 'hy_cw']
                    fw.ts('dve', cv[k][:], raw[:, k, 0:Lq], cw[:, ci, 0:1], cw[:, ci, 3:4], ALU.mult, ALU.add, reads=rk, writes=[('hy_cv', k)])
                    fw.stt('dve', cv[k][:], raw[:, k, 1:Lq + 1], cw[:, ci, 1:2], cv[k][:], ALU.mult, ALU.add, reads=rk, writes=[('hy_cv', k)])
                    fw.stt('dve', cv[k][:], raw[:, k, 2:Lq + 2], cw[:, ci, 2:3], cv[k][:], ALU.mult, ALU.add, reads=rk, writes=[('hy_cv', k)])
                fw.tt('pool', u[:, m, :], cv[2][:], cv[0][:], ALU.mult, reads=[('hy_cv', 2), ('hy_cv', 0)], writes=[('hy_u', m)])
                fw.cp('act', x1c[:, m, :], cv[1][:], reads=[('hy_cv', 1)], writes=[('hy_x1', m)])
            uk = [('hy_u', m) for m in range(4)]
            for sc in range(nsc):
                pi = sc % 2
                for m in range(4):
                    fw.tr(P.ps[pi][:, m * 128:(m + 1) * 128], u[:, m, sc * 128:(sc + 1) * 128], S.ident[:], reads=uk + ['ident'], writes=[('ps', pi)])
                fw.cp('act' if sc % 2 == 0 else 'dve', utok[:, sc, :], P.ps[pi][:, :], reads=[('ps', pi)], writes=[('hy_ut', sc)])
            fw.barrier()
        with ExitStack() as es2:
            cf = es2.enter_context(P.sb('hy_cf', [128, 2, nsc, 128], BF16))
            sf = es2.enter_context(P.sb('hy_sf', [128, 2, nsc, 128], BF16))
            kk = es2.enter_context(P.sb('hy_kk', [128, 2, 2, 256], F32))
            tq = [es2.enter_context(P.sb('hy_t%d' % q, [128, 2, 256], F32)) for q in range(4)]
            for fc in range(nfc):
                fsz = 128 if fc < nfc - 1 else 1
                k = fc % 2
                fw.dma('sp', cf[:, k, :, :], I['cf_' + tag][fc], writes=[('hy_cf', k)])
                fw.dma('sp', sf[:, k, :, :], I['sf_' + tag][fc], writes=[('hy_sf', k)])
                for v in range(2):
                    fw.dma('sp', kk[:fsz, k, v, :], S.khat[tag][v, fc * 128:fc * 128 + fsz, :], writes=[('hy_kk', k, v)])
                pa, pb = 2 + 2 * k, 3 + 2 * k
                for sc in range(nsc):
                    fw.mm(P.ps[pa][:fsz, :], cf[:, k, sc, :fsz], utok[:, sc, :], start=(sc == 0), stop=(sc == nsc - 1),
                          reads=[('hy_cf', k)] if sc in (0, nsc - 1) else (), writes=[('ps', pa)])
                for sc in range(nsc):
                    fw.mm(P.ps[pb][:fsz, :], sf[:, k, sc, :fsz], utok[:, sc, :], start=(sc == 0), stop=(sc == nsc - 1),
                          reads=[('hy_sf', k)] if sc in (0, nsc - 1) else (), writes=[('ps', pb)])
                ure = P.ps[pa][:fsz, :].rearrange("p (b c) -> p b c", b=2)
                uim = P.ps[pb][:fsz, :].rearrange("p (b c) -> p b c", b=2)
                kre = kk[:fsz, k, 0, :].unsqueeze(1).to_broadcast([fsz, 2, 256])
                kim = kk[:fsz, k, 1, :].unsqueeze(1).to_broadcast([fsz, 2, 256])
                kr = [('hy_kk', k, 0), ('hy_kk', k, 1)]
                fw.tt('dve', tq[0][:fsz], ure, kre, ALU.mult, reads=[('ps', pa)] + kr, writes=[('hy_t', 0)])
                fw.tt('dve', tq[1][:fsz], uim, kim, ALU.mult, reads=[('ps', pb)] + kr, writes=[('hy_t', 1)])
                fw.tt('dve', tq[2][:fsz], ure, kim, ALU.mult, reads=[('ps', pa)] + kr, writes=[('hy_t', 2)])
                fw.tt('dve', tq[3][:fsz], uim, kre, ALU.mult, reads=[('ps', pb)] + kr, writes=[('hy_t', 3)])
                fw.tt('pool', wre[:fsz, fc, :].rearrange("p (b c) -> p b c", b=2), tq[0][:fsz], tq[1][:fsz], ALU.subtract,
                      reads=[('hy_t', 0), ('hy_t', 1)], writes=[('hy_wre', fc)])
                fw.tt('pool', wim[:fsz, fc, :].rearrange("p (b c) -> p b c", b=2), tq[2][:fsz], tq[3][:fsz], ALU.add,
                      reads=[('hy_t', 2), ('hy_t', 3)], writes=[('hy_wim', fc)])
            fw.barrier()
        with ExitStack() as es2:
            ci_t = es2.enter_context(P.sb('hy_ci', [128, nfc, TB], BF16))
            si_t = es2.enter_context(P.sb('hy_si', [128, nfc, TB], BF16))
            at = es2.enter_context(P.sb('hy_at', [128, 2, TB], F32))
            ob = es2.enter_context(P.sb('hy_o', [128, 2, TB], BF16))
            oi = 0
            for tb in range(ntb):
                fw.dma('sp', ci_t[:], I['ci_' + tag][tb], writes=['hy_ci'])
                fw.dma('sp', si_t[:], I['si_' + tag][tb], writes=['hy_si'])
                for m in range(4):
                    b, ch = m // 2, m % 2
                    pi = m % 2
                    pt = P.ps[pi][:, :TB]
                    for fc in range(nfc):
                        fsz = 128 if fc < nfc - 1 else 1
                        fw.mm(pt, wre[:fsz, fc, m * 128:(m + 1) * 128], ci_t[:fsz, fc, :], start=(fc == 0), stop=False,
                              reads=['hy_ci'] if fc in (0, nfc - 1) else (), writes=[('ps', pi)])
                    for fc in range(nfc - 1):
                        fw.mm(pt, wim[:, fc, m * 128:(m + 1) * 128], si_t[:, fc, :], start=False, stop=(fc == nfc - 2),
                              reads=['hy_si'] if fc in (0, nfc - 2) else (), writes=[('ps', pi)])
                    a = at[:, oi % 2, :]
                    o = ob[:, oi % 2, :]
                    ak, ok = ('hy_at', oi % 2), ('hy_o', oi % 2)
                    oi += 1
                    fw.ts('dve', a, pt, S.hyn[:, ch, li:li + 1], None, ALU.mult, reads=[('ps', pi)], writes=[ak])
                    fw.stt('dve', a, u[:, m, tb * TB:(tb + 1) * TB], hb[:, ch:ch + 1], a, ALU.mult, ALU.add, reads=['hy_hb'], writes=[ak])
                    fw.tt('pool', o, a, x1c[:, m, tb * TB:(tb + 1) * TB], ALU.mult, reads=[ak], writes=[ok])
                    fw.dma('sp', S.mix[(b, tag)][ch * 128:(ch + 1) * 128, tb * TB:(tb + 1) * TB], o, reads=[ok], writes=[('mixhy', b, ch, tb)])
            fw.barrier()


def phase_ssd(P, I, S, l, ctx_out):
    nc, fw = P.nc, P.fw
    ps = P.ps
    with ExitStack() as es:
        E = lambda n, sh, dt=F32: es.enter_context(P.sb(n, sh, dt))
        cw = E('sd_cw', [128, 8, 4]); dtb = E('sd_dtb', [16, 1]); arow = E('sd_arow', [128, 16]); dskb = E('sd_dsk', [128, 8])
        sng = E('sd_sng', [128, 4])
        Hs = [E('sd_H%d' % d, [128, 512]) for d in range(2)]
        xtok = E('sd_xtok', [128, 16, 512]); btok = E('sd_btok', [128, 16, 256]); cbt = E('sd_cbt', [128, 16, 2, 128])
        ct = E('sd_ct', [128, 2, L]); dttok = E('sd_dttok', [128, 16, 16]); ytok = E('sd_ytok', [128, 16, 512])
        raw = E('sd_raw', [128, 8, 514]); xa = E('sd_xa', [128, 8, 512]); dtT = E('sd_dtT', [16, L])
        dtA = E('sd_dtA', [128, 16]); ac16 = E('sd_acum', [128, 16]); acum = ac16[:, 0:8]; t8 = E('sd_t8', [128, 8]); dend = E('sd_dend', [128, 8])
        cdec = E('sd_cdec', [128, 8]); eac = E('sd_eac', [128, 8]); xdt = E('sd_xdt', [128, 8, 64]); xdte = E('sd_xdte', [128, 8, 64])
        segt8 = E('sd_seg8', [128, 8, 128]); dtAb8 = E('sd_dtAb8', [128, 8, 128]); mt8 = E('sd_mt8', [128, 8, 128])
        yo = E('sd_yo', [128, 8, 64]); tmpy = E('sd_tmpy', [128, 512]); hsc = E('sd_hsc', [128, 8, 64])
        zt = E('sd_zt', [128, 4, 128]); yg = E('sd_yg', [128, 4, 128]); sqg = E('sd_sqg', [128, 4, 128]); rs = E('sd_rs', [128, 2, 128])
        og = E('sd_og', [128, 4, 128], BF16)
        fw.dma('sp', cw[:], I['sscw'][l], writes=['sd_cw'])
        fw.dma('sp', dtb[:], I['dtb'][l], writes=['sd_dtb'])
        fw.dma('sp', arow[:], I['alog'][l].partition_broadcast(128), writes=['sd_arow'])
        fw.dma('sp', dskb[:], I['dskip'][l].partition_broadcast(128), writes=['sd_dsk'])
        fw.dma('sp', sng[:], I['sng'][l], writes=['sd_sng'])
        fw.act(arow[:], arow[:], AF.Exp, reads=['sd_arow'], writes=['sd_arow'])
        fw.ts('dve', arow[:], arow[:], -1.0, None, ALU.mult, reads=['sd_arow'], writes=['sd_arow'])
        tri = [S.uinc, S.linc]
        msk = [S.maskf, S.maskb]

        def stage_a(seq):
            Lq = seq_len(seq)
            T = min(512, Lq)
            plv = S.pl[seq][OFF_XBC:OFF_XBC + 1024, :].rearrange("(cc p) t -> p cc t", p=128)
            fw.dma('sp', dtT[:, :Lq], S.pl[seq][OFF_DT:OFF_DT + 16, :], writes=['sd_dtT'])
            fw.act(dtT[:, :Lq], dtT[:, :Lq], AF.Exp, bias=dtb[:, 0:1], reads=['sd_dtT', 'sd_dtb'], writes=['sd_dtT'])
            fw.ts('dve', dtT[:, :Lq], dtT[:, :Lq], 1.0, None, ALU.add, reads=['sd_dtT'], writes=['sd_dtT'])
            fw.act(dtT[:, :Lq], dtT[:, :Lq], AF.Ln, reads=['sd_dtT'], writes=['sd_dtT'])
            for tb in range(Lq // T):
                t0 = tb * T
                lo, hi = max(0, t0 - 1), min(Lq, t0 + T + 1)
                d0 = lo - (t0 - 1)
                if tb == 0:
                    fw.memset('pool', raw[:, :, 0:1], 0.0, writes=['sd_raw'])
                if tb == Lq // T - 1:
                    fw.memset('pool', raw[:, :, T + 1:T + 2], 0.0, writes=['sd_raw'])
                fw.dma('sp', raw[:, :, d0:d0 + hi - lo], plv[:, :, lo:hi], writes=['sd_raw'])
                for cc in range(8):
                    k = ('sd_xa', cc)
                    fw.ts('dve', xa[:, cc, :T], raw[:, cc, 0:T], cw[:, cc, 0:1], cw[:, cc, 3:4], ALU.mult, ALU.add, reads=['sd_raw', 'sd_cw'], writes=[k])
                    fw.stt('dve', xa[:, cc, :T], raw[:, cc, 1:T + 1], cw[:, cc, 1:2], xa[:, cc, :T], ALU.mult, ALU.add, reads=['sd_raw'], writes=[k])
                    fw.stt('dve', xa[:, cc, :T], raw[:, cc, 2:T + 2], cw[:, cc, 2:3], xa[:, cc, :T], ALU.mult, ALU.add, reads=['sd_raw'], writes=[k])
                    fw.act(xa[:, cc, :T], xa[:, cc, :T], AF.Silu, reads=[k], writes=[k])
                xk = [('sd_xa', cc) for cc in range(8)]
                fw.cp('pool', ct[:, :, t0:t0 + T], xa[:, 6:8, :T], reads=xk, writes=['sd_ct'])
                for j in range(T // 128):
                    c = tb * (T // 128) + j
                    sl = slice(j * 128, (j + 1) * 128)
                    for k in range(4):
                        fw.tr(ps[1][:, k * 128:(k + 1) * 128], xa[:, k, sl], S.ident[:], reads=xk, writes=[('ps', 1)])
                    fw.cp('act', xtok[:, c, :], ps[1][:, :], reads=[('ps', 1)], writes=[('sd_xtok', c)])
                    for g in range(2):
                        fw.tr(ps[2][:, g * 128:(g + 1) * 128], xa[:, 4 + g, sl], S.ident[:], reads=xk, writes=[('ps', 2)])
                        fw.mm(ps[2][:, 256 + g * 128:256 + (g + 1) * 128], xa[:, 4 + g, sl], xa[:, 6 + g, sl], reads=xk, writes=[('ps', 2)])
                    fw.cp('dve', btok[:, c, :], ps[2][:, 0:256], reads=[('ps', 2)], writes=[('sd_btok', c)])
                    fw.cp('dve', cbt[:, c, :, :], ps[2][:, 256:512].rearrange("p (g i) -> p g i", g=2), reads=[('ps', 2)], writes=[('sd_cbt', c)])
                    fw.tr(ps[3][:, 0:16], dtT[:16, t0 + j * 128:t0 + (j + 1) * 128], S.ident[:16, :16], reads=['sd_dtT'], writes=[('ps', 3)])
                    fw.cp('dve', dttok[:, c, :], ps[3][:, 0:16], reads=[('ps', 3)], writes=[('sd_dttok', c)])

        def scan(seq, d, with_output, final_out):
            Lq = seq_len(seq)
            ncq = Lq // 128
            H = Hs[d]
            hk = 'sd_H%d' % d
            d8 = slice(d * 8, (d + 1) * 8)
            order = range(ncq) if d == 0 else range(ncq - 1, -1, -1)
            for c in order:
                fw.tt('pool', dtA[:], dttok[:, c, :], arow[:], ALU.mult, reads=[('sd_dttok', c), 'sd_arow'], writes=['sd_dtA'])
                fw.mm(ps[0][:, 0:8], tri[d][:], dtA[:, d8], reads=['sd_dtA'], writes=[('ps', 0)])
                fw.mm(ps[0][:, 8:16], S.ones[:], dtA[:, d8], reads=['sd_dtA'], writes=[('ps', 0)])
                fw.cp('dve', ac16[:], ps[0][:, 0:16], reads=[('ps', 0)], writes=['sd_acum'])
                fw.tt('pool', t8[:], ac16[:, 8:16], acum, ALU.subtract, reads=['sd_acum'], writes=['sd_t8'])
                fw.act(dend[:], t8[:], AF.Exp, reads=['sd_t8'], writes=['sd_dend'])
                fw.act(cdec[:], ac16[:, 8:16], AF.Exp, reads=['sd_acum'], writes=['sd_cdec'])
                fw.tt('pool', xdt[:], xtok[:, c, :].rearrange("p (h q) -> p h q", h=8),
                      dttok[:, c, d8].unsqueeze(2).to_broadcast([128, 8, 64]), ALU.mult, reads=[('sd_xtok', c), ('sd_dttok', c)], writes=['sd_xdt'])
                if with_output:
                    fw.act(eac[:], acum,  AF.Exp, reads=['sd_acum'], writes=['sd_eac'])
                    fw.cp('dve', dtAb8[:], dtA[:, d8].unsqueeze(2).to_broadcast([128, 8, 128]), reads=['sd_dtA'], writes=['sd_dtAb'])
                    for hd in range(8):
                        pa = 1 + hd // 4
                        fw.mm(ps[pa][:, (hd % 4) * 128:(hd % 4 + 1) * 128], dtAb8[:, hd, :], tri[d][:], reads=['sd_dtAb'], writes=[('ps', pa)])
                    for hf in range(2):
                        fw.tt('dve', segt8[:, 4 * hf:4 * hf + 4, :], ps[1 + hf][:, :].rearrange("p (h i) -> p h i", h=4),
                              acum[:, 4 * hf:4 * hf + 4].unsqueeze(2).to_broadcast([128, 4, 128]), ALU.subtract,
                              reads=[('ps', 1 + hf), 'sd_acum'], writes=['sd_seg8'])
                    fw.tt('dve', segt8[:], segt8[:], msk[d][:, :].unsqueeze(1).to_broadcast([128, 8, 128]), ALU.min,
                          reads=['sd_seg8'], writes=['sd_seg8'])
                    fw.act(segt8[:], segt8[:], AF.Exp, reads=['sd_seg8'], writes=['sd_seg8'])
                    fw.tt('dve', mt8[:].rearrange("p (g h) i -> p g h i", g=2), segt8[:].rearrange("p (g h) i -> p g h i", g=2),
                          cbt[:, c, :, :].unsqueeze(2).to_broadcast([128, 2, 4, 128]), ALU.mult, reads=['sd_seg8', ('sd_cbt', c)], writes=['sd_mt8'])
                    for hd in range(8):
                        fw.mm(ps[3][:, hd * 64:(hd + 1) * 64], mt8[:, hd, :], xdt[:, hd, :], reads=['sd_mt8', 'sd_xdt'], writes=[('ps', 3)])
                    for g in range(2):
                        fw.mm(ps[4][:, g * 256:(g + 1) * 256], ct[:, g, c * 128:(c + 1) * 128], H[:, g * 256:(g + 1) * 256],
                              reads=['sd_ct', hk], writes=[('ps', 4)])
                    fw.tt('dve', yo[:], ps[4][:, :].rearrange("p (h q) -> p h q", h=8), eac[:, :].unsqueeze(2).to_broadcast([128, 8, 64]),
                          ALU.mult, reads=[('ps', 4), 'sd_eac'], writes=['sd_yo'])
                    yflat = yo[:].rearrange("p h q -> p (h q)")
                    if d == 0:
                        fw.tt('dve', ytok[:, c, :], ps[3][:, :], yflat, ALU.add, reads=[('ps', 3), 'sd_yo'], writes=[('sd_ytok', c)])
                        fw.tt('pool', tmpy[:].rearrange("p (h q) -> p h q", h=8), xtok[:, c, :].rearrange("p (h q) -> p h q", h=8),
                              dskb[:, :].unsqueeze(2).to_broadcast([128, 8, 64]), ALU.mult, reads=[('sd_xtok', c), 'sd_dsk'], writes=['sd_tmpy'])
                        fw.tt('pool', ytok[:, c, :], ytok[:, c, :], tmpy[:], ALU.add, reads=['sd_tmpy'], writes=[('sd_ytok', c)])
                    else:
                        fw.tt('dve', tmpy[:], ps[3][:, :], yflat, ALU.add, reads=[('ps', 3), 'sd_yo'], writes=['sd_tmpy'])
                        fw.tt('pool', ytok[:, c, :], ytok[:, c, :], tmpy[:], ALU.add, reads=['sd_tmpy'], writes=[('sd_ytok', c)])
                fw.tt('pool', xdte[:], xdt[:], dend[:, :].unsqueeze(2).to_broadcast([128, 8, 64]), ALU.mult, reads=['sd_xdt', 'sd_dend'], writes=['sd_xdte'])
                for g in range(2):
                    fw.mm(ps[5][:, g * 256:(g + 1) * 256], btok[:, c, g * 128:(g + 1) * 128],
                          xdte[:, g * 4:(g + 1) * 4, :].rearrange("p h q -> p (h q)"), reads=[('sd_btok', c), 'sd_xdte'], writes=[('ps', 5)])
                fw.tt('pool', hsc[:], H[:].rearrange("p (h q) -> p h q", h=8), cdec[:, :].unsqueeze(2).to_broadcast([128, 8, 64]), ALU.mult,
                      reads=[hk, 'sd_cdec'], writes=['sd_hsc'])
                fw.tt('dve', H[:], hsc[:].rearrange("p h q -> p (h q)"), ps[5][:, :], ALU.add, reads=['sd_hsc', ('ps', 5)], writes=[hk])
                if final_out:
                    tk = slice(c * 128, (c + 1) * 128)
                    for k in range(4):
                        fw.tr(ps[6][:, k * 128:(k + 1) * 128], ytok[:, c, k * 128:(k + 1) * 128], S.ident[:], reads=[('sd_ytok', c)], writes=[('ps', 6)])
                    fw.dma('sp', zt[:], S.pl[seq][OFF_Z:OFF_Z + 512, tk].rearrange("(k p) t -> p k t", p=128), writes=['sd_zt'])
                    fw.act(zt[:], zt[:], AF.Silu, reads=['sd_zt'], writes=['sd_zt'])
                    fw.tt('dve', yg[:], ps[6][:, :].rearrange("p (k t) -> p k t", k=4), zt[:], ALU.mult, reads=[('ps', 6), 'sd_zt'], writes=['sd_yg'])
                    fw.act(sqg[:], yg[:], AF.Square, reads=['sd_yg'], writes=['sd_sqg'])
                    for g in range(2):
                        fw.mm(ps[7][:, g * 128:(g + 1) * 128], S.ones[:], sqg[:, 2 * g, :], start=True, stop=False, reads=['sd_sqg'], writes=[('ps', 7)])
                        fw.mm(ps[7][:, g * 128:(g + 1) * 128], S.ones[:], sqg[:, 2 * g + 1, :], start=False, stop=True, reads=['sd_sqg'], writes=[('ps', 7)])
                    fw.act(rs[:], ps[7][:, 0:256].rearrange("p (g t) -> p g t", g=2), AF.Sqrt, bias=S.epsc[:, 0:1], scale=1.0 / 256.0,
                           reads=[('ps', 7)], writes=['sd_rs'])
                    fw.op('dve', lambda e: e.reciprocal(rs[:], rs[:]), reads=['sd_rs'], writes=['sd_rs'])
                    for k in range(4):
                        fw.stt('dve', og[:, k, :], yg[:, k, :], sng[:, k:k + 1], rs[:, k // 2, :], ALU.mult, ALU.mult,
                               reads=['sd_yg', 'sd_rs', 'sd_sng'], writes=['sd_og'])
                    fw.dma('sp', S.mix[seq][256:768, tk].rearrange("(k p) t -> p k t", p=128), og[:], reads=['sd_og'], writes=[('mixssd', seq, c)])

        for b in range(2):
            for d in range(2):
                fw.memset('pool', Hs[d][:], 0.0, writes=['sd_H%d' % d])
            stage_a((b, 'c'))
            if SSD_MODE != 'a':
                scan((b, 'c'), 0, ctx_out, False)
                scan((b, 'c'), 1, ctx_out, ctx_out)
            stage_a((b, 'l'))
            if SSD_MODE != 'a':
                scan((b, 'l'), 0, True, False)
                scan((b, 'l'), 1, True, True)
        fw.barrier()


def phase_peer(P, I, S, l, ctx_out):
    nc, fw = P.nc, P.fw
    ps = P.ps
    T = 256
    NB = 4
    with ExitStack() as es2:
        cs = [es2.enter_context(P.sb('pe_cs%d' % i, [128, 2048], F32)) for i in range(4)]
        cbs = [es2.enter_context(P.sb('pe_cb%d' % i, [128, 2048], BF16)) for i in range(4)]
        uv = I['uT'][l].rearrange("(a p) e -> p a e", p=128)
        n = 0
        for blk in range(64):
            k = n % 4
            n += 1
            fw.dma('sp', cs[k][:].rearrange("p (a e) -> p a e", a=8), uv[:, :, blk * 256:(blk + 1) * 256], writes=['pe_cs%d' % k])
            fw.cp('dve' if k % 2 == 0 else 'pool', cbs[k][:], cs[k][:], reads=['pe_cs%d' % k], writes=['pe_cb%d' % k])
            fw.dma('act', S.ubf[blk].rearrange("p a e -> p (a e)"), cbs[k][:], reads=['pe_cb%d' % k], writes=[('ubf', blk)])
        for blk in range(64):
            k = n % 4
            n += 1
            fw.dma('sp', cs[k][:].rearrange("p (ii d) -> p ii d", ii=2),
                   I['v'][l][blk * 256:(blk + 1) * 256, :].rearrange("(ii j) d -> j ii d", j=128), writes=['pe_cs%d' % k])
            fw.cp('dve' if k % 2 == 0 else 'pool', cbs[k][:], cs[k][:], reads=['pe_cs%d' % k], writes=['pe_cb%d' % k])
            fw.dma('act', S.vbf[blk].rearrange("p ii d -> p (ii d)"), cbs[k][:], reads=['pe_cb%d' % k], writes=[('vbf', blk)])
        fw.barrier()
    with ExitStack() as es:
        E = lambda n, sh, dt=F32: es.enter_context(P.sb(n, sh, dt))
        wq = E('pe_wq', [128, 8, 2048], BF16)
        st = [E('pe_s%d' % i, [128, 8, 256]) for i in range(2)]
        k1 = E('pe_k1', [128, 128], BF16); k2 = E('pe_k2', [128, 128], BF16)
        gbuf = E('pe_g', [128, 128, T], BF16)
        ub = E('pe_ub', [128, 2, 8, 256], BF16); vb = E('pe_vb', [128, 2, 2, 1024], BF16)
        h2 = E('pe_h2', [128, 8, T], BF16); sq = E('pe_sq', [128, 8, T]); rstd = E('pe_rstd', [128, T])
        qT = E('pe_qT', [128, 16, T], BF16); s12 = E('pe_s12', [128, 16, 128]); v12 = E('pe_v12', [128, 16, 16])
        wk = E('pe_wk', [128, 4, 128]); wk2 = E('pe_wk2', [128, 4, 256]); t16 = E('pe_t16', [128, 8, 16])
        e16 = E('pe_e16', [128, 8, 16]); zz = E('pe_z', [128, 8]); mz = E('pe_mz', [128, 8])
        thr = E('pe_thr', [128, 8, 16]); bia = E('pe_bia', [128, 8, 16]); pc = E('pe_pc', [128, 3, T])
        Et = [E('pe_E%d' % i, [128, 256]) for i in range(4)]
        Wt = [E('pe_W%d' % i, [128, 128], BF16) for i in range(8)]
        Pt = [E('pe_P%d' % i, [128, 128], BF16) for i in range(8)]
        qr1 = E('pe_qr1', [128, 2, 4, 128], BF16)
        qr2 = E('pe_qr2', [128, 2, 4, 128], BF16)
        gst = [E('pe_gs%d' % i, [128, T]) for i in range(2)]
        At = [E('pe_A%d' % i, [128, T], BF16) for i in range(2)]
        xo = E('pe_xo', [128, 8, T])
        cand = sq
        wv = I['wq'][l].rearrange("(dc p) n -> p dc n", p=128)
        n = 0
        for blk in range(8):
            k = n % 2
            n += 1
            fw.dma('sp', st[k][:], wv[:, :, blk * 256:(blk + 1) * 256], writes=['pe_s%d' % k])
            fw.cp('dve' if k == 0 else 'pool', wq[:, :, blk * 256:(blk + 1) * 256], st[k][:], reads=['pe_s%d' % k], writes=[('pe_wq', blk)])
        for (kt, nm, key) in ((k1, 'k1T', 'pe_k1'), (k2, 'k2T', 'pe_k2')):
            k = n % 2
            n += 1
            fw.dma('sp', st[k][:, 0, 0:128], I[nm][l], writes=['pe_s%d' % k])
            fw.cp('dve', kt[:], st[k][:, 0, 0:128], reads=['pe_s%d' % k], writes=[key])
        fw.barrier()

        def load_tables(blk):
            bb = blk % 2
            fw.dma('sp', ub[:, bb, :, :], S.ubf[blk], writes=[('pe_ub', bb)])
            fw.dma('sp', vb[:, bb, :, :], S.vbf[blk], writes=[('pe_vb', bb)])

        gi = 0
        for seq in seqs_of(ctx_out):
            Lq = seq_len(seq)
            col = seq_col(seq)
            rv = S.res[seq].rearrange("(dc p) t -> p dc t", p=128)
            for grp in range(Lq // T):
                g0 = grp * T
                xt = st[gi % 2]
                xk = 'pe_s%d' % (gi % 2)
                gi += 1
                fw.dma('sp', xt[:], rv[:, :, g0:g0 + T], writes=[xk])
                hk = normmod(P, S, xt, xk, T, S.G2[:, :, col], S.modT[:, 24:32, col], h2, 'pe_h2', sq, rstd, ('ps', 4), ps[4])
                for qc in range(16):
                    pi = 4 + qc % 4
                    for dc in range(8):
                        fw.mm(ps[pi][:, :T], wq[:, dc, qc * 128:(qc + 1) * 128], h2[:, dc, :], start=(dc == 0), stop=(dc == 7),
                              reads=hk if dc in (0, 7) else (), writes=[('ps', pi)])
                    fw.cp('act' if qc % 2 == 0 else 'dve', qT[:, qc, :], ps[pi][:, :T], reads=[('ps', pi)], writes=[('pe_qT', qc)])
                qk = [('pe_qT', qc) for qc in range(16)]
                sqk = [('nm_sq', dc) for dc in range(8)]
                for tt in range(2):
                    tsl = slice(tt * 128, (tt + 1) * 128)
                    for h in range(8):
                        fw.mm(ps[4 + h // 4][:, (h % 4) * 128:(h % 4 + 1) * 128], qT[:, 2 * h, tsl], k1[:], reads=qk + ['pe_k1'], writes=[('ps', 4 + h // 4)])
                        fw.mm(ps[6 + h // 4][:, (h % 4) * 128:(h % 4 + 1) * 128], qT[:, 2 * h + 1, tsl], k2[:], reads=qk + ['pe_k2'], writes=[('ps', 6 + h // 4)])
                    for q4 in range(4):
                        fw.cp('dve' if q4 % 2 == 0 else 'act', s12[:, q4 * 4:(q4 + 1) * 4, :], ps[4 + q4][:, :].rearrange("p (h i) -> p h i", h=4),
                              reads=[('ps', 4 + q4)], writes=[('pe_s12', q4)])
                    sk = [('pe_s12', q4) for q4 in range(4)]
                    for base in range(0, 16, 4):
                        for idx in range(base, base + 4):
                            fw.op('dve', lambda e, idx=idx: e.max(out=v12[:, idx, 0:8], in_=s12[:, idx, :]), reads=sk, writes=[('pe_v12', idx)])
                        for idx in range(base, base + 4):
                            fw.op('dve', lambda e, idx=idx: e.match_replace(out=wk[:, idx % 4, :], in_to_replace=v12[:, idx, 0:8], in_values=s12[:, idx, :], imm_value=-1e30),
                                  reads=[('pe_v12', idx)], writes=[('pe_wk', idx % 4)])
                        for idx in range(base, base + 4):
                            fw.op('dve', lambda e, idx=idx: e.max(out=v12[:, idx, 8:16], in_=wk[:, idx % 4, :]), reads=[('pe_wk', idx % 4)], writes=[('pe_v12b', idx)])
                    v12k = [('pe_v12', i) for i in range(16)] + [('pe_v12b', i) for i in range(16)]
                    fw.tt('dve', cand[:].rearrange("p h (a b) -> p h a b", a=16), v12[:, 0:8, :].unsqueeze(3).to_broadcast([128, 8, 16, 16]),
                          v12[:, 8:16, :].unsqueeze(2).to_broadcast([128, 8, 16, 16]), ALU.add, reads=v12k, writes=sqk)
                    for base in range(0, 8, 4):
                        for h in range(base, base + 4):
                            fw.op('dve', lambda e, h=h: e.max(out=t16[:, h, 0:8], in_=cand[:, h, :]), reads=sqk, writes=[('pe_t16', h)])
                        for h in range(base, base + 4):
                            fw.op('dve', lambda e, h=h: e.match_replace(out=wk2[:, h % 4, :], in_to_replace=t16[:, h, 0:8], in_values=cand[:, h, :], imm_value=-1e30),
                                  reads=[('pe_t16', h)] + sqk, writes=[('pe_wk2', h % 4)])
                        for h in range(base, base + 4):
                            fw.op('dve', lambda e, h=h: e.max(out=t16[:, h, 8:16], in_=wk2[:, h % 4, :]), reads=[('pe_wk2', h % 4)], writes=[('pe_t16b', h)])
                    t16k = [('pe_t16', i) for i in range(8)] + [('pe_t16b', i) for i in range(8)]
                    fw.tt('dve', e16[:], t16[:], t16[:, :, 0:1].to_broadcast([128, 8, 16]), ALU.subtract, reads=t16k, writes=['pe_e16'])
                    fw.act(e16[:], e16[:], AF.Exp, reads=['pe_e16'], writes=['pe_e16'])
                    fw.op('dve', lambda e: e.reduce_sum(out=zz[:], in_=e16[:], axis=AX.X), reads=['pe_e16'], writes=['pe_z'])
                    fw.act(zz[:], zz[:], AF.Ln, reads=['pe_z'], writes=['pe_z'])
                    fw.tt('dve', mz[:], t16[:, :, 0], zz[:], ALU.add, reads=t16k + ['pe_z'], writes=['pe_mz'])
                    fw.tt('dve', thr[:], t16[:, :, 15:16].to_broadcast([128, 8, 16]), v12[:, 0:8, :], ALU.subtract, reads=t16k + v12k, writes=['pe_thr'])
                    fw.tt('dve', bia[:], v12[:, 0:8, :], mz[:, :].unsqueeze(2).to_broadcast([128, 8, 16]), ALU.subtract, reads=['pe_mz'] + v12k, writes=['pe_bia'])
                    fw.act(bia[:], bia[:], AF.Exp, reads=['pe_bia'], writes=['pe_bia'])
                    srcs = [(v12[:, 0:8, :], v12k), (thr[:], ['pe_thr']), (bia[:], ['pe_bia'])]
                    for slot, (sap, skey) in enumerate(srcs):
                        fw.tr(ps[4][:, slot * 128:(slot + 1) * 128], sap.rearrange("p h a -> p (h a)"), S.ident[:], reads=skey, writes=[('ps', 4)])
                    fw.cp('dve', pc[:, :, tsl], ps[4][:, 0:384].rearrange("p (s t) -> p s t", s=3), reads=[('ps', 4)], writes=['pe_pc'])
                load_tables(0)
                load_tables(1)
                NQ = T // 4

                def quad_F(u):
                    qb = u % 2
                    t0 = 4 * u
                    for half, qr, kk_ in ((0, qr1, k1), (1, qr2, k2)):
                        src_ap = qT[:, half:16:2, t0:t0 + 4].rearrange("p h t -> p t h").unsqueeze(3).to_broadcast([128, 4, 8, 16])
                        fw.cp('pool' if half == 0 else 'dve', qr[:, qb, :, :].rearrange("p t (h a) -> p t h a", h=8), src_ap,
                              reads=qk if u < 2 else (), writes=[('pe_qr%d' % half, qb)])
                    for k in range(4):
                        t = t0 + k
                        xb = (t // 2) % 4
                        c1 = (t % 2) * 128
                        fw.mm(ps[xb][:, c1:c1 + 128], qr1[:, qb, k, :], k1[:], reads=[('pe_qr0', qb)], writes=[('ps', xb)])
                        fw.mm(ps[xb][:, 256 + c1:256 + c1 + 128], qr2[:, qb, k, :], k2[:], reads=[('pe_qr1', qb)], writes=[('ps', xb)])

                def quad_B(u):
                    t0 = 4 * u
                    gb = 4 + u % 2
                    for pr in range(2):
                        tp = t0 + 2 * pr
                        xb = (tp // 2) % 4
                        eq = (tp // 2) % 4
                        ek = ('pe_E', eq)
                        fw.act(Et[eq][:], ps[xb][:, 256:512], AF.Exp, reads=[('ps', xb)], writes=[ek])
                        for kk in range(2):
                            t = tp + kk
                            k = 2 * pr + kk
                            c1 = kk * 128
                            q = t % 8
                            wkk, pk = ('pe_W', q), ('pe_P', q)
                            fw.stt('dve', Wt[q][:], ps[xb][:, 256 + c1:256 + c1 + 128], pc[:, 1, t:t + 1], Et[eq][:, c1:c1 + 128], ALU.is_ge, ALU.mult,
                                   reads=[('ps', xb), ek, 'pe_pc'], writes=[wkk])
                            fw.ts('dve', Pt[q][:], ps[xb][:, c1:c1 + 128], pc[:, 0, t:t + 1], pc[:, 2, t:t + 1], ALU.is_equal, ALU.mult,
                                  reads=[('ps', xb), ek, 'pe_pc'], writes=[pk])
                            fw.mm(ps[gb][:, k * 128:(k + 1) * 128], Wt[q][:], Pt[q][:], reads=[wkk, pk], writes=[('ps', gb)])

                def quad_C(u):
                    gb = 4 + u % 2
                    fw.cp('act', gbuf[:, :, 4 * u:4 * u + 4].rearrange("p i t -> p t i"),
                          ps[gb][:, :].rearrange("p (t i) -> p t i", t=4), reads=[('ps', gb)], writes=[('pe_g', u % 2)])

                for u in range(NQ + 2):
                    if u < NQ:
                        quad_F(u)
                    if 1 <= u <= NQ:
                        quad_B(u - 1)
                    if u >= 2:
                        quad_C(u - 2)
                gk = [('pe_g', q) for q in range(2)]

                def dense_S(i):
                    bb, ii = (i // 2) % 2, i % 2
                    pi = 4 + i % 2
                    for dc in range(8):
                        fw.mm(ps[pi][:, :T], ub[:, bb, dc, ii * 128:(ii + 1) * 128], h2[:, dc, :], start=(dc == 0), stop=(dc == 7),
                              reads=[('pe_ub', bb)] if dc in (0, 7) else (), writes=[('ps', pi)])

                def dense_O(i):
                    bb, ii = (i // 2) % 2, i % 2
                    pi = 4 + i % 2
                    fw.act(gst[i % 2][:], ps[pi][:, :T], AF.Gelu_apprx_tanh, reads=[('ps', pi)], writes=[('pe_gs', i % 2)])
                    fw.tt('pool' if i % 2 == 0 else 'dve', At[i % 2][:], gst[i % 2][:], gbuf[:, i, :], ALU.mult,
                          reads=[('pe_gs', i % 2)] + (gk if i < 2 else []), writes=[('pe_A', i % 2)])
                    for dch in range(8):
                        bk, rg = dch // 2, (dch % 2) * 256
                        fw.mm(ps[bk][:, rg:rg + T], vb[:, bb, ii, dch * 128:(dch + 1) * 128], At[i % 2][:],
                              start=(i == 0 and dch % 2 == 0), stop=(i == 127),
                              reads=[('pe_A', i % 2), ('pe_vb', bb)] if dch in (0, 7) else (), writes=[('ps', bk)])

                dense_S(0)
                for i in range(128):
                    if i + 1 < 128:
                        dense_S(i + 1)
                    dense_O(i)
                    if i % 2 == 1 and (i + 1) // 2 + 1 < 64:
                        load_tables((i + 1) // 2 + 1)
                for dch in range(8):
                    bk, rg = dch // 2, (dch % 2) * 256
                    fw.stt('dve', xo[:, dch, :], ps[bk][:, rg:rg + T], S.modT[:, 40 + dch, col:col + 1], xt[:, dch, :], ALU.mult, ALU.add,
                           reads=[('ps', bk), xk], writes=[('pe_xo', dch)])
                fw.dma('sp', rv[:, :, g0:g0 + T], xo[:], reads=[('pe_xo', d8) for d8 in range(8)], writes=[('resp', seq, grp)])
        fw.barrier()
```

```python
import math
from contextlib import ExitStack
import numpy as np
import ml_dtypes
import concourse.bass as bass
import concourse.mybir as mybir
from concourse.bass_utils import run_bass_kernel_spmd

F32 = mybir.dt.float32
BF16 = mybir.dt.bfloat16
AF = mybir.ActivationFunctionType
ALU = mybir.AluOpType
AX = mybir.AxisListType

NDS = 20
D = 1024
L = 2048
LC = 256
NLAYER = 2
EPS = 1e-6
NCOL = 2576
OFF_Z, OFF_XBC, OFF_DT, OFF_FN = 768, 1280, 2304, 2320
PI = math.pi
import os
SSD_MODE = os.environ.get('SSD_MODE', 'full')


class FW:
    def __init__(self, nc):
        self.nc = nc
        self.eng = dict(pe=nc.tensor, dve=nc.vector, act=nc.scalar, pool=nc.gpsimd, sp=nc.sync)
        self.esem = {k: nc.alloc_semaphore("es_" + k) for k in self.eng}
        self.ecnt = {k: 0 for k in self.eng}
        self.dsem = [nc.alloc_semaphore("ds_%d" % i) for i in range(NDS)]
        self.dcnt = [0] * NDS
        self.waited = {k: {} for k in self.eng}
        self.buf = {}
        self.dsem_of = {}
        self.rr = 0
        self.ninst = 0

    def _b(self, key):
        b = self.buf.get(key)
        if b is None:
            b = dict(w=None, r={})
            self.buf[key] = b
        return b

    def _wait(self, en, tok):
        if tok is None:
            return
        kind, who, n = tok
        if kind == 'e':
            if who == en and en == 'pe':
                return
            if self.waited[en].get(('e', who), 0) >= n:
                return
            self.eng[en].wait_ge(self.esem[who], n)
            self.waited[en][('e', who)] = n
        else:
            val = self.dcnt[who]
            if self.waited[en].get(('d', who), 0) >= n:
                return
            self.eng[en].wait_ge(self.dsem[who], 16 * val)
            self.waited[en][('d', who)] = val

    def _deps(self, en, reads, writes):
        for k in reads:
            self._wait(en, self._b(k)['w'])
        for k in writes:
            b = self._b(k)
            self._wait(en, b['w'])
            for t in b['r'].values():
                self._wait(en, t)

    def _commit(self, tok, reads, writes):
        for k in writes:
            self.buf[k] = dict(w=tok, r={})
        for k in reads:
            if k in writes:
                continue
            self._b(k)['r'][(tok[0], tok[1])] = tok

    def op(self, en, fn, reads=(), writes=()):
        self._deps(en, reads, writes)
        ins = fn(self.eng[en])
        self.ecnt[en] += 1
        ins.then_inc(self.esem[en], 1)
        tok = ('e', en, self.ecnt[en])
        self._commit(tok, reads, writes)
        self.ninst += 1
        return tok

    def dma(self, en, out, in_, reads=(), writes=(), **kw):
        self._deps(en, reads, writes)
        key = writes[0] if writes else ('anon',)
        idx = self.dsem_of.get(key)
        if idx is None:
            idx = self.rr % NDS
            self.rr += 1
            self.dsem_of[key] = idx
        ins = self.eng[en].dma_start(out=out, in_=in_, **kw)
        self.dcnt[idx] += 1
        ins.then_inc(self.dsem[idx], 16)
        tok = ('d', idx, self.dcnt[idx])
        self._commit(tok, reads, writes)
        self.ninst += 1
        return tok

    def barrier(self):
        for en in self.eng:
            for who in self.eng:
                if who != en and self.ecnt[who] > self.waited[en].get(('e', who), 0):
                    self.eng[en].wait_ge(self.esem[who], self.ecnt[who])
                    self.waited[en][('e', who)] = self.ecnt[who]
            for i in range(NDS):
                if self.dcnt[i] > self.waited[en].get(('d', i), 0):
                    self.eng[en].wait_ge(self.dsem[i], 16 * self.dcnt[i])
                    self.waited[en][('d', i)] = self.dcnt[i]
        self.buf = {}

    def mm(self, out, lhsT, rhs, start=True, stop=True, reads=(), writes=()):
        return self.op('pe', lambda e: e.matmul(out, lhsT, rhs, start=start, stop=stop), reads, writes)

    def tr(self, out, in_, ident, reads=(), writes=()):
        return self.op('pe', lambda e: e.transpose(out, in_, ident), reads, writes)

    def act(self, out, in_, func, bias=None, scale=None, reads=(), writes=()):
        kw = {}
        if bias is not None:
            kw['bias'] = bias
        if scale is not None:
            kw['scale'] = scale
        return self.op('act', lambda e: e.activation(out=out, in_=in_, func=func, **kw), reads, writes)

    def ts(self, en, out, in0, s1, s2, op0, op1=None, reads=(), writes=()):
        kw = {}
        if op1 is not None:
            kw['op1'] = op1
        return self.op(en, lambda e: e.tensor_scalar(out, in0, s1, s2, op0, **kw), reads, writes)

    def tt(self, en, out, in0, in1, op, reads=(), writes=()):
        return self.op(en, lambda e: e.tensor_tensor(out, in0, in1, op), reads, writes)

    def stt(self, en, out, in0, scalar, in1, op0, op1, reads=(), writes=()):
        en = 'dve'
        return self.op(en, lambda e: e.scalar_tensor_tensor(out, in0, scalar, in1, op0, op1), reads, writes)

    def cp(self, en, out, in_, reads=(), writes=()):
        if en == 'act':
            return self.op('act', lambda e: e.copy(out, in_), reads, writes)
        return self.op(en, lambda e: e.tensor_copy(out, in_), reads, writes)

    def memset(self, en, ap, val, writes=()):
        return self.op(en, lambda e: e.memset(ap, val), (), writes)


_CONST = None


def _bf(a):
    return np.ascontiguousarray(a.astype(ml_dtypes.bfloat16))


def _consts():
    global _CONST
    if _CONST is not None:
        return _CONST
    c = {}
    c['ident'] = np.eye(128, dtype=np.float32)
    j = np.arange(128)[:, None]
    i = np.arange(128)[None, :]
    c['uinc'] = (j <= i).astype(np.float32)
    c['linc'] = (j >= i).astype(np.float32)
    c['maskf'] = np.where(i >= j, 0.0, -1e4).astype(np.float32)
    c['maskb'] = np.where(j >= i, 0.0, -1e4).astype(np.float32)
    c['ones'] = np.ones((128, 128), np.float32)
    a = np.arange(64)
    ang = 2 * np.pi * np.outer(a, a) / 64.0
    cb = np.zeros((128, 128)); sb = np.zeros((128, 128))
    for g in range(2):
        cb[g * 64:(g + 1) * 64, g * 64:(g + 1) * 64] = np.cos(ang) / 8.0
        sb[g * 64:(g + 1) * 64, g * 64:(g + 1) * 64] = np.sin(ang) / 8.0
    c['cbd'] = cb.astype(np.float32)
    c['sbd'] = sb.astype(np.float32)
    for tag, Lq in (('l', L), ('c', LC)):
        nsc = Lq // 128
        N = 2 * Lq
        nf = Lq + 1
        nfc = (nf + 127) // 128
        t = np.linspace(0.0, 1.0, Lq, dtype=np.float32)[:, None]
        w = (2.0 * np.pi * np.arange(Lq, dtype=np.float32)[:, None] / Lq).astype(np.float32)
        f = np.linspace(1e-4, 15, 16, dtype=np.float32)[None, :]
        z = np.concatenate([t, np.cos(f * w), -np.sin(f * w)], axis=-1).astype(np.float32)
        c['zT_' + tag] = np.ascontiguousarray(z.T)
        max_decay = math.log(1e-2) / 0.3
        min_decay = math.log(1e-2) / 1.5
        deltas = np.abs(np.linspace(min_decay, max_decay, 256, dtype=np.float32))
        win = np.exp(-t * deltas).astype(np.float32)
        winb = win.copy()
        winb[0] = 0.0
        lay = lambda m: np.ascontiguousarray(m.reshape(nsc, 128, 256).transpose(1, 0, 2))
        c['win_' + tag] = np.stack([lay(win), lay(winb)]).astype(np.float32)
        s = np.arange(Lq, dtype=np.float64)[:, None]
        ff = np.arange(nfc * 128, dtype=np.float64)[None, :]
        th = 2 * np.pi * s * ff / N
        valid = (ff < nf)
        Cf = np.cos(th) * valid
        Sf = -np.sin(th) * valid
        fl = lambda m: np.ascontiguousarray(m.reshape(nsc, 128, nfc, 128).transpose(2, 1, 0, 3))
        c['cf_' + tag] = _bf(fl(Cf))
        c['sf_' + tag] = _bf(fl(Sf))
        TB = min(512, Lq)
        ntb = Lq // TB
        fcol = np.arange(nfc * 128, dtype=np.float64)[:, None]
        tt = np.arange(Lq, dtype=np.float64)[None, :]
        wgt = np.where((fcol == 0) | (fcol == Lq), 1.0, 2.0) * (fcol < nf) / N
        th2 = 2 * np.pi * fcol * tt / N
        Ci = wgt * np.cos(th2)
        Si = -wgt * np.sin(th2)
        il = lambda m: np.ascontiguousarray(m.reshape(nfc, 128, ntb, TB).transpose(2, 1, 0, 3))
        c['ci_' + tag] = _bf(il(Ci))
        c['si_' + tag] = _bf(il(Si))
        t1 = np.arange(Lq, dtype=np.float64)
        th3 = 2 * np.pi * np.outer(t1, t1) / Lq
        CL = np.cos(th3) / math.sqrt(Lq)
        SLn = -np.sin(th3) / math.sqrt(Lq)
        ll = lambda m: np.ascontiguousarray(m.reshape(nsc, 128, ntb, TB).transpose(2, 1, 0, 3))
        c['cl_' + tag] = _bf(ll(CL))
        c['sl_' + tag] = _bf(ll(SLn))
    _CONST = c
    return c


def _pc(v, n):
    return np.ascontiguousarray(np.asarray(v, np.float32).reshape(n, 128).T)


class Prog:
    def __init__(self, layers=(0, 1), phases=None, dbg=False):
        self.layers = layers
        self.phases = phases
        self.dbg = dbg
        nc = bass.Bass("TRN2", target_bir_lowering=False)
        self.nc = nc
        self.fw = FW(nc)
        self.inputs = {}
        self.uid = 0
        self.ps = [nc.alloc_psum_tensor("psb%d" % i, [128, 512], F32) for i in range(8)]

    def inp(self, name, shape, dt=F32):
        t = self.nc.dram_tensor(name, list(shape), dt, kind="ExternalInput").ap()
        self.inputs[name] = t
        return t

    def scratch(self, name, shape, dt=F32, out=False):
        kind = "ExternalOutput" if (out or self.dbg) else "Internal"
        return self.nc.dram_tensor(name, list(shape), dt, kind=kind).ap()

    def sb(self, name, shape, dt):
        self.uid += 1
        return self.nc.sbuf_tensor('%s_u%d' % (name, self.uid), shape, dt)

    def want(self, ph):
        return self.phases is None or ph in self.phases


def _declare(P):
    c = _consts()
    I = {}
    I['xT'] = P.inp('xT', [2, D, L])
    I['ctxT'] = P.inp('ctxT', [2, D, LC])
    I['cT'] = P.inp('cT', [128, 8, 3])
    I['w_ada'] = P.inp('w_ada', [NLAYER, D, 6 * D])
    I['b_adaT'] = P.inp('b_adaT', [NLAYER, 128, 48])
    I['gn1'] = P.inp('gn1', [NLAYER, 128, 8])
    I['gn2'] = P.inp('gn2', [NLAYER, 128, 8])
    I['gfin'] = P.inp('gfin', [128, 8])
    I['w_in'] = P.inp('w_in', [NLAYER, D, NCOL])
    I['hycw'] = P.inp('hycw', [NLAYER, 128, 6, 4])
    I['hfw1'] = P.inp('hfw1', [NLAYER, 33, 64])
    I['hfw2'] = P.inp('hfw2', [NLAYER, 64, 64])
    I['hfw3'] = P.inp('hfw3', [NLAYER, 64, 512])
    I['hfv'] = P.inp('hfv', [NLAYER, 64, 3])
    I['hybias'] = P.inp('hybias', [NLAYER, 128, 2])
    I['sscw'] = P.inp('sscw', [NLAYER, 128, 8, 4])
    I['dtb'] = P.inp('dtb', [NLAYER, 16, 1])
    I['alog'] = P.inp('alog', [NLAYER, 1, 16])
    I['dskip'] = P.inp('dskip', [NLAYER, 1, 8])
    I['sng'] = P.inp('sng', [NLAYER, 128, 4])
    I['w_out'] = P.inp('w_out', [NLAYER, D, D])
    I['wq'] = P.inp('wq', [NLAYER, D, 2048])
    I['k1T'] = P.inp('k1T', [NLAYER, 128, 128])
    I['k2T'] = P.inp('k2T', [NLAYER, 128, 128])
    I['uT'] = P.inp('uT', [NLAYER, D, 16384])
    I['v'] = P.inp('v', [NLAYER, 16384, D])
    for k, a in c.items():
        I[k] = P.inp('c_' + k, a.shape, BF16 if a.dtype == ml_dtypes.bfloat16 else F32)
    return I


class Ctx:
    pass


def build(layers=(0, 1), phases=None, dbg=False, final=True):
    P = Prog(layers, phases, dbg)
    nc, fw = P.nc, P.fw
    I = _declare(P)
    S = Ctx()
    S.res = {}
    for b in range(2):
        S.res[(b, 'l')] = P.scratch('res_l%d' % b, [D, L])
        S.res[(b, 'c')] = P.scratch('res_c%d' % b, [D, LC])
    S.pl = {}
    S.mix = {}
    for b in range(2):
        S.pl[(b, 'l')] = P.scratch('pl_l%d' % b, [NCOL, L])
        S.pl[(b, 'c')] = P.scratch('pl_c%d' % b, [NCOL, LC])
        S.mix[(b, 'l')] = P.scratch('mix_l%d' % b, [D, L], BF16)
        S.mix[(b, 'c')] = P.scratch('mix_c%d' % b, [D, LC], BF16)
    S.khat = {'l': P.scratch('khat_l', [2, 17 * 128, 256]), 'c': P.scratch('khat_c', [2, 3 * 128, 256])}
    S.ubf = P.scratch('ubf', [64, 128, 8, 256], BF16)
    S.vbf = P.scratch('vbf', [64, 128, 2, 1024], BF16)
    S.outT = P.scratch('outT', [2, D, L], F32, out=True)

    A = lambda n, sh, dt=F32: nc.alloc_sbuf_tensor('sb_' + n, sh, dt)
    S.ident = A('ident', [128, 128]); S.ones = A('ones', [128, 128])
    S.uinc = A('uinc', [128, 128]); S.linc = A('linc', [128, 128])
    S.maskf = A('maskf', [128, 128]); S.maskb = A('maskb', [128, 128])
    S.modT = A('modT', [128, 48, 3])
    S.G1 = A('G1', [128, 8, 3]); S.G2 = A('G2', [128, 8, 3])
    S.gfin = A('gfin', [128, 8]); S.zero8 = A('zero8', [128, 8])
    S.hyn = A('hyn', [128, 2, 2])
    for nm in ('ident', 'ones', 'uinc', 'linc', 'maskf', 'maskb'):
        fw.dma('sp', getattr(S, nm)[:], I[nm], writes=[nm])
    fw.dma('sp', S.gfin[:], I['gfin'], writes=['gfin'])
    fw.memset('pool', S.zero8[:], 0.0, writes=['zero8'])
    S.epsc = A('epsc', [128, 1])
    fw.memset('pool', S.epsc[:], EPS, writes=['epsc'])
    S.negpi = A('negpi', [128, 1])
    fw.memset('pool', S.negpi[:], -PI, writes=['negpi'])
    fw.barrier()

    def src(l, seq):
        b, kind = seq
        if l == layers[0] and l == 0:
            return I['xT'][b] if kind == 'l' else I['ctxT'][b]
        return S.res[seq]

    for l in layers:
        ctx_out = l < NLAYER - 1
        if P.want('mod'):
            phase_mod(P, I, S, l)
        if P.want('proj'):
            phase_proj(P, I, S, l, src)
        if P.want('filt'):
            phase_filt(P, I, S, l, 'l')
            if ctx_out:
                phase_filt(P, I, S, l, 'c')
        if P.want('hy'):
            phase_hy(P, I, S, l, 'l')
            if ctx_out:
                phase_hy(P, I, S, l, 'c')
        if P.want('fn'):
            phase_fn(P, I, S, l, 'l')
            if ctx_out:
                phase_fn(P, I, S, l, 'c')
        if P.want('ssd'):
            phase_ssd(P, I, S, l, ctx_out)
        if P.want('out'):
            phase_out(P, I, S, l, src, ctx_out)
        if P.want('peer'):
            phase_peer(P, I, S, l, ctx_out)
    if final and P.want('final'):
        phase_final(P, I, S)
    if dbg:
        dh = P.scratch('dbg_hyn', [128, 4])
        fw.dma('sp', dh, S.hyn[:].rearrange("p a b -> p (a b)"), writes=['dbg_hyn'])
    fw.barrier()
    return P


def phase_mod(P, I, S, l):
    nc, fw = P.nc, P.fw
    with ExitStack() as es:
        cin = es.enter_context(P.sb('m_c', [128, 8, 3], F32))
        sc = es.enter_context(P.sb('m_sc', [128, 8, 3], F32))
        w0 = es.enter_context(P.sb('m_w0', [128, 8, 512], F32))
        w1 = es.enter_context(P.sb('m_w1', [128, 8, 512], F32))
        bada = es.enter_context(P.sb('m_b', [128, 48], F32))
        g1 = es.enter_context(P.sb('m_g1', [128, 8], F32))
        g2 = es.enter_context(P.sb('m_g2', [128, 8], F32))
        tmp = es.enter_context(P.sb('m_t', [128, 8, 3], F32))
        wb = [w0, w1]
        fw.dma('sp', cin[:], I['cT'], writes=['m_c'])
        fw.dma('sp', bada[:], I['b_adaT'][l], writes=['m_b'])
        fw.dma('sp', g1[:], I['gn1'][l], writes=['m_g1'])
        fw.dma('sp', g2[:], I['gn2'][l], writes=['m_g2'])
        fw.act(sc[:], cin[:], AF.Silu, reads=['m_c'], writes=['m_sc'])
        wv = I['w_ada'][l].rearrange("(dc p) n -> p dc n", p=128)
        for blk in range(12):
            w = wb[blk % 2]
            wk = 'm_w%d' % (blk % 2)
            fw.dma('sp', w[:], wv[:, :, blk * 512:(blk + 1) * 512], writes=[wk])
            for j in range(4):
                cc = blk * 4 + j
                pk = ('ps', cc % 2)
                pt = P.ps[cc % 2][:, 0:3]
                for dc in range(8):
                    fw.mm(pt, w[:, dc, j * 128:(j + 1) * 128], sc[:, dc, :], start=(dc == 0), stop=(dc == 7),
                          reads=[wk, 'm_sc'], writes=[pk])
                fw.ts('dve', S.modT[:, cc, :], pt, bada[:, cc:cc + 1], None, ALU.add, reads=[pk, 'm_b'], writes=['modT'])
        for (G, g, gk, c0, nm) in ((S.G1, g1, 'm_g1', 8, 'G1'), (S.G2, g2, 'm_g2', 32, 'G2')):
            fw.ts('dve', tmp[:], S.modT[:, c0:c0 + 8, :], 1.0, None, ALU.add, reads=['modT'], writes=['m_t'])
            fw.tt('dve', G[:], tmp[:], g[:, :].unsqueeze(2).to_broadcast([128, 8, 3]), ALU.mult, reads=['m_t', gk], writes=[nm])
        fw.barrier()


def seqs_of(ctx_too=True):
    out = []
    for b in range(2):
        out.append((b, 'l'))
        if ctx_too:
            out.append((b, 'c'))
    return out


def seq_len(seq):
    return L if seq[1] == 'l' else LC


def seq_col(seq):
    return seq[0] if seq[1] == 'l' else 2


def normmod(P, S, xt, xk, T, Gap, shap, hm, hk, sq, rstd, psk, pst):
    fw = P.fw
    fw.act(sq[:, :, :T], xt[:, :, :T], AF.Square, reads=[xk], writes=[('nm_sq', dc) for dc in range(8)])
    for dc in range(8):
        fw.mm(pst[:, :T], S.ones[:], sq[:, dc, :T], start=(dc == 0), stop=(dc == 7), reads=[('nm_sq', dc), 'ones'], writes=[psk])
    fw.act(rstd[:, :T], pst[:, :T], AF.Sqrt, bias=S.epsc[:, 0:1], scale=1.0 / D, reads=[psk, 'epsc'], writes=['nm_rstd'])
    fw.op('dve', lambda e: e.reciprocal(rstd[:, :T], rstd[:, :T]), reads=['nm_rstd'], writes=['nm_rstd'])
    for dc in range(8):
        en = 'dve' if dc % 2 == 0 else 'pool'
        fw.stt(en, sq[:, dc, :T], xt[:, dc, :T], Gap[:, dc:dc + 1], rstd[:, :T], ALU.mult, ALU.mult,
               reads=[xk, 'nm_rstd'], writes=[('nm_sq', dc)])
        fw.act(hm[:, dc, :T], sq[:, dc, :T], AF.Identity, bias=shap[:, dc:dc + 1], reads=[('nm_sq', dc)], writes=[(hk, dc)])
    return [(hk, dc) for dc in range(8)]


def load_cast_weight(P, dst, dstk, srcv, ncols, stg, stgk):
    fw = P.fw
    nb = (ncols + 511) // 512
    for blk in range(nb):
        c0 = blk * 512
        c1 = min(ncols, c0 + 512)
        st = stg[blk % 2]
        sk = stgk[blk % 2]
        fw.dma('sp', st[:, :, :c1 - c0], srcv[:, :, c0:c1], writes=[sk])
        fw.cp('dve' if blk % 2 == 0 else 'pool', dst[:, :, c0:c1], st[:, :, :c1 - c0], reads=[sk], writes=[(dstk, blk)])
    return [(dstk, blk) for blk in range(nb)]


def phase_proj(P, I, S, l, src):
    nc, fw = P.nc, P.fw
    with ExitStack() as es:
        winb = es.enter_context(P.sb('p_win', [128, 8, NCOL], BF16))
        s0 = es.enter_context(P.sb('p_s0', [128, 8, 512], F32))
        s1 = es.enter_context(P.sb('p_s1', [128, 8, 512], F32))
        sq = es.enter_context(P.sb('p_sq', [128, 8, 512], F32))
        rstd = es.enter_context(P.sb('p_rstd', [128, 512], F32))
        hm = es.enter_context(P.sb('p_hm', [128, 8, 512], BF16))
        ob = es.enter_context(P.sb('p_o', [128, 4, 512], F32))
        wkeys = load_cast_weight(P, winb, 'p_win', I['w_in'][l].rearrange("(dc p) n -> p dc n", p=128), NCOL, [s0, s1], ['p_s0', 'p_s1'])
        xb = [s0, s1]
        chunks = [(c0, min(128, NCOL - c0)) for c0 in range(0, OFF_DT, 128)] + [(OFF_DT, 16)] + [(OFF_FN, 128), (OFF_FN + 128, 128)]
        it = 0
        oi = 0
        for seq in seqs_of(True):
            Lq = seq_len(seq)
            T = min(512, Lq)
            col = seq_col(seq)
            xv = src(l, seq).rearrange("(dc p) t -> p dc t", p=128)
            for tb in range(Lq // T):
                xt = xb[it % 2]
                xk = 'p_s%d' % (it % 2)
                it += 1
                fw.dma('sp', xt[:, :, :T], xv[:, :, tb * T:(tb + 1) * T], reads=[('res', seq)], writes=[xk])
                hk = normmod(P, S, xt, xk, T, S.G1[:, :, col], S.modT[:, 0:8, col], hm, 'p_hm', sq, rstd, ('ps', 0), P.ps[0])
                for ci, (c0, cw) in enumerate(chunks):
                    pb = 1 + ci % 3
                    pt = P.ps[pb][:cw, :T]
                    for dc in range(8):
                        fw.mm(pt, winb[:, dc, c0:c0 + cw], hm[:, dc, :T], start=(dc == 0), stop=(dc == 7),
                              reads=wkeys + hk if dc in (0, 7) else (), writes=[('ps', pb)])
                    o = ob[:cw, oi % 4, :T]
                    ok = ('p_o', oi % 4)
                    oi += 1
                    if ci % 2 == 0:
                        fw.cp('act', o, pt, reads=[('ps', pb)], writes=[ok])
                    else:
                        fw.cp('dve', o, pt, reads=[('ps', pb)], writes=[ok])
                    fw.dma('sp', S.pl[seq][c0:c0 + cw, tb * T:(tb + 1) * T], o, reads=[ok], writes=[('pl', seq, ci, tb)])
        fw.barrier()


def phase_final(P, I, S):
    nc, fw = P.nc, P.fw
    with ExitStack() as es:
        s0 = es.enter_context(P.sb('f_s0', [128, 8, 512], F32))
        s1 = es.enter_context(P.sb('f_s1', [128, 8, 512], F32))
        sq = es.enter_context(P.sb('f_sq', [128, 8, 512], F32))
        rstd = es.enter_context(P.sb('f_rstd', [128, 512], F32))
        o0 = es.enter_context(P.sb('f_o0', [128, 8, 512], F32))
        o1 = es.enter_context(P.sb('f_o1', [128, 8, 512], F32))
        xb = [s0, s1]
        ob = [o0, o1]
        it = 0
        for b in range(2):
            xv = S.res[(b, 'l')].rearrange("(dc p) t -> p dc t", p=128)
            ov = S.outT[b].rearrange("(dc p) t -> p dc t", p=128)
            for tb in range(L // 512):
                xt = xb[it % 2]; xk = 'f_s%d' % (it % 2)
                o = ob[it % 2]; ok = 'f_o%d' % (it % 2)
                it += 1
                fw.dma('sp', xt[:], xv[:, :, tb * 512:(tb + 1) * 512], writes=[xk])
                hk = normmod(P, S, xt, xk, 512, S.gfin, S.zero8, o, ok + 'h', sq, rstd, ('ps', 0), P.ps[0])
                fw.dma('sp', ov[:, :, tb * 512:(tb + 1) * 512], o[:], reads=hk, writes=[('outT', b, tb)])
        fw.barrier()


def prep_shared(inp):
    f = lambda a: np.ascontiguousarray(np.asarray(a, np.float32))
    sh = {}
    sh['w_ada'] = f(inp['w_ada'])
    sh['b_adaT'] = np.stack([_pc(inp['b_ada'][l], 48) for l in range(NLAYER)])
    sh['gn1'] = np.stack([_pc(inp['g_norm1'][l], 8) for l in range(NLAYER)])
    sh['gn2'] = np.stack([_pc(inp['g_norm2'][l], 8) for l in range(NLAYER)])
    sh['gfin'] = _pc(inp['g_final'], 8)
    sh['w_in'] = f(inp['w_in'])
    hy = []
    for l in range(NLAYER):
        m = np.concatenate([np.asarray(inp['hy_conv_w'][l], np.float32), np.asarray(inp['hy_conv_b'][l], np.float32)[None]], 0)
        hy.append(np.ascontiguousarray(m.reshape(4, 6, 128).transpose(2, 1, 0)))
    sh['hycw'] = np.stack(hy)
    sh['hfw1'] = f(inp['hf_w1']); sh['hfw2'] = f(inp['hf_w2']); sh['hfw3'] = f(inp['hf_w3'])
    sh['hfv'] = np.ascontiguousarray(np.stack([inp['hf_b1'], inp['hf_b2'], inp['hf_freq']], axis=-1).astype(np.float32))
    sh['hybias'] = np.stack([_pc(inp['hy_bias'][l], 2) for l in range(NLAYER)])
    ss = []
    for l in range(NLAYER):
        m = np.concatenate([np.asarray(inp['ssd_conv_w'][l], np.float32), np.asarray(inp['ssd_conv_b'][l], np.float32)[None]], 0)
        ss.append(np.ascontiguousarray(m.reshape(4, 8, 128).transpose(2, 1, 0)))
    sh['sscw'] = np.stack(ss)
    sh['dtb'] = f(np.asarray(inp['ssd_dt_bias']).reshape(NLAYER, 16, 1))
    sh['alog'] = f(np.asarray(inp['ssd_a_log']).reshape(NLAYER, 1, 16))
    sh['dskip'] = f(np.asarray(inp['ssd_d']).reshape(NLAYER, 1, 8))
    sh['sng'] = np.stack([_pc(inp['ssd_norm_g'][l], 4) for l in range(NLAYER)])
    sh['w_out'] = f(inp['w_out'])
    sh['wq'] = f(inp['peer_wq'])
    sh['k1T'] = np.ascontiguousarray(np.asarray(inp['peer_k1'], np.float32).transpose(0, 2, 1))
    sh['k2T'] = np.ascontiguousarray(np.asarray(inp['peer_k2'], np.float32).transpose(0, 2, 1))
    sh['uT'] = np.ascontiguousarray(np.asarray(inp['peer_u'], np.float32).transpose(0, 2, 1))
    sh['v'] = f(inp['peer_v'])
    for k, a in _consts().items():
        sh['c_' + k] = a
    return sh


def prep_core(inp, core):
    m = {}
    x = np.asarray(inp['x'], np.float32)[2 * core:2 * core + 2]
    cx = np.asarray(inp['ctx'], np.float32)[2 * core:2 * core + 2]
    m['xT'] = np.ascontiguousarray(x.transpose(0, 2, 1))
    m['ctxT'] = np.ascontiguousarray(cx.transpose(0, 2, 1))
    cv = np.stack([np.asarray(inp['c'], np.float32)[2 * core], np.asarray(inp['c'], np.float32)[2 * core + 1],
                   np.asarray(inp['c_ctx'], np.float32)], axis=-1)
    m['cT'] = np.ascontiguousarray(cv.reshape(8, 128, 3).transpose(1, 0, 2))
    return m


_PROG = None


def kernel(**inputs):
    global _PROG
    if _PROG is None:
        _PROG = build()
    P = _PROG
    sh = prep_shared(inputs)
    in_maps = []
    for core in range(8):
        m = dict(sh)
        m.update(prep_core(inputs, core))
        in_maps.append({k: m[k] for k in P.inputs})
    res = run_bass_kernel_spmd(P.nc, in_maps, core_ids=list(range(8)))
    outs = [np.asarray(r['outT']).transpose(0, 2, 1) for r in res.results]
    return np.ascontiguousarray(np.concatenate(outs, axis=0).astype(np.float32))


def phase_fn(P, I, S, l, tag):
    nc, fw = P.nc, P.fw
    Lq = L if tag == 'l' else LC
    nsc = Lq // 128
    TB = min(512, Lq)
    ntb = Lq // TB
    with ExitStack() as es:
        ut = es.enter_context(P.sb('fn_ut', [128, 4, Lq], F32))
        cbd = es.enter_context(P.sb('fn_cbd', [128, 128], F32))
        sbd = es.enter_context(P.sb('fn_sbd', [128, 128], F32))
        atok = es.enter_context(P.sb('fn_a', [128, nsc, 512], BF16))
        btok = es.enter_context(P.sb('fn_b', [128, nsc, 512], BF16))
        cl = es.enter_context(P.sb('fn_cl', [128, nsc, TB], BF16))
        sl = es.enter_context(P.sb('fn_sl', [128, nsc, TB], BF16))
        ob = es.enter_context(P.sb('fn_o', [128, 2, TB], BF16))
        fw.dma('sp', cbd[:], I['cbd'], writes=['fn_cbd'])
        fw.dma('sp', sbd[:], I['sbd'], writes=['fn_sbd'])
        for b in range(2):
            for ch in range(2):
                fw.dma('sp', ut[:, b * 2 + ch, :], S.pl[(b, tag)][OFF_FN + ch * 128:OFF_FN + (ch + 1) * 128, :], writes=[('fn_ut', b * 2 + ch)])
        utk = [('fn_ut', m) for m in range(4)]
        for tc in range(nsc):
            pa, pb = 2 * (tc % 2), 2 * (tc % 2) + 1
            for m in range(4):
                fw.mm(P.ps[pa][:, m * 128:(m + 1) * 128], ut[:, m, tc * 128:(tc + 1) * 128], cbd[:], reads=utk + ['fn_cbd'], writes=[('ps', pa)])
                fw.mm(P.ps[pb][:, m * 128:(m + 1) * 128], ut[:, m, tc * 128:(tc + 1) * 128], sbd[:], reads=utk + ['fn_sbd'], writes=[('ps', pb)])
            fw.cp('act', atok[:, tc, :], P.ps[pa][:, :], reads=[('ps', pa)], writes=[('fn_a', tc)])
            fw.cp('dve', btok[:, tc, :], P.ps[pb][:, :], reads=[('ps', pb)], writes=[('fn_b', tc)])
        ak = [('fn_a', tc) for tc in range(nsc)]
        bk = [('fn_b', tc) for tc in range(nsc)]
        oi = 0
        for tb in range(ntb):
            fw.dma('sp', cl[:], I['cl_' + tag][tb], writes=['fn_cl'])
            fw.dma('sp', sl[:], I['sl_' + tag][tb], writes=['fn_sl'])
            for m in range(4):
                b, ch = m // 2, m % 2
                pi = 4 + m % 2
                pt = P.ps[pi][:, :TB]
                for tc in range(nsc):
                    fw.mm(pt, atok[:, tc, m * 128:(m + 1) * 128], cl[:, tc, :], start=(tc == 0), stop=False,
                          reads=ak + ['fn_cl'] if tc in (0, nsc - 1) else (), writes=[('ps', pi)])
                for tc in range(nsc):
                    fw.mm(pt, btok[:, tc, m * 128:(m + 1) * 128], sl[:, tc, :], start=False, stop=(tc == nsc - 1),
                          reads=bk + ['fn_sl'] if tc in (0, nsc - 1) else (), writes=[('ps', pi)])
                o = ob[:, oi % 2, :]
                ok = ('fn_o', oi % 2)
                oi += 1
                fw.cp('act' if m % 2 == 0 else 'dve', o, pt, reads=[('ps', pi)], writes=[ok])
                fw.dma('sp', S.mix[(b, tag)][768 + ch * 128:768 + (ch + 1) * 128, tb * TB:(tb + 1) * TB], o, reads=[ok], writes=[('mixfn', b, ch, tb)])
        fw.barrier()


def phase_out(P, I, S, l, src, ctx_out):
    nc, fw = P.nc, P.fw
    with ExitStack() as es:
        wout = es.enter_context(P.sb('o_w', [128, 8, 1024], BF16))
        s0 = es.enter_context(P.sb('o_s0', [128, 8, 512], F32))
        s1 = es.enter_context(P.sb('o_s1', [128, 8, 512], F32))
        m0 = es.enter_context(P.sb('o_m0', [128, 8, 512], BF16))
        m1 = es.enter_context(P.sb('o_m1', [128, 8, 512], BF16))
        x0 = es.enter_context(P.sb('o_x0', [128, 8, 512], F32))
        x1 = es.enter_context(P.sb('o_x1', [128, 8, 512], F32))
        wkeys = load_cast_weight(P, wout, 'o_w', I['w_out'][l].rearrange("(dc p) n -> p dc n", p=128), 1024, [s0, s1], ['o_s0', 'o_s1'])
        xb, mb, ob = [s0, s1], [m0, m1], [x0, x1]
        it = 0
        for seq in seqs_of(ctx_out):
            Lq = seq_len(seq)
            T = min(512, Lq)
            col = seq_col(seq)
            xv = src(l, seq).rearrange("(dc p) t -> p dc t", p=128)
            mv = S.mix[seq].rearrange("(dc p) t -> p dc t", p=128)
            rv = S.res[seq].rearrange("(dc p) t -> p dc t", p=128)
            for tb in range(Lq // T):
                k = it % 2
                it += 1
                xt, mx, xo = xb[k], mb[k], ob[k]
                fw.dma('sp', xt[:, :, :T], xv[:, :, tb * T:(tb + 1) * T], writes=['o_s%d' % k])
                fw.dma('sp', mx[:, :, :T], mv[:, :, tb * T:(tb + 1) * T], writes=['o_m%d' % k])
                for dch in range(8):
                    pi = dch % 4
                    pt = P.ps[pi][:, :T]
                    for cc in range(8):
                        fw.mm(pt, wout[:, cc, dch * 128:(dch + 1) * 128], mx[:, cc, :T], start=(cc == 0), stop=(cc == 7),
                              reads=wkeys + ['o_m%d' % k] if cc in (0, 7) else (), writes=[('ps', pi)])
                    fw.stt('dve', xo[:, dch, :T], pt, S.modT[:, 16 + dch, col:col + 1], xt[:, dch, :T], ALU.mult, ALU.add,
                           reads=[('ps', pi), 'o_s%d' % k, 'modT'], writes=[('o_x%d' % k, dch)])
                fw.dma('sp', rv[:, :, tb * T:(tb + 1) * T], xo[:, :, :T], reads=[('o_x%d' % k, d8) for d8 in range(8)], writes=[('resw', seq, tb)])
        fw.barrier()


def phase_filt(P, I, S, l, tag):
    nc, fw = P.nc, P.fw
    Lq = L if tag == 'l' else LC
    li = 0 if tag == 'l' else 1
    nsc = Lq // 128
    nfc = (Lq + 1 + 127) // 128
    T = min(512, Lq)
    with ExitStack() as es:
        zT = es.enter_context(P.sb('fl_z', [33, Lq], F32))
        w1 = es.enter_context(P.sb('fl_w1', [33, 64], F32))
        w2 = es.enter_context(P.sb('fl_w2', [64, 64], F32))
        w3 = es.enter_context(P.sb('fl_w3', [64, 512], F32))
        hv = es.enter_context(P.sb('fl_hv', [64, 3], F32))
        fb = es.enter_context(P.sb('fl_fb', [64, 2], F32))
        h1 = es.enter_context(P.sb('fl_h1', [64, Lq], F32))
        h2 = es.enter_context(P.sb('fl_h2', [64, Lq], F32))
        win = es.enter_context(P.sb('fl_win', [128, 2, nsc, 256], F32))
        pm = es.enter_context(P.sb('fl_pm', [128, nsc, 256], BF16))
        mmn = es.enter_context(P.sb('fl_mm', [128, nsc, 256], BF16))
        acc = es.enter_context(P.sb('fl_acc', [128, 256], F32))
        t1 = es.enter_context(P.sb('fl_t1', [128, 2, 256], F32))
        t2 = es.enter_context(P.sb('fl_t2', [128, 2, 256], F32))
        tmp = es.enter_context(P.sb('fl_tmp', [64, 512], F32))
        tmpk = es.enter_context(P.sb('fl_tmpk', [64, 512], F32))
        cf = es.enter_context(P.sb('fl_cf', [128, 2, nsc, 128], BF16))
        sf = es.enter_context(P.sb('fl_sf', [128, 2, nsc, 128], BF16))
        ko = es.enter_context(P.sb('fl_ko', [128, 2, 2, 256], F32))
        ntmp = es.enter_context(P.sb('fl_n', [128, 2], F32))
        fw.dma('sp', zT[:], I['zT_' + tag], writes=['fl_z'])
        fw.dma('sp', w1[:], I['hfw1'][l], writes=['fl_w1'])
        fw.dma('sp', w2[:], I['hfw2'][l], writes=['fl_w2'])
        fw.dma('sp', w3[:], I['hfw3'][l], writes=['fl_w3'])
        fw.dma('sp', hv[:], I['hfv'][l], writes=['fl_hv'])
        for v in range(2):
            fw.dma('sp', win[:, v, :, :], I['win_' + tag][v], writes=[('fl_win', v)])
        fw.ts('dve', fb[:], hv[:, 0:2], hv[:, 2:3], None, ALU.mult, reads=['fl_hv'], writes=['fl_fb'])
        fw.memset('pool', acc[:], 0.0, writes=['fl_acc'])

        def sin_layer(dst, dk, w, wk, K, srcT, sk, col):
            for blk in range(Lq // T):
                pi = blk % 2
                pt = P.ps[pi][:64, :T]
                fw.mm(pt, w[:K, :64], srcT[:K, blk * T:(blk + 1) * T], reads=[wk, sk], writes=[('ps', pi)])
                fw.ts('dve', tmp[:, :T], pt, hv[:, 2:3], fb[:, col:col + 1], ALU.mult, ALU.add, reads=[('ps', pi), 'fl_hv', 'fl_fb'], writes=['fl_tmp'])
                MAGIC = 12582912.0
                fw.ts('dve', tmpk[:, :T], tmp[:, :T], 1.0 / (2.0 * PI), MAGIC, ALU.mult, ALU.add, reads=['fl_tmp'], writes=['fl_tmpk'])
                fw.ts('dve', tmpk[:, :T], tmpk[:, :T], MAGIC, None, ALU.subtract, reads=['fl_tmpk'], writes=['fl_tmpk'])
                fw.stt('dve', tmp[:, :T], tmpk[:, :T], -2.0 * PI, tmp[:, :T], ALU.mult, ALU.add, reads=['fl_tmpk', 'fl_tmp'], writes=['fl_tmp'])
                fw.act(dst[:, blk * T:(blk + 1) * T], tmp[:, :T], AF.Sin, reads=['fl_tmp'], writes=[dk])

        sin_layer(h1, 'fl_h1', w1, 'fl_w1', 33, zT, 'fl_z', 0)
        sin_layer(h2, 'fl_h2', w2, 'fl_w2', 64, h1, 'fl_h1', 1)
        for sc in range(nsc):
            pi = 2 + sc % 2
            pt = P.ps[pi]
            fw.mm(pt[:, :], h2[:64, sc * 128:(sc + 1) * 128], w3[:64, :], reads=['fl_h2', 'fl_w3'], writes=[('ps', pi)])
            fw.tt('dve', t1[:], pt[:, :].rearrange("p (v c) -> p v c", v=2), win[:, :, sc, :], ALU.mult,
                  reads=[('ps', pi), ('fl_win', 0), ('fl_win', 1)], writes=['fl_t1'])
            fw.tt('pool', pm[:, sc, :], t1[:, 0, :], t1[:, 1, :], ALU.add, reads=['fl_t1'], writes=[('fl_pm', sc)])
            fw.tt('pool', mmn[:, sc, :], t1[:, 0, :], t1[:, 1, :], ALU.subtract, reads=['fl_t1'], writes=[('fl_mm', sc)])
            fw.act(t2[:], t1[:], AF.Square, reads=['fl_t1'], writes=['fl_t2'])
            fw.tt('pool', acc[:], acc[:], t2[:, 0, :], ALU.add, reads=['fl_t2'], writes=['fl_acc'])
            fw.tt('pool', acc[:], acc[:], t2[:, 1, :], ALU.add, reads=['fl_t2'], writes=['fl_acc'])
        for ch in range(2):
            fw.mm(P.ps[0][:, ch:ch + 1], acc[:, ch * 128:(ch + 1) * 128], S.ones[:, 0:1], reads=['fl_acc', 'ones'], writes=[('ps', 0)])
        fw.act(ntmp[:], P.ps[0][:, 0:2], AF.Sqrt, bias=S.epsc[:, 0:1], reads=[('ps', 0), 'epsc'], writes=['fl_n'])
        fw.op('dve', lambda e: e.reciprocal(S.hyn[:, :, li], ntmp[:]), reads=['fl_n'], writes=[('hyn', li)])
        pmk = [('fl_pm', sc) for sc in range(nsc)]
        mmk = [('fl_mm', sc) for sc in range(nsc)]
        for fc in range(nfc):
            fsz = 128 if fc < nfc - 1 else 1
            k = fc % 2
            fw.dma('sp', cf[:, k, :, :], I['cf_' + tag][fc], writes=[('fl_cf', k)])
            fw.dma('sp', sf[:, k, :, :], I['sf_' + tag][fc], writes=[('fl_sf', k)])
            pa, pb = 4 + 2 * k, 5 + 2 * k
            for sc in range(nsc):
                fw.mm(P.ps[pa][:fsz, :256], cf[:, k, sc, :fsz], pm[:, sc, :], start=(sc == 0), stop=(sc == nsc - 1),
                      reads=pmk + [('fl_cf', k)] if sc in (0, nsc - 1) else (), writes=[('ps', pa)])
            for sc in range(nsc):
                fw.mm(P.ps[pb][:fsz, :256], sf[:, k, sc, :fsz], mmn[:, sc, :], start=(sc == 0), stop=(sc == nsc - 1),
                      reads=mmk + [('fl_sf', k)] if sc in (0, nsc - 1) else (), writes=[('ps', pb)])
            fw.cp('act', ko[:fsz, k, 0, :], P.ps[pa][:fsz, :256], reads=[('ps', pa)], writes=[('fl_ko', k, 0)])
            fw.cp('dve', ko[:fsz, k, 1, :], P.ps[pb][:fsz, :256], reads=[('ps', pb)], writes=[('fl_ko', k, 1)])
            for v in range(2):
                fw.dma('sp', S.khat[tag][v, fc * 128:fc * 128 + fsz, :], ko[:fsz, k, v, :], reads=[('fl_ko', k, v)], writes=[('khat', tag, v, fc)])
        fw.barrier()


def phase_hy(P, I, S, l, tag):
    nc, fw = P.nc, P.fw
    Lq = L if tag == 'l' else LC
    li = 0 if tag == 'l' else 1
    nsc = Lq // 128
    nfc = (Lq + 1 + 127) // 128
    TB = min(512, Lq)
    ntb = Lq // TB
    with ExitStack() as es:
        u = es.enter_context(P.sb('hy_u', [128, 4, Lq], F32))
        x1c = es.enter_context(P.sb('hy_x1', [128, 4, Lq], BF16))
        utok = es.enter_context(P.sb('hy_ut', [128, nsc, 512], BF16))
        wre = es.enter_context(P.sb('hy_wre', [128, nfc, 512], BF16))
        wim = es.enter_context(P.sb('hy_wim', [128, nfc, 512], BF16))
        cw = es.enter_context(P.sb('hy_cw', [128, 6, 4], F32))
        hb = es.enter_context(P.sb('hy_hb', [128, 2], F32))
        fw.dma('sp', cw[:], I['hycw'][l], writes=['hy_cw'])
        fw.dma('sp', hb[:], I['hybias'][l], writes=['hy_hb'])
        with ExitStack() as es2:
            raw = es2.enter_context(P.sb('hy_raw', [128, 3, Lq + 2], F32))
            cv = [es2.enter_context(P.sb('hy_cv%d' % k, [128, Lq], F32)) for k in range(3)]
            fw.memset('pool', raw[:, :, 0:1], 0.0, writes=['hy_raw_h0'])
            fw.memset('pool', raw[:, :, Lq + 1:Lq + 2], 0.0, writes=['hy_raw_h1'])
            for m in range(4):
                b, ch = m // 2, m % 2
                for k in range(3):
                    r0 = k * 256 + ch * 128
                    fw.dma('sp', raw[:, k, 1:Lq + 1], S.pl[(b, tag)][r0:r0 + 128, :], writes=[('hy_raw', k)])
                    ci = 2 * k + ch
                    rk = [('hy_raw', k), 'hy_raw_h0', 'hy_raw_h1', 'hy_cw']
                    fw.ts('dve', cv[k][:], raw[:, k, 0:Lq], cw[:, ci, 0:1], cw[:, ci, 3:4], ALU.mult, ALU.add, reads=rk, writes=[('hy_cv', k)])
                    fw.stt('dve', cv[k][:], raw[:, k, 1:Lq + 1], cw[:, ci, 1:2], cv[k][:], ALU.mult, ALU.add, reads=rk, writes=[('hy_cv', k)])
                    fw.stt('dve', cv[k][:], raw[:, k, 2:Lq + 2], cw[:, ci, 2:3], cv[k][:], ALU.mult, ALU.add, reads=rk, writes=[('hy_cv', k)])
                fw.tt('pool', u[:, m, :], cv[2][:], cv[0][:], ALU.mult, reads=[('hy_cv', 2), ('hy_cv', 0)], writes=[('hy_u', m)])
                fw.cp('act', x1c[:, m, :], cv[1][:], reads=[('hy_cv', 1)], writes=[('hy_x1', m)])
            uk = [('hy_u', m) for m in range(4)]
            for sc in range(nsc):
                pi = sc % 2
                for m in range(4):
                    fw.tr(P.ps[pi][:, m * 128:(m + 1) * 128], u[:, m, sc * 128:(sc + 1) * 128], S.ident[:], reads=uk + ['ident'], writes=[('ps', pi)])
                fw.cp('act' if sc % 2 == 0 else 'dve', utok[:, sc, :], P.ps[pi][:, :], reads=[('ps', pi)], writes=[('hy_ut', sc)])
            fw.barrier()
        with ExitStack() as es2:
            cf = es2.enter_context(P.sb('hy_cf', [128, 2, nsc, 128], BF16))
            sf = es2.enter_context(P.sb('hy_sf', [128, 2, nsc, 128], BF16))
            kk = es2.enter_context(P.sb('hy_kk', [128, 2, 2, 256], F32))
            tq = [es2.enter_context(P.sb('hy_t%d' % q, [128, 2, 256], F32)) for q in range(4)]
            for fc in range(nfc):
                fsz = 128 if fc < nfc - 1 else 1
                k = fc % 2
                fw.dma('sp', cf[:, k, :, :], I['cf_' + tag][fc], writes=[('hy_cf', k)])
                fw.dma('sp', sf[:, k, :, :], I['sf_' + tag][fc], writes=[('hy_sf', k)])
                for v in range(2):
                    fw.dma('sp', kk[:fsz, k, v, :], S.khat[tag][v, fc * 128:fc * 128 + fsz, :], writes=[('hy_kk', k, v)])
                pa, pb = 2 + 2 * k, 3 + 2 * k
                for sc in range(nsc):
                    fw.mm(P.ps[pa][:fsz, :], cf[:, k, sc, :fsz], utok[:, sc, :], start=(sc == 0), stop=(sc == nsc - 1),
                          reads=[('hy_cf', k)] if sc in (0, nsc - 1) else (), writes=[('ps', pa)])
                for sc in range(nsc):
                    fw.mm(P.ps[pb][:fsz, :], sf[:, k, sc, :fsz], utok[:, sc, :], start=(sc == 0), stop=(sc == nsc - 1),
                          reads=[('hy_sf', k)] if sc in (0, nsc - 1) else (), writes=[('ps', pb)])
                ure = P.ps[pa][:fsz, :].rearrange("p (b c) -> p b c", b=2)
                uim = P.ps[pb][:fsz, :].rearrange("p (b c) -> p b c", b=2)
                kre = kk[:fsz, k, 0, :].unsqueeze(1).to_broadcast([fsz, 2, 256])
                kim = kk[:fsz, k, 1, :].unsqueeze(1).to_broadcast([fsz, 2, 256])
                kr = [('hy_kk', k, 0), ('hy_kk', k, 1)]
                fw.tt('dve', tq[0][:fsz], ure, kre, ALU.mult, reads=[('ps', pa)] + kr, writes=[('hy_t', 0)])
                fw.tt('dve', tq[1][:fsz], uim, kim, ALU.mult, reads=[('ps', pb)] + kr, writes=[('hy_t', 1)])
                fw.tt('dve', tq[2][:fsz], ure, kim, ALU.mult, reads=[('ps', pa)] + kr, writes=[('hy_t', 2)])
                fw.tt('dve', tq[3][:fsz], uim, kre, ALU.mult, reads=[('ps', pb)] + kr, writes=[('hy_t', 3)])
                fw.tt('pool', wre[:fsz, fc, :].rearrange("p (b c) -> p b c", b=2), tq[0][:fsz], tq[1][:fsz], ALU.subtract,
                      reads=[('hy_t', 0), ('hy_t', 1)], writes=[('hy_wre', fc)])
                fw.tt('pool', wim[:fsz, fc, :].rearrange("p (b c) -> p b c", b=2), tq[2][:fsz], tq[3][:fsz], ALU.add,
                      reads=[('hy_t', 2), ('hy_t', 3)], writes=[('hy_wim', fc)])
            fw.barrier()
        with ExitStack() as es2:
            ci_t = es2.enter_context(P.sb('hy_ci', [128, nfc, TB], BF16))
            si_t = es2.enter_context(P.sb('hy_si', [128, nfc, TB], BF16))
            at = es2.enter_context(P.sb('hy_at', [128, 2, TB], F32))
            ob = es2.enter_context(P.sb('hy_o', [128, 2, TB], BF16))
            oi = 0
            for tb in range(ntb):
                fw.dma('sp', ci_t[:], I['ci_' + tag][tb], writes=['hy_ci'])
                fw.dma('sp', si_t[:], I['si_' + tag][tb], writes=['hy_si'])
                for m in range(4):
                    b, ch = m // 2, m % 2
                    pi = m % 2
                    pt = P.ps[pi][:, :TB]
                    for fc in range(nfc):
                        fsz = 128 if fc < nfc - 1 else 1
                        fw.mm(pt, wre[:fsz, fc, m * 128:(m + 1) * 128], ci_t[:fsz, fc, :], start=(fc == 0), stop=False,
                              reads=['hy_ci'] if fc in (0, nfc - 1) else (), writes=[('ps', pi)])
                    for fc in range(nfc - 1):
                        fw.mm(pt, wim[:, fc, m * 128:(m + 1) * 128], si_t[:, fc, :], start=False, stop=(fc == nfc - 2),
                              reads=['hy_si'] if fc in (0, nfc - 2) else (), writes=[('ps', pi)])
                    a = at[:, oi % 2, :]
                    o = ob[:, oi % 2, :]
                    ak, ok = ('hy_at', oi % 2), ('hy_o', oi % 2)
                    oi += 1
                    fw.ts('dve', a, pt, S.hyn[:, ch, li:li + 1], None, ALU.mult, reads=[('ps', pi)], writes=[ak])
                    fw.stt('dve', a, u[:, m, tb * TB:(tb + 1) * TB], hb[:, ch:ch + 1], a, ALU.mult, ALU.add, reads=['hy_hb'], writes=[ak])
                    fw.tt('pool', o, a, x1c[:, m, tb * TB:(tb + 1) * TB], ALU.mult, reads=[ak], writes=[ok])
                    fw.dma('sp', S.mix[(b, tag)][ch * 128:(ch + 1) * 128, tb * TB:(tb + 1) * TB], o, reads=[ok], writes=[('mixhy', b, ch, tb)])
            fw.barrier()


def phase_ssd(P, I, S, l, ctx_out):
    nc, fw = P.nc, P.fw
    ps = P.ps
    with ExitStack() as es:
        E = lambda n, sh, dt=F32: es.enter_context(P.sb(n, sh, dt))
        cw = E('sd_cw', [128, 8, 4]); dtb = E('sd_dtb', [16, 1]); arow = E('sd_arow', [128, 16]); dskb = E('sd_dsk', [128, 8])
        sng = E('sd_sng', [128, 4])
        Hs = [E('sd_H%d' % d, [128, 512]) for d in range(2)]
        xtok = E('sd_xtok', [128, 16, 512]); btok = E('sd_btok', [128, 16, 256]); cbt = E('sd_cbt', [128, 16, 2, 128])
        ct = E('sd_ct', [128, 2, L]); dttok = E('sd_dttok', [128, 16, 16]); ytok = E('sd_ytok', [128, 16, 512])
        raw = E('sd_raw', [128, 8, 514]); xa = E('sd_xa', [128, 8, 512]); dtT = E('sd_dtT', [16, L])
        dtA = E('sd_dtA', [128, 16]); ac16 = E('sd_acum', [128, 16]); acum = ac16[:, 0:8]; t8 = E('sd_t8', [128, 8]); dend = E('sd_dend', [128, 8])
        cdec = E('sd_cdec', [128, 8]); eac = E('sd_eac', [128, 8]); xdt = E('sd_xdt', [128, 8, 64]); xdte = E('sd_xdte', [128, 8, 64])
        segt8 = E('sd_seg8', [128, 8, 128]); dtAb8 = E('sd_dtAb8', [128, 8, 128]); mt8 = E('sd_mt8', [128, 8, 128])
        yo = E('sd_yo', [128, 8, 64]); tmpy = E('sd_tmpy', [128, 512]); hsc = E('sd_hsc', [128, 8, 64])
        zt = E('sd_zt', [128, 4, 128]); yg = E('sd_yg', [128, 4, 128]); sqg = E('sd_sqg', [128, 4, 128]); rs = E('sd_rs', [128, 2, 128])
        og = E('sd_og', [128, 4, 128], BF16)
        fw.dma('sp', cw[:], I['sscw'][l], writes=['sd_cw'])
        fw.dma('sp', dtb[:], I['dtb'][l], writes=['sd_dtb'])
        fw.dma('sp', arow[:], I['alog'][l].partition_broadcast(128), writes=['sd_arow'])
        fw.dma('sp', dskb[:], I['dskip'][l].partition_broadcast(128), writes=['sd_dsk'])
        fw.dma('sp', sng[:], I['sng'][l], writes=['sd_sng'])
        fw.act(arow[:], arow[:], AF.Exp, reads=['sd_arow'], writes=['sd_arow'])
        fw.ts('dve', arow[:], arow[:], -1.0, None, ALU.mult, reads=['sd_arow'], writes=['sd_arow'])
        tri = [S.uinc, S.linc]
        msk = [S.maskf, S.maskb]

        def stage_a(seq):
            Lq = seq_len(seq)
            T = min(512, Lq)
            plv = S.pl[seq][OFF_XBC:OFF_XBC + 1024, :].rearrange("(cc p) t -> p cc t", p=128)
            fw.dma('sp', dtT[:, :Lq], S.pl[seq][OFF_DT:OFF_DT + 16, :], writes=['sd_dtT'])
            fw.act(dtT[:, :Lq], dtT[:, :Lq], AF.Exp, bias=dtb[:, 0:1], reads=['sd_dtT', 'sd_dtb'], writes=['sd_dtT'])
            fw.ts('dve', dtT[:, :Lq], dtT[:, :Lq], 1.0, None, ALU.add, reads=['sd_dtT'], writes=['sd_dtT'])
            fw.act(dtT[:, :Lq], dtT[:, :Lq], AF.Ln, reads=['sd_dtT'], writes=['sd_dtT'])
            for tb in range(Lq // T):
                t0 = tb * T
                lo, hi = max(0, t0 - 1), min(Lq, t0 + T + 1)
                d0 = lo - (t0 - 1)
                if tb == 0:
                    fw.memset('pool', raw[:, :, 0:1], 0.0, writes=['sd_raw'])
                if tb == Lq // T - 1:
                    fw.memset('pool', raw[:, :, T + 1:T + 2], 0.0, writes=['sd_raw'])
                fw.dma('sp', raw[:, :, d0:d0 + hi - lo], plv[:, :, lo:hi], writes=['sd_raw'])
                for cc in range(8):
                    k = ('sd_xa', cc)
                    fw.ts('dve', xa[:, cc, :T], raw[:, cc, 0:T], cw[:, cc, 0:1], cw[:, cc, 3:4], ALU.mult, ALU.add, reads=['sd_raw', 'sd_cw'], writes=[k])
                    fw.stt('dve', xa[:, cc, :T], raw[:, cc, 1:T + 1], cw[:, cc, 1:2], xa[:, cc, :T], ALU.mult, ALU.add, reads=['sd_raw'], writes=[k])
                    fw.stt('dve', xa[:, cc, :T], raw[:, cc, 2:T + 2], cw[:, cc, 2:3], xa[:, cc, :T], ALU.mult, ALU.add, reads=['sd_raw'], writes=[k])
                    fw.act(xa[:, cc, :T], xa[:, cc, :T], AF.Silu, reads=[k], writes=[k])
                xk = [('sd_xa', cc) for cc in range(8)]
                fw.cp('pool', ct[:, :, t0:t0 + T], xa[:, 6:8, :T], reads=xk, writes=['sd_ct'])
                for j in range(T // 128):
                    c = tb * (T // 128) + j
                    sl = slice(j * 128, (j + 1) * 128)
                    for k in range(4):
                        fw.tr(ps[1][:, k * 128:(k + 1) * 128], xa[:, k, sl], S.ident[:], reads=xk, writes=[('ps', 1)])
                    fw.cp('act', xtok[:, c, :], ps[1][:, :], reads=[('ps', 1)], writes=[('sd_xtok', c)])
                    for g in range(2):
                        fw.tr(ps[2][:, g * 128:(g + 1) * 128], xa[:, 4 + g, sl], S.ident[:], reads=xk, writes=[('ps', 2)])
                        fw.mm(ps[2][:, 256 + g * 128:256 + (g + 1) * 128], xa[:, 4 + g, sl], xa[:, 6 + g, sl], reads=xk, writes=[('ps', 2)])
                    fw.cp('dve', btok[:, c, :], ps[2][:, 0:256], reads=[('ps', 2)], writes=[('sd_btok', c)])
                    fw.cp('dve', cbt[:, c, :, :], ps[2][:, 256:512].rearrange("p (g i) -> p g i", g=2), reads=[('ps', 2)], writes=[('sd_cbt', c)])
                    fw.tr(ps[3][:, 0:16], dtT[:16, t0 + j * 128:t0 + (j + 1) * 128], S.ident[:16, :16], reads=['sd_dtT'], writes=[('ps', 3)])
                    fw.cp('dve', dttok[:, c, :], ps[3][:, 0:16], reads=[('ps', 3)], writes=[('sd_dttok', c)])

        def scan(seq, d, with_output, final_out):
            Lq = seq_len(seq)
            ncq = Lq // 128
            H = Hs[d]
            hk = 'sd_H%d' % d
            d8 = slice(d * 8, (d + 1) * 8)
            order = range(ncq) if d == 0 else range(ncq - 1, -1, -1)
            for c in order:
                fw.tt('pool', dtA[:], dttok[:, c, :], arow[:], ALU.mult, reads=[('sd_dttok', c), 'sd_arow'], writes=['sd_dtA'])
                fw.mm(ps[0][:, 0:8], tri[d][:], dtA[:, d8], reads=['sd_dtA'], writes=[('ps', 0)])
                fw.mm(ps[0][:, 8:16], S.ones[:], dtA[:, d8], reads=['sd_dtA'], writes=[('ps', 0)])
                fw.cp('dve', ac16[:], ps[0][:, 0:16], reads=[('ps', 0)], writes=['sd_acum'])
                fw.tt('pool', t8[:], ac16[:, 8:16], acum, ALU.subtract, reads=['sd_acum'], writes=['sd_t8'])
                fw.act(dend[:], t8[:], AF.Exp, reads=['sd_t8'], writes=['sd_dend'])
                fw.act(cdec[:], ac16[:, 8:16], AF.Exp, reads=['sd_acum'], writes=['sd_cdec'])
                fw.tt('pool', xdt[:], xtok[:, c, :].rearrange("p (h q) -> p h q", h=8),
                      dttok[:, c, d8].unsqueeze(2).to_broadcast([128, 8, 64]), ALU.mult, reads=[('sd_xtok', c), ('sd_dttok', c)], writes=['sd_xdt'])
                if with_output:
                    fw.act(eac[:], acum,  AF.Exp, reads=['sd_acum'], writes=['sd_eac'])
                    fw.cp('dve', dtAb8[:], dtA[:, d8].unsqueeze(2).to_broadcast([128, 8, 128]), reads=['sd_dtA'], writes=['sd_dtAb'])
                    for hd in range(8):
                        pa = 1 + hd // 4
                        fw.mm(ps[pa][:, (hd % 4) * 128:(hd % 4 + 1) * 128], dtAb8[:, hd, :], tri[d][:], reads=['sd_dtAb'], writes=[('ps', pa)])
                    for hf in range(2):
                        fw.tt('dve', segt8[:, 4 * hf:4 * hf + 4, :], ps[1 + hf][:, :].rearrange("p (h i) -> p h i", h=4),
                              acum[:, 4 * hf:4 * hf + 4].unsqueeze(2).to_broadcast([128, 4, 128]), ALU.subtract,
                              reads=[('ps', 1 + hf), 'sd_acum'], writes=['sd_seg8'])
                    fw.tt('dve', segt8[:], segt8[:], msk[d][:, :].unsqueeze(1).to_broadcast([128, 8, 128]), ALU.min,
                          reads=['sd_seg8'], writes=['sd_seg8'])
                    fw.act(segt8[:], segt8[:], AF.Exp, reads=['sd_seg8'], writes=['sd_seg8'])
                    fw.tt('dve', mt8[:].rearrange("p (g h) i -> p g h i", g=2), segt8[:].rearrange("p (g h) i -> p g h i", g=2),
                          cbt[:, c, :, :].unsqueeze(2).to_broadcast([128, 2, 4, 128]), ALU.mult, reads=['sd_seg8', ('sd_cbt', c)], writes=['sd_mt8'])
                    for hd in range(8):
                        fw.mm(ps[3][:, hd * 64:(hd + 1) * 64], mt8[:, hd, :], xdt[:, hd, :], reads=['sd_mt8', 'sd_xdt'], writes=[('ps', 3)])
                    for g in range(2):
                        fw.mm(ps[4][:, g * 256:(g + 1) * 256], ct[:, g, c * 128:(c + 1) * 128], H[:, g * 256:(g + 1) * 256],
                              reads=['sd_ct', hk], writes=[('ps', 4)])
                    fw.tt('dve', yo[:], ps[4][:, :].rearrange("p (h q) -> p h q", h=8), eac[:, :].unsqueeze(2).to_broadcast([128, 8, 64]),
                          ALU.mult, reads=[('ps', 4), 'sd_eac'], writes=['sd_yo'])
                    yflat = yo[:].rearrange("p h q -> p (h q)")
                    if d == 0:
                        fw.tt('dve', ytok[:, c, :], ps[3][:, :], yflat, ALU.add, reads=[('ps', 3), 'sd_yo'], writes=[('sd_ytok', c)])
                        fw.tt('pool', tmpy[:].rearrange("p (h q) -> p h q", h=8), xtok[:, c, :].rearrange("p (h q) -> p h q", h=8),
                              dskb[:, :].unsqueeze(2).to_broadcast([128, 8, 64]), ALU.mult, reads=[('sd_xtok', c), 'sd_dsk'], writes=['sd_tmpy'])
                        fw.tt('pool', ytok[:, c, :], ytok[:, c, :], tmpy[:], ALU.add, reads=['sd_tmpy'], writes=[('sd_ytok', c)])
                    else:
                        fw.tt('dve', tmpy[:], ps[3][:, :], yflat, ALU.add, reads=[('ps', 3), 'sd_yo'], writes=['sd_tmpy'])
                        fw.tt('pool', ytok[:, c, :], ytok[:, c, :], tmpy[:], ALU.add, reads=['sd_tmpy'], writes=[('sd_ytok', c)])
                fw.tt('pool', xdte[:], xdt[:], dend[:, :].unsqueeze(2).to_broadcast([128, 8, 64]), ALU.mult, reads=['sd_xdt', 'sd_dend'], writes=['sd_xdte'])
                for g in range(2):
                    fw.mm(ps[5][:, g * 256:(g + 1) * 256], btok[:, c, g * 128:(g + 1) * 128],
                          xdte[:, g * 4:(g + 1) * 4, :].rearrange("p h q -> p (h q)"), reads=[('sd_btok', c), 'sd_xdte'], writes=[('ps', 5)])
                fw.tt('pool', hsc[:], H[:].rearrange("p (h q) -> p h q", h=8), cdec[:, :].unsqueeze(2).to_broadcast([128, 8, 64]), ALU.mult,
                      reads=[hk, 'sd_cdec'], writes=['sd_hsc'])
                fw.tt('dve', H[:], hsc[:].rearrange("p h q -> p (h q)"), ps[5][:, :], ALU.add, reads=['sd_hsc', ('ps', 5)], writes=[hk])
                if final_out:
                    tk = slice(c * 128, (c + 1) * 128)
                    for k in range(4):
                        fw.tr(ps[6][:, k * 128:(k + 1) * 128], ytok[:, c, k * 128:(k + 1) * 128], S.ident[:], reads=[('sd_ytok', c)], writes=[('ps', 6)])
                    fw.dma('sp', zt[:], S.pl[seq][OFF_Z:OFF_Z + 512, tk].rearrange("(k p) t -> p k t", p=128), writes=['sd_zt'])
                    fw.act(zt[:], zt[:], AF.Silu, reads=['sd_zt'], writes=['sd_zt'])
                    fw.tt('dve', yg[:], ps[6][:, :].rearrange("p (k t) -> p k t", k=4), zt[:], ALU.mult, reads=[('ps', 6), 'sd_zt'], writes=['sd_yg'])
                    fw.act(sqg[:], yg[:], AF.Square, reads=['sd_yg'], writes=['sd_sqg'])
                    for g in range(2):
                        fw.mm(ps[7][:, g * 128:(g + 1) * 128], S.ones[:], sqg[:, 2 * g, :], start=True, stop=False, reads=['sd_sqg'], writes=[('ps', 7)])
                        fw.mm(ps[7][:, g * 128:(g + 1) * 128], S.ones[:], sqg[:, 2 * g + 1, :], start=False, stop=True, reads=['sd_sqg'], writes=[('ps', 7)])
                    fw.act(rs[:], ps[7][:, 0:256].rearrange("p (g t) -> p g t", g=2), AF.Sqrt, bias=S.epsc[:, 0:1], scale=1.0 / 256.0,
                           reads=[('ps', 7)], writes=['sd_rs'])
                    fw.op('dve', lambda e: e.reciprocal(rs[:], rs[:]), reads=['sd_rs'], writes=['sd_rs'])
                    for k in range(4):
                        fw.stt('dve', og[:, k, :], yg[:, k, :], sng[:, k:k + 1], rs[:, k // 2, :], ALU.mult, ALU.mult,
                               reads=['sd_yg', 'sd_rs', 'sd_sng'], writes=['sd_og'])
                    fw.dma('sp', S.mix[seq][256:768, tk].rearrange("(k p) t -> p k t", p=128), og[:], reads=['sd_og'], writes=[('mixssd', seq, c)])

        for b in range(2):
            for d in range(2):
                fw.memset('pool', Hs[d][:], 0.0, writes=['sd_H%d' % d])
            stage_a((b, 'c'))
            if SSD_MODE != 'a':
                scan((b, 'c'), 0, ctx_out, False)
                scan((b, 'c'), 1, ctx_out, ctx_out)
            stage_a((b, 'l'))
            if SSD_MODE != 'a':
                scan((b, 'l'), 0, True, False)
                scan((b, 'l'), 1, True, True)
        fw.barrier()


def phase_peer(P, I, S, l, ctx_out):
    nc, fw = P.nc, P.fw
    ps = P.ps
    T = 256
    NB = 4
    with ExitStack() as es2:
        cs = [es2.enter_context(P.sb('pe_cs%d' % i, [128, 2048], F32)) for i in range(4)]
        cbs = [es2.enter_context(P.sb('pe_cb%d' % i, [128, 2048], BF16)) for i in range(4)]
        uv = I['uT'][l].rearrange("(a p) e -> p a e", p=128)
        n = 0
        for blk in range(64):
            k = n % 4
            n += 1
            fw.dma('sp', cs[k][:].rearrange("p (a e) -> p a e", a=8), uv[:, :, blk * 256:(blk + 1) * 256], writes=['pe_cs%d' % k])
            fw.cp('dve' if k % 2 == 0 else 'pool', cbs[k][:], cs[k][:], reads=['pe_cs%d' % k], writes=['pe_cb%d' % k])
            fw.dma('act', S.ubf[blk].rearrange("p a e -> p (a e)"), cbs[k][:], reads=['pe_cb%d' % k], writes=[('ubf', blk)])
        for blk in range(64):
            k = n % 4
            n += 1
            fw.dma('sp', cs[k][:].rearrange("p (ii d) -> p ii d", ii=2),
                   I['v'][l][blk * 256:(blk + 1) * 256, :].rearrange("(ii j) d -> j ii d", j=128), writes=['pe_cs%d' % k])
            fw.cp('dve' if k % 2 == 0 else 'pool', cbs[k][:], cs[k][:], reads=['pe_cs%d' % k], writes=['pe_cb%d' % k])
            fw.dma('act', S.vbf[blk].rearrange("p ii d -> p (ii d)"), cbs[k][:], reads=['pe_cb%d' % k], writes=[('vbf', blk)])
        fw.barrier()
    with ExitStack() as es:
        E = lambda n, sh, dt=F32: es.enter_context(P.sb(n, sh, dt))
        wq = E('pe_wq', [128, 8, 2048], BF16)
        st = [E('pe_s%d' % i, [128, 8, 256]) for i in range(2)]
        k1 = E('pe_k1', [128, 128], BF16); k2 = E('pe_k2', [128, 128], BF16)
        gbuf = E('pe_g', [128, 128, T], BF16)
        ub = E('pe_ub', [128, 2, 8, 256], BF16); vb = E('pe_vb', [128, 2, 2, 1024], BF16)
        h2s = [E('pe_h2%d' % i, [128, 8, T], BF16) for i in range(2)]; sq = E('pe_sq', [128, 8, T]); rstd = E('pe_rstd', [128, T])
        qT = E('pe_qT', [128, 16, T], BF16); s12 = E('pe_s12', [128, 16, 128]); v12 = E('pe_v12', [128, 16, 16])
        wk = E('pe_wk', [128, 4, 128]); wk2 = E('pe_wk2', [128, 4, 256]); t16 = E('pe_t16', [128, 8, 16])
        e16 = E('pe_e16', [128, 8, 16]); zz = E('pe_z', [128, 8]); mz = E('pe_mz', [128, 8])
        thr = E('pe_thr', [128, 8, 16]); bia = E('pe_bia', [128, 8, 16]); pc = E('pe_pc', [128, 3, T])
        Et = [E('pe_E%d' % i, [128, 256]) for i in range(4)]
        Wt = [E('pe_W%d' % i, [128, 128], BF16) for i in range(8)]
        Pt = [E('pe_P%d' % i, [128, 128], BF16) for i in range(8)]
        qr1 = E('pe_qr1', [128, 2, 4, 128], BF16)
        qr2 = E('pe_qr2', [128, 2, 4, 128], BF16)
        gst = [E('pe_gs%d' % i, [128, T]) for i in range(2)]
        At = [E('pe_A%d' % i, [128, T], BF16) for i in range(2)]
        xo = E('pe_xo', [128, 8, T])
        cand = sq
        wv = I['wq'][l].rearrange("(dc p) n -> p dc n", p=128)
        n = 0
        for blk in range(8):
            k = n % 2
            n += 1
            fw.dma('sp', st[k][:], wv[:, :, blk * 256:(blk + 1) * 256], writes=['pe_s%d' % k])
            fw.cp('dve' if k == 0 else 'pool', wq[:, :, blk * 256:(blk + 1) * 256], st[k][:], reads=['pe_s%d' % k], writes=[('pe_wq', blk)])
        for (kt, nm, key) in ((k1, 'k1T', 'pe_k1'), (k2, 'k2T', 'pe_k2')):
            k = n % 2
            n += 1
            fw.dma('sp', st[k][:, 0, 0:128], I[nm][l], writes=['pe_s%d' % k])
            fw.cp('dve', kt[:], st[k][:, 0, 0:128], reads=['pe_s%d' % k], writes=[key])
        fw.barrier()

        def load_tables(blk):
            bb = blk % 2
            fw.dma('sp', ub[:, bb, :, :], S.ubf[blk], writes=[('pe_ub', bb)])
            fw.dma('sp', vb[:, bb, :, :], S.vbf[blk], writes=[('pe_vb', bb)])

        qk = [('pe_qT', qc) for qc in range(16)]
        sqk = [('nm_sq', dc) for dc in range(8)]

        def nk_gen(seq, grp, gi):
            col = seq_col(seq)
            rv = S.res[seq].rearrange("(dc p) t -> p dc t", p=128)
            g0 = grp * T
            xt = st[gi % 2]
            xk = 'pe_s%d' % (gi % 2)
            h2 = h2s[gi % 2]
            fw.dma('sp', xt[:], rv[:, :, g0:g0 + T], writes=[xk])
            hk = normmod(P, S, xt, xk, T, S.G2[:, :, col], S.modT[:, 24:32, col], h2, 'pe_h2_%d' % (gi % 2), sq, rstd, ('ps', 6), ps[6])
            yield
            for qc in range(16):
                pi = 6 + qc % 2
                for dc in range(8):
                    fw.mm(ps[pi][:, :T], wq[:, dc, qc * 128:(qc + 1) * 128], h2[:, dc, :], start=(dc == 0), stop=(dc == 7),
                          reads=hk if dc in (0, 7) else (), writes=[('ps', pi)])
                fw.cp('act' if qc % 2 == 0 else 'dve', qT[:, qc, :], ps[pi][:, :T], reads=[('ps', pi)], writes=[('pe_qT', qc)])
                yield
            for tt in range(2):
                tsl = slice(tt * 128, (tt + 1) * 128)
                for half, kt, kkey in ((0, k1, 'pe_k1'), (1, k2, 'pe_k2')):
                    for h in range(8):
                        fw.mm(ps[6 + h // 4][:, (h % 4) * 128:(h % 4 + 1) * 128], qT[:, 2 * h + half, tsl], kt[:], reads=qk + [kkey], writes=[('ps', 6 + h // 4)])
                    for q2 in range(2):
                        q4 = half * 2 + q2
                        fw.cp('dve' if q2 == 0 else 'act', s12[:, q4 * 4:(q4 + 1) * 4, :], ps[6 + q2][:, :].rearrange("p (h i) -> p h i", h=4),
                              reads=[('ps', 6 + q2)], writes=[('pe_s12', q4)])
                    yield
                sk = [('pe_s12', q4) for q4 in range(4)]
                for base in range(0, 16, 4):
                    for idx in range(base, base + 4):
                        fw.op('dve', lambda e, idx=idx: e.max(out=v12[:, idx, 0:8], in_=s12[:, idx, :]), reads=sk, writes=[('pe_v12', idx)])
                    for idx in range(base, base + 4):
                        fw.op('dve', lambda e, idx=idx: e.match_replace(out=wk[:, idx % 4, :], in_to_replace=v12[:, idx, 0:8], in_values=s12[:, idx, :], imm_value=-1e30),
                              reads=[('pe_v12', idx)], writes=[('pe_wk', idx % 4)])
                    for idx in range(base, base + 4):
                        fw.op('dve', lambda e, idx=idx: e.max(out=v12[:, idx, 8:16], in_=wk[:, idx % 4, :]), reads=[('pe_wk', idx % 4)], writes=[('pe_v12b', idx)])
                yield
                v12k = [('pe_v12', i) for i in range(16)] + [('pe_v12b', i) for i in range(16)]
                fw.tt('dve', cand[:].rearrange("p h (a b) -> p h a b", a=16), v12[:, 0:8, :].unsqueeze(3).to_broadcast([128, 8, 16, 16]),
                      v12[:, 8:16, :].unsqueeze(2).to_broadcast([128, 8, 16, 16]), ALU.add, reads=v12k, writes=sqk)
                for base in range(0, 8, 4):
                    for h in range(base, base + 4):
                        fw.op('dve', lambda e, h=h: e.max(out=t16[:, h, 0:8], in_=cand[:, h, :]), reads=sqk, writes=[('pe_t16', h)])
                    for h in range(base, base + 4):
                        fw.op('dve', lambda e, h=h: e.match_replace(out=wk2[:, h % 4, :], in_to_replace=t16[:, h, 0:8], in_values=cand[:, h, :], imm_value=-1e30),
                              reads=[('pe_t16', h)] + sqk, writes=[('pe_wk2', h % 4)])
                    for h in range(base, base + 4):
                        fw.op('dve', lambda e, h=h: e.max(out=t16[:, h, 8:16], in_=wk2[:, h % 4, :]), reads=[('pe_wk2', h % 4)], writes=[('pe_t16b', h)])
                yield
                t16k = [('pe_t16', i) for i in range(8)] + [('pe_t16b', i) for i in range(8)]
                yield
                fw.tt('dve', e16[:], t16[:], t16[:, :, 0:1].to_broadcast([128, 8, 16]), ALU.subtract, reads=t16k, writes=['pe_e16'])
                fw.act(e16[:], e16[:], AF.Exp, reads=['pe_e16'], writes=['pe_e16'])
                fw.op('dve', lambda e: e.reduce_sum(out=zz[:], in_=e16[:], axis=AX.X), reads=['pe_e16'], writes=['pe_z'])
                fw.act(zz[:], zz[:], AF.Ln, reads=['pe_z'], writes=['pe_z'])
                fw.tt('dve', mz[:], t16[:, :, 0], zz[:], ALU.add, reads=t16k + ['pe_z'], writes=['pe_mz'])
                fw.tt('dve', thr[:], t16[:, :, 15:16].to_broadcast([128, 8, 16]), v12[:, 0:8, :], ALU.subtract, reads=t16k + v12k, writes=['pe_thr'])
                fw.tt('dve', bia[:], v12[:, 0:8, :], mz[:, :].unsqueeze(2).to_broadcast([128, 8, 16]), ALU.subtract, reads=['pe_mz'] + v12k, writes=['pe_bia'])
                fw.act(bia[:], bia[:], AF.Exp, reads=['pe_bia'], writes=['pe_bia'])
                srcs = [(v12[:, 0:8, :], v12k), (thr[:], ['pe_thr']), (bia[:], ['pe_bia'])]
                for slot, (sap, skey) in enumerate(srcs):
                    fw.tr(ps[6][:, slot * 128:(slot + 1) * 128], sap.rearrange("p h a -> p (h a)"), S.ident[:], reads=skey, writes=[('ps', 6)])
                fw.cp('dve', pc[:, :, tsl], ps[6][:, 0:384].rearrange("p (s t) -> p s t", s=3), reads=[('ps', 6)], writes=['pe_pc'])
                yield

        groups = [(seq, grp) for seq in seqs_of(ctx_out) for grp in range(seq_len(seq) // T)]
        for _ in nk_gen(groups[0][0], groups[0][1], 0):
            pass
        for gi, (seq, grp) in enumerate(groups):
            if True:
                col = seq_col(seq)
                rv = S.res[seq].rearrange("(dc p) t -> p dc t", p=128)
                g0 = grp * T
                xt = st[gi % 2]
                xk = 'pe_s%d' % (gi % 2)
                h2 = h2s[gi % 2]
                nxt = nk_gen(groups[gi + 1][0], groups[gi + 1][1], gi + 1) if gi + 1 < len(groups) else iter(())
                load_tables(0)
                load_tables(1)
                NQ = T // 4

                def quad_F(u):
                    qb = u % 2
                    t0 = 4 * u
                    for half, qr, kk_ in ((0, qr1, k1), (1, qr2, k2)):
                        src_ap = qT[:, half:16:2, t0:t0 + 4].rearrange("p h t -> p t h").unsqueeze(3).to_broadcast([128, 4, 8, 16])
                        fw.cp('pool' if half == 0 else 'dve', qr[:, qb, :, :].rearrange("p t (h a) -> p t h a", h=8), src_ap,
                              reads=qk if u < 2 else (), writes=[('pe_qr%d' % half, qb)])
                    for k in range(4):
                        t = t0 + k
                        xb = (t // 2) % 4
                        c1 = (t % 2) * 128
                        fw.mm(ps[xb][:, c1:c1 + 128], qr1[:, qb, k, :], k1[:], reads=[('pe_qr0', qb)], writes=[('ps', xb)])
                        fw.mm(ps[xb][:, 256 + c1:256 + c1 + 128], qr2[:, qb, k, :], k2[:], reads=[('pe_qr1', qb)], writes=[('ps', xb)])

                def quad_B(u):
                    t0 = 4 * u
                    gb = 4 + u % 2
                    for pr in range(2):
                        tp = t0 + 2 * pr
                        xb = (tp // 2) % 4
                        eq = (tp // 2) % 4
                        ek = ('pe_E', eq)
                        fw.act(Et[eq][:], ps[xb][:, 256:512], AF.Exp, reads=[('ps', xb)], writes=[ek])
                        for kk in range(2):
                            t = tp + kk
                            k = 2 * pr + kk
                            c1 = kk * 128
                            q = t % 8
                            wkk, pk = ('pe_W', q), ('pe_P', q)
                            fw.stt('dve', Wt[q][:], ps[xb][:, 256 + c1:256 + c1 + 128], pc[:, 1, t:t + 1], Et[eq][:, c1:c1 + 128], ALU.is_ge, ALU.mult,
                                   reads=[('ps', xb), ek, 'pe_pc'], writes=[wkk])
                            fw.ts('dve', Pt[q][:], ps[xb][:, c1:c1 + 128], pc[:, 0, t:t + 1], pc[:, 2, t:t + 1], ALU.is_equal, ALU.mult,
                                  reads=[('ps', xb), ek, 'pe_pc'], writes=[pk])
                            fw.mm(ps[gb][:, k * 128:(k + 1) * 128], Wt[q][:], Pt[q][:], reads=[wkk, pk], writes=[('ps', gb)])

                def quad_C(u):
                    gb = 4 + u % 2
                    fw.cp('act', gbuf[:, :, 4 * u:4 * u + 4].rearrange("p i t -> p t i"),
                          ps[gb][:, :].rearrange("p (t i) -> p t i", t=4), reads=[('ps', gb)], writes=[('pe_g', u % 2)])

                for u in range(NQ + 2):
                    if u < NQ:
                        quad_F(u)
                    if 1 <= u <= NQ:
                        quad_B(u - 1)
                    if u >= 2:
                        quad_C(u - 2)
                gk = [('pe_g', q) for q in range(2)]

                def dense_S(i):
                    bb, ii = (i // 2) % 2, i % 2
                    pi = 4 + i % 2
                    for dc in range(8):
                        fw.mm(ps[pi][:, :T], ub[:, bb, dc, ii * 128:(ii + 1) * 128], h2[:, dc, :], start=(dc == 0), stop=(dc == 7),
                              reads=[('pe_ub', bb)] if dc in (0, 7) else (), writes=[('ps', pi)])

                def dense_O(i):
                    bb, ii = (i // 2) % 2, i % 2
                    pi = 4 + i % 2
                    fw.act(gst[i % 2][:], ps[pi][:, :T], AF.Gelu_apprx_tanh, reads=[('ps', pi)], writes=[('pe_gs', i % 2)])
                    fw.tt('pool', At[i % 2][:], gst[i % 2][:], gbuf[:, i, :], ALU.mult,
                          reads=[('pe_gs', i % 2)] + (gk if i < 2 else []), writes=[('pe_A', i % 2)])
                    for dch in range(8):
                        bk, rg = dch // 2, (dch % 2) * 256
                        fw.mm(ps[bk][:, rg:rg + T], vb[:, bb, ii, dch * 128:(dch + 1) * 128], At[i % 2][:],
                              start=(i == 0 and dch % 2 == 0), stop=(i == 127),
                              reads=[('pe_A', i % 2), ('pe_vb', bb)] if dch in (0, 7) else (), writes=[('ps', bk)])

                dense_S(0)
                for i in range(128):
                    if i + 1 < 128:
                        dense_S(i + 1)
                    dense_O(i)
                    if i >= 4:
                        next(nxt, None)
                    if i % 2 == 1 and (i + 1) // 2 + 1 < 64:
                        load_tables((i + 1) // 2 + 1)
                for _ in nxt:
                    pass
                for dch in range(8):
                    bk, rg = dch // 2, (dch % 2) * 256
                    fw.stt('dve', xo[:, dch, :], ps[bk][:, rg:rg + T], S.modT[:, 40 + dch, col:col + 1], xt[:, dch, :], ALU.mult, ALU.add,
                           reads=[('ps', bk), xk], writes=[('pe_xo', dch)])
                fw.dma('sp', rv[:, :, g0:g0 + T], xo[:], reads=[('pe_xo', d8) for d8 in range(8)], writes=[('resp', seq, grp)])
        fw.barrier()
```

```python
import math
from contextlib import ExitStack
import numpy as np
import ml_dtypes
import concourse.bass as bass
import concourse.mybir as mybir
from concourse.bass_utils import run_bass_kernel_spmd

F32 = mybir.dt.float32
BF16 = mybir.dt.bfloat16
AF = mybir.ActivationFunctionType
ALU = mybir.AluOpType
AX = mybir.AxisListType

NDS = 20
D = 1024
L = 2048
LC = 256
NLAYER = 2
EPS = 1e-6
NCOL = 2576
OFF_Z, OFF_XBC, OFF_DT, OFF_FN = 768, 1280, 2304, 2320
PI = math.pi
import os
SSD_MODE = os.environ.get('SSD_MODE', 'full')


class FW:
    def __init__(self, nc):
        self.nc = nc
        self.eng = dict(pe=nc.tensor, dve=nc.vector, act=nc.scalar, pool=nc.gpsimd, sp=nc.sync)
        self.esem = {k: nc.alloc_semaphore("es_" + k) for k in self.eng}
        self.ecnt = {k: 0 for k in self.eng}
        self.dsem = [nc.alloc_semaphore("ds_%d" % i) for i in range(NDS)]
        self.dcnt = [0] * NDS
        self.waited = {k: {} for k in self.eng}
        self.buf = {}
        self.dsem_of = {}
        self.rr = 0
        self.ninst = 0

    def _b(self, key):
        b = self.buf.get(key)
        if b is None:
            b = dict(w=None, r={})
            self.buf[key] = b
        return b

    def _wait(self, en, tok):
        if tok is None:
            return
        kind, who, n = tok
        if kind == 'e':
            if who == en and en == 'pe':
                return
            if self.waited[en].get(('e', who), 0) >= n:
                return
            self.eng[en].wait_ge(self.esem[who], n)
            self.waited[en][('e', who)] = n
        else:
            val = self.dcnt[who]
            if self.waited[en].get(('d', who), 0) >= n:
                return
            self.eng[en].wait_ge(self.dsem[who], 16 * val)
            self.waited[en][('d', who)] = val

    def _deps(self, en, reads, writes):
        for k in reads:
            self._wait(en, self._b(k)['w'])
        for k in writes:
            b = self._b(k)
            self._wait(en, b['w'])
            for t in b['r'].values():
                self._wait(en, t)

    def _commit(self, tok, reads, writes):
        for k in writes:
            self.buf[k] = dict(w=tok, r={})
        for k in reads:
            if k in writes:
                continue
            self._b(k)['r'][(tok[0], tok[1])] = tok

    def op(self, en, fn, reads=(), writes=()):
        self._deps(en, reads, writes)
        ins = fn(self.eng[en])
        self.ecnt[en] += 1
        ins.then_inc(self.esem[en], 1)
        tok = ('e', en, self.ecnt[en])
        self._commit(tok, reads, writes)
        self.ninst += 1
        return tok

    def dma(self, en, out, in_, reads=(), writes=(), **kw):
        self._deps(en, reads, writes)
        key = writes[0] if writes else ('anon',)
        idx = self.dsem_of.get(key)
        if idx is None:
            idx = self.rr % NDS
            self.rr += 1
            self.dsem_of[key] = idx
        ins = self.eng[en].dma_start(out=out, in_=in_, **kw)
        self.dcnt[idx] += 1
        ins.then_inc(self.dsem[idx], 16)
        tok = ('d', idx, self.dcnt[idx])
        self._commit(tok, reads, writes)
        self.ninst += 1
        return tok

    def barrier(self):
        for en in self.eng:
            for who in self.eng:
                if who != en and self.ecnt[who] > self.waited[en].get(('e', who), 0):
                    self.eng[en].wait_ge(self.esem[who], self.ecnt[who])
                    self.waited[en][('e', who)] = self.ecnt[who]
            for i in range(NDS):
                if self.dcnt[i] > self.waited[en].get(('d', i), 0):
                    self.eng[en].wait_ge(self.dsem[i], 16 * self.dcnt[i])
                    self.waited[en][('d', i)] = self.dcnt[i]
        self.buf = {}

    def mm(self, out, lhsT, rhs, start=True, stop=True, reads=(), writes=()):
        return self.op('pe', lambda e: e.matmul(out, lhsT, rhs, start=start, stop=stop), reads, writes)

    def tr(self, out, in_, ident, reads=(), writes=()):
        return self.op('pe', lambda e: e.transpose(out, in_, ident), reads, writes)

    def act(self, out, in_, func, bias=None, scale=None, reads=(), writes=()):
        kw = {}
        if bias is not None:
            kw['bias'] = bias
        if scale is not None:
            kw['scale'] = scale
        return self.op('act', lambda e: e.activation(out=out, in_=in_, func=func, **kw), reads, writes)

    def ts(self, en, out, in0, s1, s2, op0, op1=None, reads=(), writes=()):
        kw = {}
        if op1 is not None:
            kw['op1'] = op1
        return self.op(en, lambda e: e.tensor_scalar(out, in0, s1, s2, op0, **kw), reads, writes)

    def tt(self, en, out, in0, in1, op, reads=(), writes=()):
        return self.op(en, lambda e: e.tensor_tensor(out, in0, in1, op), reads, writes)

    def stt(self, en, out, in0, scalar, in1, op0, op1, reads=(), writes=()):
        en = 'dve'
        return self.op(en, lambda e: e.scalar_tensor_tensor(out, in0, scalar, in1, op0, op1), reads, writes)

    def cp(self, en, out, in_, reads=(), writes=()):
        if en == 'act':
            return self.op('act', lambda e: e.copy(out, in_), reads, writes)
        return self.op(en, lambda e: e.tensor_copy(out, in_), reads, writes)

    def memset(self, en, ap, val, writes=()):
        return self.op(en, lambda e: e.memset(ap, val), (), writes)


_CONST = None


def _bf(a):
    return np.ascontiguousarray(a.astype(ml_dtypes.bfloat16))


def _consts():
    global _CONST
    if _CONST is not None:
        return _CONST
    c = {}
    c['ident'] = np.eye(128, dtype=np.float32)
    j = np.arange(128)[:, None]
    i = np.arange(128)[None, :]
    c['uinc'] = (j <= i).astype(np.float32)
    c['linc'] = (j >= i).astype(np.float32)
    c['maskf'] = np.where(i >= j, 0.0, -1e4).astype(np.float32)
    c['maskb'] = np.where(j >= i, 0.0, -1e4).astype(np.float32)
    c['ones'] = np.ones((128, 128), np.float32)
    a = np.arange(64)
    ang = 2 * np.pi * np.outer(a, a) / 64.0
    cb = np.zeros((128, 128)); sb = np.zeros((128, 128))
    for g in range(2):
        cb[g * 64:(g + 1) * 64, g * 64:(g + 1) * 64] = np.cos(ang) / 8.0
        sb[g * 64:(g + 1) * 64, g * 64:(g + 1) * 64] = np.sin(ang) / 8.0
    c['cbd'] = cb.astype(np.float32)
    c['sbd'] = sb.astype(np.float32)
    for tag, Lq in (('l', L), ('c', LC)):
        nsc = Lq // 128
        N = 2 * Lq
        nf = Lq + 1
        nfc = (nf + 127) // 128
        t = np.linspace(0.0, 1.0, Lq, dtype=np.float32)[:, None]
        w = (2.0 * np.pi * np.arange(Lq, dtype=np.float32)[:, None] / Lq).astype(np.float32)
        f = np.linspace(1e-4, 15, 16, dtype=np.float32)[None, :]
        z = np.concatenate([t, np.cos(f * w), -np.sin(f * w)], axis=-1).astype(np.float32)
        c['zT_' + tag] = np.ascontiguousarray(z.T)
        max_decay = math.log(1e-2) / 0.3
        min_decay = math.log(1e-2) / 1.5
        deltas = np.abs(np.linspace(min_decay, max_decay, 256, dtype=np.float32))
        win = np.exp(-t * deltas).astype(np.float32)
        winb = win.copy()
        winb[0] = 0.0
        lay = lambda m: np.ascontiguousarray(m.reshape(nsc, 128, 256).transpose(1, 0, 2))
        c['win_' + tag] = np.stack([lay(win), lay(winb)]).astype(np.float32)
        s = np.arange(Lq, dtype=np.float64)[:, None]
        ff = np.arange(nfc * 128, dtype=np.float64)[None, :]
        th = 2 * np.pi * s * ff / N
        valid = (ff < nf)
        Cf = np.cos(th) * valid
        Sf = -np.sin(th) * valid
        fl = lambda m: np.ascontiguousarray(m.reshape(nsc, 128, nfc, 128).transpose(2, 1, 0, 3))
        c['cf_' + tag] = _bf(fl(Cf))
        c['sf_' + tag] = _bf(fl(Sf))
        TB = min(512, Lq)
        ntb = Lq // TB
        fcol = np.arange(nfc * 128, dtype=np.float64)[:, None]
        tt = np.arange(Lq, dtype=np.float64)[None, :]
        wgt = np.where((fcol == 0) | (fcol == Lq), 1.0, 2.0) * (fcol < nf) / N
        th2 = 2 * np.pi * fcol * tt / N
        Ci = wgt * np.cos(th2)
        Si = -wgt * np.sin(th2)
        il = lambda m: np.ascontiguousarray(m.reshape(nfc, 128, ntb, TB).transpose(2, 1, 0, 3))
        c['ci_' + tag] = _bf(il(Ci))
        c['si_' + tag] = _bf(il(Si))
        t1 = np.arange(Lq, dtype=np.float64)
        th3 = 2 * np.pi * np.outer(t1, t1) / Lq
        CL = np.cos(th3) / math.sqrt(Lq)
        SLn = -np.sin(th3) / math.sqrt(Lq)
        ll = lambda m: np.ascontiguousarray(m.reshape(nsc, 128, ntb, TB).transpose(2, 1, 0, 3))
        c['cl_' + tag] = _bf(ll(CL))
        c['sl_' + tag] = _bf(ll(SLn))
    _CONST = c
    return c


def _pc(v, n):
    return np.ascontiguousarray(np.asarray(v, np.float32).reshape(n, 128).T)


class Prog:
    def __init__(self, layers=(0, 1), phases=None, dbg=False):
        self.layers = layers
        self.phases = phases
        self.dbg = dbg
        nc = bass.Bass("TRN2", target_bir_lowering=False)
        self.nc = nc
        self.fw = FW(nc)
        self.inputs = {}
        self.uid = 0
        self.ps = [nc.alloc_psum_tensor("psb%d" % i, [128, 512], F32) for i in range(8)]

    def inp(self, name, shape, dt=F32):
        t = self.nc.dram_tensor(name, list(shape), dt, kind="ExternalInput").ap()
        self.inputs[name] = t
        return t

    def scratch(self, name, shape, dt=F32, out=False):
        kind = "ExternalOutput" if (out or self.dbg) else "Internal"
        return self.nc.dram_tensor(name, list(shape), dt, kind=kind).ap()

    def sb(self, name, shape, dt):
        self.uid += 1
        return self.nc.sbuf_tensor('%s_u%d' % (name, self.uid), shape, dt)

    def want(self, ph):
        return self.phases is None or ph in self.phases


def _declare(P):
    c = _consts()
    I = {}
    I['xT'] = P.inp('xT', [2, D, L])
    I['ctxT'] = P.inp('ctxT', [2, D, LC])
    I['cT'] = P.inp('cT', [128, 8, 3])
    I['w_ada'] = P.inp('w_ada', [NLAYER, D, 6 * D])
    I['b_adaT'] = P.inp('b_adaT', [NLAYER, 128, 48])
    I['gn1'] = P.inp('gn1', [NLAYER, 128, 8])
    I['gn2'] = P.inp('gn2', [NLAYER, 128, 8])
    I['gfin'] = P.inp('gfin', [128, 8])
    I['w_in'] = P.inp('w_in', [NLAYER, D, NCOL])
    I['hycw'] = P.inp('hycw', [NLAYER, 128, 6, 4])
    I['hfw1'] = P.inp('hfw1', [NLAYER, 33, 64])
    I['hfw2'] = P.inp('hfw2', [NLAYER, 64, 64])
    I['hfw3'] = P.inp('hfw3', [NLAYER, 64, 512])
    I['hfv'] = P.inp('hfv', [NLAYER, 64, 3])
    I['hybias'] = P.inp('hybias', [NLAYER, 128, 2])
    I['sscw'] = P.inp('sscw', [NLAYER, 128, 8, 4])
    I['dtb'] = P.inp('dtb', [NLAYER, 16, 1])
    I['alog'] = P.inp('alog', [NLAYER, 1, 16])
    I['dskip'] = P.inp('dskip', [NLAYER, 1, 8])
    I['sng'] = P.inp('sng', [NLAYER, 128, 4])
    I['w_out'] = P.inp('w_out', [NLAYER, D, D])
    I['wq'] = P.inp('wq', [NLAYER, D, 2048])
    I['k1T'] = P.inp('k1T', [NLAYER, 128, 128])
    I['k2T'] = P.inp('k2T', [NLAYER, 128, 128])
    I['uT'] = P.inp('uT', [NLAYER, D, 16384])
    I['v'] = P.inp('v', [NLAYER, 16384, D])
    for k, a in c.items():
        I[k] = P.inp('c_' + k, a.shape, BF16 if a.dtype == ml_dtypes.bfloat16 else F32)
    return I


class Ctx:
    pass


def build(layers=(0, 1), phases=None, dbg=False, final=True):
    P = Prog(layers, phases, dbg)
    nc, fw = P.nc, P.fw
    I = _declare(P)
    S = Ctx()
    S.res = {}
    for b in range(2):
        S.res[(b, 'l')] = P.scratch('res_l%d' % b, [D, L])
        S.res[(b, 'c')] = P.scratch('res_c%d' % b, [D, LC])
    S.pl = {}
    S.mix = {}
    for b in range(2):
        S.pl[(b, 'l')] = P.scratch('pl_l%d' % b, [NCOL, L])
        S.pl[(b, 'c')] = P.scratch('pl_c%d' % b, [NCOL, LC])
        S.mix[(b, 'l')] = P.scratch('mix_l%d' % b, [D, L], BF16)
        S.mix[(b, 'c')] = P.scratch('mix_c%d' % b, [D, LC], BF16)
    S.khat = {'l': P.scratch('khat_l', [2, 17 * 128, 256]), 'c': P.scratch('khat_c', [2, 3 * 128, 256])}
    S.ubf = P.scratch('ubf', [64, 128, 8, 256], BF16)
    S.vbf = P.scratch('vbf', [64, 128, 2, 1024], BF16)
    S.outT = P.scratch('outT', [2, D, L], F32, out=True)

    A = lambda n, sh, dt=F32: nc.alloc_sbuf_tensor('sb_' + n, sh, dt)
    S.ident = A('ident', [128, 128]); S.ones = A('ones', [128, 128])
    S.uinc = A('uinc', [128, 128]); S.linc = A('linc', [128, 128])
    S.maskf = A('maskf', [128, 128]); S.maskb = A('maskb', [128, 128])
    S.modT = A('modT', [128, 48, 3])
    S.G1 = A('G1', [128, 8, 3]); S.G2 = A('G2', [128, 8, 3])
    S.gfin = A('gfin', [128, 8]); S.zero8 = A('zero8', [128, 8])
    S.hyn = A('hyn', [128, 2, 2])
    for nm in ('ident', 'ones', 'uinc', 'linc', 'maskf', 'maskb'):
        fw.dma('sp', getattr(S, nm)[:], I[nm], writes=[nm])
    fw.dma('sp', S.gfin[:], I['gfin'], writes=['gfin'])
    fw.memset('pool', S.zero8[:], 0.0, writes=['zero8'])
    S.epsc = A('epsc', [128, 1])
    fw.memset('pool', S.epsc[:], EPS, writes=['epsc'])
    S.negpi = A('negpi', [128, 1])
    fw.memset('pool', S.negpi[:], -PI, writes=['negpi'])
    fw.barrier()

    def src(l, seq):
        b, kind = seq
        if l == layers[0] and l == 0:
            return I['xT'][b] if kind == 'l' else I['ctxT'][b]
        return S.res[seq]

    for l in layers:
        ctx_out = l < NLAYER - 1
        if P.want('mod'):
            phase_mod(P, I, S, l)
        if P.want('proj'):
            phase_proj(P, I, S, l, src)
        if P.want('filt'):
            phase_filt(P, I, S, l, 'l')
            if ctx_out:
                phase_filt(P, I, S, l, 'c')
        if P.want('hy'):
            phase_hy(P, I, S, l, 'l')
            if ctx_out:
                phase_hy(P, I, S, l, 'c')
        if P.want('fn'):
            phase_fn(P, I, S, l, 'l')
            if ctx_out:
                phase_fn(P, I, S, l, 'c')
        if P.want('ssd'):
            phase_ssd(P, I, S, l, ctx_out)
        if P.want('out'):
            phase_out(P, I, S, l, src, ctx_out)
        if P.want('peer'):
            phase_peer(P, I, S, l, ctx_out)
    if final and P.want('final'):
        phase_final(P, I, S)
    if dbg:
        dh = P.scratch('dbg_hyn', [128, 4])
        fw.dma('sp', dh, S.hyn[:].rearrange("p a b -> p (a b)"), writes=['dbg_hyn'])
    fw.barrier()
    return P


def phase_mod(P, I, S, l):
    nc, fw = P.nc, P.fw
    with ExitStack() as es:
        cin = es.enter_context(P.sb('m_c', [128, 8, 3], F32))
        sc = es.enter_context(P.sb('m_sc', [128, 8, 3], F32))
        w0 = es.enter_context(P.sb('m_w0', [128, 8, 512], F32))
        w1 = es.enter_context(P.sb('m_w1', [128, 8, 512], F32))
        bada = es.enter_context(P.sb('m_b', [128, 48], F32))
        g1 = es.enter_context(P.sb('m_g1', [128, 8], F32))
        g2 = es.enter_context(P.sb('m_g2', [128, 8], F32))
        tmp = es.enter_context(P.sb('m_t', [128, 8, 3], F32))
        wb = [w0, w1]
        fw.dma('sp', cin[:], I['cT'], writes=['m_c'])
        fw.dma('sp', bada[:], I['b_adaT'][l], writes=['m_b'])
        fw.dma('sp', g1[:], I['gn1'][l], writes=['m_g1'])
        fw.dma('sp', g2[:], I['gn2'][l], writes=['m_g2'])
        fw.act(sc[:], cin[:], AF.Silu, reads=['m_c'], writes=['m_sc'])
        wv = I['w_ada'][l].rearrange("(dc p) n -> p dc n", p=128)
        for blk in range(12):
            w = wb[blk % 2]
            wk = 'm_w%d' % (blk % 2)
            fw.dma('sp', w[:], wv[:, :, blk * 512:(blk + 1) * 512], writes=[wk])
            for j in range(4):
                cc = blk * 4 + j
                pk = ('ps', cc % 2)
                pt = P.ps[cc % 2][:, 0:3]
                for dc in range(8):
                    fw.mm(pt, w[:, dc, j * 128:(j + 1) * 128], sc[:, dc, :], start=(dc == 0), stop=(dc == 7),
                          reads=[wk, 'm_sc'], writes=[pk])
                fw.ts('dve', S.modT[:, cc, :], pt, bada[:, cc:cc + 1], None, ALU.add, reads=[pk, 'm_b'], writes=['modT'])
        for (G, g, gk, c0, nm) in ((S.G1, g1, 'm_g1', 8, 'G1'), (S.G2, g2, 'm_g2', 32, 'G2')):
            fw.ts('dve', tmp[:], S.modT[:, c0:c0 + 8, :], 1.0, None, ALU.add, reads=['modT'], writes=['m_t'])
            fw.tt('dve', G[:], tmp[:], g[:, :].unsqueeze(2).to_broadcast([128, 8, 3]), ALU.mult, reads=['m_t', gk], writes=[nm])
        fw.barrier()


def seqs_of(ctx_too=True):
    out = []
    for b in range(2):
        out.append((b, 'l'))
        if ctx_too:
            out.append((b, 'c'))
    return out


def seq_len(seq):
    return L if seq[1] == 'l' else LC


def seq_col(seq):
    return seq[0] if seq[1] == 'l' else 2


def normmod(P, S, xt, xk, T, Gap, shap, hm, hk, sq, rstd, psk, pst):
    fw = P.fw
    fw.act(sq[:, :, :T], xt[:, :, :T], AF.Square, reads=[xk], writes=[('nm_sq', dc) for dc in range(8)])
    for dc in range(8):
        fw.mm(pst[:, :T], S.ones[:], sq[:, dc, :T], start=(dc == 0), stop=(dc == 7), reads=[('nm_sq', dc), 'ones'], writes=[psk])
    fw.act(rstd[:, :T], pst[:, :T], AF.Sqrt, bias=S.epsc[:, 0:1], scale=1.0 / D, reads=[psk, 'epsc'], writes=['nm_rstd'])
    fw.op('dve', lambda e: e.reciprocal(rstd[:, :T], rstd[:, :T]), reads=['nm_rstd'], writes=['nm_rstd'])
    for dc in range(8):
        en = 'dve' if dc % 2 == 0 else 'pool'
        fw.stt(en, sq[:, dc, :T], xt[:, dc, :T], Gap[:, dc:dc + 1], rstd[:, :T], ALU.mult, ALU.mult,
               reads=[xk, 'nm_rstd'], writes=[('nm_sq', dc)])
        fw.act(hm[:, dc, :T], sq[:, dc, :T], AF.Identity, bias=shap[:, dc:dc + 1], reads=[('nm_sq', dc)], writes=[(hk, dc)])
    return [(hk, dc) for dc in range(8)]


def load_cast_weight(P, dst, dstk, srcv, ncols, stg, stgk):
    fw = P.fw
    nb = (ncols + 511) // 512
    for blk in range(nb):
        c0 = blk * 512
        c1 = min(ncols, c0 + 512)
        st = stg[blk % 2]
        sk = stgk[blk % 2]
        fw.dma('sp', st[:, :, :c1 - c0], srcv[:, :, c0:c1], writes=[sk])
        fw.cp('dve' if blk % 2 == 0 else 'pool', dst[:, :, c0:c1], st[:, :, :c1 - c0], reads=[sk], writes=[(dstk, blk)])
    return [(dstk, blk) for blk in range(nb)]


def phase_proj(P, I, S, l, src):
    nc, fw = P.nc, P.fw
    with ExitStack() as es:
        winb = es.enter_context(P.sb('p_win', [128, 8, NCOL], BF16))
        s0 = es.enter_context(P.sb('p_s0', [128, 8, 512], F32))
        s1 = es.enter_context(P.sb('p_s1', [128, 8, 512], F32))
        sq = es.enter_context(P.sb('p_sq', [128, 8, 512], F32))
        rstd = es.enter_context(P.sb('p_rstd', [128, 512], F32))
        hm = es.enter_context(P.sb('p_hm', [128, 8, 512], BF16))
        ob = es.enter_context(P.sb('p_o', [128, 4, 512], F32))
        wkeys = load_cast_weight(P, winb, 'p_win', I['w_in'][l].rearrange("(dc p) n -> p dc n", p=128), NCOL, [s0, s1], ['p_s0', 'p_s1'])
        xb = [s0, s1]
        chunks = [(c0, min(128, NCOL - c0)) for c0 in range(0, OFF_DT, 128)] + [(OFF_DT, 16)] + [(OFF_FN, 128), (OFF_FN + 128, 128)]
        it = 0
        oi = 0
        for seq in seqs_of(True):
            Lq = seq_len(seq)
            T = min(512, Lq)
            col = seq_col(seq)
            xv = src(l, seq).rearrange("(dc p) t -> p dc t", p=128)
            for tb in range(Lq // T):
                xt = xb[it % 2]
                xk = 'p_s%d' % (it % 2)
                it += 1
                fw.dma('sp', xt[:, :, :T], xv[:, :, tb * T:(tb + 1) * T], reads=[('res', seq)], writes=[xk])
                hk = normmod(P, S, xt, xk, T, S.G1[:, :, col], S.modT[:, 0:8, col], hm, 'p_hm', sq, rstd, ('ps', 0), P.ps[0])
                for ci, (c0, cw) in enumerate(chunks):
                    pb = 1 + ci % 3
                    pt = P.ps[pb][:cw, :T]
                    for dc in range(8):
                        fw.mm(pt, winb[:, dc, c0:c0 + cw], hm[:, dc, :T], start=(dc == 0), stop=(dc == 7),
                              reads=wkeys + hk if dc in (0, 7) else (), writes=[('ps', pb)])
                    o = ob[:cw, oi % 4, :T]
                    ok = ('p_o', oi % 4)
                    oi += 1
                    if ci % 2 == 0:
                        fw.cp('act', o, pt, reads=[('ps', pb)], writes=[ok])
                    else:
                        fw.cp('dve', o, pt, reads=[('ps', pb)], writes=[ok])
                    fw.dma('sp', S.pl[seq][c0:c0 + cw, tb * T:(tb + 1) * T], o, reads=[ok], writes=[('pl', seq, ci, tb)])
        fw.barrier()


def phase_final(P, I, S):
    nc, fw = P.nc, P.fw
    with ExitStack() as es:
        s0 = es.enter_context(P.sb('f_s0', [128, 8, 512], F32))
        s1 = es.enter_context(P.sb('f_s1', [128, 8, 512], F32))
        sq = es.enter_context(P.sb('f_sq', [128, 8, 512], F32))
        rstd = es.enter_context(P.sb('f_rstd', [128, 512], F32))
        o0 = es.enter_context(P.sb('f_o0', [128, 8, 512], F32))
        o1 = es.enter_context(P.sb('f_o1', [128, 8, 512], F32))
        xb = [s0, s1]
        ob = [o0, o1]
        it = 0
        for b in range(2):
            xv = S.res[(b, 'l')].rearrange("(dc p) t -> p dc t", p=128)
            ov = S.outT[b].rearrange("(dc p) t -> p dc t", p=128)
            for tb in range(L // 512):
                xt = xb[it % 2]; xk = 'f_s%d' % (it % 2)
                o = ob[it % 2]; ok = 'f_o%d' % (it % 2)
                it += 1
                fw.dma('sp', xt[:], xv[:, :, tb * 512:(tb + 1) * 512], writes=[xk])
                hk = normmod(P, S, xt, xk, 512, S.gfin, S.zero8, o, ok + 'h', sq, rstd, ('ps', 0), P.ps[0])
                fw.dma('sp', ov[:, :, tb * 512:(tb + 1) * 512], o[:], reads=hk, writes=[('outT', b, tb)])
        fw.barrier()


def prep_shared(inp):
    f = lambda a: np.ascontiguousarray(np.asarray(a, np.float32))
    sh = {}
    sh['w_ada'] = f(inp['w_ada'])
    sh['b_adaT'] = np.stack([_pc(inp['b_ada'][l], 48) for l in range(NLAYER)])
    sh['gn1'] = np.stack([_pc(inp['g_norm1'][l], 8) for l in range(NLAYER)])
    sh['gn2'] = np.stack([_pc(inp['g_norm2'][l], 8) for l in range(NLAYER)])
    sh['gfin'] = _pc(inp['g_final'], 8)
    sh['w_in'] = f(inp['w_in'])
    hy = []
    for l in range(NLAYER):
        m = np.concatenate([np.asarray(inp['hy_conv_w'][l], np.float32), np.asarray(inp['hy_conv_b'][l], np.float32)[None]], 0)
        hy.append(np.ascontiguousarray(m.reshape(4, 6, 128).transpose(2, 1, 0)))
    sh['hycw'] = np.stack(hy)
    sh['hfw1'] = f(inp['hf_w1']); sh['hfw2'] = f(inp['hf_w2']); sh['hfw3'] = f(inp['hf_w3'])
    sh['hfv'] = np.ascontiguousarray(np.stack([inp['hf_b1'], inp['hf_b2'], inp['hf_freq']], axis=-1).astype(np.float32))
    sh['hybias'] = np.stack([_pc(inp['hy_bias'][l], 2) for l in range(NLAYER)])
    ss = []
    for l in range(NLAYER):
        m = np.concatenate([np.asarray(inp['ssd_conv_w'][l], np.float32), np.asarray(inp['ssd_conv_b'][l], np.float32)[None]], 0)
        ss.append(np.ascontiguousarray(m.reshape(4, 8, 128).transpose(2, 1, 0)))
    sh['sscw'] = np.stack(ss)
    sh['dtb'] = f(np.asarray(inp['ssd_dt_bias']).reshape(NLAYER, 16, 1))
    sh['alog'] = f(np.asarray(inp['ssd_a_log']).reshape(NLAYER, 1, 16))
    sh['dskip'] = f(np.asarray(inp['ssd_d']).reshape(NLAYER, 1, 8))
    sh['sng'] = np.stack([_pc(inp['ssd_norm_g'][l], 4) for l in range(NLAYER)])
    sh['w_out'] = f(inp['w_out'])
    sh['wq'] = f(inp['peer_wq'])
    sh['k1T'] = np.ascontiguousarray(np.asarray(inp['peer_k1'], np.float32).transpose(0, 2, 1))
    sh['k2T'] = np.ascontiguousarray(np.asarray(inp['peer_k2'], np.float32).transpose(0, 2, 1))
    sh['uT'] = np.ascontiguousarray(np.asarray(inp['peer_u'], np.float32).transpose(0, 2, 1))
    sh['v'] = f(inp['peer_v'])
    for k, a in _consts().items():
        sh['c_' + k] = a
    return sh


def prep_core(inp, core):
    m = {}
    x = np.asarray(inp['x'], np.float32)[2 * core:2 * core + 2]
    cx = np.asarray(inp['ctx'], np.float32)[2 * core:2 * core + 2]
    m['xT'] = np.ascontiguousarray(x.transpose(0, 2, 1))
    m['ctxT'] = np.ascontiguousarray(cx.transpose(0, 2, 1))
    cv = np.stack([np.asarray(inp['c'], np.float32)[2 * core], np.asarray(inp['c'], np.float32)[2 * core + 1],
                   np.asarray(inp['c_ctx'], np.float32)], axis=-1)
    m['cT'] = np.ascontiguousarray(cv.reshape(8, 128, 3).transpose(1, 0, 2))
    return m


_PROG = None


def kernel(**inputs):
    global _PROG
    if _PROG is None:
        _PROG = build()
    P = _PROG
    sh = prep_shared(inputs)
    in_maps = []
    for core in range(8):
        m = dict(sh)
        m.update(prep_core(inputs, core))
        in_maps.append({k: m[k] for k in P.inputs})
    res = run_bass_kernel_spmd(P.nc, in_maps, core_ids=list(range(8)))
    outs = [np.asarray(r['outT']).transpose(0, 2, 1) for r in res.results]
    return np.ascontiguousarray(np.concatenate(outs, axis=0).astype(np.float32))


def phase_fn(P, I, S, l, tag):
    nc, fw = P.nc, P.fw
    Lq = L if tag == 'l' else LC
    nsc = Lq // 128
    TB = min(512, Lq)
    ntb = Lq // TB
    with ExitStack() as es:
        ut = es.enter_context(P.sb('fn_ut', [128, 4, Lq], F32))
        cbd = es.enter_context(P.sb('fn_cbd', [128, 128], F32))
        sbd = es.enter_context(P.sb('fn_sbd', [128, 128], F32))
        atok = es.enter_context(P.sb('fn_a', [128, nsc, 512], BF16))
        btok = es.enter_context(P.sb('fn_b', [128, nsc, 512], BF16))
        cl = es.enter_context(P.sb('fn_cl', [128, nsc, TB], BF16))
        sl = es.enter_context(P.sb('fn_sl', [128, nsc, TB], BF16))
        ob = es.enter_context(P.sb('fn_o', [128, 2, TB], BF16))
        fw.dma('sp', cbd[:], I['cbd'], writes=['fn_cbd'])
        fw.dma('sp', sbd[:], I['sbd'], writes=['fn_sbd'])
        for b in range(2):
            for ch in range(2):
                fw.dma('sp', ut[:, b * 2 + ch, :], S.pl[(b, tag)][OFF_FN + ch * 128:OFF_FN + (ch + 1) * 128, :], writes=[('fn_ut', b * 2 + ch)])
        utk = [('fn_ut', m) for m in range(4)]
        for tc in range(nsc):
            pa, pb = 2 * (tc % 2), 2 * (tc % 2) + 1
            for m in range(4):
                fw.mm(P.ps[pa][:, m * 128:(m + 1) * 128], ut[:, m, tc * 128:(tc + 1) * 128], cbd[:], reads=utk + ['fn_cbd'], writes=[('ps', pa)])
                fw.mm(P.ps[pb][:, m * 128:(m + 1) * 128], ut[:, m, tc * 128:(tc + 1) * 128], sbd[:], reads=utk + ['fn_sbd'], writes=[('ps', pb)])
            fw.cp('act', atok[:, tc, :], P.ps[pa][:, :], reads=[('ps', pa)], writes=[('fn_a', tc)])
            fw.cp('dve', btok[:, tc, :], P.ps[pb][:, :], reads=[('ps', pb)], writes=[('fn_b', tc)])
        ak = [('fn_a', tc) for tc in range(nsc)]
        bk = [('fn_b', tc) for tc in range(nsc)]
        oi = 0
        for tb in range(ntb):
            fw.dma('sp', cl[:], I['cl_' + tag][tb], writes=['fn_cl'])
            fw.dma('sp', sl[:], I['sl_' + tag][tb], writes=['fn_sl'])
            for m in range(4):
                b, ch = m // 2, m % 2
                pi = 4 + m % 2
                pt = P.ps[pi][:, :TB]
                for tc in range(nsc):
                    fw.mm(pt, atok[:, tc, m * 128:(m + 1) * 128], cl[:, tc, :], start=(tc == 0), stop=False,
                          reads=ak + ['fn_cl'] if tc in (0, nsc - 1) else (), writes=[('ps', pi)])
                for tc in range(nsc):
                    fw.mm(pt, btok[:, tc, m * 128:(m + 1) * 128], sl[:, tc, :], start=False, stop=(tc == nsc - 1),
                          reads=bk + ['fn_sl'] if tc in (0, nsc - 1) else (), writes=[('ps', pi)])
                o = ob[:, oi % 2, :]
                ok = ('fn_o', oi % 2)
                oi += 1
                fw.cp('act' if m % 2 == 0 else 'dve', o, pt, reads=[('ps', pi)], writes=[ok])
                fw.dma('sp', S.mix[(b, tag)][768 + ch * 128:768 + (ch + 1) * 128, tb * TB:(tb + 1) * TB], o, reads=[ok], writes=[('mixfn', b, ch, tb)])
        fw.barrier()


def phase_out(P, I, S, l, src, ctx_out):
    nc, fw = P.nc, P.fw
    with ExitStack() as es:
        wout = es.enter_context(P.sb('o_w', [128, 8, 1024], BF16))
        s0 = es.enter_context(P.sb('o_s0', [128, 8, 512], F32))
        s1 = es.enter_context(P.sb('o_s1', [128, 8, 512], F32))
        m0 = es.enter_context(P.sb('o_m0', [128, 8, 512], BF16))
        m1 = es.enter_context(P.sb('o_m1', [128, 8, 512], BF16))
        x0 = es.enter_context(P.sb('o_x0', [128, 8, 512], F32))
        x1 = es.enter_context(P.sb('o_x1', [128, 8, 512], F32))
        wkeys = load_cast_weight(P, wout, 'o_w', I['w_out'][l].rearrange("(dc p) n -> p dc n", p=128), 1024, [s0, s1], ['o_s0', 'o_s1'])
        xb, mb, ob = [s0, s1], [m0, m1], [x0, x1]
        it = 0
        for seq in seqs_of(ctx_out):
            Lq = seq_len(seq)
            T = min(512, Lq)
            col = seq_col(seq)
            xv = src(l, seq).rearrange("(dc p) t -> p dc t", p=128)
            mv = S.mix[seq].rearrange("(dc p) t -> p dc t", p=128)
            rv = S.res[seq].rearrange("(dc p) t -> p dc t", p=128)
            for tb in range(Lq // T):
                k = it % 2
                it += 1
                xt, mx, xo = xb[k], mb[k], ob[k]
                fw.dma('sp', xt[:, :, :T], xv[:, :, tb * T:(tb + 1) * T], writes=['o_s%d' % k])
                fw.dma('sp', mx[:, :, :T], mv[:, :, tb * T:(tb + 1) * T], writes=['o_m%d' % k])
                for dch in range(8):
                    pi = dch % 4
                    pt = P.ps[pi][:, :T]
                    for cc in range(8):
                        fw.mm(pt, wout[:, cc, dch * 128:(dch + 1) * 128], mx[:, cc, :T], start=(cc == 0), stop=(cc == 7),
                              reads=wkeys + ['o_m%d' % k] if cc in (0, 7) else (), writes=[('ps', pi)])
                    fw.stt('dve', xo[:, dch, :T], pt, S.modT[:, 16 + dch, col:col + 1], xt[:, dch, :T], ALU.mult, ALU.add,
                           reads=[('ps', pi), 'o_s%d' % k, 'modT'], writes=[('o_x%d' % k, dch)])
                fw.dma('sp', rv[:, :, tb * T:(tb + 1) * T], xo[:, :, :T], reads=[('o_x%d' % k, d8) for d8 in range(8)], writes=[('resw', seq, tb)])
        fw.barrier()


def phase_filt(P, I, S, l, tag):
    nc, fw = P.nc, P.fw
    Lq = L if tag == 'l' else LC
    li = 0 if tag == 'l' else 1
    nsc = Lq // 128
    nfc = (Lq + 1 + 127) // 128
    T = min(512, Lq)
    with ExitStack() as es:
        zT = es.enter_context(P.sb('fl_z', [33, Lq], F32))
        w1 = es.enter_context(P.sb('fl_w1', [33, 64], F32))
        w2 = es.enter_context(P.sb('fl_w2', [64, 64], F32))
        w3 = es.enter_context(P.sb('fl_w3', [64, 512], F32))
        hv = es.enter_context(P.sb('fl_hv', [64, 3], F32))
        fb = es.enter_context(P.sb('fl_fb', [64, 2], F32))
        h1 = es.enter_context(P.sb('fl_h1', [64, Lq], F32))
        h2 = es.enter_context(P.sb('fl_h2', [64, Lq], F32))
        win = es.enter_context(P.sb('fl_win', [128, 2, nsc, 256], F32))
        pm = es.enter_context(P.sb('fl_pm', [128, nsc, 256], BF16))
        mmn = es.enter_context(P.sb('fl_mm', [128, nsc, 256], BF16))
        acc = es.enter_context(P.sb('fl_acc', [128, 256], F32))
        t1 = es.enter_context(P.sb('fl_t1', [128, 2, 256], F32))
        t2 = es.enter_context(P.sb('fl_t2', [128, 2, 256], F32))
        tmp = es.enter_context(P.sb('fl_tmp', [64, 512], F32))
        tmpk = es.enter_context(P.sb('fl_tmpk', [64, 512], F32))
        cf = es.enter_context(P.sb('fl_cf', [128, 2, nsc, 128], BF16))
        sf = es.enter_context(P.sb('fl_sf', [128, 2, nsc, 128], BF16))
        ko = es.enter_context(P.sb('fl_ko', [128, 2, 2, 256], F32))
        ntmp = es.enter_context(P.sb('fl_n', [128, 2], F32))
        fw.dma('sp', zT[:], I['zT_' + tag], writes=['fl_z'])
        fw.dma('sp', w1[:], I['hfw1'][l], writes=['fl_w1'])
        fw.dma('sp', w2[:], I['hfw2'][l], writes=['fl_w2'])
        fw.dma('sp', w3[:], I['hfw3'][l], writes=['fl_w3'])
        fw.dma('sp', hv[:], I['hfv'][l], writes=['fl_hv'])
        for v in range(2):
            fw.dma('sp', win[:, v, :, :], I['win_' + tag][v], writes=[('fl_win', v)])
        fw.ts('dve', fb[:], hv[:, 0:2], hv[:, 2:3], None, ALU.mult, reads=['fl_hv'], writes=['fl_fb'])
        fw.memset('pool', acc[:], 0.0, writes=['fl_acc'])

        def sin_layer(dst, dk, w, wk, K, srcT, sk, col):
            for blk in range(Lq // T):
                pi = blk % 2
                pt = P.ps[pi][:64, :T]
                fw.mm(pt, w[:K, :64], srcT[:K, blk * T:(blk + 1) * T], reads=[wk, sk], writes=[('ps', pi)])
                fw.ts('dve', tmp[:, :T], pt, hv[:, 2:3], fb[:, col:col + 1], ALU.mult, ALU.add, reads=[('ps', pi), 'fl_hv', 'fl_fb'], writes=['fl_tmp'])
                MAGIC = 12582912.0
                fw.ts('dve', tmpk[:, :T], tmp[:, :T], 1.0 / (2.0 * PI), MAGIC, ALU.mult, ALU.add, reads=['fl_tmp'], writes=['fl_tmpk'])
                fw.ts('dve', tmpk[:, :T], tmpk[:, :T], MAGIC, None, ALU.subtract, reads=['fl_tmpk'], writes=['fl_tmpk'])
                fw.stt('dve', tmp[:, :T], tmpk[:, :T], -2.0 * PI, tmp[:, :T], ALU.mult, ALU.add, reads=['fl_tmpk', 'fl_tmp'], writes=['fl_tmp'])
                fw.act(dst[:, blk * T:(blk + 1) * T], tmp[:, :T], AF.Sin, reads=['fl_tmp'], writes=[dk])

        sin_layer(h1, 'fl_h1', w1, 'fl_w1', 33, zT, 'fl_z', 0)
        sin_layer(h2, 'fl_h2', w2, 'fl_w2', 64, h1, 'fl_h1', 1)
        for sc in range(nsc):
            pi = 2 + sc % 2
            pt = P.ps[pi]
            fw.mm(pt[:, :], h2[:64, sc * 128:(sc + 1) * 128], w3[:64, :], reads=['fl_h2', 'fl_w3'], writes=[('ps', pi)])
            fw.tt('dve', t1[:], pt[:, :].rearrange("p (v c) -> p v c", v=2), win[:, :, sc, :], ALU.mult,
                  reads=[('ps', pi), ('fl_win', 0), ('fl_win', 1)], writes=['fl_t1'])
            fw.tt('pool', pm[:, sc, :], t1[:, 0, :], t1[:, 1, :], ALU.add, reads=['fl_t1'], writes=[('fl_pm', sc)])
            fw.tt('pool', mmn[:, sc, :], t1[:, 0, :], t1[:, 1, :], ALU.subtract, reads=['fl_t1'], writes=[('fl_mm', sc)])
            fw.act(t2[:], t1[:], AF.Square, reads=['fl_t1'], writes=['fl_t2'])
            fw.tt('pool', acc[:], acc[:], t2[:, 0, :], ALU.add, reads=['fl_t2'], writes=['fl_acc'])
            fw.tt('pool', acc[:], acc[:], t2[:, 1, :], ALU.add, reads=['fl_t2'], writes=['fl_acc'])
        for ch in range(2):
            fw.mm(P.ps[0][:, ch:ch + 1], acc[:, ch * 128:(ch + 1) * 128], S.ones[:, 0:1], reads=['fl_acc', 'ones'], writes=[('ps', 0)])
        fw.act(ntmp[:], P.ps[0][:, 0:2], AF.Sqrt, bias=S.epsc[:, 0:1], reads=[('ps', 0), 'epsc'], writes=['fl_n'])
        fw.op('dve', lambda e: e.reciprocal(S.hyn[:, :, li], ntmp[:]), reads=['fl_n'], writes=[('hyn', li)])
        pmk = [('fl_pm', sc) for sc in range(nsc)]
        mmk = [('fl_mm', sc) for sc in range(nsc)]
        for fc in range(nfc):
            fsz = 128 if fc < nfc - 1 else 1
            k = fc % 2
            fw.dma('sp', cf[:, k, :, :], I['cf_' + tag][fc], writes=[('fl_cf', k)])
            fw.dma('sp', sf[:, k, :, :], I['sf_' + tag][fc], writes=[('fl_sf', k)])
            pa, pb = 4 + 2 * k, 5 + 2 * k
            for sc in range(nsc):
                fw.mm(P.ps[pa][:fsz, :256], cf[:, k, sc, :fsz], pm[:, sc, :], start=(sc == 0), stop=(sc == nsc - 1),
                      reads=pmk + [('fl_cf', k)] if sc in (0, nsc - 1) else (), writes=[('ps', pa)])
            for sc in range(nsc):
                fw.mm(P.ps[pb][:fsz, :256], sf[:, k, sc, :fsz], mmn[:, sc, :], start=(sc == 0), stop=(sc == nsc - 1),
                      reads=mmk + [('fl_sf', k)] if sc in (0, nsc - 1) else (), writes=[('ps', pb)])
            fw.cp('act', ko[:fsz, k, 0, :], P.ps[pa][:fsz, :256], reads=[('ps', pa)], writes=[('fl_ko', k, 0)])
            fw.cp('dve', ko[:fsz, k, 1, :], P.ps[pb][:fsz, :256], reads=[('ps', pb)], writes=[('fl_ko', k, 1)])
            for v in range(2):
                fw.dma('sp', S.khat[tag][v, fc * 128:fc * 128 + fsz, :], ko[:fsz, k, v, :], reads=[('fl_ko', k, v)], writes=[('khat', tag, v, fc)])
        fw.barrier()


def phase_hy(P, I, S, l, tag):
    nc, fw = P.nc, P.fw
    Lq = L if tag == 'l' else LC
    li = 0 if tag == 'l' else 1
    nsc = Lq // 128
    nfc = (Lq + 1 + 127) // 128
    TB = min(512, Lq)
    ntb = Lq // TB
    with ExitStack() as es:
        u = es.enter_context(P.sb('hy_u', [128, 4, Lq], F32))
        x1c = es.enter_context(P.sb('hy_x1', [128, 4, Lq], BF16))
        utok = es.enter_context(P.sb('hy_ut', [128, nsc, 512], BF16))
        wre = es.enter_context(P.sb('hy_wre', [128, nfc, 512], BF16))
        wim = es.enter_context(P.sb('hy_wim', [128, nfc, 512], BF16))
        cw = es.enter_context(P.sb('hy_cw', [128, 6, 4], F32))
        hb = es.enter_context(P.sb('hy_hb', [128, 2], F32))
        fw.dma('sp', cw[:], I['hycw'][l], writes=['hy_cw'])
        fw.dma('sp', hb[:], I['hybias'][l], writes=['hy_hb'])
        with ExitStack() as es2:
            raw = es2.enter_context(P.sb('hy_raw', [128, 3, Lq + 2], F32))
            cv = [es2.enter_context(P.sb('hy_cv%d' % k, [128, Lq], F32)) for k in range(3)]
            fw.memset('pool', raw[:, :, 0:1], 0.0, writes=['hy_raw_h0'])
            fw.memset('pool', raw[:, :, Lq + 1:Lq + 2], 0.0, writes=['hy_raw_h1'])
            for m in range(4):
                b, ch = m // 2, m % 2
                for k in range(3):
                    r0 = k * 256 + ch * 128
                    fw.dma('sp', raw[:, k, 1:Lq + 1], S.pl[(b, tag)][r0:r0 + 128, :], writes=[('hy_raw', k)])
                    ci = 2 * k + ch
                    rk = [('hy_raw', k), 'hy_raw_h0', 'hy_raw_h1', 'hy_cw']
                    fw.ts('dve', cv[k][:], raw[:, k, 0:Lq], cw[:, ci, 0:1], cw[:, ci, 3:4], ALU.mult, ALU.add, reads=rk, writes=[('hy_cv', k)])
                    fw.stt('dve', cv[k][:], raw[:, k, 1:Lq + 1], cw[:, ci, 1:2], cv[k][:], ALU.mult, ALU.add, reads=rk, writes=[('hy_cv', k)])
                    fw.stt('dve', cv[k][:], raw[:, k, 2:Lq + 2], cw[:, ci, 2:3], cv[k][:], ALU.mult, ALU.add, reads=rk, writes=[('hy_cv', k)])
                fw.tt('pool', u[:, m, :], cv[2][:], cv[0][:], ALU.mult, reads=[('hy_cv', 2), ('hy_cv', 0)], writes=[('hy_u', m)])
                fw.cp('act', x1c[:, m, :], cv[1][:], reads=[('hy_cv', 1)], writes=[('hy_x1', m)])
            uk = [('hy_u', m) for m in range(4)]
            for sc in range(nsc):
                pi = sc % 2
                for m in range(4):
                    fw.tr(P.ps[pi][:, m * 128:(m + 1) * 128], u[:, m, sc * 128:(sc + 1) * 128], S.ident[:], reads=uk + ['ident'], writes=[('ps', pi)])
                fw.cp('act' if sc % 2 == 0 else 'dve', utok[:, sc, :], P.ps[pi][:, :], reads=[('ps', pi)], writes=[('hy_ut', sc)])
            fw.barrier()
        with ExitStack() as es2:
            cf = es2.enter_context(P.sb('hy_cf', [128, 2, nsc, 128], BF16))
            sf = es2.enter_context(P.sb('hy_sf', [128, 2, nsc, 128], BF16))
            kk = es2.enter_context(P.sb('hy_kk', [128, 2, 2, 256], F32))
            tq = [es2.enter_context(P.sb('hy_t%d' % q, [128, 2, 256], F32)) for q in range(4)]
            for fc in range(nfc):
                fsz = 128 if fc < nfc - 1 else 1
                k = fc % 2
                fw.dma('sp', cf[:, k, :, :], I['cf_' + tag][fc], writes=[('hy_cf', k)])
                fw.dma('sp', sf[:, k, :, :], I['sf_' + tag][fc], writes=[('hy_sf', k)])
                for v in range(2):
                    fw.dma('sp', kk[:fsz, k, v, :], S.khat[tag][v, fc * 128:fc * 128 + fsz, :], writes=[('hy_kk', k, v)])
                pa, pb = 2 + 2 * k, 3 + 2 * k
                for sc in range(nsc):
                    fw.mm(P.ps[pa][:fsz, :], cf[:, k, sc, :fsz], utok[:, sc, :], start=(sc == 0), stop=(sc == nsc - 1),
                          reads=[('hy_cf', k)] if sc in (0, nsc - 1) else (), writes=[('ps', pa)])
                for sc in range(nsc):
                    fw.mm(P.ps[pb][:fsz, :], sf[:, k, sc, :fsz], utok[:, sc, :], start=(sc == 0), stop=(sc == nsc - 1),
                          reads=[('hy_sf', k)] if sc in (0, nsc - 1) else (), writes=[('ps', pb)])
                ure = P.ps[pa][:fsz, :].rearrange("p (b c) -> p b c", b=2)
                uim = P.ps[pb][:fsz, :].rearrange("p (b c) -> p b c", b=2)
                kre = kk[:fsz, k, 0, :].unsqueeze(1).to_broadcast([fsz, 2, 256])
                kim = kk[:fsz, k, 1, :].unsqueeze(1).to_broadcast([fsz, 2, 256])
                kr = [('hy_kk', k, 0), ('hy_kk', k, 1)]
                fw.tt('dve', tq[0][:fsz], ure, kre, ALU.mult, reads=[('ps', pa)] + kr, writes=[('hy_t', 0)])
                fw.tt('dve', tq[1][:fsz], uim, kim, ALU.mult, reads=[('ps', pb)] + kr, writes=[('hy_t', 1)])
                fw.tt('dve', tq[2][:fsz], ure, kim, ALU.mult, reads=[('ps', pa)] + kr, writes=[('hy_t', 2)])
                fw.tt('dve', tq[3][:fsz], uim, kre, ALU.mult, reads=[('ps', pb)] + kr, writes=[('hy_t', 3)])
                fw.tt('pool', wre[:fsz, fc, :].rearrange("p (b c) -> p b c", b=2), tq[0][:fsz], tq[1][:fsz], ALU.subtract,
                      reads=[('hy_t', 0), ('hy_t', 1)], writes=[('hy_wre', fc)])
                fw.tt('pool', wim[:fsz, fc, :].rearrange("p (b c) -> p b c", b=2), tq[2][:fsz], tq[3][:fsz], ALU.add,
                      reads=[('hy_t', 2), ('hy_t', 3)], writes=[('hy_wim', fc)])
            fw.barrier()
        with ExitStack() as es2:
            ci_t = es2.enter_context(P.sb('hy_ci', [128, nfc, TB], BF16))
            si_t = es2.enter_context(P.sb('hy_si', [128, nfc, TB], BF16))
            at = es2.enter_context(P.sb('hy_at', [128, 2, TB], F32))
            ob = es2.enter_context(P.sb('hy_o', [128, 2, TB], BF16))
            oi = 0
            for tb in range(ntb):
                fw.dma('sp', ci_t[:], I['ci_' + tag][tb], writes=['hy_ci'])
                fw.dma('sp', si_t[:], I['si_' + tag][tb], writes=['hy_si'])
                for m in range(4):
                    b, ch = m // 2, m % 2
                    pi = m % 2
                    pt = P.ps[pi][:, :TB]
                    for fc in range(nfc):
                        fsz = 128 if fc < nfc - 1 else 1
                        fw.mm(pt, wre[:fsz, fc, m * 128:(m + 1) * 128], ci_t[:fsz, fc, :], start=(fc == 0), stop=False,
                              reads=['hy_ci'] if fc in (0, nfc - 1) else (), writes=[('ps', pi)])
                    for fc in range(nfc - 1):
                        fw.mm(pt, wim[:, fc, m * 128:(m + 1) * 128], si_t[:, fc, :], start=False, stop=(fc == nfc - 2),
                              reads=['hy_si'] if fc in (0, nfc - 2) else (), writes=[('ps', pi)])
                    a = at[:, oi % 2, :]
                    o = ob[:, oi % 2, :]
                    ak, ok = ('hy_at', oi % 2), ('hy_o', oi % 2)
                    oi += 1
                    fw.ts('dve', a, pt, S.hyn[:, ch, li:li + 1], None, ALU.mult, reads=[('ps', pi)], writes=[ak])
                    fw.stt('dve', a, u[:, m, tb * TB:(tb + 1) * TB], hb[:, ch:ch + 1], a, ALU.mult, ALU.add, reads=['hy_hb'], writes=[ak])
                    fw.tt('pool', o, a, x1c[:, m, tb * TB:(tb + 1) * TB], ALU.mult, reads=[ak], writes=[ok])
                    fw.dma('sp', S.mix[(b, tag)][ch * 128:(ch + 1) * 128, tb * TB:(tb + 1) * TB], o, reads=[ok], writes=[('mixhy', b, ch, tb)])
            fw.barrier()


def phase_ssd(P, I, S, l, ctx_out):
    nc, fw = P.nc, P.fw
    ps = P.ps
    with ExitStack() as es:
        E = lambda n, sh, dt=F32: es.enter_context(P.sb(n, sh, dt))
        cw = E('sd_cw', [128, 8, 4]); dtb = E('sd_dtb', [16, 1]); arow = E('sd_arow', [128, 16]); dskb = E('sd_dsk', [128, 8])
        sng = E('sd_sng', [128, 4])
        Hs = [E('sd_H%d' % d, [128, 512]) for d in range(2)]
        xtok = E('sd_xtok', [128, 16, 512]); btok = E('sd_btok', [128, 16, 256]); cbt = E('sd_cbt', [128, 16, 2, 128])
        ct = E('sd_ct', [128, 2, L]); dttok = E('sd_dttok', [128, 16, 16]); ytok = E('sd_ytok', [128, 16, 512])
        raw = E('sd_raw', [128, 8, 514]); xa = E('sd_xa', [128, 8, 512]); dtT = E('sd_dtT', [16, L])
        dtA = E('sd_dtA', [128, 16]); ac16 = E('sd_acum', [128, 16]); acum = ac16[:, 0:8]; t8 = E('sd_t8', [128, 8]); dend = E('sd_dend', [128, 8])
        cdec = E('sd_cdec', [128, 8]); eac = E('sd_eac', [128, 8]); xdt = E('sd_xdt', [128, 8, 64]); xdte = E('sd_xdte', [128, 8, 64])
        segt8 = E('sd_seg8', [128, 8, 128]); dtAb8 = E('sd_dtAb8', [128, 8, 128]); mt8 = E('sd_mt8', [128, 8, 128])
        yo = E('sd_yo', [128, 8, 64]); tmpy = E('sd_tmpy', [128, 512]); hsc = E('sd_hsc', [128, 8, 64])
        zt = E('sd_zt', [128, 4, 128]); yg = E('sd_yg', [128, 4, 128]); sqg = E('sd_sqg', [128, 4, 128]); rs = E('sd_rs', [128, 2, 128])
        og = E('sd_og', [128, 4, 128], BF16)
        fw.dma('sp', cw[:], I['sscw'][l], writes=['sd_cw'])
        fw.dma('sp', dtb[:], I['dtb'][l], writes=['sd_dtb'])
        fw.dma('sp', arow[:], I['alog'][l].partition_broadcast(128), writes=['sd_arow'])
        fw.dma('sp', dskb[:], I['dskip'][l].partition_broadcast(128), writes=['sd_dsk'])
        fw.dma('sp', sng[:], I['sng'][l], writes=['sd_sng'])
        fw.act(arow[:], arow[:], AF.Exp, reads=['sd_arow'], writes=['sd_arow'])
        fw.ts('dve', arow[:], arow[:], -1.0, None, ALU.mult, reads=['sd_arow'], writes=['sd_arow'])
        tri = [S.uinc, S.linc]
        msk = [S.maskf, S.maskb]

        def stage_a(seq):
            Lq = seq_len(seq)
            T = min(512, Lq)
            plv = S.pl[seq][OFF_XBC:OFF_XBC + 1024, :].rearrange("(cc p) t -> p cc t", p=128)
            fw.dma('sp', dtT[:, :Lq], S.pl[seq][OFF_DT:OFF_DT + 16, :], writes=['sd_dtT'])
            fw.act(dtT[:, :Lq], dtT[:, :Lq], AF.Exp, bias=dtb[:, 0:1], reads=['sd_dtT', 'sd_dtb'], writes=['sd_dtT'])
            fw.ts('dve', dtT[:, :Lq], dtT[:, :Lq], 1.0, None, ALU.add, reads=['sd_dtT'], writes=['sd_dtT'])
            fw.act(dtT[:, :Lq], dtT[:, :Lq], AF.Ln, reads=['sd_dtT'], writes=['sd_dtT'])
            for tb in range(Lq // T):
                t0 = tb * T
                lo, hi = max(0, t0 - 1), min(Lq, t0 + T + 1)
                d0 = lo - (t0 - 1)
                if tb == 0:
                    fw.memset('pool', raw[:, :, 0:1], 0.0, writes=['sd_raw'])
                if tb == Lq // T - 1:
                    fw.memset('pool', raw[:, :, T + 1:T + 2], 0.0, writes=['sd_raw'])
                fw.dma('sp', raw[:, :, d0:d0 + hi - lo], plv[:, :, lo:hi], writes=['sd_raw'])
                for cc in range(8):
                    k = ('sd_xa', cc)
                    fw.ts('dve', xa[:, cc, :T], raw[:, cc, 0:T], cw[:, cc, 0:1], cw[:, cc, 3:4], ALU.mult, ALU.add, reads=['sd_raw', 'sd_cw'], writes=[k])
                    fw.stt('dve', xa[:, cc, :T], raw[:, cc, 1:T + 1], cw[:, cc, 1:2], xa[:, cc, :T], ALU.mult, ALU.add, reads=['sd_raw'], writes=[k])
                    fw.stt('dve', xa[:, cc, :T], raw[:, cc, 2:T + 2], cw[:, cc, 2:3], xa[:, cc, :T], ALU.mult, ALU.add, reads=['sd_raw'], writes=[k])
                    fw.act(xa[:, cc, :T], xa[:, cc, :T], AF.Silu, reads=[k], writes=[k])
                xk = [('sd_xa', cc) for cc in range(8)]
                fw.cp('pool', ct[:, :, t0:t0 + T], xa[:, 6:8, :T], reads=xk, writes=['sd_ct'])
                for j in range(T // 128):
                    c = tb * (T // 128) + j
                    sl = slice(j * 128, (j + 1) * 128)
                    for k in range(4):
                        fw.tr(ps[1][:, k * 128:(k + 1) * 128], xa[:, k, sl], S.ident[:], reads=xk, writes=[('ps', 1)])
                    fw.cp('act', xtok[:, c, :], ps[1][:, :], reads=[('ps', 1)], writes=[('sd_xtok', c)])
                    for g in range(2):
                        fw.tr(ps[2][:, g * 128:(g + 1) * 128], xa[:, 4 + g, sl], S.ident[:], reads=xk, writes=[('ps', 2)])
                        fw.mm(ps[2][:, 256 + g * 128:256 + (g + 1) * 128], xa[:, 4 + g, sl], xa[:, 6 + g, sl], reads=xk, writes=[('ps', 2)])
                    fw.cp('dve', btok[:, c, :], ps[2][:, 0:256], reads=[('ps', 2)], writes=[('sd_btok', c)])
                    fw.cp('dve', cbt[:, c, :, :], ps[2][:, 256:512].rearrange("p (g i) -> p g i", g=2), reads=[('ps', 2)], writes=[('sd_cbt', c)])
                    fw.tr(ps[3][:, 0:16], dtT[:16, t0 + j * 128:t0 + (j + 1) * 128], S.ident[:16, :16], reads=['sd_dtT'], writes=[('ps', 3)])
                    fw.cp('dve', dttok[:, c, :], ps[3][:, 0:16], reads=[('ps', 3)], writes=[('sd_dttok', c)])

        def scan(seq, d, with_output, final_out):
            Lq = seq_len(seq)
            ncq = Lq // 128
            H = Hs[d]
            hk = 'sd_H%d' % d
            d8 = slice(d * 8, (d + 1) * 8)
            order = range(ncq) if d == 0 else range(ncq - 1, -1, -1)
            for c in order:
                fw.tt('pool', dtA[:], dttok[:, c, :], arow[:], ALU.mult, reads=[('sd_dttok', c), 'sd_arow'], writes=['sd_dtA'])
                fw.mm(ps[0][:, 0:8], tri[d][:], dtA[:, d8], reads=['sd_dtA'], writes=[('ps', 0)])
                fw.mm(ps[0][:, 8:16], S.ones[:], dtA[:, d8], reads=['sd_dtA'], writes=[('ps', 0)])
                fw.cp('dve', ac16[:], ps[0][:, 0:16], reads=[('ps', 0)], writes=['sd_acum'])
                fw.tt('pool', t8[:], ac16[:, 8:16], acum, ALU.subtract, reads=['sd_acum'], writes=['sd_t8'])
                fw.act(dend[:], t8[:], AF.Exp, reads=['sd_t8'], writes=['sd_dend'])
                fw.act(cdec[:], ac16[:, 8:16], AF.Exp, reads=['sd_acum'], writes=['sd_cdec'])
                fw.tt('pool', xdt[:], xtok[:, c, :].rearrange("p (h q) -> p h q", h=8),
                      dttok[:, c, d8].unsqueeze(2).to_broadcast([128, 8, 64]), ALU.mult, reads=[('sd_xtok', c), ('sd_dttok', c)], writes=['sd_xdt'])
                if with_output:
                    fw.act(eac[:], acum,  AF.Exp, reads=['sd_acum'], writes=['sd_eac'])
                    fw.cp('dve', dtAb8[:], dtA[:, d8].unsqueeze(2).to_broadcast([128, 8, 128]), reads=['sd_dtA'], writes=['sd_dtAb'])
                    for hd in range(8):
                        pa = 1 + hd // 4
                        fw.mm(ps[pa][:, (hd % 4) * 128:(hd % 4 + 1) * 128], dtAb8[:, hd, :], tri[d][:], reads=['sd_dtAb'], writes=[('ps', pa)])
                    for hf in range(2):
                        fw.tt('dve', segt8[:, 4 * hf:4 * hf + 4, :], ps[1 + hf][:, :].rearrange("p (h i) -> p h i", h=4),
                              acum[:, 4 * hf:4 * hf + 4].unsqueeze(2).to_broadcast([128, 4, 128]), ALU.subtract,
                              reads=[('ps', 1 + hf), 'sd_acum'], writes=['sd_seg8'])
                    fw.tt('dve', segt8[:], segt8[:], msk[d][:, :].unsqueeze(1).to_broadcast([128, 8, 128]), ALU.min,
                          reads=['sd_seg8'], writes=['sd_seg8'])
                    fw.act(segt8[:], segt8[:], AF.Exp, reads=['sd_seg8'], writes=['sd_seg8'])
                    fw.tt('dve', mt8[:].rearrange("p (g h) i -> p g h i", g=2), segt8[:].rearrange("p (g h) i -> p g h i", g=2),
                          cbt[:, c, :, :].unsqueeze(2).to_broadcast([128, 2, 4, 128]), ALU.mult, reads=['sd_seg8', ('sd_cbt', c)], writes=['sd_mt8'])
                    for hd in range(8):
                        fw.mm(ps[3][:, hd * 64:(hd + 1) * 64], mt8[:, hd, :], xdt[:, hd, :], reads=['sd_mt8', 'sd_xdt'], writes=[('ps', 3)])
                    for g in range(2):
                        fw.mm(ps[4][:, g * 256:(g + 1) * 256], ct[:, g, c * 128:(c + 1) * 128], H[:, g * 256:(g + 1) * 256],
                              reads=['sd_ct', hk], writes=[('ps', 4)])
                    fw.tt('dve', yo[:], ps[4][:, :].rearrange("p (h q) -> p h q", h=8), eac[:, :].unsqueeze(2).to_broadcast([128, 8, 64]),
                          ALU.mult, reads=[('ps', 4), 'sd_eac'], writes=['sd_yo'])
                    yflat = yo[:].rearrange("p h q -> p (h q)")
                    if d == 0:
                        fw.tt('dve', ytok[:, c, :], ps[3][:, :], yflat, ALU.add, reads=[('ps', 3), 'sd_yo'], writes=[('sd_ytok', c)])
                        fw.tt('pool', tmpy[:].rearrange("p (h q) -> p h q", h=8), xtok[:, c, :].rearrange("p (h q) -> p h q", h=8),
                              dskb[:, :].unsqueeze(2).to_broadcast([128, 8, 64]), ALU.mult, reads=[('sd_xtok', c), 'sd_dsk'], writes=['sd_tmpy'])
                        fw.tt('pool', ytok[:, c, :], ytok[:, c, :], tmpy[:], ALU.add, reads=['sd_tmpy'], writes=[('sd_ytok', c)])
                    else:
                        fw.tt('dve', tmpy[:], ps[3][:, :], yflat, ALU.add, reads=[('ps', 3), 'sd_yo'], writes=['sd_tmpy'])
                        fw.tt('pool', ytok[:, c, :], ytok[:, c, :], tmpy[:], ALU.add, reads=['sd_tmpy'], writes=[('sd_ytok', c)])
                fw.tt('pool', xdte[:], xdt[:], dend[:, :].unsqueeze(2).to_broadcast([128, 8, 64]), ALU.mult, reads=['sd_xdt', 'sd_dend'], writes=['sd_xdte'])
                for g in range(2):
                    fw.mm(ps[5][:, g * 256:(g + 1) * 256], btok[:, c, g * 128:(g + 1) * 128],
                          xdte[:, g * 4:(g + 1) * 4, :].rearrange("p h q -> p (h q)"), reads=[('sd_btok', c), 'sd_xdte'], writes=[('ps', 5)])
                fw.tt('pool', hsc[:], H[:].rearrange("p (h q) -> p h q", h=8), cdec[:, :].unsqueeze(2).to_broadcast([128, 8, 64]), ALU.mult,
                      reads=[hk, 'sd_cdec'], writes=['sd_hsc'])
                fw.tt('dve', H[:], hsc[:].rearrange("p h q -> p (h q)"), ps[5][:, :], ALU.add, reads=['sd_hsc', ('ps', 5)], writes=[hk])
                if final_out:
                    tk = slice(c * 128, (c + 1) * 128)
                    for k in range(4):
                        fw.tr(ps[6][:, k * 128:(k + 1) * 128], ytok[:, c, k * 128:(k + 1) * 128], S.ident[:], reads=[('sd_ytok', c)], writes=[('ps', 6)])
                    fw.dma('sp', zt[:], S.pl[seq][OFF_Z:OFF_Z + 512, tk].rearrange("(k p) t -> p k t", p=128), writes=['sd_zt'])
                    fw.act(zt[:], zt[:], AF.Silu, reads=['sd_zt'], writes=['sd_zt'])
                    fw.tt('dve', yg[:], ps[6][:, :].rearrange("p (k t) -> p k t", k=4), zt[:], ALU.mult, reads=[('ps', 6), 'sd_zt'], writes=['sd_yg'])
                    fw.act(sqg[:], yg[:], AF.Square, reads=['sd_yg'], writes=['sd_sqg'])
                    for g in range(2):
                        fw.mm(ps[7][:, g * 128:(g + 1) * 128], S.ones[:], sqg[:, 2 * g, :], start=True, stop=False, reads=['sd_sqg'], writes=[('ps', 7)])
                        fw.mm(ps[7][:, g * 128:(g + 1) * 128], S.ones[:], sqg[:, 2 * g + 1, :], start=False, stop=True, reads=['sd_sqg'], writes=[('ps', 7)])
                    fw.act(rs[:], ps[7][:, 0:256].rearrange("p (g t) -> p g t", g=2), AF.Sqrt, bias=S.epsc[:, 0:1], scale=1.0 / 256.0,
                           reads=[('ps', 7)], writes=['sd_rs'])
                    fw.op('dve', lambda e: e.reciprocal(rs[:], rs[:]), reads=['sd_rs'], writes=['sd_rs'])
                    for k in range(4):
                        fw.stt('dve', og[:, k, :], yg[:, k, :], sng[:, k:k + 1], rs[:, k // 2, :], ALU.mult, ALU.mult,
                               reads=['sd_yg', 'sd_rs', 'sd_sng'], writes=['sd_og'])
                    fw.dma('sp', S.mix[seq][256:768, tk].rearrange("(k p) t -> p k t", p=128), og[:], reads=['sd_og'], writes=[('mixssd', seq, c)])

        for b in range(2):
            for d in range(2):
                fw.memset('pool', Hs[d][:], 0.0, writes=['sd_H%d' % d])
            stage_a((b, 'c'))
            if SSD_MODE != 'a':
                scan((b, 'c'), 0, ctx_out, False)
                scan((b, 'c'), 1, ctx_out, ctx_out)
            stage_a((b, 'l'))
            if SSD_MODE != 'a':
                scan((b, 'l'), 0, True, False)
                scan((b, 'l'), 1, True, True)
        fw.barrier()


def phase_peer(P, I, S, l, ctx_out):
    nc, fw = P.nc, P.fw
    ps = P.ps
    T = 256
    NB = 4
    with ExitStack() as es2:
        cs = [es2.enter_context(P.sb('pe_cs%d' % i, [128, 2048], F32)) for i in range(4)]
        cbs = [es2.enter_context(P.sb('pe_cb%d' % i, [128, 2048], BF16)) for i in range(4)]
        uv = I['uT'][l].rearrange("(a p) e -> p a e", p=128)
        n = 0
        for blk in range(64):
            k = n % 4
            n += 1
            fw.dma('sp', cs[k][:].rearrange("p (a e) -> p a e", a=8), uv[:, :, blk * 256:(blk + 1) * 256], writes=['pe_cs%d' % k])
            fw.cp('dve' if k % 2 == 0 else 'pool', cbs[k][:], cs[k][:], reads=['pe_cs%d' % k], writes=['pe_cb%d' % k])
            fw.dma('act', S.ubf[blk].rearrange("p a e -> p (a e)"), cbs[k][:], reads=['pe_cb%d' % k], writes=[('ubf', blk)])
        for blk in range(64):
            k = n % 4
            n += 1
            fw.dma('sp', cs[k][:].rearrange("p (ii d) -> p ii d", ii=2),
                   I['v'][l][blk * 256:(blk + 1) * 256, :].rearrange("(ii j) d -> j ii d", j=128), writes=['pe_cs%d' % k])
            fw.cp('dve' if k % 2 == 0 else 'pool', cbs[k][:], cs[k][:], reads=['pe_cs%d' % k], writes=['pe_cb%d' % k])
            fw.dma('act', S.vbf[blk].rearrange("p ii d -> p (ii d)"), cbs[k][:], reads=['pe_cb%d' % k], writes=[('vbf', blk)])
        fw.barrier()
    with ExitStack() as es:
        E = lambda n, sh, dt=F32: es.enter_context(P.sb(n, sh, dt))
        wq = E('pe_wq', [128, 8, 2048], BF16)
        st = [E('pe_s%d' % i, [128, 8, 256]) for i in range(2)]
        k1 = E('pe_k1', [128, 128], BF16); k2 = E('pe_k2', [128, 128], BF16)
        gbuf = E('pe_g', [128, 128, T], BF16)
        ub = E('pe_ub', [128, 2, 8, 256], BF16); vb = E('pe_vb', [128, 2, 2, 1024], BF16)
        h2s = [E('pe_h2%d' % i, [128, 8, T], BF16) for i in range(2)]; sq = E('pe_sq', [128, 8, T]); rstd = E('pe_rstd', [128, T])
        qT = E('pe_qT', [128, 16, T], BF16); s12 = E('pe_s12', [128, 16, 128]); v12s = [E('pe_v12%d' % i, [128, 16, 16]) for i in range(2)]
        wk = E('pe_wk', [128, 4, 128]); wk2 = E('pe_wk2', [128, 4, 256]); t16s = [E('pe_t16%d' % i, [128, 8, 16]) for i in range(2)]
        e16 = E('pe_e16', [128, 8, 16]); zz = E('pe_z', [128, 8]); mz = E('pe_mz', [128, 8])
        thr = E('pe_thr', [128, 8, 16]); bia = E('pe_bia', [128, 8, 16]); pc = E('pe_pc', [128, 3, T])
        Et = [E('pe_E%d' % i, [128, 256]) for i in range(4)]
        Wt = [E('pe_W%d' % i, [128, 128], BF16) for i in range(8)]
        Pt = [E('pe_P%d' % i, [128, 128], BF16) for i in range(8)]
        qr1 = E('pe_qr1', [128, 2, 4, 128], BF16)
        qr2 = E('pe_qr2', [128, 2, 4, 128], BF16)
        gst = [E('pe_gs%d' % i, [128, T]) for i in range(2)]
        At = [E('pe_A%d' % i, [128, T], BF16) for i in range(2)]
        xo = E('pe_xo', [128, 8, T])
        cand = sq
        wv = I['wq'][l].rearrange("(dc p) n -> p dc n", p=128)
        n = 0
        for blk in range(8):
            k = n % 2
            n += 1
            fw.dma('sp', st[k][:], wv[:, :, blk * 256:(blk + 1) * 256], writes=['pe_s%d' % k])
            fw.cp('dve' if k == 0 else 'pool', wq[:, :, blk * 256:(blk + 1) * 256], st[k][:], reads=['pe_s%d' % k], writes=[('pe_wq', blk)])
        for (kt, nm, key) in ((k1, 'k1T', 'pe_k1'), (k2, 'k2T', 'pe_k2')):
            k = n % 2
            n += 1
            fw.dma('sp', st[k][:, 0, 0:128], I[nm][l], writes=['pe_s%d' % k])
            fw.cp('dve', kt[:], st[k][:, 0, 0:128], reads=['pe_s%d' % k], writes=[key])
        fw.barrier()

        def load_tables(blk):
            bb = blk % 2
            fw.dma('sp', ub[:, bb, :, :], S.ubf[blk], writes=[('pe_ub', bb)])
            fw.dma('sp', vb[:, bb, :, :], S.vbf[blk], writes=[('pe_vb', bb)])

        qk = [('pe_qT', qc) for qc in range(16)]
        sqk = [('nm_sq', dc) for dc in range(8)]

        def v12k_of(tt):
            return [('pe_v12', tt, i) for i in range(16)] + [('pe_v12b', tt, i) for i in range(16)]

        def t16k_of(tt):
            return [('pe_t16', tt, i) for i in range(8)] + [('pe_t16b', tt, i) for i in range(8)]

        def part_A(seq, grp, gi):
            col = seq_col(seq)
            rv = S.res[seq].rearrange("(dc p) t -> p dc t", p=128)
            g0 = grp * T
            xt = st[gi % 2]
            xk = 'pe_s%d' % (gi % 2)
            h2 = h2s[gi % 2]
            fw.dma('sp', xt[:], rv[:, :, g0:g0 + T], writes=[xk])
            hk = normmod(P, S, xt, xk, T, S.G2[:, :, col], S.modT[:, 24:32, col], h2, 'pe_h2_%d' % (gi % 2), sq, rstd, ('ps', 4), ps[4])
            for qc in range(16):
                pi = 4 + qc % 4
                for dc in range(8):
                    fw.mm(ps[pi][:, :T], wq[:, dc, qc * 128:(qc + 1) * 128], h2[:, dc, :], start=(dc == 0), stop=(dc == 7),
                          reads=hk if dc in (0, 7) else (), writes=[('ps', pi)])
                fw.cp('act' if qc % 2 == 0 else 'dve', qT[:, qc, :], ps[pi][:, :T], reads=[('ps', pi)], writes=[('pe_qT', qc)])
            for tt in range(2):
                tsl = slice(tt * 128, (tt + 1) * 128)
                v12 = v12s[tt]
                t16 = t16s[tt]
                for h in range(8):
                    fw.mm(ps[4 + h // 4][:, (h % 4) * 128:(h % 4 + 1) * 128], qT[:, 2 * h, tsl], k1[:], reads=qk + ['pe_k1'], writes=[('ps', 4 + h // 4)])
                    fw.mm(ps[6 + h // 4][:, (h % 4) * 128:(h % 4 + 1) * 128], qT[:, 2 * h + 1, tsl], k2[:], reads=qk + ['pe_k2'], writes=[('ps', 6 + h // 4)])
                for q4 in range(4):
                    fw.cp('dve' if q4 % 2 == 0 else 'act', s12[:, q4 * 4:(q4 + 1) * 4, :], ps[4 + q4][:, :].rearrange("p (h i) -> p h i", h=4),
                          reads=[('ps', 4 + q4)], writes=[('pe_s12', q4)])
                sk = [('pe_s12', q4) for q4 in range(4)]
                for base in range(0, 16, 4):
                    for idx in range(base, base + 4):
                        fw.op('dve', lambda e, idx=idx: e.max(out=v12[:, idx, 0:8], in_=s12[:, idx, :]), reads=sk, writes=[('pe_v12', tt, idx)])
                    for idx in range(base, base + 4):
                        fw.op('dve', lambda e, idx=idx: e.match_replace(out=wk[:, idx % 4, :], in_to_replace=v12[:, idx, 0:8], in_values=s12[:, idx, :], imm_value=-1e30),
                              reads=[('pe_v12', tt, idx)], writes=[('pe_wk', idx % 4)])
                    for idx in range(base, base + 4):
                        fw.op('dve', lambda e, idx=idx: e.max(out=v12[:, idx, 8:16], in_=wk[:, idx % 4, :]), reads=[('pe_wk', idx % 4)], writes=[('pe_v12b', tt, idx)])
                v12k = v12k_of(tt)
                fw.tt('dve', cand[:].rearrange("p h (a b) -> p h a b", a=16), v12[:, 0:8, :].unsqueeze(3).to_broadcast([128, 8, 16, 16]),
                      v12[:, 8:16, :].unsqueeze(2).to_broadcast([128, 8, 16, 16]), ALU.add, reads=v12k, writes=sqk)
                for base in range(0, 8, 4):
                    for h in range(base, base + 4):
                        fw.op('dve', lambda e, h=h: e.max(out=t16[:, h, 0:8], in_=cand[:, h, :]), reads=sqk, writes=[('pe_t16', tt, h)])
                    for h in range(base, base + 4):
                        fw.op('dve', lambda e, h=h: e.match_replace(out=wk2[:, h % 4, :], in_to_replace=t16[:, h, 0:8], in_values=cand[:, h, :], imm_value=-1e30),
                              reads=[('pe_t16', tt, h)] + sqk, writes=[('pe_wk2', h % 4)])
                    for h in range(base, base + 4):
                        fw.op('dve', lambda e, h=h: e.max(out=t16[:, h, 8:16], in_=wk2[:, h % 4, :]), reads=[('pe_wk2', h % 4)], writes=[('pe_t16b', tt, h)])

        def part_B():
            for tt in range(2):
                tsl = slice(tt * 128, (tt + 1) * 128)
                v12 = v12s[tt]
                t16 = t16s[tt]
                v12k = v12k_of(tt)
                t16k = t16k_of(tt)
                fw.tt('dve', e16[:], t16[:], t16[:, :, 0:1].to_broadcast([128, 8, 16]), ALU.subtract, reads=t16k, writes=['pe_e16'])
                fw.act(e16[:], e16[:], AF.Exp, reads=['pe_e16'], writes=['pe_e16'])
                fw.op('dve', lambda e: e.reduce_sum(out=zz[:], in_=e16[:], axis=AX.X), reads=['pe_e16'], writes=['pe_z'])
                fw.act(zz[:], zz[:], AF.Ln, reads=['pe_z'], writes=['pe_z'])
                fw.tt('dve', mz[:], t16[:, :, 0], zz[:], ALU.add, reads=t16k + ['pe_z'], writes=['pe_mz'])
                fw.tt('dve', thr[:], t16[:, :, 15:16].to_broadcast([128, 8, 16]), v12[:, 0:8, :], ALU.subtract, reads=t16k + v12k, writes=['pe_thr'])
                fw.tt('dve', bia[:], v12[:, 0:8, :], mz[:, :].unsqueeze(2).to_broadcast([128, 8, 16]), ALU.subtract, reads=['pe_mz'] + v12k, writes=['pe_bia'])
                fw.act(bia[:], bia[:], AF.Exp, reads=['pe_bia'], writes=['pe_bia'])
                srcs = [(v12[:, 0:8, :], v12k), (thr[:], ['pe_thr']), (bia[:], ['pe_bia'])]
                for slot, (sap, skey) in enumerate(srcs):
                    fw.tr(ps[6][:, slot * 128:(slot + 1) * 128], sap.rearrange("p h a -> p (h a)"), S.ident[:], reads=skey, writes=[('ps', 6)])
                fw.cp('dve', pc[:, :, tsl], ps[6][:, 0:384].rearrange("p (s t) -> p s t", s=3), reads=[('ps', 6)], writes=['pe_pc'])

        groups = [(seq, grp) for seq in seqs_of(ctx_out) for grp in range(seq_len(seq) // T)]
        part_A(groups[0][0], groups[0][1], 0)
        part_B()
        for gi, (seq, grp) in enumerate(groups):
            if True:
                col = seq_col(seq)
                rv = S.res[seq].rearrange("(dc p) t -> p dc t", p=128)
                g0 = grp * T
                xt = st[gi % 2]
                xk = 'pe_s%d' % (gi % 2)
                h2 = h2s[gi % 2]
                has_next = gi + 1 < len(groups)
                load_tables(0)
                load_tables(1)
                NQ = T // 4

                def quad_F(u):
                    qb = u % 2
                    t0 = 4 * u
                    for half, qr, kk_ in ((0, qr1, k1), (1, qr2, k2)):
                        src_ap = qT[:, half:16:2, t0:t0 + 4].rearrange("p h t -> p t h").unsqueeze(3).to_broadcast([128, 4, 8, 16])
                        fw.cp('pool' if half == 0 else 'dve', qr[:, qb, :, :].rearrange("p t (h a) -> p t h a", h=8), src_ap,
                              reads=qk if u < 2 else (), writes=[('pe_qr%d' % half, qb)])
                    for k in range(4):
                        t = t0 + k
                        xb = (t // 2) % 4
                        c1 = (t % 2) * 128
                        fw.mm(ps[xb][:, c1:c1 + 128], qr1[:, qb, k, :], k1[:], reads=[('pe_qr0', qb)], writes=[('ps', xb)])
                        fw.mm(ps[xb][:, 256 + c1:256 + c1 + 128], qr2[:, qb, k, :], k2[:], reads=[('pe_qr1', qb)], writes=[('ps', xb)])

                def quad_B(u):
                    t0 = 4 * u
                    gb = 4 + u % 2
                    for pr in range(2):
                        tp = t0 + 2 * pr
                        xb = (tp // 2) % 4
                        eq = (tp // 2) % 4
                        ek = ('pe_E', eq)
                        fw.act(Et[eq][:], ps[xb][:, 256:512], AF.Exp, reads=[('ps', xb)], writes=[ek])
                        for kk in range(2):
                            t = tp + kk
                            k = 2 * pr + kk
                            c1 = kk * 128
                            q = t % 8
                            wkk, pk = ('pe_W', q), ('pe_P', q)
                            fw.stt('dve', Wt[q][:], ps[xb][:, 256 + c1:256 + c1 + 128], pc[:, 1, t:t + 1], Et[eq][:, c1:c1 + 128], ALU.is_ge, ALU.mult,
                                   reads=[('ps', xb), ek, 'pe_pc'], writes=[wkk])
                            fw.ts('dve', Pt[q][:], ps[xb][:, c1:c1 + 128], pc[:, 0, t:t + 1], pc[:, 2, t:t + 1], ALU.is_equal, ALU.mult,
                                  reads=[('ps', xb), ek, 'pe_pc'], writes=[pk])
                            fw.mm(ps[gb][:, k * 128:(k + 1) * 128], Wt[q][:], Pt[q][:], reads=[wkk, pk], writes=[('ps', gb)])

                def quad_C(u):
                    gb = 4 + u % 2
                    fw.cp('act', gbuf[:, :, 4 * u:4 * u + 4].rearrange("p i t -> p t i"),
                          ps[gb][:, :].rearrange("p (t i) -> p t i", t=4), reads=[('ps', gb)], writes=[('pe_g', u % 2)])

                for u in range(NQ + 2):
                    if u < NQ:
                        quad_F(u)
                    if 1 <= u <= NQ:
                        quad_B(u - 1)
                    if u >= 2:
                        quad_C(u - 2)
                gk = [('pe_g', q) for q in range(2)]
                if has_next:
                    part_A(groups[gi + 1][0], groups[gi + 1][1], gi + 1)

                def dense_S(i):
                    bb, ii = (i // 2) % 2, i % 2
                    pi = 4 + i % 2
                    for dc in range(8):
                        fw.mm(ps[pi][:, :T], ub[:, bb, dc, ii * 128:(ii + 1) * 128], h2[:, dc, :], start=(dc == 0), stop=(dc == 7),
                              reads=[('pe_ub', bb)] if dc in (0, 7) else (), writes=[('ps', pi)])

                def dense_O(i):
                    bb, ii = (i // 2) % 2, i % 2
                    pi = 4 + i % 2
                    fw.act(gst[i % 2][:], ps[pi][:, :T], AF.Gelu_apprx_tanh, reads=[('ps', pi)], writes=[('pe_gs', i % 2)])
                    fw.tt('pool' if i % 2 == 0 else 'dve', At[i % 2][:], gst[i % 2][:], gbuf[:, i, :], ALU.mult,
                          reads=[('pe_gs', i % 2)] + (gk if i < 2 else []), writes=[('pe_A', i % 2)])
                    for dch in range(8):
                        bk, rg = dch // 2, (dch % 2) * 256
                        fw.mm(ps[bk][:, rg:rg + T], vb[:, bb, ii, dch * 128:(dch + 1) * 128], At[i % 2][:],
                              start=(i == 0 and dch % 2 == 0), stop=(i == 127),
                              reads=[('pe_A', i % 2), ('pe_vb', bb)] if dch in (0, 7) else (), writes=[('ps', bk)])

                dense_S(0)
                for i in range(128):
                    if i + 1 < 128:
                        dense_S(i + 1)
                    dense_O(i)
                    if i % 2 == 1 and (i + 1) // 2 + 1 < 64:
                        load_tables((i + 1) // 2 + 1)
                for dch in range(8):
                    bk, rg = dch // 2, (dch % 2) * 256
                    fw.stt('dve', xo[:, dch, :], ps[bk][:, rg:rg + T], S.modT[:, 40 + dch, col:col + 1], xt[:, dch, :], ALU.mult, ALU.add,
                           reads=[('ps', bk), xk], writes=[('pe_xo', dch)])
                fw.dma('sp', rv[:, :, g0:g0 + T], xo[:], reads=[('pe_xo', d8) for d8 in range(8)], writes=[('resp', seq, grp)])
                if has_next:
                    part_B()
        fw.barrier()
```

```python
import math
from contextlib import ExitStack
import numpy as np
import ml_dtypes
import concourse.bass as bass
import concourse.mybir as mybir
from concourse.bass_utils import run_bass_kernel_spmd

F32 = mybir.dt.float32
BF16 = mybir.dt.bfloat16
AF = mybir.ActivationFunctionType
ALU = mybir.AluOpType
AX = mybir.AxisListType

NDS = 20
D = 1024
L = 2048
LC = 256
NLAYER = 2
EPS = 1e-6
NCOL = 2576
OFF_Z, OFF_XBC, OFF_DT, OFF_FN = 768, 1280, 2304, 2320
PI = math.pi
import os
SSD_MODE = os.environ.get('SSD_MODE', 'full')


class FW:
    def __init__(self, nc):
        self.nc = nc
        self.eng = dict(pe=nc.tensor, dve=nc.vector, act=nc.scalar, pool=nc.gpsimd, sp=nc.sync)
        self.esem = {k: nc.alloc_semaphore("es_" + k) for k in self.eng}
        self.ecnt = {k: 0 for k in self.eng}
        self.dsem = [nc.alloc_semaphore("ds_%d" % i) for i in range(NDS)]
        self.dcnt = [0] * NDS
        self.waited = {k: {} for k in self.eng}
        self.buf = {}
        self.dsem_of = {}
        self.rr = 0
        self.ninst = 0

    def _b(self, key):
        b = self.buf.get(key)
        if b is None:
            b = dict(w=None, r={})
            self.buf[key] = b
        return b

    def _wait(self, en, tok):
        if tok is None:
            return
        kind, who, n = tok
        if kind == 'e':
            if who == en and en == 'pe':
                return
            if self.waited[en].get(('e', who), 0) >= n:
                return
            self.eng[en].wait_ge(self.esem[who], n)
            self.waited[en][('e', who)] = n
        else:
            val = self.dcnt[who]
            if self.waited[en].get(('d', who), 0) >= n:
                return
            self.eng[en].wait_ge(self.dsem[who], 16 * val)
            self.waited[en][('d', who)] = val

    def _deps(self, en, reads, writes):
        for k in reads:
            self._wait(en, self._b(k)['w'])
        for k in writes:
            b = self._b(k)
            self._wait(en, b['w'])
            for t in b['r'].values():
                self._wait(en, t)

    def _commit(self, tok, reads, writes):
        for k in writes:
            self.buf[k] = dict(w=tok, r={})
        for k in reads:
            if k in writes:
                continue
            self._b(k)['r'][(tok[0], tok[1])] = tok

    def op(self, en, fn, reads=(), writes=()):
        self._deps(en, reads, writes)
        ins = fn(self.eng[en])
        self.ecnt[en] += 1
        ins.then_inc(self.esem[en], 1)
        tok = ('e', en, self.ecnt[en])
        self._commit(tok, reads, writes)
        self.ninst += 1
        return tok

    def dma(self, en, out, in_, reads=(), writes=(), **kw):
        self._deps(en, reads, writes)
        key = writes[0] if writes else ('anon',)
        idx = self.dsem_of.get(key)
        if idx is None:
            idx = self.rr % NDS
            self.rr += 1
            self.dsem_of[key] = idx
        ins = self.eng[en].dma_start(out=out, in_=in_, **kw)
        self.dcnt[idx] += 1
        ins.then_inc(self.dsem[idx], 16)
        tok = ('d', idx, self.dcnt[idx])
        self._commit(tok, reads, writes)
        self.ninst += 1
        return tok

    def barrier(self):
        for en in self.eng:
            for who in self.eng:
                if who != en and self.ecnt[who] > self.waited[en].get(('e', who), 0):
                    self.eng[en].wait_ge(self.esem[who], self.ecnt[who])
                    self.waited[en][('e', who)] = self.ecnt[who]
            for i in range(NDS):
                if self.dcnt[i] > self.waited[en].get(('d', i), 0):
                    self.eng[en].wait_ge(self.dsem[i], 16 * self.dcnt[i])
                    self.waited[en][('d', i)] = self.dcnt[i]
        self.buf = {}

    def mm(self, out, lhsT, rhs, start=True, stop=True, reads=(), writes=()):
        return self.op('pe', lambda e: e.matmul(out, lhsT, rhs, start=start, stop=stop), reads, writes)

    def tr(self, out, in_, ident, reads=(), writes=()):
        return self.op('pe', lambda e: e.transpose(out, in_, ident), reads, writes)

    def act(self, out, in_, func, bias=None, scale=None, reads=(), writes=()):
        kw = {}
        if bias is not None:
            kw['bias'] = bias
        if scale is not None:
            kw['scale'] = scale
        return self.op('act', lambda e: e.activation(out=out, in_=in_, func=func, **kw), reads, writes)

    def ts(self, en, out, in0, s1, s2, op0, op1=None, reads=(), writes=()):
        kw = {}
        if op1 is not None:
            kw['op1'] = op1
        return self.op(en, lambda e: e.tensor_scalar(out, in0, s1, s2, op0, **kw), reads, writes)

    def tt(self, en, out, in0, in1, op, reads=(), writes=()):
        return self.op(en, lambda e: e.tensor_tensor(out, in0, in1, op), reads, writes)

    def stt(self, en, out, in0, scalar, in1, op0, op1, reads=(), writes=()):
        en = 'dve'
        return self.op(en, lambda e: e.scalar_tensor_tensor(out, in0, scalar, in1, op0, op1), reads, writes)

    def cp(self, en, out, in_, reads=(), writes=()):
        if en == 'act':
            return self.op('act', lambda e: e.copy(out, in_), reads, writes)
        return self.op(en, lambda e: e.tensor_copy(out, in_), reads, writes)

    def memset(self, en, ap, val, writes=()):
        return self.op(en, lambda e: e.memset(ap, val), (), writes)


_CONST = None


def _bf(a):
    return np.ascontiguousarray(a.astype(ml_dtypes.bfloat16))


def _consts():
    global _CONST
    if _CONST is not None:
        return _CONST
    c = {}
    c['ident'] = np.eye(128, dtype=np.float32)
    j = np.arange(128)[:, None]
    i = np.arange(128)[None, :]
    c['uinc'] = (j <= i).astype(np.float32)
    c['linc'] = (j >= i).astype(np.float32)
    c['maskf'] = np.where(i >= j, 0.0, -1e4).astype(np.float32)
    c['maskb'] = np.where(j >= i, 0.0, -1e4).astype(np.float32)
    c['ones'] = np.ones((128, 128), np.float32)
    a = np.arange(64)
    ang = 2 * np.pi * np.outer(a, a) / 64.0
    cb = np.zeros((128, 128)); sb = np.zeros((128, 128))
    for g in range(2):
        cb[g * 64:(g + 1) * 64, g * 64:(g + 1) * 64] = np.cos(ang) / 8.0
        sb[g * 64:(g + 1) * 64, g * 64:(g + 1) * 64] = np.sin(ang) / 8.0
    c['cbd'] = cb.astype(np.float32)
    c['sbd'] = sb.astype(np.float32)
    for tag, Lq in (('l', L), ('c', LC)):
        nsc = Lq // 128
        N = 2 * Lq
        nf = Lq + 1
        nfc = (nf + 127) // 128
        t = np.linspace(0.0, 1.0, Lq, dtype=np.float32)[:, None]
        w = (2.0 * np.pi * np.arange(Lq, dtype=np.float32)[:, None] / Lq).astype(np.float32)
        f = np.linspace(1e-4, 15, 16, dtype=np.float32)[None, :]
        z = np.concatenate([t, np.cos(f * w), -np.sin(f * w)], axis=-1).astype(np.float32)
        c['zT_' + tag] = np.ascontiguousarray(z.T)
        max_decay = math.log(1e-2) / 0.3
        min_decay = math.log(1e-2) / 1.5
        deltas = np.abs(np.linspace(min_decay, max_decay, 256, dtype=np.float32))
        win = np.exp(-t * deltas).astype(np.float32)
        winb = win.copy()
        winb[0] = 0.0
        lay = lambda m: np.ascontiguousarray(m.reshape(nsc, 128, 256).transpose(1, 0, 2))
        c['win_' + tag] = np.stack([lay(win), lay(winb)]).astype(np.float32)
        s = np.arange(Lq, dtype=np.float64)[:, None]
        ff = np.arange(nfc * 128, dtype=np.float64)[None, :]
        th = 2 * np.pi * s * ff / N
        valid = (ff < nf)
        Cf = np.cos(th) * valid
        Sf = -np.sin(th) * valid
        fl = lambda m: np.ascontiguousarray(m.reshape(nsc, 128, nfc, 128).transpose(2, 1, 0, 3))
        c['cf_' + tag] = _bf(fl(Cf))
        c['sf_' + tag] = _bf(fl(Sf))
        TB = min(512, Lq)
        ntb = Lq // TB
        fcol = np.arange(nfc * 128, dtype=np.float64)[:, None]
        tt = np.arange(Lq, dtype=np.float64)[None, :]
        wgt = np.where((fcol == 0) | (fcol == Lq), 1.0, 2.0) * (fcol < nf) / N
        th2 = 2 * np.pi * fcol * tt / N
        Ci = wgt * np.cos(th2)
        Si = -wgt * np.sin(th2)
        il = lambda m: np.ascontiguousarray(m.reshape(nfc, 128, ntb, TB).transpose(2, 1, 0, 3))
        c['ci_' + tag] = _bf(il(Ci))
        c['si_' + tag] = _bf(il(Si))
        t1 = np.arange(Lq, dtype=np.float64)
        th3 = 2 * np.pi * np.outer(t1, t1) / Lq
        CL = np.cos(th3) / math.sqrt(Lq)
        SLn = -np.sin(th3) / math.sqrt(Lq)
        ll = lambda m: np.ascontiguousarray(m.reshape(nsc, 128, ntb, TB).transpose(2, 1, 0, 3))
        c['cl_' + tag] = _bf(ll(CL))
        c['sl_' + tag] = _bf(ll(SLn))
    _CONST = c
    return c


def _pc(v, n):
    return np.ascontiguousarray(np.asarray(v, np.float32).reshape(n, 128).T)


class Prog:
    def __init__(self, layers=(0, 1), phases=None, dbg=False):
        self.layers = layers
        self.phases = phases
        self.dbg = dbg
        nc = bass.Bass("TRN2", target_bir_lowering=False)
        self.nc = nc
        self.fw = FW(nc)
        self.inputs = {}
        self.uid = 0
        self.ps = [nc.alloc_psum_tensor("psb%d" % i, [128, 512], F32) for i in range(8)]

    def inp(self, name, shape, dt=F32):
        t = self.nc.dram_tensor(name, list(shape), dt, kind="ExternalInput").ap()
        self.inputs[name] = t
        return t

    def scratch(self, name, shape, dt=F32, out=False):
        kind = "ExternalOutput" if (out or self.dbg) else "Internal"
        return self.nc.dram_tensor(name, list(shape), dt, kind=kind).ap()

    def sb(self, name, shape, dt):
        self.uid += 1
        return self.nc.sbuf_tensor('%s_u%d' % (name, self.uid), shape, dt)

    def want(self, ph):
        return self.phases is None or ph in self.phases


def _declare(P):
    c = _consts()
    I = {}
    I['xT'] = P.inp('xT', [2, D, L])
    I['ctxT'] = P.inp('ctxT', [2, D, LC])
    I['cT'] = P.inp('cT', [128, 8, 3])
    I['w_ada'] = P.inp('w_ada', [NLAYER, D, 6 * D])
    I['b_adaT'] = P.inp('b_adaT', [NLAYER, 128, 48])
    I['gn1'] = P.inp('gn1', [NLAYER, 128, 8])
    I['gn2'] = P.inp('gn2', [NLAYER, 128, 8])
    I['gfin'] = P.inp('gfin', [128, 8])
    I['w_in'] = P.inp('w_in', [NLAYER, D, NCOL])
    I['hycw'] = P.inp('hycw', [NLAYER, 128, 6, 4])
    I['hfw1'] = P.inp('hfw1', [NLAYER, 33, 64])
    I['hfw2'] = P.inp('hfw2', [NLAYER, 64, 64])
    I['hfw3'] = P.inp('hfw3', [NLAYER, 64, 512])
    I['hfv'] = P.inp('hfv', [NLAYER, 64, 3])
    I['hybias'] = P.inp('hybias', [NLAYER, 128, 2])
    I['sscw'] = P.inp('sscw', [NLAYER, 128, 8, 4])
    I['dtb'] = P.inp('dtb', [NLAYER, 16, 1])
    I['alog'] = P.inp('alog', [NLAYER, 1, 16])
    I['dskip'] = P.inp('dskip', [NLAYER, 1, 8])
    I['sng'] = P.inp('sng', [NLAYER, 128, 4])
    I['w_out'] = P.inp('w_out', [NLAYER, D, D])
    I['wq'] = P.inp('wq', [NLAYER, D, 2048])
    I['k1T'] = P.inp('k1T', [NLAYER, 128, 128])
    I['k2T'] = P.inp('k2T', [NLAYER, 128, 128])
    I['uT'] = P.inp('uT', [NLAYER, D, 16384])
    I['v'] = P.inp('v', [NLAYER, 16384, D])
    for k, a in c.items():
        I[k] = P.inp('c_' + k, a.shape, BF16 if a.dtype == ml_dtypes.bfloat16 else F32)
    return I


class Ctx:
    pass


def build(layers=(0, 1), phases=None, dbg=False, final=True):
    P = Prog(layers, phases, dbg)
    nc, fw = P.nc, P.fw
    I = _declare(P)
    S = Ctx()
    S.res = {}
    for b in range(2):
        S.res[(b, 'l')] = P.scratch('res_l%d' % b, [D, L])
        S.res[(b, 'c')] = P.scratch('res_c%d' % b, [D, LC])
    S.pl = {}
    S.mix = {}
    for b in range(2):
        S.pl[(b, 'l')] = P.scratch('pl_l%d' % b, [NCOL, L])
        S.pl[(b, 'c')] = P.scratch('pl_c%d' % b, [NCOL, LC])
        S.mix[(b, 'l')] = P.scratch('mix_l%d' % b, [D, L], BF16)
        S.mix[(b, 'c')] = P.scratch('mix_c%d' % b, [D, LC], BF16)
    S.khat = {'l': P.scratch('khat_l', [2, 17 * 128, 256]), 'c': P.scratch('khat_c', [2, 3 * 128, 256])}
    S.ubf = P.scratch('ubf', [64, 128, 8, 256], BF16)
    S.vbf = P.scratch('vbf', [64, 128, 2, 1024], BF16)
    S.outT = P.scratch('outT', [2, D, L], F32, out=True)

    A = lambda n, sh, dt=F32: nc.alloc_sbuf_tensor('sb_' + n, sh, dt)
    S.ident = A('ident', [128, 128]); S.ones = A('ones', [128, 128])
    S.uinc = A('uinc', [128, 128]); S.linc = A('linc', [128, 128])
    S.maskf = A('maskf', [128, 128]); S.maskb = A('maskb', [128, 128])
    S.modT = A('modT', [128, 48, 3])
    S.G1 = A('G1', [128, 8, 3]); S.G2 = A('G2', [128, 8, 3])
    S.gfin = A('gfin', [128, 8]); S.zero8 = A('zero8', [128, 8])
    S.hyn = A('hyn', [128, 2, 2])
    for nm in ('ident', 'ones', 'uinc', 'linc', 'maskf', 'maskb'):
        fw.dma('sp', getattr(S, nm)[:], I[nm], writes=[nm])
    fw.dma('sp', S.gfin[:], I['gfin'], writes=['gfin'])
    fw.memset('pool', S.zero8[:], 0.0, writes=['zero8'])
    S.epsc = A('epsc', [128, 1])
    fw.memset('pool', S.epsc[:], EPS, writes=['epsc'])
    S.negpi = A('negpi', [128, 1])
    fw.memset('pool', S.negpi[:], -PI, writes=['negpi'])
    fw.barrier()

    def src(l, seq):
        b, kind = seq
        if l == layers[0] and l == 0:
            return I['xT'][b] if kind == 'l' else I['ctxT'][b]
        return S.res[seq]

    for l in layers:
        ctx_out = l < NLAYER - 1
        if P.want('mod'):
            phase_mod(P, I, S, l)
        if P.want('proj'):
            phase_proj(P, I, S, l, src)
        if P.want('filt'):
            phase_filt(P, I, S, l, 'l')
            if ctx_out:
                phase_filt(P, I, S, l, 'c')
        if P.want('hy'):
            phase_hy(P, I, S, l, 'l')
            if ctx_out:
                phase_hy(P, I, S, l, 'c')
        if P.want('fn'):
            phase_fn(P, I, S, l, 'l')
            if ctx_out:
                phase_fn(P, I, S, l, 'c')
        if P.want('ssd'):
            phase_ssd(P, I, S, l, ctx_out)
        if P.want('out'):
            phase_out(P, I, S, l, src, ctx_out)
        if P.want('peer'):
            phase_peer(P, I, S, l, ctx_out)
    if final and P.want('final'):
        phase_final(P, I, S)
    if dbg:
        dh = P.scratch('dbg_hyn', [128, 4])
        fw.dma('sp', dh, S.hyn[:].rearrange("p a b -> p (a b)"), writes=['dbg_hyn'])
    fw.barrier()
    return P


def phase_mod(P, I, S, l):
    nc, fw = P.nc, P.fw
    with ExitStack() as es:
        cin = es.enter_context(P.sb('m_c', [128, 8, 3], F32))
        sc = es.enter_context(P.sb('m_sc', [128, 8, 3], F32))
        w0 = es.enter_context(P.sb('m_w0', [128, 8, 512], F32))
        w1 = es.enter_context(P.sb('m_w1', [128, 8, 512], F32))
        bada = es.enter_context(P.sb('m_b', [128, 48], F32))
        g1 = es.enter_context(P.sb('m_g1', [128, 8], F32))
        g2 = es.enter_context(P.sb('m_g2', [128, 8], F32))
        tmp = es.enter_context(P.sb('m_t', [128, 8, 3], F32))
        wb = [w0, w1]
        fw.dma('sp', cin[:], I['cT'], writes=['m_c'])
        fw.dma('sp', bada[:], I['b_adaT'][l], writes=['m_b'])
        fw.dma('sp', g1[:], I['gn1'][l], writes=['m_g1'])
        fw.dma('sp', g2[:], I['gn2'][l], writes=['m_g2'])
        fw.act(sc[:], cin[:], AF.Silu, reads=['m_c'], writes=['m_sc'])
        wv = I['w_ada'][l].rearrange("(dc p) n -> p dc n", p=128)
        for blk in range(12):
            w = wb[blk % 2]
            wk = 'm_w%d' % (blk % 2)
            fw.dma('sp', w[:], wv[:, :, blk * 512:(blk + 1) * 512], writes=[wk])
            for j in range(4):
                cc = blk * 4 + j
                pk = ('ps', cc % 2)
                pt = P.ps[cc % 2][:, 0:3]
                for dc in range(8):
                    fw.mm(pt, w[:, dc, j * 128:(j + 1) * 128], sc[:, dc, :], start=(dc == 0), stop=(dc == 7),
                          reads=[wk, 'm_sc'], writes=[pk])
                fw.ts('dve', S.modT[:, cc, :], pt, bada[:, cc:cc + 1], None, ALU.add, reads=[pk, 'm_b'], writes=['modT'])
        for (G, g, gk, c0, nm) in ((S.G1, g1, 'm_g1', 8, 'G1'), (S.G2, g2, 'm_g2', 32, 'G2')):
            fw.ts('dve', tmp[:], S.modT[:, c0:c0 + 8, :], 1.0, None, ALU.add, reads=['modT'], writes=['m_t'])
            fw.tt('dve', G[:], tmp[:], g[:, :].unsqueeze(2).to_broadcast([128, 8, 3]), ALU.mult, reads=['m_t', gk], writes=[nm])
        fw.barrier()


def seqs_of(ctx_too=True):
    out = []
    for b in range(2):
        out.append((b, 'l'))
        if ctx_too:
            out.append((b, 'c'))
    return out


def seq_len(seq):
    return L if seq[1] == 'l' else LC


def seq_col(seq):
    return seq[0] if seq[1] == 'l' else 2


def normmod(P, S, xt, xk, T, Gap, shap, hm, hk, sq, rstd, psk, pst):
    fw = P.fw
    fw.act(sq[:, :, :T], xt[:, :, :T], AF.Square, reads=[xk], writes=[('nm_sq', dc) for dc in range(8)])
    for dc in range(8):
        fw.mm(pst[:, :T], S.ones[:], sq[:, dc, :T], start=(dc == 0), stop=(dc == 7), reads=[('nm_sq', dc), 'ones'], writes=[psk])
    fw.act(rstd[:, :T], pst[:, :T], AF.Sqrt, bias=S.epsc[:, 0:1], scale=1.0 / D, reads=[psk, 'epsc'], writes=['nm_rstd'])
    fw.op('dve', lambda e: e.reciprocal(rstd[:, :T], rstd[:, :T]), reads=['nm_rstd'], writes=['nm_rstd'])
    for dc in range(8):
        en = 'dve' if dc % 2 == 0 else 'pool'
        fw.stt(en, sq[:, dc, :T], xt[:, dc, :T], Gap[:, dc:dc + 1], rstd[:, :T], ALU.mult, ALU.mult,
               reads=[xk, 'nm_rstd'], writes=[('nm_sq', dc)])
        fw.act(hm[:, dc, :T], sq[:, dc, :T], AF.Identity, bias=shap[:, dc:dc + 1], reads=[('nm_sq', dc)], writes=[(hk, dc)])
    return [(hk, dc) for dc in range(8)]


def load_cast_weight(P, dst, dstk, srcv, ncols, stg, stgk):
    fw = P.fw
    nb = (ncols + 511) // 512
    for blk in range(nb):
        c0 = blk * 512
        c1 = min(ncols, c0 + 512)
        st = stg[blk % 2]
        sk = stgk[blk % 2]
        fw.dma('sp', st[:, :, :c1 - c0], srcv[:, :, c0:c1], writes=[sk])
        fw.cp('dve' if blk % 2 == 0 else 'pool', dst[:, :, c0:c1], st[:, :, :c1 - c0], reads=[sk], writes=[(dstk, blk)])
    return [(dstk, blk) for blk in range(nb)]


def phase_proj(P, I, S, l, src):
    nc, fw = P.nc, P.fw
    with ExitStack() as es:
        winb = es.enter_context(P.sb('p_win', [128, 8, NCOL], BF16))
        s0 = es.enter_context(P.sb('p_s0', [128, 8, 512], F32))
        s1 = es.enter_context(P.sb('p_s1', [128, 8, 512], F32))
        sq = es.enter_context(P.sb('p_sq', [128, 8, 512], F32))
        rstd = es.enter_context(P.sb('p_rstd', [128, 512], F32))
        hm = es.enter_context(P.sb('p_hm', [128, 8, 512], BF16))
        ob = es.enter_context(P.sb('p_o', [128, 4, 512], F32))
        wkeys = load_cast_weight(P, winb, 'p_win', I['w_in'][l].rearrange("(dc p) n -> p dc n", p=128), NCOL, [s0, s1], ['p_s0', 'p_s1'])
        xb = [s0, s1]
        chunks = [(c0, min(128, NCOL - c0)) for c0 in range(0, OFF_DT, 128)] + [(OFF_DT, 16)] + [(OFF_FN, 128), (OFF_FN + 128, 128)]
        it = 0
        oi = 0
        for seq in seqs_of(True):
            Lq = seq_len(seq)
            T = min(512, Lq)
            col = seq_col(seq)
            xv = src(l, seq).rearrange("(dc p) t -> p dc t", p=128)
            for tb in range(Lq // T):
                xt = xb[it % 2]
                xk = 'p_s%d' % (it % 2)
                it += 1
                fw.dma('sp', xt[:, :, :T], xv[:, :, tb * T:(tb + 1) * T], reads=[('res', seq)], writes=[xk])
                hk = normmod(P, S, xt, xk, T, S.G1[:, :, col], S.modT[:, 0:8, col], hm, 'p_hm', sq, rstd, ('ps', 0), P.ps[0])
                for ci, (c0, cw) in enumerate(chunks):
                    pb = 1 + ci % 3
                    pt = P.ps[pb][:cw, :T]
                    for dc in range(8):
                        fw.mm(pt, winb[:, dc, c0:c0 + cw], hm[:, dc, :T], start=(dc == 0), stop=(dc == 7),
                              reads=wkeys + hk if dc in (0, 7) else (), writes=[('ps', pb)])
                    o = ob[:cw, oi % 4, :T]
                    ok = ('p_o', oi % 4)
                    oi += 1
                    if ci % 2 == 0:
                        fw.cp('act', o, pt, reads=[('ps', pb)], writes=[ok])
                    else:
                        fw.cp('dve', o, pt, reads=[('ps', pb)], writes=[ok])
                    fw.dma('sp', S.pl[seq][c0:c0 + cw, tb * T:(tb + 1) * T], o, reads=[ok], writes=[('pl', seq, ci, tb)])
        fw.barrier()


def phase_final(P, I, S):
    nc, fw = P.nc, P.fw
    with ExitStack() as es:
        s0 = es.enter_context(P.sb('f_s0', [128, 8, 512], F32))
        s1 = es.enter_context(P.sb('f_s1', [128, 8, 512], F32))
        sq = es.enter_context(P.sb('f_sq', [128, 8, 512], F32))
        rstd = es.enter_context(P.sb('f_rstd', [128, 512], F32))
        o0 = es.enter_context(P.sb('f_o0', [128, 8, 512], F32))
        o1 = es.enter_context(P.sb('f_o1', [128, 8, 512], F32))
        xb = [s0, s1]
        ob = [o0, o1]
        it = 0
        for b in range(2):
            xv = S.res[(b, 'l')].rearrange("(dc p) t -> p dc t", p=128)
            ov = S.outT[b].rearrange("(dc p) t -> p dc t", p=128)
            for tb in range(L // 512):
                xt = xb[it % 2]; xk = 'f_s%d' % (it % 2)
                o = ob[it % 2]; ok = 'f_o%d' % (it % 2)
                it += 1
                fw.dma('sp', xt[:], xv[:, :, tb * 512:(tb + 1) * 512], writes=[xk])
                hk = normmod(P, S, xt, xk, 512, S.gfin, S.zero8, o, ok + 'h', sq, rstd, ('ps', 0), P.ps[0])
                fw.dma('sp', ov[:, :, tb * 512:(tb + 1) * 512], o[:], reads=hk, writes=[('outT', b, tb)])
        fw.barrier()


def prep_shared(inp):
    f = lambda a: np.ascontiguousarray(np.asarray(a, np.float32))
    sh = {}
    sh['w_ada'] = f(inp['w_ada'])
    sh['b_adaT'] = np.stack([_pc(inp['b_ada'][l], 48) for l in range(NLAYER)])
    sh['gn1'] = np.stack([_pc(inp['g_norm1'][l], 8) for l in range(NLAYER)])
    sh['gn2'] = np.stack([_pc(inp['g_norm2'][l], 8) for l in range(NLAYER)])
    sh['gfin'] = _pc(inp['g_final'], 8)
    sh['w_in'] = f(inp['w_in'])
    hy = []
    for l in range(NLAYER):
        m = np.concatenate([np.asarray(inp['hy_conv_w'][l], np.float32), np.asarray(inp['hy_conv_b'][l], np.float32)[None]], 0)
        hy.append(np.ascontiguousarray(m.reshape(4, 6, 128).transpose(2, 1, 0)))
    sh['hycw'] = np.stack(hy)
    sh['hfw1'] = f(inp['hf_w1']); sh['hfw2'] = f(inp['hf_w2']); sh['hfw3'] = f(inp['hf_w3'])
    sh['hfv'] = np.ascontiguousarray(np.stack([inp['hf_b1'], inp['hf_b2'], inp['hf_freq']], axis=-1).astype(np.float32))
    sh['hybias'] = np.stack([_pc(inp['hy_bias'][l], 2) for l in range(NLAYER)])
    ss = []
    for l in range(NLAYER):
        m = np.concatenate([np.asarray(inp['ssd_conv_w'][l], np.float32), np.asarray(inp['ssd_conv_b'][l], np.float32)[None]], 0)
        ss.append(np.ascontiguousarray(m.reshape(4, 8, 128).transpose(2, 1, 0)))
    sh['sscw'] = np.stack(ss)
    sh['dtb'] = f(np.asarray(inp['ssd_dt_bias']).reshape(NLAYER, 16, 1))
    sh['alog'] = f(np.asarray(inp['ssd_a_log']).reshape(NLAYER, 1, 16))
    sh['dskip'] = f(np.asarray(inp['ssd_d']).reshape(NLAYER, 1, 8))
    sh['sng'] = np.stack([_pc(inp['ssd_norm_g'][l], 4) for l in range(NLAYER)])
    sh['w_out'] = f(inp['w_out'])
    sh['wq'] = f(inp['peer_wq'])
    sh['k1T'] = np.ascontiguousarray(np.asarray(inp['peer_k1'], np.float32).transpose(0, 2, 1))
    sh['k2T'] = np.ascontiguousarray(np.asarray(inp['peer_k2'], np.float32).transpose(0, 2, 1))
    sh['uT'] = np.ascontiguousarray(np.asarray(inp['peer_u'], np.float32).transpose(0, 2, 1))
    sh['v'] = f(inp['peer_v'])
    for k, a in _consts().items():
        sh['c_' + k] = a
    return sh


def prep_core(inp, core):
    m = {}
    x = np.asarray(inp['x'], np.float32)[2 * core:2 * core + 2]
    cx = np.asarray(inp['ctx'], np.float32)[2 * core:2 * core + 2]
    m['xT'] = np.ascontiguousarray(x.transpose(0, 2, 1))
    m['ctxT'] = np.ascontiguousarray(cx.transpose(0, 2, 1))
    cv = np.stack([np.asarray(inp['c'], np.float32)[2 * core], np.asarray(inp['c'], np.float32)[2 * core + 1],
                   np.asarray(inp['c_ctx'], np.float32)], axis=-1)
    m['cT'] = np.ascontiguousarray(cv.reshape(8, 128, 3).transpose(1, 0, 2))
    return m


_PROG = None


def kernel(**inputs):
    global _PROG
    if _PROG is None:
        _PROG = build()
    P = _PROG
    sh = prep_shared(inputs)
    in_maps = []
    for core in range(8):
        m = dict(sh)
        m.update(prep_core(inputs, core))
        in_maps.append({k: m[k] for k in P.inputs})
    res = run_bass_kernel_spmd(P.nc, in_maps, core_ids=list(range(8)))
    outs = [np.asarray(r['outT']).transpose(0, 2, 1) for r in res.results]
    return np.ascontiguousarray(np.concatenate(outs, axis=0).astype(np.float32))


def phase_fn(P, I, S, l, tag):
    nc, fw = P.nc, P.fw
    Lq = L if tag == 'l' else LC
    nsc = Lq // 128
    TB = min(512, Lq)
    ntb = Lq // TB
    with ExitStack() as es:
        ut = es.enter_context(P.sb('fn_ut', [128, 4, Lq], F32))
        cbd = es.enter_context(P.sb('fn_cbd', [128, 128], F32))
        sbd = es.enter_context(P.sb('fn_sbd', [128, 128], F32))
        atok = es.enter_context(P.sb('fn_a', [128, nsc, 512], BF16))
        btok = es.enter_context(P.sb('fn_b', [128, nsc, 512], BF16))
        cl = es.enter_context(P.sb('fn_cl', [128, nsc, TB], BF16))
        sl = es.enter_context(P.sb('fn_sl', [128, nsc, TB], BF16))
        ob = es.enter_context(P.sb('fn_o', [128, 2, TB], BF16))
        fw.dma('sp', cbd[:], I['cbd'], writes=['fn_cbd'])
        fw.dma('sp', sbd[:], I['sbd'], writes=['fn_sbd'])
        for b in range(2):
            for ch in range(2):
                fw.dma('sp', ut[:, b * 2 + ch, :], S.pl[(b, tag)][OFF_FN + ch * 128:OFF_FN + (ch + 1) * 128, :], writes=[('fn_ut', b * 2 + ch)])
        utk = [('fn_ut', m) for m in range(4)]
        for tc in range(nsc):
            pa, pb = 2 * (tc % 2), 2 * (tc % 2) + 1
            for m in range(4):
                fw.mm(P.ps[pa][:, m * 128:(m + 1) * 128], ut[:, m, tc * 128:(tc + 1) * 128], cbd[:], reads=utk + ['fn_cbd'], writes=[('ps', pa)])
                fw.mm(P.ps[pb][:, m * 128:(m + 1) * 128], ut[:, m, tc * 128:(tc + 1) * 128], sbd[:], reads=utk + ['fn_sbd'], writes=[('ps', pb)])
            fw.cp('act', atok[:, tc, :], P.ps[pa][:, :], reads=[('ps', pa)], writes=[('fn_a', tc)])
            fw.cp('dve', btok[:, tc, :], P.ps[pb][:, :], reads=[('ps', pb)], writes=[('fn_b', tc)])
        ak = [('fn_a', tc) for tc in range(nsc)]
        bk = [('fn_b', tc) for tc in range(nsc)]
        oi = 0
        for tb in range(ntb):
            fw.dma('sp', cl[:], I['cl_' + tag][tb], writes=['fn_cl'])
            fw.dma('sp', sl[:], I['sl_' + tag][tb], writes=['fn_sl'])
            for m in range(4):
                b, ch = m // 2, m % 2
                pi = 4 + m % 2
                pt = P.ps[pi][:, :TB]
                for tc in range(nsc):
                    fw.mm(pt, atok[:, tc, m * 128:(m + 1) * 128], cl[:, tc, :], start=(tc == 0), stop=False,
                          reads=ak + ['fn_cl'] if tc in (0, nsc - 1) else (), writes=[('ps', pi)])
                for tc in range(nsc):
                    fw.mm(pt, btok[:, tc, m * 128:(m + 1) * 128], sl[:, tc, :], start=False, stop=(tc == nsc - 1),
                          reads=bk + ['fn_sl'] if tc in (0, nsc - 1) else (), writes=[('ps', pi)])
                o = ob[:, oi % 2, :]
                ok = ('fn_o', oi % 2)
                oi += 1
                fw.cp('act' if m % 2 == 0 else 'dve', o, pt, reads=[('ps', pi)], writes=[ok])
                fw.dma('sp', S.mix[(b, tag)][768 + ch * 128:768 + (ch + 1) * 128, tb * TB:(tb + 1) * TB], o, reads=[ok], writes=[('mixfn', b, ch, tb)])
        fw.barrier()


def phase_out(P, I, S, l, src, ctx_out):
    nc, fw = P.nc, P.fw
    with ExitStack() as es:
        wout = es.enter_context(P.sb('o_w', [128, 8, 1024], BF16))
        s0 = es.enter_context(P.sb('o_s0', [128, 8, 512], F32))
        s1 = es.enter_context(P.sb('o_s1', [128, 8, 512], F32))
        m0 = es.enter_context(P.sb('o_m0', [128, 8, 512], BF16))
        m1 = es.enter_context(P.sb('o_m1', [128, 8, 512], BF16))
        x0 = es.enter_context(P.sb('o_x0', [128, 8, 512], F32))
        x1 = es.enter_context(P.sb('o_x1', [128, 8, 512], F32))
        wkeys = load_cast_weight(P, wout, 'o_w', I['w_out'][l].rearrange("(dc p) n -> p dc n", p=128), 1024, [s0, s1], ['o_s0', 'o_s1'])
        xb, mb, ob = [s0, s1], [m0, m1], [x0, x1]
        it = 0
        for seq in seqs_of(ctx_out):
            Lq = seq_len(seq)
            T = min(512, Lq)
            col = seq_col(seq)
            xv = src(l, seq).rearrange("(dc p) t -> p dc t", p=128)
            mv = S.mix[seq].rearrange("(dc p) t -> p dc t", p=128)
            rv = S.res[seq].rearrange("(dc p) t -> p dc t", p=128)
            for tb in range(Lq // T):
                k = it % 2
                it += 1
                xt, mx, xo = xb[k], mb[k], ob[k]
                fw.dma('sp', xt[:, :, :T], xv[:, :, tb * T:(tb + 1) * T], writes=['o_s%d' % k])
                fw.dma('sp', mx[:, :, :T], mv[:, :, tb * T:(tb + 1) * T], writes=['o_m%d' % k])
                for dch in range(8):
                    pi = dch % 4
                    pt = P.ps[pi][:, :T]
                    for cc in range(8):
                        fw.mm(pt, wout[:, cc, dch * 128:(dch + 1) * 128], mx[:, cc, :T], start=(cc == 0), stop=(cc == 7),
                              reads=wkeys + ['o_m%d' % k] if cc in (0, 7) else (), writes=[('ps', pi)])
                    fw.stt('dve', xo[:, dch, :T], pt, S.modT[:, 16 + dch, col:col + 1], xt[:, dch, :T], ALU.mult, ALU.add,
                           reads=[('ps', pi), 'o_s%d' % k, 'modT'], writes=[('o_x%d' % k, dch)])
                fw.dma('sp', rv[:, :, tb * T:(tb + 1) * T], xo[:, :, :T], reads=[('o_x%d' % k, d8) for d8 in range(8)], writes=[('resw', seq, tb)])
        fw.barrier()


def phase_filt(P, I, S, l, tag):
    nc, fw = P.nc, P.fw
    Lq = L if tag == 'l' else LC
    li = 0 if tag == 'l' else 1
    nsc = Lq // 128
    nfc = (Lq + 1 + 127) // 128
    T = min(512, Lq)
    with ExitStack() as es:
        zT = es.enter_context(P.sb('fl_z', [33, Lq], F32))
        w1 = es.enter_context(P.sb('fl_w1', [33, 64], F32))
        w2 = es.enter_context(P.sb('fl_w2', [64, 64], F32))
        w3 = es.enter_context(P.sb('fl_w3', [64, 512], F32))
        hv = es.enter_context(P.sb('fl_hv', [64, 3], F32))
        fb = es.enter_context(P.sb('fl_fb', [64, 2], F32))
        h1 = es.enter_context(P.sb('fl_h1', [64, Lq], F32))
        h2 = es.enter_context(P.sb('fl_h2', [64, Lq], F32))
        win = es.enter_context(P.sb('fl_win', [128, 2, nsc, 256], F32))
        pm = es.enter_context(P.sb('fl_pm', [128, nsc, 256], BF16))
        mmn = es.enter_context(P.sb('fl_mm', [128, nsc, 256], BF16))
        acc = es.enter_context(P.sb('fl_acc', [128, 256], F32))
        t1 = es.enter_context(P.sb('fl_t1', [128, 2, 256], F32))
        t2 = es.enter_context(P.sb('fl_t2', [128, 2, 256], F32))
        tmp = es.enter_context(P.sb('fl_tmp', [64, 512], F32))
        tmpk = es.enter_context(P.sb('fl_tmpk', [64, 512], F32))
        cf = es.enter_context(P.sb('fl_cf', [128, 2, nsc, 128], BF16))
        sf = es.enter_context(P.sb('fl_sf', [128, 2, nsc, 128], BF16))
        ko = es.enter_context(P.sb('fl_ko', [128, 2, 2, 256], F32))
        ntmp = es.enter_context(P.sb('fl_n', [128, 2], F32))
        fw.dma('sp', zT[:], I['zT_' + tag], writes=['fl_z'])
        fw.dma('sp', w1[:], I['hfw1'][l], writes=['fl_w1'])
        fw.dma('sp', w2[:], I['hfw2'][l], writes=['fl_w2'])
        fw.dma('sp', w3[:], I['hfw3'][l], writes=['fl_w3'])
        fw.dma('sp', hv[:], I['hfv'][l], writes=['fl_hv'])
        for v in range(2):
            fw.dma('sp', win[:, v, :, :], I['win_' + tag][v], writes=[('fl_win', v)])
        fw.ts('dve', fb[:], hv[:, 0:2], hv[:, 2:3], None, ALU.mult, reads=['fl_hv'], writes=['fl_fb'])
        fw.memset('pool', acc[:], 0.0, writes=['fl_acc'])

        def sin_layer(dst, dk, w, wk, K, srcT, sk, col):
            for blk in range(Lq // T):
                pi = blk % 2
                pt = P.ps[pi][:64, :T]
                fw.mm(pt, w[:K, :64], srcT[:K, blk * T:(blk + 1) * T], reads=[wk, sk], writes=[('ps', pi)])
                fw.ts('dve', tmp[:, :T], pt, hv[:, 2:3], fb[:, col:col + 1], ALU.mult, ALU.add, reads=[('ps', pi), 'fl_hv', 'fl_fb'], writes=['fl_tmp'])
                MAGIC = 12582912.0
                fw.ts('dve', tmpk[:, :T], tmp[:, :T], 1.0 / (2.0 * PI), MAGIC, ALU.mult, ALU.add, reads=['fl_tmp'], writes=['fl_tmpk'])
                fw.ts('dve', tmpk[:, :T], tmpk[:, :T], MAGIC, None, ALU.subtract, reads=['fl_tmpk'], writes=['fl_tmpk'])
                fw.stt('dve', tmp[:, :T], tmpk[:, :T], -2.0 * PI, tmp[:, :T], ALU.mult, ALU.add, reads=['fl_tmpk', 'fl_tmp'], writes=['fl_tmp'])
                fw.act(dst[:, blk * T:(blk + 1) * T], tmp[:, :T], AF.Sin, reads=['fl_tmp'], writes=[dk])

        sin_layer(h1, 'fl_h1', w1, 'fl_w1', 33, zT, 'fl_z', 0)
        sin_layer(h2, 'fl_h2', w2, 'fl_w2', 64, h1, 'fl_h1', 1)
        for sc in range(nsc):
            pi = 2 + sc % 2
            pt = P.ps[pi]
            fw.mm(pt[:, :], h2[:64, sc * 128:(sc + 1) * 128], w3[:64, :], reads=['fl_h2', 'fl_w3'], writes=[('ps', pi)])
            fw.tt('dve', t1[:], pt[:, :].rearrange("p (v c) -> p v c", v=2), win[:, :, sc, :], ALU.mult,
                  reads=[('ps', pi), ('fl_win', 0), ('fl_win', 1)], writes=['fl_t1'])
            fw.tt('pool', pm[:, sc, :], t1[:, 0, :], t1[:, 1, :], ALU.add, reads=['fl_t1'], writes=[('fl_pm', sc)])
            fw.tt('pool', mmn[:, sc, :], t1[:, 0, :], t1[:, 1, :], ALU.subtract, reads=['fl_t1'], writes=[('fl_mm', sc)])
            fw.act(t2[:], t1[:], AF.Square, reads=['fl_t1'], writes=['fl_t2'])
            fw.tt('pool', acc[:], acc[:], t2[:, 0, :], ALU.add, reads=['fl_t2'], writes=['fl_acc'])
            fw.tt('pool', acc[:], acc[:], t2[:, 1, :], ALU.add, reads=['fl_t2'], writes=['fl_acc'])
        for ch in range(2):
            fw.mm(P.ps[0][:, ch:ch + 1], acc[:, ch * 128:(ch + 1) * 128], S.ones[:, 0:1], reads=['fl_acc', 'ones'], writes=[('ps', 0)])
        fw.act(ntmp[:], P.ps[0][:, 0:2], AF.Sqrt, bias=S.epsc[:, 0:1], reads=[('ps', 0), 'epsc'], writes=['fl_n'])
        fw.op('dve', lambda e: e.reciprocal(S.hyn[:, :, li], ntmp[:]), reads=['fl_n'], writes=[('hyn', li)])
        pmk = [('fl_pm', sc) for sc in range(nsc)]
        mmk = [('fl_mm', sc) for sc in range(nsc)]
        for fc in range(nfc):
            fsz = 128 if fc < nfc - 1 else 1
            k = fc % 2
            fw.dma('sp', cf[:, k, :, :], I['cf_' + tag][fc], writes=[('fl_cf', k)])
            fw.dma('sp', sf[:, k, :, :], I['sf_' + tag][fc], writes=[('fl_sf', k)])
            pa, pb = 4 + 2 * k, 5 + 2 * k
            for sc in range(nsc):
                fw.mm(P.ps[pa][:fsz, :256], cf[:, k, sc, :fsz], pm[:, sc, :], start=(sc == 0), stop=(sc == nsc - 1),
                      reads=pmk + [('fl_cf', k)] if sc in (0, nsc - 1) else (), writes=[('ps', pa)])
            for sc in range(nsc):
                fw.mm(P.ps[pb][:fsz, :256], sf[:, k, sc, :fsz], mmn[:, sc, :], start=(sc == 0), stop=(sc == nsc - 1),
                      reads=mmk + [('fl_sf', k)] if sc in (0, nsc - 1) else (), writes=[('ps', pb)])
            fw.cp('act', ko[:fsz, k, 0, :], P.ps[pa][:fsz, :256], reads=[('ps', pa)], writes=[('fl_ko', k, 0)])
            fw.cp('dve', ko[:fsz, k, 1, :], P.ps[pb][:fsz, :256], reads=[('ps', pb)], writes=[('fl_ko', k, 1)])
            for v in range(2):
                fw.dma('sp', S.khat[tag][v, fc * 128:fc * 128 + fsz, :], ko[:fsz, k, v, :], reads=[('fl_ko', k, v)], writes=[('khat', tag, v, fc)])
        fw.barrier()


def phase_hy(P, I, S, l, tag):
    nc, fw = P.nc, P.fw
    Lq = L if tag == 'l' else LC
    li = 0 if tag == 'l' else 1
    nsc = Lq // 128
    nfc = (Lq + 1 + 127) // 128
    TB = min(512, Lq)
    ntb = Lq // TB
    with ExitStack() as es:
        u = es.enter_context(P.sb('hy_u', [128, 4, Lq], F32))
        x1c = es.enter_context(P.sb('hy_x1', [128, 4, Lq], BF16))
        utok = es.enter_context(P.sb('hy_ut', [128, nsc, 512], BF16))
        wre = es.enter_context(P.sb('hy_wre', [128, nfc, 512], BF16))
        wim = es.enter_context(P.sb('hy_wim', [128, nfc, 512], BF16))
        cw = es.enter_context(P.sb('hy_cw', [128, 6, 4], F32))
        hb = es.enter_context(P.sb('hy_hb', [128, 2], F32))
        fw.dma('sp', cw[:], I['hycw'][l], writes=['hy_cw'])
        fw.dma('sp', hb[:], I['hybias'][l], writes=['hy_hb'])
        with ExitStack() as es2:
            raw = es2.enter_context(P.sb('hy_raw', [128, 3, Lq + 2], F32))
            cv = [es2.enter_context(P.sb('hy_cv%d' % k, [128, Lq], F32)) for k in range(3)]
            fw.memset('pool', raw[:, :, 0:1], 0.0, writes=['hy_raw_h0'])
            fw.memset('pool', raw[:, :, Lq + 1:Lq + 2], 0.0, writes=['hy_raw_h1'])
            for m in range(4):
                b, ch = m // 2, m % 2
                for k in range(3):
                    r0 = k * 256 + ch * 128
                    fw.dma('sp', raw[:, k, 1:Lq + 1], S.pl[(b, tag)][r0:r0 + 128, :], writes=[('hy_raw', k)])
                    ci = 2 * k + ch
                    rk = [('hy_raw', k), 'hy_raw_h0', 'hy_raw_h1', 'hy_cw']
                    fw.ts('dve', cv[k][:], raw[:, k, 0:Lq], cw[:, ci, 0:1], cw[:, ci, 3:4], ALU.mult, ALU.add, reads=rk, writes=[('hy_cv', k)])
                    fw.stt('dve', cv[k][:], raw[:, k, 1:Lq + 1], cw[:, ci, 1:2], cv[k][:], ALU.mult, ALU.add, reads=rk, writes=[('hy_cv', k)])
                    fw.stt('dve', cv[k][:], raw[:, k, 2:Lq + 2], cw[:, ci, 2:3], cv[k][:], ALU.mult, ALU.add, reads=rk, writes=[('hy_cv', k)])
                fw.tt('pool', u[:, m, :], cv[2][:], cv[0][:], ALU.mult, reads=[('hy_cv', 2), ('hy_cv', 0)], writes=[('hy_u', m)])
                fw.cp('act', x1c[:, m, :], cv[1][:], reads=[('hy_cv', 1)], writes=[('hy_x1', m)])
            uk = [('hy_u', m) for m in range(4)]
            for sc in range(nsc):
                pi = sc % 2
                for m in range(4):
                    fw.tr(P.ps[pi][:, m * 128:(m + 1) * 128], u[:, m, sc * 128:(sc + 1) * 128], S.ident[:], reads=uk + ['ident'], writes=[('ps', pi)])
                fw.cp('act' if sc % 2 == 0 else 'dve', utok[:, sc, :], P.ps[pi][:, :], reads=[('ps', pi)], writes=[('hy_ut', sc)])
            fw.barrier()
        with ExitStack() as es2:
            cf = es2.enter_context(P.sb('hy_cf', [128, 2, nsc, 128], BF16))
            sf = es2.enter_context(P.sb('hy_sf', [128, 2, nsc, 128], BF16))
            kk = es2.enter_context(P.sb('hy_kk', [128, 2, 2, 256], F32))
            tq = [es2.enter_context(P.sb('hy_t%d' % q, [128, 2, 256], F32)) for q in range(4)]
            for fc in range(nfc):
                fsz = 128 if fc < nfc - 1 else 1
                k = fc % 2
                fw.dma('sp', cf[:, k, :, :], I['cf_' + tag][fc], writes=[('hy_cf', k)])
                fw.dma('sp', sf[:, k, :, :], I['sf_' + tag][fc], writes=[('hy_sf', k)])
                for v in range(2):
                    fw.dma('sp', kk[:fsz, k, v, :], S.khat[tag][v, fc * 128:fc * 128 + fsz, :], writes=[('hy_kk', k, v)])
                pa, pb = 2 + 2 * k, 3 + 2 * k
                for sc in range(nsc):
                    fw.mm(P.ps[pa][:fsz, :], cf[:, k, sc, :fsz], utok[:, sc, :], start=(sc == 0), stop=(sc == nsc - 1),
                          reads=[('hy_cf', k)] if sc in (0, nsc - 1) else (), writes=[('ps', pa)])
                for sc in range(nsc):
                    fw.mm(P.ps[pb][:fsz, :], sf[:, k, sc, :fsz], utok[:, sc, :], start=(sc == 0), stop=(sc == nsc - 1),
                          reads=[('hy_sf', k)] if sc in (0, nsc - 1) else (), writes=[('ps', pb)])
                ure = P.ps[pa][:fsz, :].rearrange("p (b c) -> p b c", b=2)
                uim = P.ps[pb][:fsz, :].rearrange("p (b c) -> p b c", b=2)
                kre = kk[:fsz, k, 0, :].unsqueeze(1).to_broadcast([fsz, 2, 256])
                kim = kk[:fsz, k, 1, :].unsqueeze(1).to_broadcast([fsz, 2, 256])
                kr = [('hy_kk', k, 0), ('hy_kk', k, 1)]
                fw.tt('dve', tq[0][:fsz], ure, kre, ALU.mult, reads=[('ps', pa)] + kr, writes=[('hy_t', 0)])
                fw.tt('dve', tq[1][:fsz], uim, kim, ALU.mult, reads=[('ps', pb)] + kr, writes=[('hy_t', 1)])
                fw.tt('dve', tq[2][:fsz], ure, kim, ALU.mult, reads=[('ps', pa)] + kr, writes=[('hy_t', 2)])
                fw.tt('dve', tq[3][:fsz], uim, kre, ALU.mult, reads=[('ps', pb)] + kr, writes=[('hy_t', 3)])
                fw.tt('pool', wre[:fsz, fc, :].rearrange("p (b c) -> p b c", b=2), tq[0][:fsz], tq[1][:fsz], ALU.subtract,
                      reads=[('hy_t', 0), ('hy_t', 1)], writes=[('hy_wre', fc)])
                fw.tt('pool', wim[:fsz, fc, :].rearrange("p (b c) -> p b c", b=2), tq[2][:fsz], tq[3][:fsz], ALU.add,
                      reads=[('hy_t', 2), ('hy_t', 3)], writes=[('hy_wim', fc)])
            fw.barrier()
        with ExitStack() as es2:
            ci_t = es2.enter_context(P.sb('hy_ci', [128, nfc, TB], BF16))
            si_t = es2.enter_context(P.sb('hy_si', [128, nfc, TB], BF16))
            at = es2.enter_context(P.sb('hy_at', [128, 2, TB], F32))
            ob = es2.enter_context(P.sb('hy_o', [128, 2, TB], BF16))
            oi = 0
            for tb in range(ntb):
                fw.dma('sp', ci_t[:], I['ci_' + tag][tb], writes=['hy_ci'])
                fw.dma('sp', si_t[:], I['si_' + tag][tb], writes=['hy_si'])
                for m in range(4):
                    b, ch = m // 2, m % 2
                    pi = m % 2
                    pt = P.ps[pi][:, :TB]
                    for fc in range(nfc):
                        fsz = 128 if fc < nfc - 1 else 1
                        fw.mm(pt, wre[:fsz, fc, m * 128:(m + 1) * 128], ci_t[:fsz, fc, :], start=(fc == 0), stop=False,
                              reads=['hy_ci'] if fc in (0, nfc - 1) else (), writes=[('ps', pi)])
                    for fc in range(nfc - 1):
                        fw.mm(pt, wim[:, fc, m * 128:(m + 1) * 128], si_t[:, fc, :], start=False, stop=(fc == nfc - 2),
                              reads=['hy_si'] if fc in (0, nfc - 2) else (), writes=[('ps', pi)])
                    a = at[:, oi % 2, :]
                    o = ob[:, oi % 2, :]
                    ak, ok = ('hy_at', oi % 2), ('hy_o', oi % 2)
                    oi += 1
                    fw.ts('dve', a, pt, S.hyn[:, ch, li:li + 1], None, ALU.mult, reads=[('ps', pi)], writes=[ak])
                    fw.stt('dve', a, u[:, m, tb * TB:(tb + 1) * TB], hb[:, ch:ch + 1], a, ALU.mult, ALU.add, reads=['hy_hb'], writes=[ak])
                    fw.tt('pool', o, a, x1c[:, m, tb * TB:(tb + 1) * TB], ALU.mult, reads=[ak], writes=[ok])
                    fw.dma('sp', S.mix[(b, tag)][ch * 128:(ch + 1) * 128, tb * TB:(tb + 1) * TB], o, reads=[ok], writes=[('mixhy', b, ch, tb)])
            fw.barrier()


def phase_ssd(P, I, S, l, ctx_out):
    nc, fw = P.nc, P.fw
    ps = P.ps
    with ExitStack() as es:
        E = lambda n, sh, dt=F32: es.enter_context(P.sb(n, sh, dt))
        cw = E('sd_cw', [128, 8, 4]); dtb = E('sd_dtb', [16, 1]); arow = E('sd_arow', [128, 16]); dskb = E('sd_dsk', [128, 8])
        sng = E('sd_sng', [128, 4])
        Hs = [E('sd_H%d' % d, [128, 512]) for d in range(2)]
        xtok = E('sd_xtok', [128, 16, 512]); btok = E('sd_btok', [128, 16, 256]); cbt = E('sd_cbt', [128, 16, 2, 128])
        ct = E('sd_ct', [128, 2, L]); dttok = E('sd_dttok', [128, 16, 16]); ytok = E('sd_ytok', [128, 16, 512])
        raw = E('sd_raw', [128, 8, 514]); xa = E('sd_xa', [128, 8, 512]); dtT = E('sd_dtT', [16, L])
        dtA = E('sd_dtA', [128, 16]); ac16 = E('sd_acum', [128, 16]); acum = ac16[:, 0:8]; t8 = E('sd_t8', [128, 8]); dend = E('sd_dend', [128, 8])
        cdec = E('sd_cdec', [128, 8]); eac = E('sd_eac', [128, 8]); xdt = E('sd_xdt', [128, 8, 64]); xdte = E('sd_xdte', [128, 8, 64])
        segt8 = E('sd_seg8', [128, 8, 128]); dtAb8 = E('sd_dtAb8', [128, 8, 128]); mt8 = E('sd_mt8', [128, 8, 128])
        yo = E('sd_yo', [128, 8, 64]); tmpy = E('sd_tmpy', [128, 512]); hsc = E('sd_hsc', [128, 8, 64])
        zt = E('sd_zt', [128, 4, 128]); yg = E('sd_yg', [128, 4, 128]); sqg = E('sd_sqg', [128, 4, 128]); rs = E('sd_rs', [128, 2, 128])
        og = E('sd_og', [128, 4, 128], BF16)
        fw.dma('sp', cw[:], I['sscw'][l], writes=['sd_cw'])
        fw.dma('sp', dtb[:], I['dtb'][l], writes=['sd_dtb'])
        fw.dma('sp', arow[:], I['alog'][l].partition_broadcast(128), writes=['sd_arow'])
        fw.dma('sp', dskb[:], I['dskip'][l].partition_broadcast(128), writes=['sd_dsk'])
        fw.dma('sp', sng[:], I['sng'][l], writes=['sd_sng'])
        fw.act(arow[:], arow[:], AF.Exp, reads=['sd_arow'], writes=['sd_arow'])
        fw.ts('dve', arow[:], arow[:], -1.0, None, ALU.mult, reads=['sd_arow'], writes=['sd_arow'])
        tri = [S.uinc, S.linc]
        msk = [S.maskf, S.maskb]

        def stage_a(seq):
            Lq = seq_len(seq)
            T = min(512, Lq)
            plv = S.pl[seq][OFF_XBC:OFF_XBC + 1024, :].rearrange("(cc p) t -> p cc t", p=128)
            fw.dma('sp', dtT[:, :Lq], S.pl[seq][OFF_DT:OFF_DT + 16, :], writes=['sd_dtT'])
            fw.act(dtT[:, :Lq], dtT[:, :Lq], AF.Exp, bias=dtb[:, 0:1], reads=['sd_dtT', 'sd_dtb'], writes=['sd_dtT'])
            fw.ts('dve', dtT[:, :Lq], dtT[:, :Lq], 1.0, None, ALU.add, reads=['sd_dtT'], writes=['sd_dtT'])
            fw.act(dtT[:, :Lq], dtT[:, :Lq], AF.Ln, reads=['sd_dtT'], writes=['sd_dtT'])
            for tb in range(Lq // T):
                t0 = tb * T
                lo, hi = max(0, t0 - 1), min(Lq, t0 + T + 1)
                d0 = lo - (t0 - 1)
                if tb == 0:
                    fw.memset('pool', raw[:, :, 0:1], 0.0, writes=['sd_raw'])
                if tb == Lq // T - 1:
                    fw.memset('pool', raw[:, :, T + 1:T + 2], 0.0, writes=['sd_raw'])
                fw.dma('sp', raw[:, :, d0:d0 + hi - lo], plv[:, :, lo:hi], writes=['sd_raw'])
                for cc in range(8):
                    k = ('sd_xa', cc)
                    fw.ts('dve', xa[:, cc, :T], raw[:, cc, 0:T], cw[:, cc, 0:1], cw[:, cc, 3:4], ALU.mult, ALU.add, reads=['sd_raw', 'sd_cw'], writes=[k])
                    fw.stt('dve', xa[:, cc, :T], raw[:, cc, 1:T + 1], cw[:, cc, 1:2], xa[:, cc, :T], ALU.mult, ALU.add, reads=['sd_raw'], writes=[k])
                    fw.stt('dve', xa[:, cc, :T], raw[:, cc, 2:T + 2], cw[:, cc, 2:3], xa[:, cc, :T], ALU.mult, ALU.add, reads=['sd_raw'], writes=[k])
                    fw.act(xa[:, cc, :T], xa[:, cc, :T], AF.Silu, reads=[k], writes=[k])
                xk = [('sd_xa', cc) for cc in range(8)]
                fw.cp('pool', ct[:, :, t0:t0 + T], xa[:, 6:8, :T], reads=xk, writes=['sd_ct'])
                for j in range(T // 128):
                    c = tb * (T // 128) + j
                    sl = slice(j * 128, (j + 1) * 128)
                    for k in range(4):
                        fw.tr(ps[1][:, k * 128:(k + 1) * 128], xa[:, k, sl], S.ident[:], reads=xk, writes=[('ps', 1)])
                    fw.cp('act', xtok[:, c, :], ps[1][:, :], reads=[('ps', 1)], writes=[('sd_xtok', c)])
                    for g in range(2):
                        fw.tr(ps[2][:, g * 128:(g + 1) * 128], xa[:, 4 + g, sl], S.ident[:], reads=xk, writes=[('ps', 2)])
                        fw.mm(ps[2][:, 256 + g * 128:256 + (g + 1) * 128], xa[:, 4 + g, sl], xa[:, 6 + g, sl], reads=xk, writes=[('ps', 2)])
                    fw.cp('dve', btok[:, c, :], ps[2][:, 0:256], reads=[('ps', 2)], writes=[('sd_btok', c)])
                    fw.cp('dve', cbt[:, c, :, :], ps[2][:, 256:512].rearrange("p (g i) -> p g i", g=2), reads=[('ps', 2)], writes=[('sd_cbt', c)])
                    fw.tr(ps[3][:, 0:16], dtT[:16, t0 + j * 128:t0 + (j + 1) * 128], S.ident[:16, :16], reads=['sd_dtT'], writes=[('ps', 3)])
                    fw.cp('dve', dttok[:, c, :], ps[3][:, 0:16], reads=[('ps', 3)], writes=[('sd_dttok', c)])

        def scan(seq, d, with_output, final_out):
            Lq = seq_len(seq)
            ncq = Lq // 128
            H = Hs[d]
            hk = 'sd_H%d' % d
            d8 = slice(d * 8, (d + 1) * 8)
            order = range(ncq) if d == 0 else range(ncq - 1, -1, -1)
            for c in order:
                fw.tt('pool', dtA[:], dttok[:, c, :], arow[:], ALU.mult, reads=[('sd_dttok', c), 'sd_arow'], writes=['sd_dtA'])
                fw.mm(ps[0][:, 0:8], tri[d][:], dtA[:, d8], reads=['sd_dtA'], writes=[('ps', 0)])
                fw.mm(ps[0][:, 8:16], S.ones[:], dtA[:, d8], reads=['sd_dtA'], writes=[('ps', 0)])
                fw.cp('dve', ac16[:], ps[0][:, 0:16], reads=[('ps', 0)], writes=['sd_acum'])
                fw.tt('pool', t8[:], ac16[:, 8:16], acum, ALU.subtract, reads=['sd_acum'], writes=['sd_t8'])
                fw.act(dend[:], t8[:], AF.Exp, reads=['sd_t8'], writes=['sd_dend'])
                fw.act(cdec[:], ac16[:, 8:16], AF.Exp, reads=['sd_acum'], writes=['sd_cdec'])
                fw.tt('pool', xdt[:], xtok[:, c, :].rearrange("p (h q) -> p h q", h=8),
                      dttok[:, c, d8].unsqueeze(2).to_broadcast([128, 8, 64]), ALU.mult, reads=[('sd_xtok', c), ('sd_dttok', c)], writes=['sd_xdt'])
                if with_output:
                    fw.act(eac[:], acum,  AF.Exp, reads=['sd_acum'], writes=['sd_eac'])
                    fw.cp('dve', dtAb8[:], dtA[:, d8].unsqueeze(2).to_broadcast([128, 8, 128]), reads=['sd_dtA'], writes=['sd_dtAb'])
                    for hd in range(8):
                        pa = 1 + hd // 4
                        fw.mm(ps[pa][:, (hd % 4) * 128:(hd % 4 + 1) * 128], dtAb8[:, hd, :], tri[d][:], reads=['sd_dtAb'], writes=[('ps', pa)])
                    for hf in range(2):
                        fw.tt('dve', segt8[:, 4 * hf:4 * hf + 4, :], ps[1 + hf][:, :].rearrange("p (h i) -> p h i", h=4),
                              acum[:, 4 * hf:4 * hf + 4].unsqueeze(2).to_broadcast([128, 4, 128]), ALU.subtract,
                              reads=[('ps', 1 + hf), 'sd_acum'], writes=['sd_seg8'])
                    fw.tt('dve', segt8[:], segt8[:], msk[d][:, :].unsqueeze(1).to_broadcast([128, 8, 128]), ALU.min,
                          reads=['sd_seg8'], writes=['sd_seg8'])
                    fw.act(segt8[:], segt8[:], AF.Exp, reads=['sd_seg8'], writes=['sd_seg8'])
                    fw.tt('dve', mt8[:].rearrange("p (g h) i -> p g h i", g=2), segt8[:].rearrange("p (g h) i -> p g h i", g=2),
                          cbt[:, c, :, :].unsqueeze(2).to_broadcast([128, 2, 4, 128]), ALU.mult, reads=['sd_seg8', ('sd_cbt', c)], writes=['sd_mt8'])
                    for hd in range(8):
                        fw.mm(ps[3][:, hd * 64:(hd + 1) * 64], mt8[:, hd, :], xdt[:, hd, :], reads=['sd_mt8', 'sd_xdt'], writes=[('ps', 3)])
                    for g in range(2):
                        fw.mm(ps[4][:, g * 256:(g + 1) * 256], ct[:, g, c * 128:(c + 1) * 128], H[:, g * 256:(g + 1) * 256],
                              reads=['sd_ct', hk], writes=[('ps', 4)])
                    fw.tt('dve', yo[:], ps[4][:, :].rearrange("p (h q) -> p h q", h=8), eac[:, :].unsqueeze(2).to_broadcast([128, 8, 64]),
                          ALU.mult, reads=[('ps', 4), 'sd_eac'], writes=['sd_yo'])
                    yflat = yo[:].rearrange("p h q -> p (h q)")
                    if d == 0:
                        fw.tt('dve', ytok[:, c, :], ps[3][:, :], yflat, ALU.add, reads=[('ps', 3), 'sd_yo'], writes=[('sd_ytok', c)])
                        fw.tt('pool', tmpy[:].rearrange("p (h q) -> p h q", h=8), xtok[:, c, :].rearrange("p (h q) -> p h q", h=8),
                              dskb[:, :].unsqueeze(2).to_broadcast([128, 8, 64]), ALU.mult, reads=[('sd_xtok', c), 'sd_dsk'], writes=['sd_tmpy'])
                        fw.tt('pool', ytok[:, c, :], ytok[:, c, :], tmpy[:], ALU.add, reads=['sd_tmpy'], writes=[('sd_ytok', c)])
                    else:
                        fw.tt('dve', tmpy[:], ps[3][:, :], yflat, ALU.add, reads=[('ps', 3), 'sd_yo'], writes=['sd_tmpy'])
                        fw.tt('pool', ytok[:, c, :], ytok[:, c, :], tmpy[:], ALU.add, reads=['sd_tmpy'], writes=[('sd_ytok', c)])
                fw.tt('pool', xdte[:], xdt[:], dend[:, :].unsqueeze(2).to_broadcast([128, 8, 64]), ALU.mult, reads=['sd_xdt', 'sd_dend'], writes=['sd_xdte'])
                for g in range(2):
                    fw.mm(ps[5][:, g * 256:(g + 1) * 256], btok[:, c, g * 128:(g + 1) * 128],
                          xdte[:, g * 4:(g + 1) * 4, :].rearrange("p h q -> p (h q)"), reads=[('sd_btok', c), 'sd_xdte'], writes=[('ps', 5)])
                fw.tt('pool', hsc[:], H[:].rearrange("p (h q) -> p h q", h=8), cdec[:, :].unsqueeze(2).to_broadcast([128, 8, 64]), ALU.mult,
                      reads=[hk, 'sd_cdec'], writes=['sd_hsc'])
                fw.tt('dve', H[:], hsc[:].rearrange("p h q -> p (h q)"), ps[5][:, :], ALU.add, reads=['sd_hsc', ('ps', 5)], writes=[hk])
                if final_out:
                    tk = slice(c * 128, (c + 1) * 128)
                    for k in range(4):
                        fw.tr(ps[6][:, k * 128:(k + 1) * 128], ytok[:, c, k * 128:(k + 1) * 128], S.ident[:], reads=[('sd_ytok', c)], writes=[('ps', 6)])
                    fw.dma('sp', zt[:], S.pl[seq][OFF_Z:OFF_Z + 512, tk].rearrange("(k p) t -> p k t", p=128), writes=['sd_zt'])
                    fw.act(zt[:], zt[:], AF.Silu, reads=['sd_zt'], writes=['sd_zt'])
                    fw.tt('dve', yg[:], ps[6][:, :].rearrange("p (k t) -> p k t", k=4), zt[:], ALU.mult, reads=[('ps', 6), 'sd_zt'], writes=['sd_yg'])
                    fw.act(sqg[:], yg[:], AF.Square, reads=['sd_yg'], writes=['sd_sqg'])
                    for g in range(2):
                        fw.mm(ps[7][:, g * 128:(g + 1) * 128], S.ones[:], sqg[:, 2 * g, :], start=True, stop=False, reads=['sd_sqg'], writes=[('ps', 7)])
                        fw.mm(ps[7][:, g * 128:(g + 1) * 128], S.ones[:], sqg[:, 2 * g + 1, :], start=False, stop=True, reads=['sd_sqg'], writes=[('ps', 7)])
                    fw.act(rs[:], ps[7][:, 0:256].rearrange("p (g t) -> p g t", g=2), AF.Sqrt, bias=S.epsc[:, 0:1], scale=1.0 / 256.0,
                           reads=[('ps', 7)], writes=['sd_rs'])
                    fw.op('dve', lambda e: e.reciprocal(rs[:], rs[:]), reads=['sd_rs'], writes=['sd_rs'])
                    for k in range(4):
                        fw.stt('dve', og[:, k, :], yg[:, k, :], sng[:, k:k + 1], rs[:, k // 2, :], ALU.mult, ALU.mult,
                               reads=['sd_yg', 'sd_rs', 'sd_sng'], writes=['sd_og'])
                    fw.dma('sp', S.mix[seq][256:768, tk].rearrange("(k p) t -> p k t", p=128), og[:], reads=['sd_og'], writes=[('mixssd', seq, c)])

        for b in range(2):
            for d in range(2):
                fw.memset('pool', Hs[d][:], 0.0, writes=['sd_H%d' % d])
            stage_a((b, 'c'))
            if SSD_MODE != 'a':
                scan((b, 'c'), 0, ctx_out, False)
                scan((b, 'c'), 1, ctx_out, ctx_out)
            stage_a((b, 'l'))
            if SSD_MODE != 'a':
                scan((b, 'l'), 0, True, False)
                scan((b, 'l'), 1, True, True)
        fw.barrier()


def phase_peer(P, I, S, l, ctx_out):
    nc, fw = P.nc, P.fw
    ps = P.ps
    T = 256
    NB = 4
    with ExitStack() as es2:
        cs = [es2.enter_context(P.sb('pe_cs%d' % i, [128, 2048], F32)) for i in range(4)]
        cbs = [es2.enter_context(P.sb('pe_cb%d' % i, [128, 2048], BF16)) for i in range(4)]
        uv = I['uT'][l].rearrange("(a p) e -> p a e", p=128)
        n = 0
        for blk in range(64):
            k = n % 4
            n += 1
            fw.dma('sp', cs[k][:].rearrange("p (a e) -> p a e", a=8), uv[:, :, blk * 256:(blk + 1) * 256], writes=['pe_cs%d' % k])
            fw.cp('dve' if k % 2 == 0 else 'pool', cbs[k][:], cs[k][:], reads=['pe_cs%d' % k], writes=['pe_cb%d' % k])
            fw.dma('act', S.ubf[blk].rearrange("p a e -> p (a e)"), cbs[k][:], reads=['pe_cb%d' % k], writes=[('ubf', blk)])
        for blk in range(64):
            k = n % 4
            n += 1
            fw.dma('sp', cs[k][:].rearrange("p (ii d) -> p ii d", ii=2),
                   I['v'][l][blk * 256:(blk + 1) * 256, :].rearrange("(ii j) d -> j ii d", j=128), writes=['pe_cs%d' % k])
            fw.cp('dve' if k % 2 == 0 else 'pool', cbs[k][:], cs[k][:], reads=['pe_cs%d' % k], writes=['pe_cb%d' % k])
            fw.dma('act', S.vbf[blk].rearrange("p ii d -> p (ii d)"), cbs[k][:], reads=['pe_cb%d' % k], writes=[('vbf', blk)])
        fw.barrier()
    with ExitStack() as es:
        E = lambda n, sh, dt=F32: es.enter_context(P.sb(n, sh, dt))
        wq = E('pe_wq', [128, 8, 2048], BF16)
        st = [E('pe_s%d' % i, [128, 8, 256]) for i in range(2)]
        k1 = E('pe_k1', [128, 128], BF16); k2 = E('pe_k2', [128, 128], BF16)
        gbuf = E('pe_g', [128, 128, T], BF16)
        ub = E('pe_ub', [128, 2, 8, 256], BF16); vb = E('pe_vb', [128, 2, 2, 1024], BF16)
        h2s = [E('pe_h2%d' % i, [128, 8, T], BF16) for i in range(2)]; sq = E('pe_sq', [128, 8, T]); rstd = E('pe_rstd', [128, T])
        qT = E('pe_qT', [128, 16, T], BF16); s12 = E('pe_s12', [128, 16, 128]); v12s = [E('pe_v12%d' % i, [128, 16, 16]) for i in range(2)]
        wk = E('pe_wk', [128, 4, 128]); wk2 = E('pe_wk2', [128, 4, 256]); t16s = [E('pe_t16%d' % i, [128, 8, 16]) for i in range(2)]
        e16 = E('pe_e16', [128, 8, 16]); zz = E('pe_z', [128, 8]); mz = E('pe_mz', [128, 8])
        thr = E('pe_thr', [128, 8, 16]); bia = E('pe_bia', [128, 8, 16]); pc = E('pe_pc', [128, 3, T])
        Et = [E('pe_E%d' % i, [128, 256]) for i in range(4)]
        Wt = [E('pe_W%d' % i, [128, 128], BF16) for i in range(8)]
        Pt = [E('pe_P%d' % i, [128, 128], BF16) for i in range(8)]
        qr1 = E('pe_qr1', [128, 2, 4, 128], BF16)
        qr2 = E('pe_qr2', [128, 2, 4, 128], BF16)
        gst = [E('pe_gs%d' % i, [128, T]) for i in range(2)]
        At = [E('pe_A%d' % i, [128, T], BF16) for i in range(2)]
        xo = E('pe_xo', [128, 8, T])
        cand = sq
        wv = I['wq'][l].rearrange("(dc p) n -> p dc n", p=128)
        n = 0
        for blk in range(8):
            k = n % 2
            n += 1
            fw.dma('sp', st[k][:], wv[:, :, blk * 256:(blk + 1) * 256], writes=['pe_s%d' % k])
            fw.cp('dve' if k == 0 else 'pool', wq[:, :, blk * 256:(blk + 1) * 256], st[k][:], reads=['pe_s%d' % k], writes=[('pe_wq', blk)])
        for (kt, nm, key) in ((k1, 'k1T', 'pe_k1'), (k2, 'k2T', 'pe_k2')):
            k = n % 2
            n += 1
            fw.dma('sp', st[k][:, 0, 0:128], I[nm][l], writes=['pe_s%d' % k])
            fw.cp('dve', kt[:], st[k][:, 0, 0:128], reads=['pe_s%d' % k], writes=[key])
        fw.barrier()

        def load_tables(blk):
            bb = blk % 2
            fw.dma('sp', ub[:, bb, :, :], S.ubf[blk], writes=[('pe_ub', bb)])
            fw.dma('sp', vb[:, bb, :, :], S.vbf[blk], writes=[('pe_vb', bb)])

        qk = [('pe_qT', qc) for qc in range(16)]
        sqk = [('nm_sq', dc) for dc in range(8)]

        def v12k_of(tt):
            return [('pe_v12', tt, i) for i in range(16)] + [('pe_v12b', tt, i) for i in range(16)]

        def t16k_of(tt):
            return [('pe_t16', tt, i) for i in range(8)] + [('pe_t16b', tt, i) for i in range(8)]

        def part_A(seq, grp, gi):
            col = seq_col(seq)
            rv = S.res[seq].rearrange("(dc p) t -> p dc t", p=128)
            g0 = grp * T
            xt = st[gi % 2]
            xk = 'pe_s%d' % (gi % 2)
            h2 = h2s[gi % 2]
            fw.dma('sp', xt[:], rv[:, :, g0:g0 + T], writes=[xk])
            hk = normmod(P, S, xt, xk, T, S.G2[:, :, col], S.modT[:, 24:32, col], h2, 'pe_h2_%d' % (gi % 2), sq, rstd, ('ps', 4), ps[4])
            for qc in range(16):
                pi = 4 + qc % 4
                for dc in range(8):
                    fw.mm(ps[pi][:, :T], wq[:, dc, qc * 128:(qc + 1) * 128], h2[:, dc, :], start=(dc == 0), stop=(dc == 7),
                          reads=hk if dc in (0, 7) else (), writes=[('ps', pi)])
                fw.cp('act' if qc % 2 == 0 else 'dve', qT[:, qc, :], ps[pi][:, :T], reads=[('ps', pi)], writes=[('pe_qT', qc)])
            for tt in range(2):
                tsl = slice(tt * 128, (tt + 1) * 128)
                v12 = v12s[tt]
                t16 = t16s[tt]
                for h in range(8):
                    fw.mm(ps[4 + h // 4][:, (h % 4) * 128:(h % 4 + 1) * 128], qT[:, 2 * h, tsl], k1[:], reads=qk + ['pe_k1'], writes=[('ps', 4 + h // 4)])
                    fw.mm(ps[6 + h // 4][:, (h % 4) * 128:(h % 4 + 1) * 128], qT[:, 2 * h + 1, tsl], k2[:], reads=qk + ['pe_k2'], writes=[('ps', 6 + h // 4)])
                for q4 in range(4):
                    fw.cp('dve' if q4 % 2 == 0 else 'act', s12[:, q4 * 4:(q4 + 1) * 4, :], ps[4 + q4][:, :].rearrange("p (h i) -> p h i", h=4),
                          reads=[('ps', 4 + q4)], writes=[('pe_s12', q4)])
                sk = [('pe_s12', q4) for q4 in range(4)]
                for base in range(0, 16, 4):
                    for idx in range(base, base + 4):
                        fw.op('dve', lambda e, idx=idx: e.max(out=v12[:, idx, 0:8], in_=s12[:, idx, :]), reads=sk, writes=[('pe_v12', tt, idx)])
                    for idx in range(base, base + 4):
                        fw.op('dve', lambda e, idx=idx: e.match_replace(out=wk[:, idx % 4, :], in_to_replace=v12[:, idx, 0:8], in_values=s12[:, idx, :], imm_value=-1e30),
                              reads=[('pe_v12', tt, idx)], writes=[('pe_wk', idx % 4)])
                    for idx in range(base, base + 4):
                        fw.op('dve', lambda e, idx=idx: e.max(out=v12[:, idx, 8:16], in_=wk[:, idx % 4, :]), reads=[('pe_wk', idx % 4)], writes=[('pe_v12b', tt, idx)])
                v12k = v12k_of(tt)
                fw.tt('dve', cand[:].rearrange("p h (a b) -> p h a b", a=16), v12[:, 0:8, :].unsqueeze(3).to_broadcast([128, 8, 16, 16]),
                      v12[:, 8:16, :].unsqueeze(2).to_broadcast([128, 8, 16, 16]), ALU.add, reads=v12k, writes=sqk)
                for base in range(0, 8, 4):
                    for h in range(base, base + 4):
                        fw.op('dve', lambda e, h=h: e.max(out=t16[:, h, 0:8], in_=cand[:, h, :]), reads=sqk, writes=[('pe_t16', tt, h)])
                    for h in range(base, base + 4):
                        fw.op('dve', lambda e, h=h: e.match_replace(out=wk2[:, h % 4, :], in_to_replace=t16[:, h, 0:8], in_values=cand[:, h, :], imm_value=-1e30),
                              reads=[('pe_t16', tt, h)] + sqk, writes=[('pe_wk2', h % 4)])
                    for h in range(base, base + 4):
                        fw.op('dve', lambda e, h=h: e.max(out=t16[:, h, 8:16], in_=wk2[:, h % 4, :]), reads=[('pe_wk2', h % 4)], writes=[('pe_t16b', tt, h)])

        def part_B():
            for tt in range(2):
                tsl = slice(tt * 128, (tt + 1) * 128)
                v12 = v12s[tt]
                t16 = t16s[tt]
                v12k = v12k_of(tt)
                t16k = t16k_of(tt)
                fw.tt('dve', e16[:], t16[:], t16[:, :, 0:1].to_broadcast([128, 8, 16]), ALU.subtract, reads=t16k, writes=['pe_e16'])
                fw.act(e16[:], e16[:], AF.Exp, reads=['pe_e16'], writes=['pe_e16'])
                fw.op('dve', lambda e: e.reduce_sum(out=zz[:], in_=e16[:], axis=AX.X), reads=['pe_e16'], writes=['pe_z'])
                fw.act(zz[:], zz[:], AF.Ln, reads=['pe_z'], writes=['pe_z'])
                fw.tt('dve', mz[:], t16[:, :, 0], zz[:], ALU.add, reads=t16k + ['pe_z'], writes=['pe_mz'])
                fw.tt('dve', thr[:], t16[:, :, 15:16].to_broadcast([128, 8, 16]), v12[:, 0:8, :], ALU.subtract, reads=t16k + v12k, writes=['pe_thr'])
                fw.tt('dve', bia[:], v12[:, 0:8, :], mz[:, :].unsqueeze(2).to_broadcast([128, 8, 16]), ALU.subtract, reads=['pe_mz'] + v12k, writes=['pe_bia'])
                fw.act(bia[:], bia[:], AF.Exp, reads=['pe_bia'], writes=['pe_bia'])
                srcs = [(v12[:, 0:8, :], v12k), (thr[:], ['pe_thr']), (bia[:], ['pe_bia'])]
                for slot, (sap, skey) in enumerate(srcs):
                    fw.tr(ps[6][:, slot * 128:(slot + 1) * 128], sap.rearrange("p h a -> p (h a)"), S.ident[:], reads=skey, writes=[('ps', 6)])
                fw.cp('dve', pc[:, :, tsl], ps[6][:, 0:384].rearrange("p (s t) -> p s t", s=3), reads=[('ps', 6)], writes=['pe_pc'])

        groups = [(seq, grp) for seq in seqs_of(ctx_out) for grp in range(seq_len(seq) // T)]
        part_A(groups[0][0], groups[0][1], 0)
        part_B()
        for gi, (seq, grp) in enumerate(groups):
            if True:
                col = seq_col(seq)
                rv = S.res[seq].rearrange("(dc p) t -> p dc t", p=128)
                g0 = grp * T
                xt = st[gi % 2]
                xk = 'pe_s%d' % (gi % 2)
                h2 = h2s[gi % 2]
                has_next = gi + 1 < len(groups)
                load_tables(0)
                load_tables(1)
                NQ = T // 4

                def quad_F(u):
                    qb = u % 2
                    t0 = 4 * u
                    for half, qr, kk_ in ((0, qr1, k1), (1, qr2, k2)):
                        src_ap = qT[:, half:16:2, t0:t0 + 4].rearrange("p h t -> p t h").unsqueeze(3).to_broadcast([128, 4, 8, 16])
                        fw.cp('pool' if half == 0 else 'dve', qr[:, qb, :, :].rearrange("p t (h a) -> p t h a", h=8), src_ap,
                              reads=qk if u < 2 else (), writes=[('pe_qr%d' % half, qb)])
                    for k in range(4):
                        t = t0 + k
                        xb = (t // 2) % 4
                        c1 = (t % 2) * 128
                        fw.mm(ps[xb][:, c1:c1 + 128], qr1[:, qb, k, :], k1[:], reads=[('pe_qr0', qb)], writes=[('ps', xb)])
                        fw.mm(ps[xb][:, 256 + c1:256 + c1 + 128], qr2[:, qb, k, :], k2[:], reads=[('pe_qr1', qb)], writes=[('ps', xb)])

                def quad_B(u):
                    t0 = 4 * u
                    gb = 4 + u % 2
                    for pr in range(2):
                        tp = t0 + 2 * pr
                        xb = (tp // 2) % 4
                        eq = (tp // 2) % 4
                        ek = ('pe_E', eq)
                        fw.act(Et[eq][:], ps[xb][:, 256:512], AF.Exp, reads=[('ps', xb)], writes=[ek])
                        for kk in range(2):
                            t = tp + kk
                            k = 2 * pr + kk
                            c1 = kk * 128
                            q = t % 8
                            wkk, pk = ('pe_W', q), ('pe_P', q)
                            fw.stt('dve', Wt[q][:], ps[xb][:, 256 + c1:256 + c1 + 128], pc[:, 1, t:t + 1], Et[eq][:, c1:c1 + 128], ALU.is_ge, ALU.mult,
                                   reads=[('ps', xb), ek, 'pe_pc'], writes=[wkk])
                            fw.ts('dve', Pt[q][:], ps[xb][:, c1:c1 + 128], pc[:, 0, t:t + 1], pc[:, 2, t:t + 1], ALU.is_equal, ALU.mult,
                                  reads=[('ps', xb), ek, 'pe_pc'], writes=[pk])
                            fw.mm(ps[gb][:, k * 128:(k + 1) * 128], Wt[q][:], Pt[q][:], reads=[wkk, pk], writes=[('ps', gb)])

                def quad_C(u):
                    gb = 4 + u % 2
                    fw.cp('act', gbuf[:, :, 4 * u:4 * u + 4].rearrange("p i t -> p t i"),
                          ps[gb][:, :].rearrange("p (t i) -> p t i", t=4), reads=[('ps', gb)], writes=[('pe_g', u % 2)])

                for u in range(NQ + 2):
                    if u < NQ:
                        quad_F(u)
                    if 1 <= u <= NQ:
                        quad_B(u - 1)
                    if u >= 2:
                        quad_C(u - 2)
                gk = [('pe_g', q) for q in range(2)]
                if has_next:
                    part_A(groups[gi + 1][0], groups[gi + 1][1], gi + 1)

                def dense_S(i):
                    bb, ii = (i // 2) % 2, i % 2
                    pi = 4 + i % 2
                    for dc in range(8):
                        fw.mm(ps[pi][:, :T], ub[:, bb, dc, ii * 128:(ii + 1) * 128], h2[:, dc, :], start=(dc == 0), stop=(dc == 7),
                              reads=[('pe_ub', bb)] if dc in (0, 7) else (), writes=[('ps', pi)])

                def dense_O(i):
                    bb, ii = (i // 2) % 2, i % 2
                    pi = 4 + i % 2
                    fw.act(gst[i % 2][:], ps[pi][:, :T], AF.Gelu_apprx_tanh, reads=[('ps', pi)], writes=[('pe_gs', i % 2)])
                    fw.tt('pool', At[i % 2][:], gst[i % 2][:], gbuf[:, i, :], ALU.mult,
                          reads=[('pe_gs', i % 2)] + (gk if i < 2 else []), writes=[('pe_A', i % 2)])
                    for dch in range(8):
                        bk, rg = dch // 2, (dch % 2) * 256
                        fw.mm(ps[bk][:, rg:rg + T], vb[:, bb, ii, dch * 128:(dch + 1) * 128], At[i % 2][:],
                              start=(i == 0 and dch % 2 == 0), stop=(i == 127),
                              reads=[('pe_A', i % 2), ('pe_vb', bb)] if dch in (0, 7) else (), writes=[('ps', bk)])

                dense_S(0)
                for i in range(128):
                    if i + 1 < 128:
                        dense_S(i + 1)
                    dense_O(i)
                    if i % 2 == 1 and (i + 1) // 2 + 1 < 64:
                        load_tables((i + 1) // 2 + 1)
                for dch in range(8):
                    bk, rg = dch // 2, (dch % 2) * 256
                    fw.stt('dve', xo[:, dch, :], ps[bk][:, rg:rg + T], S.modT[:, 40 + dch, col:col + 1], xt[:, dch, :], ALU.mult, ALU.add,
                           reads=[('ps', bk), xk], writes=[('pe_xo', dch)])
                fw.dma('sp', rv[:, :, g0:g0 + T], xo[:], reads=[('pe_xo', d8) for d8 in range(8)], writes=[('resp', seq, grp)])
                if has_next:
                    part_B()
        fw.barrier()
```

```python
import math
from contextlib import ExitStack
import numpy as np
import ml_dtypes
import concourse.bass as bass
import concourse.mybir as mybir
from concourse.bass_utils import run_bass_kernel_spmd

F32 = mybir.dt.float32
BF16 = mybir.dt.bfloat16
AF = mybir.ActivationFunctionType
ALU = mybir.AluOpType
AX = mybir.AxisListType

NDS = 20
D = 1024
L = 2048
LC = 256
NLAYER = 2
EPS = 1e-6
NCOL = 2576
OFF_Z, OFF_XBC, OFF_DT, OFF_FN = 768, 1280, 2304, 2320
PI = math.pi
import os
SSD_MODE = os.environ.get('SSD_MODE', 'full')


class FW:
    def __init__(self, nc):
        self.nc = nc
        self.eng = dict(pe=nc.tensor, dve=nc.vector, act=nc.scalar, pool=nc.gpsimd, sp=nc.sync)
        self.esem = {k: nc.alloc_semaphore("es_" + k) for k in self.eng}
        self.ecnt = {k: 0 for k in self.eng}
        self.dsem = [nc.alloc_semaphore("ds_%d" % i) for i in range(NDS)]
        self.dcnt = [0] * NDS
        self.waited = {k: {} for k in self.eng}
        self.buf = {}
        self.dsem_of = {}
        self.rr = 0
        self.ninst = 0

    def _b(self, key):
        b = self.buf.get(key)
        if b is None:
            b = dict(w=None, r={})
            self.buf[key] = b
        return b

    def _wait(self, en, tok):
        if tok is None:
            return
        kind, who, n = tok
        if kind == 'e':
            if who == en and en == 'pe':
                return
            if self.waited[en].get(('e', who), 0) >= n:
                return
            self.eng[en].wait_ge(self.esem[who], n)
            self.waited[en][('e', who)] = n
        else:
            val = self.dcnt[who]
            if self.waited[en].get(('d', who), 0) >= n:
                return
            self.eng[en].wait_ge(self.dsem[who], 16 * val)
            self.waited[en][('d', who)] = val

    def _deps(self, en, reads, writes):
        for k in reads:
            self._wait(en, self._b(k)['w'])
        for k in writes:
            b = self._b(k)
            self._wait(en, b['w'])
            for t in b['r'].values():
                self._wait(en, t)

    def _commit(self, tok, reads, writes):
        for k in writes:
            self.buf[k] = dict(w=tok, r={})
        for k in reads:
            if k in writes:
                continue
            self._b(k)['r'][(tok[0], tok[1])] = tok

    def op(self, en, fn, reads=(), writes=()):
        self._deps(en, reads, writes)
        ins = fn(self.eng[en])
        self.ecnt[en] += 1
        ins.then_inc(self.esem[en], 1)
        tok = ('e', en, self.ecnt[en])
        self._commit(tok, reads, writes)
        self.ninst += 1
        return tok

    def dma(self, en, out, in_, reads=(), writes=(), **kw):
        self._deps(en, reads, writes)
        key = writes[0] if writes else ('anon',)
        idx = self.dsem_of.get(key)
        if idx is None:
            idx = self.rr % NDS
            self.rr += 1
            self.dsem_of[key] = idx
        ins = self.eng[en].dma_start(out=out, in_=in_, **kw)
        self.dcnt[idx] += 1
        ins.then_inc(self.dsem[idx], 16)
        tok = ('d', idx, self.dcnt[idx])
        self._commit(tok, reads, writes)
        self.ninst += 1
        return tok

    def barrier(self):
        for en in self.eng:
            for who in self.eng:
                if who != en and self.ecnt[who] > self.waited[en].get(('e', who), 0):
                    self.eng[en].wait_ge(self.esem[who], self.ecnt[who])
                    self.waited[en][('e', who)] = self.ecnt[who]
            for i in range(NDS):
                if self.dcnt[i] > self.waited[en].get(('d', i), 0):
                    self.eng[en].wait_ge(self.dsem[i], 16 * self.dcnt[i])
                    self.waited[en][('d', i)] = self.dcnt[i]
        self.buf = {}

    def mm(self, out, lhsT, rhs, start=True, stop=True, reads=(), writes=()):
        return self.op('pe', lambda e: e.matmul(out, lhsT, rhs, start=start, stop=stop), reads, writes)

    def tr(self, out, in_, ident, reads=(), writes=()):
        return self.op('pe', lambda e: e.transpose(out, in_, ident), reads, writes)

    def act(self, out, in_, func, bias=None, scale=None, reads=(), writes=()):
        kw = {}
        if bias is not None:
            kw['bias'] = bias
        if scale is not None:
            kw['scale'] = scale
        return self.op('act', lambda e: e.activation(out=out, in_=in_, func=func, **kw), reads, writes)

    def ts(self, en, out, in0, s1, s2, op0, op1=None, reads=(), writes=()):
        kw = {}
        if op1 is not None:
            kw['op1'] = op1
        return self.op(en, lambda e: e.tensor_scalar(out, in0, s1, s2, op0, **kw), reads, writes)

    def tt(self, en, out, in0, in1, op, reads=(), writes=()):
        return self.op(en, lambda e: e.tensor_tensor(out, in0, in1, op), reads, writes)

    def stt(self, en, out, in0, scalar, in1, op0, op1, reads=(), writes=()):
        en = 'dve'
        return self.op(en, lambda e: e.scalar_tensor_tensor(out, in0, scalar, in1, op0, op1), reads, writes)

    def cp(self, en, out, in_, reads=(), writes=()):
        if en == 'act':
            return self.op('act', lambda e: e.copy(out, in_), reads, writes)
        return self.op(en, lambda e: e.tensor_copy(out, in_), reads, writes)

    def memset(self, en, ap, val, writes=()):
        return self.op(en, lambda e: e.memset(ap, val), (), writes)


_CONST = None


def _bf(a):
    return np.ascontiguousarray(a.astype(ml_dtypes.bfloat16))


def _consts():
    global _CONST
    if _CONST is not None:
        return _CONST
    c = {}
    c['ident'] = np.eye(128, dtype=np.float32)
    j = np.arange(128)[:, None]
    i = np.arange(128)[None, :]
    c['uinc'] = (j <= i).astype(np.float32)
    c['linc'] = (j >= i).astype(np.float32)
    c['maskf'] = np.where(i >= j, 0.0, -1e4).astype(np.float32)
    c['maskb'] = np.where(j >= i, 0.0, -1e4).astype(np.float32)
    c['ones'] = np.ones((128, 128), np.float32)
    a = np.arange(64)
    ang = 2 * np.pi * np.outer(a, a) / 64.0
    cb = np.zeros((128, 128)); sb = np.zeros((128, 128))
    for g in range(2):
        cb[g * 64:(g + 1) * 64, g * 64:(g + 1) * 64] = np.cos(ang) / 8.0
        sb[g * 64:(g + 1) * 64, g * 64:(g + 1) * 64] = np.sin(ang) / 8.0
    c['cbd'] = cb.astype(np.float32)
    c['sbd'] = sb.astype(np.float32)
    for tag, Lq in (('l', L), ('c', LC)):
        nsc = Lq // 128
        N = 2 * Lq
        nf = Lq + 1
        nfc = (nf + 127) // 128
        t = np.linspace(0.0, 1.0, Lq, dtype=np.float32)[:, None]
        w = (2.0 * np.pi * np.arange(Lq, dtype=np.float32)[:, None] / Lq).astype(np.float32)
        f = np.linspace(1e-4, 15, 16, dtype=np.float32)[None, :]
        z = np.concatenate([t, np.cos(f * w), -np.sin(f * w)], axis=-1).astype(np.float32)
        c['zT_' + tag] = np.ascontiguousarray(z.T)
        max_decay = math.log(1e-2) / 0.3
        min_decay = math.log(1e-2) / 1.5
        deltas = np.abs(np.linspace(min_decay, max_decay, 256, dtype=np.float32))
        win = np.exp(-t * deltas).astype(np.float32)
        winb = win.copy()
        winb[0] = 0.0
        lay = lambda m: np.ascontiguousarray(m.reshape(nsc, 128, 256).transpose(1, 0, 2))
        c['win_' + tag] = np.stack([lay(win), lay(winb)]).astype(np.float32)
        s = np.arange(Lq, dtype=np.float64)[:, None]
        ff = np.arange(nfc * 128, dtype=np.float64)[None, :]
        th = 2 * np.pi * s * ff / N
        valid = (ff < nf)
        Cf = np.cos(th) * valid
        Sf = -np.sin(th) * valid
        fl = lambda m: np.ascontiguousarray(m.reshape(nsc, 128, nfc, 128).transpose(2, 1, 0, 3))
        c['cf_' + tag] = _bf(fl(Cf))
        c['sf_' + tag] = _bf(fl(Sf))
        TB = min(512, Lq)
        ntb = Lq // TB
        fcol = np.arange(nfc * 128, dtype=np.float64)[:, None]
        tt = np.arange(Lq, dtype=np.float64)[None, :]
        wgt = np.where((fcol == 0) | (fcol == Lq), 1.0, 2.0) * (fcol < nf) / N
        th2 = 2 * np.pi * fcol * tt / N
        Ci = wgt * np.cos(th2)
        Si = -wgt * np.sin(th2)
        il = lambda m: np.ascontiguousarray(m.reshape(nfc, 128, ntb, TB).transpose(2, 1, 0, 3))
        c['ci_' + tag] = _bf(il(Ci))
        c['si_' + tag] = _bf(il(Si))
        t1 = np.arange(Lq, dtype=np.float64)
        th3 = 2 * np.pi * np.outer(t1, t1) / Lq
        CL = np.cos(th3) / math.sqrt(Lq)
        SLn = -np.sin(th3) / math.sqrt(Lq)
        ll = lambda m: np.ascontiguousarray(m.reshape(nsc, 128, ntb, TB).transpose(2, 1, 0, 3))
        c['cl_' + tag] = _bf(ll(CL))
        c['sl_' + tag] = _bf(ll(SLn))
    _CONST = c
    return c


def _pc(v, n):
    return np.ascontiguousarray(np.asarray(v, np.float32).reshape(n, 128).T)


class Prog:
    def __init__(self, layers=(0, 1), phases=None, dbg=False):
        self.layers = layers
        self.phases = phases
        self.dbg = dbg
        nc = bass.Bass("TRN2", target_bir_lowering=False)
        self.nc = nc
        self.fw = FW(nc)
        self.inputs = {}
        self.uid = 0
        self.ps = [nc.alloc_psum_tensor("psb%d" % i, [128, 512], F32) for i in range(8)]

    def inp(self, name, shape, dt=F32):
        t = self.nc.dram_tensor(name, list(shape), dt, kind="ExternalInput").ap()
        self.inputs[name] = t
        return t

    def scratch(self, name, shape, dt=F32, out=False):
        kind = "ExternalOutput" if (out or self.dbg) else "Internal"
        return self.nc.dram_tensor(name, list(shape), dt, kind=kind).ap()

    def sb(self, name, shape, dt):
        self.uid += 1
        return self.nc.sbuf_tensor('%s_u%d' % (name, self.uid), shape, dt)

    def want(self, ph):
        return self.phases is None or ph in self.phases


def _declare(P):
    c = _consts()
    I = {}
    I['xT'] = P.inp('xT', [2, D, L])
    I['ctxT'] = P.inp('ctxT', [2, D, LC])
    I['cT'] = P.inp('cT', [128, 8, 3])
    I['w_ada'] = P.inp('w_ada', [NLAYER, D, 6 * D])
    I['b_adaT'] = P.inp('b_adaT', [NLAYER, 128, 48])
    I['gn1'] = P.inp('gn1', [NLAYER, 128, 8])
    I['gn2'] = P.inp('gn2', [NLAYER, 128, 8])
    I['gfin'] = P.inp('gfin', [128, 8])
    I['w_in'] = P.inp('w_in', [NLAYER, D, NCOL])
    I['hycw'] = P.inp('hycw', [NLAYER, 128, 6, 4])
    I['hfw1'] = P.inp('hfw1', [NLAYER, 33, 64])
    I['hfw2'] = P.inp('hfw2', [NLAYER, 64, 64])
    I['hfw3'] = P.inp('hfw3', [NLAYER, 64, 512])
    I['hfv'] = P.inp('hfv', [NLAYER, 64, 3])
    I['hybias'] = P.inp('hybias', [NLAYER, 128, 2])
    I['sscw'] = P.inp('sscw', [NLAYER, 128, 8, 4])
    I['dtb'] = P.inp('dtb', [NLAYER, 16, 1])
    I['alog'] = P.inp('alog', [NLAYER, 1, 16])
    I['dskip'] = P.inp('dskip', [NLAYER, 1, 8])
    I['sng'] = P.inp('sng', [NLAYER, 128, 4])
    I['w_out'] = P.inp('w_out', [NLAYER, D, D])
    I['wq'] = P.inp('wq', [NLAYER, D, 2048])
    I['k1T'] = P.inp('k1T', [NLAYER, 128, 128])
    I['k2T'] = P.inp('k2T', [NLAYER, 128, 128])
    I['uT'] = P.inp('uT', [NLAYER, D, 16384])
    I['v'] = P.inp('v', [NLAYER, 16384, D])
    for k, a in c.items():
        I[k] = P.inp('c_' + k, a.shape, BF16 if a.dtype == ml_dtypes.bfloat16 else F32)
    return I


class Ctx:
    pass


def build(layers=(0, 1), phases=None, dbg=False, final=True):
    P = Prog(layers, phases, dbg)
    nc, fw = P.nc, P.fw
    I = _declare(P)
    S = Ctx()
    S.res = {}
    for b in range(2):
        S.res[(b, 'l')] = P.scratch('res_l%d' % b, [D, L])
        S.res[(b, 'c')] = P.scratch('res_c%d' % b, [D, LC])
    S.pl = {}
    S.mix = {}
    for b in range(2):
        S.pl[(b, 'l')] = P.scratch('pl_l%d' % b, [NCOL, L])
        S.pl[(b, 'c')] = P.scratch('pl_c%d' % b, [NCOL, LC])
        S.mix[(b, 'l')] = P.scratch('mix_l%d' % b, [D, L], BF16)
        S.mix[(b, 'c')] = P.scratch('mix_c%d' % b, [D, LC], BF16)
    S.khat = {'l': P.scratch('khat_l', [2, 17 * 128, 256]), 'c': P.scratch('khat_c', [2, 3 * 128, 256])}
    S.ubf = P.scratch('ubf', [64, 128, 8, 256], BF16)
    S.vbf = P.scratch('vbf', [64, 128, 2, 1024], BF16)
    S.outT = P.scratch('outT', [2, D, L], F32, out=True)

    A = lambda n, sh, dt=F32: nc.alloc_sbuf_tensor('sb_' + n, sh, dt)
    S.ident = A('ident', [128, 128]); S.ones = A('ones', [128, 128])
    S.uinc = A('uinc', [128, 128]); S.linc = A('linc', [128, 128])
    S.maskf = A('maskf', [128, 128]); S.maskb = A('maskb', [128, 128])
    S.modT = A('modT', [128, 48, 3])
    S.G1 = A('G1', [128, 8, 3]); S.G2 = A('G2', [128, 8, 3])
    S.gfin = A('gfin', [128, 8]); S.zero8 = A('zero8', [128, 8])
    S.hyn = A('hyn', [128, 2, 2])
    for nm in ('ident', 'ones', 'uinc', 'linc', 'maskf', 'maskb'):
        fw.dma('sp', getattr(S, nm)[:], I[nm], writes=[nm])
    fw.dma('sp', S.gfin[:], I['gfin'], writes=['gfin'])
    fw.memset('pool', S.zero8[:], 0.0, writes=['zero8'])
    S.epsc = A('epsc', [128, 1])
    fw.memset('pool', S.epsc[:], EPS, writes=['epsc'])
    S.negpi = A('negpi', [128, 1])
    fw.memset('pool', S.negpi[:], -PI, writes=['negpi'])
    fw.barrier()

    def src(l, seq):
        b, kind = seq
        if l == layers[0] and l == 0:
            return I['xT'][b] if kind == 'l' else I['ctxT'][b]
        return S.res[seq]

    for l in layers:
        ctx_out = l < NLAYER - 1
        if P.want('mod'):
            phase_mod(P, I, S, l)
        if P.want('proj'):
            phase_proj(P, I, S, l, src)
        if P.want('filt'):
            phase_filt(P, I, S, l, 'l')
            if ctx_out:
                phase_filt(P, I, S, l, 'c')
        if P.want('hy'):
            phase_hy(P, I, S, l, 'l')
            if ctx_out:
                phase_hy(P, I, S, l, 'c')
        if P.want('fn'):
            phase_fn(P, I, S, l, 'l')
            if ctx_out:
                phase_fn(P, I, S, l, 'c')
        if P.want('ssd'):
            phase_ssd(P, I, S, l, ctx_out)
        if P.want('out'):
            phase_out(P, I, S, l, src, ctx_out)
        if P.want('peer'):
            phase_peer(P, I, S, l, ctx_out)
    if final and P.want('final'):
        phase_final(P, I, S)
    if dbg:
        dh = P.scratch('dbg_hyn', [128, 4])
        fw.dma('sp', dh, S.hyn[:].rearrange("p a b -> p (a b)"), writes=['dbg_hyn'])
    fw.barrier()
    return P


def phase_mod(P, I, S, l):
    nc, fw = P.nc, P.fw
    with ExitStack() as es:
        cin = es.enter_context(P.sb('m_c', [128, 8, 3], F32))
        sc = es.enter_context(P.sb('m_sc', [128, 8, 3], F32))
        w0 = es.enter_context(P.sb('m_w0', [128, 8, 512], F32))
        w1 = es.enter_context(P.sb('m_w1', [128, 8, 512], F32))
        bada = es.enter_context(P.sb('m_b', [128, 48], F32))
        g1 = es.enter_context(P.sb('m_g1', [128, 8], F32))
        g2 = es.enter_context(P.sb('m_g2', [128, 8], F32))
        tmp = es.enter_context(P.sb('m_t', [128, 8, 3], F32))
        wb = [w0, w1]
        fw.dma('sp', cin[:], I['cT'], writes=['m_c'])
        fw.dma('sp', bada[:], I['b_adaT'][l], writes=['m_b'])
        fw.dma('sp', g1[:], I['gn1'][l], writes=['m_g1'])
        fw.dma('sp', g2[:], I['gn2'][l], writes=['m_g2'])
        fw.act(sc[:], cin[:], AF.Silu, reads=['m_c'], writes=['m_sc'])
        wv = I['w_ada'][l].rearrange("(dc p) n -> p dc n", p=128)
        for blk in range(12):
            w = wb[blk % 2]
            wk = 'm_w%d' % (blk % 2)
            fw.dma('sp', w[:], wv[:, :, blk * 512:(blk + 1) * 512], writes=[wk])
            for j in range(4):
                cc = blk * 4 + j
                pk = ('ps', cc % 2)
                pt = P.ps[cc % 2][:, 0:3]
                for dc in range(8):
                    fw.mm(pt, w[:, dc, j * 128:(j + 1) * 128], sc[:, dc, :], start=(dc == 0), stop=(dc == 7),
                          reads=[wk, 'm_sc'], writes=[pk])
                fw.ts('dve', S.modT[:, cc, :], pt, bada[:, cc:cc + 1], None, ALU.add, reads=[pk, 'm_b'], writes=['modT'])
        for (G, g, gk, c0, nm) in ((S.G1, g1, 'm_g1', 8, 'G1'), (S.G2, g2, 'm_g2', 32, 'G2')):
            fw.ts('dve', tmp[:], S.modT[:, c0:c0 + 8, :], 1.0, None, ALU.add, reads=['modT'], writes=['m_t'])
            fw.tt('dve', G[:], tmp[:], g[:, :].unsqueeze(2).to_broadcast([128, 8, 3]), ALU.mult, reads=['m_t', gk], writes=[nm])
        fw.barrier()


def seqs_of(ctx_too=True):
    out = []
    for b in range(2):
        out.append((b, 'l'))
        if ctx_too:
            out.append((b, 'c'))
    return out


def seq_len(seq):
    return L if seq[1] == 'l' else LC


def seq_col(seq):
    return seq[0] if seq[1] == 'l' else 2


def normmod(P, S, xt, xk, T, Gap, shap, hm, hk, sq, rstd, psk, pst):
    fw = P.fw
    fw.act(sq[:, :, :T], xt[:, :, :T], AF.Square, reads=[xk], writes=[('nm_sq', dc) for dc in range(8)])
    for dc in range(8):
        fw.mm(pst[:, :T], S.ones[:], sq[:, dc, :T], start=(dc == 0), stop=(dc == 7), reads=[('nm_sq', dc), 'ones'], writes=[psk])
    fw.act(rstd[:, :T], pst[:, :T], AF.Sqrt, bias=S.epsc[:, 0:1], scale=1.0 / D, reads=[psk, 'epsc'], writes=['nm_rstd'])
    fw.op('dve', lambda e: e.reciprocal(rstd[:, :T], rstd[:, :T]), reads=['nm_rstd'], writes=['nm_rstd'])
    for dc in range(8):
        en = 'dve' if dc % 2 == 0 else 'pool'
        fw.stt(en, sq[:, dc, :T], xt[:, dc, :T], Gap[:, dc:dc + 1], rstd[:, :T], ALU.mult, ALU.mult,
               reads=[xk, 'nm_rstd'], writes=[('nm_sq', dc)])
        fw.act(hm[:, dc, :T], sq[:, dc, :T], AF.Identity, bias=shap[:, dc:dc + 1], reads=[('nm_sq', dc)], writes=[(hk, dc)])
    return [(hk, dc) for dc in range(8)]


def load_cast_weight(P, dst, dstk, srcv, ncols, stg, stgk):
    fw = P.fw
    nb = (ncols + 511) // 512
    for blk in range(nb):
        c0 = blk * 512
        c1 = min(ncols, c0 + 512)
        st = stg[blk % 2]
        sk = stgk[blk % 2]
        fw.dma('sp', st[:, :, :c1 - c0], srcv[:, :, c0:c1], writes=[sk])
        fw.cp('dve' if blk % 2 == 0 else 'pool', dst[:, :, c0:c1], st[:, :, :c1 - c0], reads=[sk], writes=[(dstk, blk)])
    return [(dstk, blk) for blk in range(nb)]


def phase_proj(P, I, S, l, src):
    nc, fw = P.nc, P.fw
    with ExitStack() as es:
        winb = es.enter_context(P.sb('p_win', [128, 8, NCOL], BF16))
        s0 = es.enter_context(P.sb('p_s0', [128, 8, 512], F32))
        s1 = es.enter_context(P.sb('p_s1', [128, 8, 512], F32))
        sq = es.enter_context(P.sb('p_sq', [128, 8, 512], F32))
        rstd = es.enter_context(P.sb('p_rstd', [128, 512], F32))
        hm = es.enter_context(P.sb('p_hm', [128, 8, 512], BF16))
        ob = es.enter_context(P.sb('p_o', [128, 4, 512], F32))
        wkeys = load_cast_weight(P, winb, 'p_win', I['w_in'][l].rearrange("(dc p) n -> p dc n", p=128), NCOL, [s0, s1], ['p_s0', 'p_s1'])
        xb = [s0, s1]
        chunks = [(c0, min(128, NCOL - c0)) for c0 in range(0, OFF_DT, 128)] + [(OFF_DT, 16)] + [(OFF_FN, 128), (OFF_FN + 128, 128)]
        it = 0
        oi = 0
        for seq in seqs_of(True):
            Lq = seq_len(seq)
            T = min(512, Lq)
            col = seq_col(seq)
            xv = src(l, seq).rearrange("(dc p) t -> p dc t", p=128)
            for tb in range(Lq // T):
                xt = xb[it % 2]
                xk = 'p_s%d' % (it % 2)
                it += 1
                fw.dma('sp', xt[:, :, :T], xv[:, :, tb * T:(tb + 1) * T], reads=[('res', seq)], writes=[xk])
                hk = normmod(P, S, xt, xk, T, S.G1[:, :, col], S.modT[:, 0:8, col], hm, 'p_hm', sq, rstd, ('ps', 0), P.ps[0])
                for ci, (c0, cw) in enumerate(chunks):
                    pb = 1 + ci % 3
                    pt = P.ps[pb][:cw, :T]
                    for dc in range(8):
                        fw.mm(pt, winb[:, dc, c0:c0 + cw], hm[:, dc, :T], start=(dc == 0), stop=(dc == 7),
                              reads=wkeys + hk if dc in (0, 7) else (), writes=[('ps', pb)])
                    o = ob[:cw, oi % 4, :T]
                    ok = ('p_o', oi % 4)
                    oi += 1
                    if ci % 2 == 0:
                        fw.cp('act', o, pt, reads=[('ps', pb)], writes=[ok])
                    else:
                        fw.cp('dve', o, pt, reads=[('ps', pb)], writes=[ok])
                    fw.dma('sp', S.pl[seq][c0:c0 + cw, tb * T:(tb + 1) * T], o, reads=[ok], writes=[('pl', seq, ci, tb)])
        fw.barrier()


def phase_final(P, I, S):
    nc, fw = P.nc, P.fw
    with ExitStack() as es:
        s0 = es.enter_context(P.sb('f_s0', [128, 8, 512], F32))
        s1 = es.enter_context(P.sb('f_s1', [128, 8, 512], F32))
        sq = es.enter_context(P.sb('f_sq', [128, 8, 512], F32))
        rstd = es.enter_context(P.sb('f_rstd', [128, 512], F32))
        o0 = es.enter_context(P.sb('f_o0', [128, 8, 512], F32))
        o1 = es.enter_context(P.sb('f_o1', [128, 8, 512], F32))
        xb = [s0, s1]
        ob = [o0, o1]
        it = 0
        for b in range(2):
            xv = S.res[(b, 'l')].rearrange("(dc p) t -> p dc t", p=128)
            ov = S.outT[b].rearrange("(dc p) t -> p dc t", p=128)
            for tb in range(L // 512):
                xt = xb[it % 2]; xk = 'f_s%d' % (it % 2)
                o = ob[it % 2]; ok = 'f_o%d' % (it % 2)
                it += 1
                fw.dma('sp', xt[:], xv[:, :, tb * 512:(tb + 1) * 512], writes=[xk])
                hk = normmod(P, S, xt, xk, 512, S.gfin, S.zero8, o, ok + 'h', sq, rstd, ('ps', 0), P.ps[0])
                fw.dma('sp', ov[:, :, tb * 512:(tb + 1) * 512], o[:], reads=hk, writes=[('outT', b, tb)])
        fw.barrier()


def prep_shared(inp):
    f = lambda a: np.ascontiguousarray(np.asarray(a, np.float32))
    sh = {}
    sh['w_ada'] = f(inp['w_ada'])
    sh['b_adaT'] = np.stack([_pc(inp['b_ada'][l], 48) for l in range(NLAYER)])
    sh['gn1'] = np.stack([_pc(inp['g_norm1'][l], 8) for l in range(NLAYER)])
    sh['gn2'] = np.stack([_pc(inp['g_norm2'][l], 8) for l in range(NLAYER)])
    sh['gfin'] = _pc(inp['g_final'], 8)
    sh['w_in'] = f(inp['w_in'])
    hy = []
    for l in range(NLAYER):
        m = np.concatenate([np.asarray(inp['hy_conv_w'][l], np.float32), np.asarray(inp['hy_conv_b'][l], np.float32)[None]], 0)
        hy.append(np.ascontiguousarray(m.reshape(4, 6, 128).transpose(2, 1, 0)))
    sh['hycw'] = np.stack(hy)
    sh['hfw1'] = f(inp['hf_w1']); sh['hfw2'] = f(inp['hf_w2']); sh['hfw3'] = f(inp['hf_w3'])
    sh['hfv'] = np.ascontiguousarray(np.stack([inp['hf_b1'], inp['hf_b2'], inp['hf_freq']], axis=-1).astype(np.float32))
    sh['hybias'] = np.stack([_pc(inp['hy_bias'][l], 2) for l in range(NLAYER)])
    ss = []
    for l in range(NLAYER):
        m = np.concatenate([np.asarray(inp['ssd_conv_w'][l], np.float32), np.asarray(inp['ssd_conv_b'][l], np.float32)[None]], 0)
        ss.append(np.ascontiguousarray(m.reshape(4, 8, 128).transpose(2, 1, 0)))
    sh['sscw'] = np.stack(ss)
    sh['dtb'] = f(np.asarray(inp['ssd_dt_bias']).reshape(NLAYER, 16, 1))
    sh['alog'] = f(np.asarray(inp['ssd_a_log']).reshape(NLAYER, 1, 16))
    sh['dskip'] = f(np.asarray(inp['ssd_d']).reshape(NLAYER, 1, 8))
    sh['sng'] = np.stack([_pc(inp['ssd_norm_g'][l], 4) for l in range(NLAYER)])
    sh['w_out'] = f(inp['w_out'])
    sh['wq'] = f(inp['peer_wq'])
    sh['k1T'] = np.ascontiguousarray(np.asarray(inp['peer_k1'], np.float32).transpose(0, 2, 1))
    sh['k2T'] = np.ascontiguousarray(np.asarray(inp['peer_k2'], np.float32).transpose(0, 2, 1))
    sh['uT'] = np.ascontiguousarray(np.asarray(inp['peer_u'], np.float32).transpose(0, 2, 1))
    sh['v'] = f(inp['peer_v'])
    for k, a in _consts().items():
        sh['c_' + k] = a
    return sh


def prep_core(inp, core):
    m = {}
    x = np.asarray(inp['x'], np.float32)[2 * core:2 * core + 2]
    cx = np.asarray(inp['ctx'], np.float32)[2 * core:2 * core + 2]
    m['xT'] = np.ascontiguousarray(x.transpose(0, 2, 1))
    m['ctxT'] = np.ascontiguousarray(cx.transpose(0, 2, 1))
    cv = np.stack([np.asarray(inp['c'], np.float32)[2 * core], np.asarray(inp['c'], np.float32)[2 * core + 1],
                   np.asarray(inp['c_ctx'], np.float32)], axis=-1)
    m['cT'] = np.ascontiguousarray(cv.reshape(8, 128, 3).transpose(1, 0, 2))
    return m


_PROG = None


def kernel(**inputs):
    global _PROG
    if _PROG is None:
        _PROG = build()
    P = _PROG
    sh = prep_shared(inputs)
    in_maps = []
    for core in range(8):
        m = dict(sh)
        m.update(prep_core(inputs, core))
        in_maps.append({k: m[k] for k in P.inputs})
    res = run_bass_kernel_spmd(P.nc, in_maps, core_ids=list(range(8)))
    outs = [np.asarray(r['outT']).transpose(0, 2, 1) for r in res.results]
    return np.ascontiguousarray(np.concatenate(outs, axis=0).astype(np.float32))


def phase_fn(P, I, S, l, tag):
    nc, fw = P.nc, P.fw
    Lq = L if tag == 'l' else LC
    nsc = Lq // 128
    TB = min(512, Lq)
    ntb = Lq // TB
    with ExitStack() as es:
        ut = es.enter_context(P.sb('fn_ut', [128, 4, Lq], F32))
        cbd = es.enter_context(P.sb('fn_cbd', [128, 128], F32))
        sbd = es.enter_context(P.sb('fn_sbd', [128, 128], F32))
        atok = es.enter_context(P.sb('fn_a', [128, nsc, 512], BF16))
        btok = es.enter_context(P.sb('fn_b', [128, nsc, 512], BF16))
        cl = es.enter_context(P.sb('fn_cl', [128, nsc, TB], BF16))
        sl = es.enter_context(P.sb('fn_sl', [128, nsc, TB], BF16))
        ob = es.enter_context(P.sb('fn_o', [128, 2, TB], BF16))
        fw.dma('sp', cbd[:], I['cbd'], writes=['fn_cbd'])
        fw.dma('sp', sbd[:], I['sbd'], writes=['fn_sbd'])
        for b in range(2):
            for ch in range(2):
                fw.dma('sp', ut[:, b * 2 + ch, :], S.pl[(b, tag)][OFF_FN + ch * 128:OFF_FN + (ch + 1) * 128, :], writes=[('fn_ut', b * 2 + ch)])
        utk = [('fn_ut', m) for m in range(4)]
        for tc in range(nsc):
            pa, pb = 2 * (tc % 2), 2 * (tc % 2) + 1
            for m in range(4):
                fw.mm(P.ps[pa][:, m * 128:(m + 1) * 128], ut[:, m, tc * 128:(tc + 1) * 128], cbd[:], reads=utk + ['fn_cbd'], writes=[('ps', pa)])
                fw.mm(P.ps[pb][:, m * 128:(m + 1) * 128], ut[:, m, tc * 128:(tc + 1) * 128], sbd[:], reads=utk + ['fn_sbd'], writes=[('ps', pb)])
            fw.cp('act', atok[:, tc, :], P.ps[pa][:, :], reads=[('ps', pa)], writes=[('fn_a', tc)])
            fw.cp('dve', btok[:, tc, :], P.ps[pb][:, :], reads=[('ps', pb)], writes=[('fn_b', tc)])
        ak = [('fn_a', tc) for tc in range(nsc)]
        bk = [('fn_b', tc) for tc in range(nsc)]
        oi = 0
        for tb in range(ntb):
            fw.dma('sp', cl[:], I['cl_' + tag][tb], writes=['fn_cl'])
            fw.dma('sp', sl[:], I['sl_' + tag][tb], writes=['fn_sl'])
            for m in range(4):
                b, ch = m // 2, m % 2
                pi = 4 + m % 2
                pt = P.ps[pi][:, :TB]
                for tc in range(nsc):
                    fw.mm(pt, atok[:, tc, m * 128:(m + 1) * 128], cl[:, tc, :], start=(tc == 0), stop=False,
                          reads=ak + ['fn_cl'] if tc in (0, nsc - 1) else (), writes=[('ps', pi)])
                for tc in range(nsc):
                    fw.mm(pt, btok[:, tc, m * 128:(m + 1) * 128], sl[:, tc, :], start=False, stop=(tc == nsc - 1),
                          reads=bk + ['fn_sl'] if tc in (0, nsc - 1) else (), writes=[('ps', pi)])
                o = ob[:, oi % 2, :]
                ok = ('fn_o', oi % 2)
                oi += 1
                fw.cp('act' if m % 2 == 0 else 'dve', o, pt, reads=[('ps', pi)], writes=[ok])
                fw.dma('sp', S.mix[(b, tag)][768 + ch * 128:768 + (ch + 1) * 128, tb * TB:(tb + 1) * TB], o, reads=[ok], writes=[('mixfn', b, ch, tb)])
        fw.barrier()


def phase_out(P, I, S, l, src, ctx_out):
    nc, fw = P.nc, P.fw
    with ExitStack() as es:
        wout = es.enter_context(P.sb('o_w', [128, 8, 1024], BF16))
        s0 = es.enter_context(P.sb('o_s0', [128, 8, 512], F32))
        s1 = es.enter_context(P.sb('o_s1', [128, 8, 512], F32))
        m0 = es.enter_context(P.sb('o_m0', [128, 8, 512], BF16))
        m1 = es.enter_context(P.sb('o_m1', [128, 8, 512], BF16))
        x0 = es.enter_context(P.sb('o_x0', [128, 8, 512], F32))
        x1 = es.enter_context(P.sb('o_x1', [128, 8, 512], F32))
        wkeys = load_cast_weight(P, wout, 'o_w', I['w_out'][l].rearrange("(dc p) n -> p dc n", p=128), 1024, [s0, s1], ['o_s0', 'o_s1'])
        xb, mb, ob = [s0, s1], [m0, m1], [x0, x1]
        it = 0
        for seq in seqs_of(ctx_out):
            Lq = seq_len(seq)
            T = min(512, Lq)
            col = seq_col(seq)
            xv = src(l, seq).rearrange("(dc p) t -> p dc t", p=128)
            mv = S.mix[seq].rearrange("(dc p) t -> p dc t", p=128)
            rv = S.res[seq].rearrange("(dc p) t -> p dc t", p=128)
            for tb in range(Lq // T):
                k = it % 2
                it += 1
                xt, mx, xo = xb[k], mb[k], ob[k]
                fw.dma('sp', xt[:, :, :T], xv[:, :, tb * T:(tb + 1) * T], writes=['o_s%d' % k])
                fw.dma('sp', mx[:, :, :T], mv[:, :, tb * T:(tb + 1) * T], writes=['o_m%d' % k])
                for dch in range(8):
                    pi = dch % 4
                    pt = P.ps[pi][:, :T]
                    for cc in range(8):
                        fw.mm(pt, wout[:, cc, dch * 128:(dch + 1) * 128], mx[:, cc, :T], start=(cc == 0), stop=(cc == 7),
                              reads=wkeys + ['o_m%d' % k] if cc in (0, 7) else (), writes=[('ps', pi)])
                    fw.stt('dve', xo[:, dch, :T], pt, S.modT[:, 16 + dch, col:col + 1], xt[:, dch, :T], ALU.mult, ALU.add,
                           reads=[('ps', pi), 'o_s%d' % k, 'modT'], writes=[('o_x%d' % k, dch)])
                fw.dma('sp', rv[:, :, tb * T:(tb + 1) * T], xo[:, :, :T], reads=[('o_x%d' % k, d8) for d8 in range(8)], writes=[('resw', seq, tb)])
        fw.barrier()


def phase_filt(P, I, S, l, tag):
    nc, fw = P.nc, P.fw
    Lq = L if tag == 'l' else LC
    li = 0 if tag == 'l' else 1
    nsc = Lq // 128
    nfc = (Lq + 1 + 127) // 128
    T = min(512, Lq)
    with ExitStack() as es:
        zT = es.enter_context(P.sb('fl_z', [33, Lq], F32))
        w1 = es.enter_context(P.sb('fl_w1', [33, 64], F32))
        w2 = es.enter_context(P.sb('fl_w2', [64, 64], F32))
        w3 = es.enter_context(P.sb('fl_w3', [64, 512], F32))
        hv = es.enter_context(P.sb('fl_hv', [64, 3], F32))
        fb = es.enter_context(P.sb('fl_fb', [64, 2], F32))
        h1 = es.enter_context(P.sb('fl_h1', [64, Lq], F32))
        h2 = es.enter_context(P.sb('fl_h2', [64, Lq], F32))
        win = es.enter_context(P.sb('fl_win', [128, 2, nsc, 256], F32))
        pm = es.enter_context(P.sb('fl_pm', [128, nsc, 256], BF16))
        mmn = es.enter_context(P.sb('fl_mm', [128, nsc, 256], BF16))
        acc = es.enter_context(P.sb('fl_acc', [128, 256], F32))
        t1 = es.enter_context(P.sb('fl_t1', [128, 2, 256], F32))
        t2 = es.enter_context(P.sb('fl_t2', [128, 2, 256], F32))
        tmp = es.enter_context(P.sb('fl_tmp', [64, 512], F32))
        tmpk = es.enter_context(P.sb('fl_tmpk', [64, 512], F32))
        cf = es.enter_context(P.sb('fl_cf', [128, 2, nsc, 128], BF16))
        sf = es.enter_context(P.sb('fl_sf', [128, 2, nsc, 128], BF16))
        ko = es.enter_context(P.sb('fl_ko', [128, 2, 2, 256], F32))
        ntmp = es.enter_context(P.sb('fl_n', [128, 2], F32))
        fw.dma('sp', zT[:], I['zT_' + tag], writes=['fl_z'])
        fw.dma('sp', w1[:], I['hfw1'][l], writes=['fl_w1'])
        fw.dma('sp', w2[:], I['hfw2'][l], writes=['fl_w2'])
        fw.dma('sp', w3[:], I['hfw3'][l], writes=['fl_w3'])
        fw.dma('sp', hv[:], I['hfv'][l], writes=['fl_hv'])
        for v in range(2):
            fw.dma('sp', win[:, v, :, :], I['win_' + tag][v], writes=[('fl_win', v)])
        fw.ts('dve', fb[:], hv[:, 0:2], hv[:, 2:3], None, ALU.mult, reads=['fl_hv'], writes=['fl_fb'])
        fw.memset('pool', acc[:], 0.0, writes=['fl_acc'])

        def sin_layer(dst, dk, w, wk, K, srcT, sk, col):
            for blk in range(Lq // T):
                pi = blk % 2
                pt = P.ps[pi][:64, :T]
                fw.mm(pt, w[:K, :64], srcT[:K, blk * T:(blk + 1) * T], reads=[wk, sk], writes=[('ps', pi)])
                fw.ts('dve', tmp[:, :T], pt, hv[:, 2:3], fb[:, col:col + 1], ALU.mult, ALU.add, reads=[('ps', pi), 'fl_hv', 'fl_fb'], writes=['fl_tmp'])
                MAGIC = 12582912.0
                fw.ts('dve', tmpk[:, :T], tmp[:, :T], 1.0 / (2.0 * PI), MAGIC, ALU.mult, ALU.add, reads=['fl_tmp'], writes=['fl_tmpk'])
                fw.ts('dve', tmpk[:, :T], tmpk[:, :T], MAGIC, None, ALU.subtract, reads=['fl_tmpk'], writes=['fl_tmpk'])
                fw.stt('dve', tmp[:, :T], tmpk[:, :T], -2.0 * PI, tmp[:, :T], ALU.mult, ALU.add, reads=['fl_tmpk', 'fl_tmp'], writes=['fl_tmp'])
                fw.act(dst[:, blk * T:(blk + 1) * T], tmp[:, :T], AF.Sin, reads=['fl_tmp'], writes=[dk])

        sin_layer(h1, 'fl_h1', w1, 'fl_w1', 33, zT, 'fl_z', 0)
        sin_layer(h2, 'fl_h2', w2, 'fl_w2', 64, h1, 'fl_h1', 1)
        for sc in range(nsc):
            pi = 2 + sc % 2
            pt = P.ps[pi]
            fw.mm(pt[:, :], h2[:64, sc * 128:(sc + 1) * 128], w3[:64, :], reads=['fl_h2', 'fl_w3'], writes=[('ps', pi)])
            fw.tt('dve', t1[:], pt[:, :].rearrange("p (v c) -> p v c", v=2), win[:, :, sc, :], ALU.mult,
                  reads=[('ps', pi), ('fl_win', 0), ('fl_win', 1)], writes=['fl_t1'])
            fw.tt('pool', pm[:, sc, :], t1[:, 0, :], t1[:, 1, :], ALU.add, reads=['fl_t1'], writes=[('fl_pm', sc)])
            fw.tt('pool', mmn[:, sc, :], t1[:, 0, :], t1[:, 1, :], ALU.subtract, reads=['fl_t1'], writes=[('fl_mm', sc)])
            fw.act(t2[:], t1[:], AF.Square, reads=['fl_t1'], writes=['fl_t2'])
            fw.tt('pool', acc[:], acc[:], t2[:, 0, :], ALU.add, reads=['fl_t2'], writes=['fl_acc'])
            fw.tt('pool', acc[:], acc[:], t2[:, 1, :], ALU.add, reads=['fl_t2'], writes=['fl_acc'])
        for ch in range(2):
            fw.mm(P.ps[0][:, ch:ch + 1], acc[:, ch * 128:(ch + 1) * 128], S.ones[:, 0:1], reads=['fl_acc', 'ones'], writes=[('ps', 0)])
        fw.act(ntmp[:], P.ps[0][:, 0:2], AF.Sqrt, bias=S.epsc[:, 0:1], reads=[('ps', 0), 'epsc'], writes=['fl_n'])
        fw.op('dve', lambda e: e.reciprocal(S.hyn[:, :, li], ntmp[:]), reads=['fl_n'], writes=[('hyn', li)])
        pmk = [('fl_pm', sc) for sc in range(nsc)]
        mmk = [('fl_mm', sc) for sc in range(nsc)]
        for fc in range(nfc):
            fsz = 128 if fc < nfc - 1 else 1
            k = fc % 2
            fw.dma('sp', cf[:, k, :, :], I['cf_' + tag][fc], writes=[('fl_cf', k)])
            fw.dma('sp', sf[:, k, :, :], I['sf_' + tag][fc], writes=[('fl_sf', k)])
            pa, pb = 4 + 2 * k, 5 + 2 * k
            for sc in range(nsc):
                fw.mm(P.ps[pa][:fsz, :256], cf[:, k, sc, :fsz], pm[:, sc, :], start=(sc == 0), stop=(sc == nsc - 1),
                      reads=pmk + [('fl_cf', k)] if sc in (0, nsc - 1) else (), writes=[('ps', pa)])
            for sc in range(nsc):
                fw.mm(P.ps[pb][:fsz, :256], sf[:, k, sc, :fsz], mmn[:, sc, :], start=(sc == 0), stop=(sc == nsc - 1),
                      reads=mmk + [('fl_sf', k)] if sc in (0, nsc - 1) else (), writes=[('ps', pb)])
            fw.cp('act', ko[:fsz, k, 0, :], P.ps[pa][:fsz, :256], reads=[('ps', pa)], writes=[('fl_ko', k, 0)])
            fw.cp('dve', ko[:fsz, k, 1, :], P.ps[pb][:fsz, :256], reads=[('ps', pb)], writes=[('fl_ko', k, 1)])
            for v in range(2):
                fw.dma('sp', S.khat[tag][v, fc * 128:fc * 128 + fsz, :], ko[:fsz, k, v, :], reads=[('fl_ko', k, v)], writes=[('khat', tag, v, fc)])
        fw.barrier()


def phase_hy(P, I, S, l, tag):
    nc, fw = P.nc, P.fw
    Lq = L if tag == 'l' else LC
    li = 0 if tag == 'l' else 1
    nsc = Lq // 128
    nfc = (Lq + 1 + 127) // 128
    TB = min(512, Lq)
    ntb = Lq // TB
    with ExitStack() as es:
        u = es.enter_context(P.sb('hy_u', [128, 4, Lq], F32))
        x1c = es.enter_context(P.sb('hy_x1', [128, 4, Lq], BF16))
        utok = es.enter_context(P.sb('hy_ut', [128, nsc, 512], BF16))
        wre = es.enter_context(P.sb('hy_wre', [128, nfc, 512], BF16))
        wim = es.enter_context(P.sb('hy_wim', [128, nfc, 512], BF16))
        cw = es.enter_context(P.sb('hy_cw', [128, 6, 4], F32))
        hb = es.enter_context(P.sb('hy_hb', [128, 2], F32))
        fw.dma('sp', cw[:], I['hycw'][l], writes=['hy_cw'])
        fw.dma('sp', hb[:], I['hybias'][l], writes=['hy_hb'])
        with ExitStack() as es2:
            raw = es2.enter_context(P.sb('hy_raw', [128, 3, Lq + 2], F32))
            cv = [es2.enter_context(P.sb('hy_cv%d' % k, [128, Lq], F32)) for k in range(3)]
            fw.memset('pool', raw[:, :, 0:1], 0.0, writes=['hy_raw_h0'])
            fw.memset('pool', raw[:, :, Lq + 1:Lq + 2], 0.0, writes=['hy_raw_h1'])
            for m in range(4):
                b, ch = m // 2, m % 2
                for k in range(3):
                    r0 = k * 256 + ch * 128
                    fw.dma('sp', raw[:, k, 1:Lq + 1], S.pl[(b, tag)][r0:r0 + 128, :], writes=[('hy_raw', k)])
                    ci = 2 * k + ch
                    rk = [('hy_raw', k), 'hy_raw_h0', 'hy_raw_h1', 'hy_cw']
                    fw.ts('dve', cv[k][:], raw[:, k, 0:Lq], cw[:, ci, 0:1], cw[:, ci, 3:4], ALU.mult, ALU.add, reads=rk, writes=[('hy_cv', k)])
                    fw.stt('dve', cv[k][:], raw[:, k, 1:Lq + 1], cw[:, ci, 1:2], cv[k][:], ALU.mult, ALU.add, reads=rk, writes=[('hy_cv', k)])
                    fw.stt('dve', cv[k][:], raw[:, k, 2:Lq + 2], cw[:, ci, 2:3], cv[k][:], ALU.mult, ALU.add, reads=rk, writes=[('hy_cv', k)])
                fw.tt('pool', u[:, m, :], cv[2][:], cv[0][:], ALU.mult, reads=[('hy_cv', 2), ('hy_cv', 0)], writes=[('hy_u', m)])
                fw.cp('act', x1c[:, m, :], cv[1][:], reads=[('hy_cv', 1)], writes=[('hy_x1', m)])
            uk = [('hy_u', m) for m in range(4)]
            for sc in range(nsc):
                pi = sc % 2
                for m in range(4):
                    fw.tr(P.ps[pi][:, m * 128:(m + 1) * 128], u[:, m, sc * 128:(sc + 1) * 128], S.ident[:], reads=uk + ['ident'], writes=[('ps', pi)])
                fw.cp('act' if sc % 2 == 0 else 'dve', utok[:, sc, :], P.ps[pi][:, :], reads=[('ps', pi)], writes=[('hy_ut', sc)])
            fw.barrier()
        with ExitStack() as es2:
            cf = es2.enter_context(P.sb('hy_cf', [128, 2, nsc, 128], BF16))
            sf = es2.enter_context(P.sb('hy_sf', [128, 2, nsc, 128], BF16))
            kk = es2.enter_context(P.sb('hy_kk', [128, 2, 2, 256], F32))
            tq = [es2.enter_context(P.sb('hy_t%d' % q, [128, 2, 256], F32)) for q in range(4)]
            for fc in range(nfc):
                fsz = 128 if fc < nfc - 1 else 1
                k = fc % 2
                fw.dma('sp', cf[:, k, :, :], I['cf_' + tag][fc], writes=[('hy_cf', k)])
                fw.dma('sp', sf[:, k, :, :], I['sf_' + tag][fc], writes=[('hy_sf', k)])
                for v in range(2):
                    fw.dma('sp', kk[:fsz, k, v, :], S.khat[tag][v, fc * 128:fc * 128 + fsz, :], writes=[('hy_kk', k, v)])
                pa, pb = 2 + 2 * k, 3 + 2 * k
                for sc in range(nsc):
                    fw.mm(P.ps[pa][:fsz, :], cf[:, k, sc, :fsz], utok[:, sc, :], start=(sc == 0), stop=(sc == nsc - 1),
                          reads=[('hy_cf', k)] if sc in (0, nsc - 1) else (), writes=[('ps', pa)])
                for sc in range(nsc):
                    fw.mm(P.ps[pb][:fsz, :], sf[:, k, sc, :fsz], utok[:, sc, :], start=(sc == 0), stop=(sc == nsc - 1),
                          reads=[('hy_sf', k)] if sc in (0, nsc - 1) else (), writes=[('ps', pb)])
                ure = P.ps[pa][:fsz, :].rearrange("p (b c) -> p b c", b=2)
                uim = P.ps[pb][:fsz, :].rearrange("p (b c) -> p b c", b=2)
                kre = kk[:fsz, k, 0, :].unsqueeze(1).to_broadcast([fsz, 2, 256])
                kim = kk[:fsz, k, 1, :].unsqueeze(1).to_broadcast([fsz, 2, 256])
                kr = [('hy_kk', k, 0), ('hy_kk', k, 1)]
                fw.tt('dve', tq[0][:fsz], ure, kre, ALU.mult, reads=[('ps', pa)] + kr, writes=[('hy_t', 0)])
                fw.tt('dve', tq[1][:fsz], uim, kim, ALU.mult, reads=[('ps', pb)] + kr, writes=[('hy_t', 1)])
                fw.tt('dve', tq[2][:fsz], ure, kim, ALU.mult, reads=[('ps', pa)] + kr, writes=[('hy_t', 2)])
                fw.tt('dve', tq[3][:fsz], uim, kre, ALU.mult, reads=[('ps', pb)] + kr, writes=[('hy_t', 3)])
                fw.tt('pool', wre[:fsz, fc, :].rearrange("p (b c) -> p b c", b=2), tq[0][:fsz], tq[1][:fsz], ALU.subtract,
                      reads=[('hy_t', 0), ('hy_t', 1)], writes=[('hy_wre', fc)])
                fw.tt('pool', wim[:fsz, fc, :].rearrange("p (b c) -> p b c", b=2), tq[2][:fsz], tq[3][:fsz], ALU.add,
                      reads=[('hy_t', 2), ('hy_t', 3)], writes=[('hy_wim', fc)])
            fw.barrier()
        with ExitStack() as es2:
            ci_t = es2.enter_context(P.sb('hy_ci', [128, nfc, TB], BF16))
            si_t = es2.enter_context(P.sb('hy_si', [128, nfc, TB], BF16))
            at = es2.enter_context(P.sb('hy_at', [128, 2, TB], F32))
            ob = es2.enter_context(P.sb('hy_o', [128, 2, TB], BF16))
            oi = 0
            for tb in range(ntb):
                fw.dma('sp', ci_t[:], I['ci_' + tag][tb], writes=['hy_ci'])
                fw.dma('sp', si_t[:], I['si_' + tag][tb], writes=['hy_si'])
                for m in range(4):
                    b, ch = m // 2, m % 2
                    pi = m % 2
                    pt = P.ps[pi][:, :TB]
                    for fc in range(nfc):
                        fsz = 128 if fc < nfc - 1 else 1
                        fw.mm(pt, wre[:fsz, fc, m * 128:(m + 1) * 128], ci_t[:fsz, fc, :], start=(fc == 0), stop=False,
                              reads=['hy_ci'] if fc in (0, nfc - 1) else (), writes=[('ps', pi)])
                    for fc in range(nfc - 1):
                        fw.mm(pt, wim[:, fc, m * 128:(m + 1) * 128], si_t[:, fc, :], start=False, stop=(fc == nfc - 2),
                              reads=['hy_si'] if fc in (0, nfc - 2) else (), writes=[('ps', pi)])
                    a = at[:, oi % 2, :]
                    o = ob[:, oi % 2, :]
                    ak, ok = ('hy_at', oi % 2), ('hy_o', oi % 2)
                    oi += 1
                    fw.ts('dve', a, pt, S.hyn[:, ch, li:li + 1], None, ALU.mult, reads=[('ps', pi)], writes=[ak])
                    fw.stt('dve', a, u[:, m, tb * TB:(tb + 1) * TB], hb[:, ch:ch + 1], a, ALU.mult, ALU.add, reads=['hy_hb'], writes=[ak])
                    fw.tt('pool', o, a, x1c[:, m, tb * TB:(tb + 1) * TB], ALU.mult, reads=[ak], writes=[ok])
                    fw.dma('sp', S.mix[(b, tag)][ch * 128:(ch + 1) * 128, tb * TB:(tb + 1) * TB], o, reads=[ok], writes=[('mixhy', b, ch, tb)])
            fw.barrier()


def phase_ssd(P, I, S, l, ctx_out):
    nc, fw = P.nc, P.fw
    ps = P.ps
    with ExitStack() as es:
        E = lambda n, sh, dt=F32: es.enter_context(P.sb(n, sh, dt))
        cw = E('sd_cw', [128, 8, 4]); dtb = E('sd_dtb', [16, 1]); arow = E('sd_arow', [128, 16]); dskb = E('sd_dsk', [128, 8])
        sng = E('sd_sng', [128, 4])
        Hs = [E('sd_H%d' % d, [128, 512]) for d in range(2)]
        xtok = E('sd_xtok', [128, 16, 512]); btok = E('sd_btok', [128, 16, 256]); cbt = E('sd_cbt', [128, 16, 2, 128])
        ct = E('sd_ct', [128, 2, L]); dttok = E('sd_dttok', [128, 16, 16]); ytok = E('sd_ytok', [128, 16, 512])
        raw = E('sd_raw', [128, 8, 514]); xa = E('sd_xa', [128, 8, 512]); dtT = E('sd_dtT', [16, L])
        dtA = E('sd_dtA', [128, 16]); ac16 = E('sd_acum', [128, 16]); acum = ac16[:, 0:8]; t8 = E('sd_t8', [128, 8]); dend = E('sd_dend', [128, 8])
        cdec = E('sd_cdec', [128, 8]); eac = E('sd_eac', [128, 8]); xdt = E('sd_xdt', [128, 8, 64]); xdte = E('sd_xdte', [128, 8, 64])
        segt8 = E('sd_seg8', [128, 8, 128]); dtAb8 = E('sd_dtAb8', [128, 8, 128]); mt8 = E('sd_mt8', [128, 8, 128])
        yo = E('sd_yo', [128, 8, 64]); tmpy = E('sd_tmpy', [128, 512]); hsc = E('sd_hsc', [128, 8, 64])
        zt = E('sd_zt', [128, 4, 128]); yg = E('sd_yg', [128, 4, 128]); sqg = E('sd_sqg', [128, 4, 128]); rs = E('sd_rs', [128, 2, 128])
        og = E('sd_og', [128, 4, 128], BF16)
        fw.dma('sp', cw[:], I['sscw'][l], writes=['sd_cw'])
        fw.dma('sp', dtb[:], I['dtb'][l], writes=['sd_dtb'])
        fw.dma('sp', arow[:], I['alog'][l].partition_broadcast(128), writes=['sd_arow'])
        fw.dma('sp', dskb[:], I['dskip'][l].partition_broadcast(128), writes=['sd_dsk'])
        fw.dma('sp', sng[:], I['sng'][l], writes=['sd_sng'])
        fw.act(arow[:], arow[:], AF.Exp, reads=['sd_arow'], writes=['sd_arow'])
        fw.ts('dve', arow[:], arow[:], -1.0, None, ALU.mult, reads=['sd_arow'], writes=['sd_arow'])
        tri = [S.uinc, S.linc]
        msk = [S.maskf, S.maskb]

        def stage_a(seq):
            Lq = seq_len(seq)
            T = min(512, Lq)
            plv = S.pl[seq][OFF_XBC:OFF_XBC + 1024, :].rearrange("(cc p) t -> p cc t", p=128)
            fw.dma('sp', dtT[:, :Lq], S.pl[seq][OFF_DT:OFF_DT + 16, :], writes=['sd_dtT'])
            fw.act(dtT[:, :Lq], dtT[:, :Lq], AF.Exp, bias=dtb[:, 0:1], reads=['sd_dtT', 'sd_dtb'], writes=['sd_dtT'])
            fw.ts('dve', dtT[:, :Lq], dtT[:, :Lq], 1.0, None, ALU.add, reads=['sd_dtT'], writes=['sd_dtT'])
            fw.act(dtT[:, :Lq], dtT[:, :Lq], AF.Ln, reads=['sd_dtT'], writes=['sd_dtT'])
            for tb in range(Lq // T):
                t0 = tb * T
                lo, hi = max(0, t0 - 1), min(Lq, t0 + T + 1)
                d0 = lo - (t0 - 1)
                if tb == 0:
                    fw.memset('pool', raw[:, :, 0:1], 0.0, writes=['sd_raw'])
                if tb == Lq // T - 1:
                    fw.memset('pool', raw[:, :, T + 1:T + 2], 0.0, writes=['sd_raw'])
                fw.dma('sp', raw[:, :, d0:d0 + hi - lo], plv[:, :, lo:hi], writes=['sd_raw'])
                for cc in range(8):
                    k = ('sd_xa', cc)
                    fw.ts('dve', xa[:, cc, :T], raw[:, cc, 0:T], cw[:, cc, 0:1], cw[:, cc, 3:4], ALU.mult, ALU.add, reads=['sd_raw', 'sd_cw'], writes=[k])
                    fw.stt('dve', xa[:, cc, :T], raw[:, cc, 1:T + 1], cw[:, cc, 1:2], xa[:, cc, :T], ALU.mult, ALU.add, reads=['sd_raw'], writes=[k])
                    fw.stt('dve', xa[:, cc, :T], raw[:, cc, 2:T + 2], cw[:, cc, 2:3], xa[:, cc, :T], ALU.mult, ALU.add, reads=['sd_raw'], writes=[k])
                    fw.act(xa[:, cc, :T], xa[:, cc, :T], AF.Silu, reads=[k], writes=[k])
                xk = [('sd_xa', cc) for cc in range(8)]
                fw.cp('pool', ct[:, :, t0:t0 + T], xa[:, 6:8, :T], reads=xk, writes=['sd_ct'])
                for j in range(T // 128):
                    c = tb * (T // 128) + j
                    sl = slice(j * 128, (j + 1) * 128)
                    for k in range(4):
                        fw.tr(ps[1][:, k * 128:(k + 1) * 128], xa[:, k, sl], S.ident[:], reads=xk, writes=[('ps', 1)])
                    fw.cp('act', xtok[:, c, :], ps[1][:, :], reads=[('ps', 1)], writes=[('sd_xtok', c)])
                    for g in range(2):
                        fw.tr(ps[2][:, g * 128:(g + 1) * 128], xa[:, 4 + g, sl], S.ident[:], reads=xk, writes=[('ps', 2)])
                        fw.mm(ps[2][:, 256 + g * 128:256 + (g + 1) * 128], xa[:, 4 + g, sl], xa[:, 6 + g, sl], reads=xk, writes=[('ps', 2)])
                    fw.cp('dve', btok[:, c, :], ps[2][:, 0:256], reads=[('ps', 2)], writes=[('sd_btok', c)])
                    fw.cp('dve', cbt[:, c, :, :], ps[2][:, 256:512].rearrange("p (g i) -> p g i", g=2), reads=[('ps', 2)], writes=[('sd_cbt', c)])
                    fw.tr(ps[3][:, 0:16], dtT[:16, t0 + j * 128:t0 + (j + 1) * 128], S.ident[:16, :16], reads=['sd_dtT'], writes=[('ps', 3)])
                    fw.cp('dve', dttok[:, c, :], ps[3][:, 0:16], reads=[('ps', 3)], writes=[('sd_dttok', c)])

        def scan(seq, d, with_output, final_out):
            Lq = seq_len(seq)
            ncq = Lq // 128
            H = Hs[d]
            hk = 'sd_H%d' % d
            d8 = slice(d * 8, (d + 1) * 8)
            order = range(ncq) if d == 0 else range(ncq - 1, -1, -1)
            for c in order:
                fw.tt('pool', dtA[:], dttok[:, c, :], arow[:], ALU.mult, reads=[('sd_dttok', c), 'sd_arow'], writes=['sd_dtA'])
                fw.mm(ps[0][:, 0:8], tri[d][:], dtA[:, d8], reads=['sd_dtA'], writes=[('ps', 0)])
                fw.mm(ps[0][:, 8:16], S.ones[:], dtA[:, d8], reads=['sd_dtA'], writes=[('ps', 0)])
                fw.cp('dve', ac16[:], ps[0][:, 0:16], reads=[('ps', 0)], writes=['sd_acum'])
                fw.tt('pool', t8[:], ac16[:, 8:16], acum, ALU.subtract, reads=['sd_acum'], writes=['sd_t8'])
                fw.act(dend[:], t8[:], AF.Exp, reads=['sd_t8'], writes=['sd_dend'])
                fw.act(cdec[:], ac16[:, 8:16], AF.Exp, reads=['sd_acum'], writes=['sd_cdec'])
                fw.tt('pool', xdt[:], xtok[:, c, :].rearrange("p (h q) -> p h q", h=8),
                      dttok[:, c, d8].unsqueeze(2).to_broadcast([128, 8, 64]), ALU.mult, reads=[('sd_xtok', c), ('sd_dttok', c)], writes=['sd_xdt'])
                if with_output:
                    fw.act(eac[:], acum,  AF.Exp, reads=['sd_acum'], writes=['sd_eac'])
                    fw.cp('dve', dtAb8[:], dtA[:, d8].unsqueeze(2).to_broadcast([128, 8, 128]), reads=['sd_dtA'], writes=['sd_dtAb'])
                    for hd in range(8):
                        pa = 1 + hd // 4
                        fw.mm(ps[pa][:, (hd % 4) * 128:(hd % 4 + 1) * 128], dtAb8[:, hd, :], tri[d][:], reads=['sd_dtAb'], writes=[('ps', pa)])
                    for hf in range(2):
                        fw.tt('dve', segt8[:, 4 * hf:4 * hf + 4, :], ps[1 + hf][:, :].rearrange("p (h i) -> p h i", h=4),
                              acum[:, 4 * hf:4 * hf + 4].unsqueeze(2).to_broadcast([128, 4, 128]), ALU.subtract,
                              reads=[('ps', 1 + hf), 'sd_acum'], writes=['sd_seg8'])
                    fw.tt('dve', segt8[:], segt8[:], msk[d][:, :].unsqueeze(1).to_broadcast([128, 8, 128]), ALU.min,
                          reads=['sd_seg8'], writes=['sd_seg8'])
                    fw.act(segt8[:], segt8[:], AF.Exp, reads=['sd_seg8'], writes=['sd_seg8'])
                    fw.tt('dve', mt8[:].rearrange("p (g h) i -> p g h i", g=2), segt8[:].rearrange("p (g h) i -> p g h i", g=2),
                          cbt[:, c, :, :].unsqueeze(2).to_broadcast([128, 2, 4, 128]), ALU.mult, reads=['sd_seg8', ('sd_cbt', c)], writes=['sd_mt8'])
                    for hd in range(8):
                        fw.mm(ps[3][:, hd * 64:(hd + 1) * 64], mt8[:, hd, :], xdt[:, hd, :], reads=['sd_mt8', 'sd_xdt'], writes=[('ps', 3)])
                    for g in range(2):
                        fw.mm(ps[4][:, g * 256:(g + 1) * 256], ct[:, g, c * 128:(c + 1) * 128], H[:, g * 256:(g + 1) * 256],
                              reads=['sd_ct', hk], writes=[('ps', 4)])
                    fw.tt('dve', yo[:], ps[4][:, :].rearrange("p (h q) -> p h q", h=8), eac[:, :].unsqueeze(2).to_broadcast([128, 8, 64]),
                          ALU.mult, reads=[('ps', 4), 'sd_eac'], writes=['sd_yo'])
                    yflat = yo[:].rearrange("p h q -> p (h q)")
                    if d == 0:
                        fw.tt('dve', ytok[:, c, :], ps[3][:, :], yflat, ALU.add, reads=[('ps', 3), 'sd_yo'], writes=[('sd_ytok', c)])
                        fw.tt('pool', tmpy[:].rearrange("p (h q) -> p h q", h=8), xtok[:, c, :].rearrange("p (h q) -> p h q", h=8),
                              dskb[:, :].unsqueeze(2).to_broadcast([128, 8, 64]), ALU.mult, reads=[('sd_xtok', c), 'sd_dsk'], writes=['sd_tmpy'])
                        fw.tt('pool', ytok[:, c, :], ytok[:, c, :], tmpy[:], ALU.add, reads=['sd_tmpy'], writes=[('sd_ytok', c)])
                    else:
                        fw.tt('dve', tmpy[:], ps[3][:, :], yflat, ALU.add, reads=[('ps', 3), 'sd_yo'], writes=['sd_tmpy'])
                        fw.tt('pool', ytok[:, c, :], ytok[:, c, :], tmpy[:], ALU.add, reads=['sd_tmpy'], writes=[('sd_ytok', c)])
                fw.tt('pool', xdte[:], xdt[:], dend[:, :].unsqueeze(2).to_broadcast([128, 8, 64]), ALU.mult, reads=['sd_xdt', 'sd_dend'], writes=['sd_xdte'])
                for g in range(2):
                    fw.mm(ps[5][:, g * 256:(g + 1) * 256], btok[:, c, g * 128:(g + 1) * 128],
                          xdte[:, g * 4:(g + 1) * 4, :].rearrange("p h q -> p (h q)"), reads=[('sd_btok', c), 'sd_xdte'], writes=[('ps', 5)])
                fw.tt('pool', hsc[:], H[:].rearrange("p (h q) -> p h q", h=8), cdec[:, :].unsqueeze(2).to_broadcast([128, 8, 64]), ALU.mult,
                      reads=[hk, 'sd_cdec'], writes=['sd_hsc'])
                fw.tt('dve', H[:], hsc[:].rearrange("p h q -> p (h q)"), ps[5][:, :], ALU.add, reads=['sd_hsc', ('ps', 5)], writes=[hk])
                if final_out:
                    tk = slice(c * 128, (c + 1) * 128)
                    for k in range(4):
                        fw.tr(ps[6][:, k * 128:(k + 1) * 128], ytok[:, c, k * 128:(k + 1) * 128], S.ident[:], reads=[('sd_ytok', c)], writes=[('ps', 6)])
                    fw.dma('sp', zt[:], S.pl[seq][OFF_Z:OFF_Z + 512, tk].rearrange("(k p) t -> p k t", p=128), writes=['sd_zt'])
                    fw.act(zt[:], zt[:], AF.Silu, reads=['sd_zt'], writes=['sd_zt'])
                    fw.tt('dve', yg[:], ps[6][:, :].rearrange("p (k t) -> p k t", k=4), zt[:], ALU.mult, reads=[('ps', 6), 'sd_zt'], writes=['sd_yg'])
                    fw.act(sqg[:], yg[:], AF.Square, reads=['sd_yg'], writes=['sd_sqg'])
                    for g in range(2):
                        fw.mm(ps[7][:, g * 128:(g + 1) * 128], S.ones[:], sqg[:, 2 * g, :], start=True, stop=False, reads=['sd_sqg'], writes=[('ps', 7)])
                        fw.mm(ps[7][:, g * 128:(g + 1) * 128], S.ones[:], sqg[:, 2 * g + 1, :], start=False, stop=True, reads=['sd_sqg'], writes=[('ps', 7)])
                    fw.act(rs[:], ps[7][:, 0:256].rearrange("p (g t) -> p g t", g=2), AF.Sqrt, bias=S.epsc[:, 0:1], scale=1.0 / 256.0,
                           reads=[('ps', 7)], writes=['sd_rs'])
                    fw.op('dve', lambda e: e.reciprocal(rs[:], rs[:]), reads=['sd_rs'], writes=['sd_rs'])
                    for k in range(4):
                        fw.stt('dve', og[:, k, :], yg[:, k, :], sng[:, k:k + 1], rs[:, k // 2, :], ALU.mult, ALU.mult,
                               reads=['sd_yg', 'sd_rs', 'sd_sng'], writes=['sd_og'])
                    fw.dma('sp', S.mix[seq][256:768, tk].rearrange("(k p) t -> p k t", p=128), og[:], reads=['sd_og'], writes=[('mixssd', seq, c)])

        for b in range(2):
            for d in range(2):
                fw.memset('pool', Hs[d][:], 0.0, writes=['sd_H%d' % d])
            stage_a((b, 'c'))
            if SSD_MODE != 'a':
                scan((b, 'c'), 0, ctx_out, False)
                scan((b, 'c'), 1, ctx_out, ctx_out)
            stage_a((b, 'l'))
            if SSD_MODE != 'a':
                scan((b, 'l'), 0, True, False)
                scan((b, 'l'), 1, True, True)
        fw.barrier()


def phase_peer(P, I, S, l, ctx_out):
    nc, fw = P.nc, P.fw
    ps = P.ps
    T = 256
    NB = 4
    with ExitStack() as es2:
        cs = [es2.enter_context(P.sb('pe_cs%d' % i, [128, 2048], F32)) for i in range(4)]
        cbs = [es2.enter_context(P.sb('pe_cb%d' % i, [128, 2048], BF16)) for i in range(4)]
        uv = I['uT'][l].rearrange("(a p) e -> p a e", p=128)
        n = 0
        for blk in range(64):
            k = n % 4
            n += 1
            fw.dma('sp', cs[k][:].rearrange("p (a e) -> p a e", a=8), uv[:, :, blk * 256:(blk + 1) * 256], writes=['pe_cs%d' % k])
            fw.cp('dve' if k % 2 == 0 else 'pool', cbs[k][:], cs[k][:], reads=['pe_cs%d' % k], writes=['pe_cb%d' % k])
            fw.dma('act', S.ubf[blk].rearrange("p a e -> p (a e)"), cbs[k][:], reads=['pe_cb%d' % k], writes=[('ubf', blk)])
        for blk in range(64):
            k = n % 4
            n += 1
            fw.dma('sp', cs[k][:].rearrange("p (ii d) -> p ii d", ii=2),
                   I['v'][l][blk * 256:(blk + 1) * 256, :].rearrange("(ii j) d -> j ii d", j=128), writes=['pe_cs%d' % k])
            fw.cp('dve' if k % 2 == 0 else 'pool', cbs[k][:], cs[k][:], reads=['pe_cs%d' % k], writes=['pe_cb%d' % k])
            fw.dma('act', S.vbf[blk].rearrange("p ii d -> p (ii d)"), cbs[k][:], reads=['pe_cb%d' % k], writes=[('vbf', blk)])
        fw.barrier()
    with ExitStack() as es:
        E = lambda n, sh, dt=F32: es.enter_context(P.sb(n, sh, dt))
        wq = E('pe_wq', [128, 8, 2048], BF16)
        st = [E('pe_s%d' % i, [128, 8, 256]) for i in range(2)]
        k1 = E('pe_k1', [128, 128], BF16); k2 = E('pe_k2', [128, 128], BF16)
        gbuf = E('pe_g', [128, 128, T], BF16)
        ub = E('pe_ub', [128, 2, 8, 256], BF16); vb = E('pe_vb', [128, 2, 2, 1024], BF16)
        h2s = [E('pe_h2%d' % i, [128, 8, T], BF16) for i in range(2)]; sq = E('pe_sq', [128, 8, T]); rstd = E('pe_rstd', [128, T])
        qT = E('pe_qT', [128, 16, T], BF16); s12s = [E('pe_s12%d' % i, [128, 16, 128]) for i in range(2)]; v12s = [E('pe_v12%d' % i, [128, 16, 16]) for i in range(2)]
        wk = E('pe_wk', [128, 4, 128]); wk2 = E('pe_wk2', [128, 4, 256]); t16s = [E('pe_t16%d' % i, [128, 8, 16]) for i in range(2)]
        e16 = E('pe_e16', [128, 8, 16]); zz = E('pe_z', [128, 8]); mz = E('pe_mz', [128, 8])
        thr = E('pe_thr', [128, 8, 16]); bia = E('pe_bia', [128, 8, 16]); pc = E('pe_pc', [128, 3, T])
        Et = [E('pe_E%d' % i, [128, 256]) for i in range(4)]
        Wt = [E('pe_W%d' % i, [128, 128], BF16) for i in range(8)]
        Pt = [E('pe_P%d' % i, [128, 128], BF16) for i in range(8)]
        qr1 = E('pe_qr1', [128, 2, 4, 128], BF16)
        qr2 = E('pe_qr2', [128, 2, 4, 128], BF16)
        gst = [E('pe_gs%d' % i, [128, T]) for i in range(2)]
        At = [E('pe_A%d' % i, [128, T], BF16) for i in range(2)]
        xo = sq
        cand = sq
        wv = I['wq'][l].rearrange("(dc p) n -> p dc n", p=128)
        n = 0
        for blk in range(8):
            k = n % 2
            n += 1
            fw.dma('sp', st[k][:], wv[:, :, blk * 256:(blk + 1) * 256], writes=['pe_s%d' % k])
            fw.cp('dve' if k == 0 else 'pool', wq[:, :, blk * 256:(blk + 1) * 256], st[k][:], reads=['pe_s%d' % k], writes=[('pe_wq', blk)])
        for (kt, nm, key) in ((k1, 'k1T', 'pe_k1'), (k2, 'k2T', 'pe_k2')):
            k = n % 2
            n += 1
            fw.dma('sp', st[k][:, 0, 0:128], I[nm][l], writes=['pe_s%d' % k])
            fw.cp('dve', kt[:], st[k][:, 0, 0:128], reads=['pe_s%d' % k], writes=[key])
        fw.barrier()

        def load_tables(blk):
            bb = blk % 2
            fw.dma('sp', ub[:, bb, :, :], S.ubf[blk], writes=[('pe_ub', bb)])
            fw.dma('sp', vb[:, bb, :, :], S.vbf[blk], writes=[('pe_vb', bb)])

        qk = [('pe_qT', qc) for qc in range(16)]
        sqk = [('nm_sq', dc) for dc in range(8)]

        def v12k_of(tt):
            return [('pe_v12', tt, i) for i in range(16)] + [('pe_v12b', tt, i) for i in range(16)]

        def t16k_of(tt):
            return [('pe_t16', tt, i) for i in range(8)] + [('pe_t16b', tt, i) for i in range(8)]

        def part_A(seq, grp, gi):
            col = seq_col(seq)
            rv = S.res[seq].rearrange("(dc p) t -> p dc t", p=128)
            g0 = grp * T
            xt = st[gi % 2]
            xk = 'pe_s%d' % (gi % 2)
            h2 = h2s[gi % 2]
            fw.dma('sp', xt[:], rv[:, :, g0:g0 + T], writes=[xk])
            hk = normmod(P, S, xt, xk, T, S.G2[:, :, col], S.modT[:, 24:32, col], h2, 'pe_h2_%d' % (gi % 2), sq, rstd, ('ps', 4), ps[4])
            for qc in range(16):
                pi = 4 + qc % 4
                for dc in range(8):
                    fw.mm(ps[pi][:, :T], wq[:, dc, qc * 128:(qc + 1) * 128], h2[:, dc, :], start=(dc == 0), stop=(dc == 7),
                          reads=hk if dc in (0, 7) else (), writes=[('ps', pi)])
                fw.cp('act' if qc % 2 == 0 else 'dve', qT[:, qc, :], ps[pi][:, :T], reads=[('ps', pi)], writes=[('pe_qT', qc)])
            for tt in range(2):
                tsl = slice(tt * 128, (tt + 1) * 128)
                s12 = s12s[tt]
                for h in range(8):
                    fw.mm(ps[4 + h // 4][:, (h % 4) * 128:(h % 4 + 1) * 128], qT[:, 2 * h, tsl], k1[:], reads=qk + ['pe_k1'], writes=[('ps', 4 + h // 4)])
                    fw.mm(ps[6 + h // 4][:, (h % 4) * 128:(h % 4 + 1) * 128], qT[:, 2 * h + 1, tsl], k2[:], reads=qk + ['pe_k2'], writes=[('ps', 6 + h // 4)])
                for q4 in range(4):
                    fw.cp('dve' if q4 % 2 == 0 else 'act', s12[:, q4 * 4:(q4 + 1) * 4, :], ps[4 + q4][:, :].rearrange("p (h i) -> p h i", h=4),
                          reads=[('ps', 4 + q4)], writes=[('pe_s12', tt, q4)])

        def chains_gen():
            for tt in range(2):
                s12 = s12s[tt]
                v12 = v12s[tt]
                t16 = t16s[tt]
                sk = [('pe_s12', tt, q4) for q4 in range(4)]
                for base in range(0, 16, 4):
                    for idx in range(base, base + 4):
                        fw.op('dve', lambda e, idx=idx: e.max(out=v12[:, idx, 0:8], in_=s12[:, idx, :]), reads=sk, writes=[('pe_v12', tt, idx)])
                    yield
                    for idx in range(base, base + 4):
                        fw.op('dve', lambda e, idx=idx: e.match_replace(out=wk[:, idx % 4, :], in_to_replace=v12[:, idx, 0:8], in_values=s12[:, idx, :], imm_value=-1e30),
                              reads=[('pe_v12', tt, idx)], writes=[('pe_wk', idx % 4)])
                    yield
                    for idx in range(base, base + 4):
                        fw.op('dve', lambda e, idx=idx: e.max(out=v12[:, idx, 8:16], in_=wk[:, idx % 4, :]), reads=[('pe_wk', idx % 4)], writes=[('pe_v12b', tt, idx)])
                    yield
                v12k = v12k_of(tt)
                fw.tt('dve', cand[:].rearrange("p h (a b) -> p h a b", a=16), v12[:, 0:8, :].unsqueeze(3).to_broadcast([128, 8, 16, 16]),
                      v12[:, 8:16, :].unsqueeze(2).to_broadcast([128, 8, 16, 16]), ALU.add, reads=v12k, writes=sqk)
                for base in range(0, 8, 4):
                    for h in range(base, base + 4):
                        fw.op('dve', lambda e, h=h: e.max(out=t16[:, h, 0:8], in_=cand[:, h, :]), reads=sqk, writes=[('pe_t16', tt, h)])
                    for h in range(base, base + 4):
                        fw.op('dve', lambda e, h=h: e.match_replace(out=wk2[:, h % 4, :], in_to_replace=t16[:, h, 0:8], in_values=cand[:, h, :], imm_value=-1e30),
                              reads=[('pe_t16', tt, h)] + sqk, writes=[('pe_wk2', h % 4)])
                    for h in range(base, base + 4):
                        fw.op('dve', lambda e, h=h: e.max(out=t16[:, h, 8:16], in_=wk2[:, h % 4, :]), reads=[('pe_wk2', h % 4)], writes=[('pe_t16b', tt, h)])
                    yield

        def part_B():
            for tt in range(2):
                tsl = slice(tt * 128, (tt + 1) * 128)
                v12 = v12s[tt]
                t16 = t16s[tt]
                v12k = v12k_of(tt)
                t16k = t16k_of(tt)
                fw.tt('dve', e16[:], t16[:], t16[:, :, 0:1].to_broadcast([128, 8, 16]), ALU.subtract, reads=t16k, writes=['pe_e16'])
                fw.act(e16[:], e16[:], AF.Exp, reads=['pe_e16'], writes=['pe_e16'])
                fw.op('dve', lambda e: e.reduce_sum(out=zz[:], in_=e16[:], axis=AX.X), reads=['pe_e16'], writes=['pe_z'])
                fw.act(zz[:], zz[:], AF.Ln, reads=['pe_z'], writes=['pe_z'])
                fw.tt('dve', mz[:], t16[:, :, 0], zz[:], ALU.add, reads=t16k + ['pe_z'], writes=['pe_mz'])
                fw.tt('dve', thr[:], t16[:, :, 15:16].to_broadcast([128, 8, 16]), v12[:, 0:8, :], ALU.subtract, reads=t16k + v12k, writes=['pe_thr'])
                fw.tt('dve', bia[:], v12[:, 0:8, :], mz[:, :].unsqueeze(2).to_broadcast([128, 8, 16]), ALU.subtract, reads=['pe_mz'] + v12k, writes=['pe_bia'])
                fw.act(bia[:], bia[:], AF.Exp, reads=['pe_bia'], writes=['pe_bia'])
                srcs = [(v12[:, 0:8, :], v12k), (thr[:], ['pe_thr']), (bia[:], ['pe_bia'])]
                for slot, (sap, skey) in enumerate(srcs):
                    fw.tr(ps[6][:, slot * 128:(slot + 1) * 128], sap.rearrange("p h a -> p (h a)"), S.ident[:], reads=skey, writes=[('ps', 6)])
                fw.cp('dve', pc[:, :, tsl], ps[6][:, 0:384].rearrange("p (s t) -> p s t", s=3), reads=[('ps', 6)], writes=['pe_pc'])

        groups = [(seq, grp) for seq in seqs_of(ctx_out) for grp in range(seq_len(seq) // T)]
        part_A(groups[0][0], groups[0][1], 0)
        for _ in chains_gen():
            pass
        part_B()
        for gi, (seq, grp) in enumerate(groups):
            if True:
                col = seq_col(seq)
                rv = S.res[seq].rearrange("(dc p) t -> p dc t", p=128)
                g0 = grp * T
                xt = st[gi % 2]
                xk = 'pe_s%d' % (gi % 2)
                h2 = h2s[gi % 2]
                has_next = gi + 1 < len(groups)
                load_tables(0)
                load_tables(1)
                NQ = T // 4

                def quad_F(u):
                    qb = u % 2
                    t0 = 4 * u
                    for half, qr, kk_ in ((0, qr1, k1), (1, qr2, k2)):
                        src_ap = qT[:, half:16:2, t0:t0 + 4].rearrange("p h t -> p t h").unsqueeze(3).to_broadcast([128, 4, 8, 16])
                        fw.cp('pool' if half == 0 else 'dve', qr[:, qb, :, :].rearrange("p t (h a) -> p t h a", h=8), src_ap,
                              reads=qk if u < 2 else (), writes=[('pe_qr%d' % half, qb)])
                    for k in range(4):
                        t = t0 + k
                        xb = (t // 2) % 4
                        c1 = (t % 2) * 128
                        fw.mm(ps[xb][:, c1:c1 + 128], qr1[:, qb, k, :], k1[:], reads=[('pe_qr0', qb)], writes=[('ps', xb)])
                        fw.mm(ps[xb][:, 256 + c1:256 + c1 + 128], qr2[:, qb, k, :], k2[:], reads=[('pe_qr1', qb)], writes=[('ps', xb)])

                def quad_B(u):
                    t0 = 4 * u
                    gb = 4 + u % 2
                    for pr in range(2):
                        tp = t0 + 2 * pr
                        xb = (tp // 2) % 4
                        eq = (tp // 2) % 4
                        ek = ('pe_E', eq)
                        fw.act(Et[eq][:], ps[xb][:, 256:512], AF.Exp, reads=[('ps', xb)], writes=[ek])
                        for kk in range(2):
                            t = tp + kk
                            k = 2 * pr + kk
                            c1 = kk * 128
                            q = t % 8
                            wkk, pk = ('pe_W', q), ('pe_P', q)
                            fw.stt('dve', Wt[q][:], ps[xb][:, 256 + c1:256 + c1 + 128], pc[:, 1, t:t + 1], Et[eq][:, c1:c1 + 128], ALU.is_ge, ALU.mult,
                                   reads=[('ps', xb), ek, 'pe_pc'], writes=[wkk])
                            fw.ts('dve', Pt[q][:], ps[xb][:, c1:c1 + 128], pc[:, 0, t:t + 1], pc[:, 2, t:t + 1], ALU.is_equal, ALU.mult,
                                  reads=[('ps', xb), ek, 'pe_pc'], writes=[pk])
                            fw.mm(ps[gb][:, k * 128:(k + 1) * 128], Wt[q][:], Pt[q][:], reads=[wkk, pk], writes=[('ps', gb)])

                def quad_C(u):
                    gb = 4 + u % 2
                    fw.cp('act', gbuf[:, :, 4 * u:4 * u + 4].rearrange("p i t -> p t i"),
                          ps[gb][:, :].rearrange("p (t i) -> p t i", t=4), reads=[('ps', gb)], writes=[('pe_g', u % 2)])

                for u in range(NQ + 2):
                    if u < NQ:
                        quad_F(u)
                    if 1 <= u <= NQ:
                        quad_B(u - 1)
                    if u >= 2:
                        quad_C(u - 2)
                gk = [('pe_g', q) for q in range(2)]
                cg = iter(())
                if has_next:
                    part_A(groups[gi + 1][0], groups[gi + 1][1], gi + 1)
                    cg = chains_gen()

                def dense_S(i):
                    bb, ii = (i // 2) % 2, i % 2
                    pi = 4 + i % 2
                    for dc in range(8):
                        fw.mm(ps[pi][:, :T], ub[:, bb, dc, ii * 128:(ii + 1) * 128], h2[:, dc, :], start=(dc == 0), stop=(dc == 7),
                              reads=[('pe_ub', bb)] if dc in (0, 7) else (), writes=[('ps', pi)])

                def dense_O(i):
                    bb, ii = (i // 2) % 2, i % 2
                    pi = 4 + i % 2
                    fw.act(gst[i % 2][:], ps[pi][:, :T], AF.Gelu_apprx_tanh, reads=[('ps', pi)], writes=[('pe_gs', i % 2)])
                    fw.tt('pool' if i % 2 == 0 else 'dve', At[i % 2][:], gst[i % 2][:], gbuf[:, i, :], ALU.mult,
                          reads=[('pe_gs', i % 2)] + (gk if i < 2 else []), writes=[('pe_A', i % 2)])
                    for dch in range(8):
                        bk, rg = dch // 2, (dch % 2) * 256
                        fw.mm(ps[bk][:, rg:rg + T], vb[:, bb, ii, dch * 128:(dch + 1) * 128], At[i % 2][:],
                              start=(i == 0 and dch % 2 == 0), stop=(i == 127),
                              reads=[('pe_A', i % 2), ('pe_vb', bb)] if dch in (0, 7) else (), writes=[('ps', bk)])

                dense_S(0)
                for i in range(128):
                    if i + 1 < 128:
                        dense_S(i + 1)
                    dense_O(i)
                    next(cg, None)
                    if i % 2 == 1 and (i + 1) // 2 + 1 < 64:
                        load_tables((i + 1) // 2 + 1)
                for _ in cg:
                    pass
                for dch in range(8):
                    bk, rg = dch // 2, (dch % 2) * 256
                    fw.stt('dve', xo[:, dch, :], ps[bk][:, rg:rg + T], S.modT[:, 40 + dch, col:col + 1], xt[:, dch, :], ALU.mult, ALU.add,
                           reads=[('ps', bk), xk], writes=[('pe_xo', dch), ('nm_sq', dch)])
                fw.dma('sp', rv[:, :, g0:g0 + T], xo[:], reads=[('pe_xo', d8) for d8 in range(8)] + [('nm_sq', d8) for d8 in range(8)],
                       writes=[('resp', seq, grp)])
                if has_next:
                    part_B()
        fw.barrier()
```
